# Optimizing a Trainium2 kernel written in Bass

```python
import math
import jax, jax.numpy as jnp
from jax import lax
import numpy as np

D_MODEL = 1024
BATCH = 8
SEQ = 8192
DEPTH = 1
DEC_BATCH = 8
DEC_SEQ = 2048
PAST_LEN = 128

GRID_W = 64
GDN_HEADS = 4
GDN_DK = 128
GDN_DV = 128
CONV_K = 5
CHUNK = 64
ATT_HEADS = 4
ATT_KV_HEADS = 2
ATT_HD = 128
ATT_GROUP = ATT_HEADS // ATT_KV_HEADS
Q_BLOCK = 128
ROPE_THETA = 10000.0
D_FF = 4 * D_MODEL
EPS = 1e-6

GDN_QK = GDN_HEADS * GDN_DK
GDN_VW = GDN_HEADS * GDN_DV
GDN_CONV_W = 2 * GDN_QK + GDN_VW
ATT_Q = ATT_HEADS * ATT_HD
ATT_KV = ATT_KV_HEADS * ATT_HD
MIX_WIDTH = GDN_VW + ATT_Q
IN_SIZES = (GDN_CONV_W, GDN_VW, GDN_HEADS, GDN_HEADS, GDN_HEADS, GDN_HEADS, ATT_Q, ATT_KV, ATT_KV)
IN_WIDTH = sum(IN_SIZES)

kernel_name = "hybrid_bidir_gdn_axial_gqa_encoder"


def _rmsnorm(x, w):
    xf = x.astype(jnp.float32)
    y = xf * lax.rsqrt(jnp.mean(xf * xf, axis=-1, keepdims=True) + EPS)
    return (y * w.astype(jnp.float32)).astype(x.dtype)


def _l2norm(x):
    return x * lax.rsqrt(jnp.sum(x * x, axis=-1, keepdims=True) + EPS)


def _split_cols(p):
    outs, off = [], 0
    for s in IN_SIZES:
        outs.append(p[..., off:off + s])
        off += s
    return outs


def _gated_delta_chunked(q, k, v, g, beta):
    B_, T, H, DK = q.shape
    DV = v.shape[-1]
    N = T // CHUNK

    def to_chunks(a):
        a = a.reshape((B_, N, CHUNK, H) + a.shape[3:])
        return jnp.moveaxis(a, 3, 1)

    qc, kc, vc = to_chunks(q), to_chunks(k), to_chunks(v)
    gc = jnp.cumsum(to_chunks(g), axis=-1)
    bc = to_chunks(beta)

    idx = jnp.arange(CHUNK)
    lower_incl = idx[:, None] >= idx[None, :]
    strict = idx[:, None] > idx[None, :]
    diff = gc[..., :, None] - gc[..., None, :]
    decay = jnp.where(lower_incl, jnp.exp(jnp.where(lower_incl, diff, 0.0)), 0.0)

    k_beta = kc * bc[..., None]
    v_beta = vc * bc[..., None]
    kk = jnp.einsum('bhnid,bhnjd->bhnij', k_beta, kc) * decay
    tri = jnp.eye(CHUNK, dtype=q.dtype) + jnp.where(strict, kk, 0.0)
    rhs = jnp.concatenate([v_beta, k_beta * jnp.exp(gc)[..., None]], axis=-1)
    sol = lax.linalg.triangular_solve(tri, rhs, left_side=True, lower=True, unit_diagonal=True)
    u = sol[..., :DV]
    w = sol[..., DV:]

    qk = jnp.einsum('bhnid,bhnjd->bhnij', qc, kc) * decay
    g_last = gc[..., -1]
    k_to_end = kc * jnp.exp(g_last[..., None] - gc)[..., None]
    q_dec = qc * jnp.exp(gc)[..., None]

    xs = tuple(jnp.moveaxis(a, 2, 0) for a in (q_dec, qk, u, w, k_to_end, g_last))

    def step(S, inp):
        q_d, a_qk, u_c, w_c, k_e, gl = inp
        v_new = u_c - jnp.einsum('bhcd,bhdv->bhcv', w_c, S)
        o = jnp.einsum('bhcd,bhdv->bhcv', q_d, S) + jnp.einsum('bhij,bhjv->bhiv', a_qk, v_new)
        S = S * jnp.exp(gl)[..., None, None] + jnp.einsum('bhcd,bhcv->bhdv', k_e, v_new)
        return S, o

    S0 = jnp.zeros((B_, H, DK, DV), q.dtype)
    _, o = lax.scan(step, S0, xs)
    o = jnp.moveaxis(o, 0, 2).reshape(B_, H, T, DV)
    return jnp.moveaxis(o, 1, 2)


def _gated_deltanet(qkv, z, b_f, b_b, a_f, a_b, conv_w, A_log_f, A_log_b, dt_bias_f, dt_bias_b, gdn_norm_w):
    B_, T, _ = qkv.shape
    qkv = lax.conv_general_dilated(qkv, conv_w[:, None, :].astype(qkv.dtype), window_strides=(1,),
                                   padding=[(CONV_K // 2, CONV_K // 2)],
                                   dimension_numbers=('NWC', 'WIO', 'NWC'),
                                   feature_group_count=GDN_CONV_W)
    qkv = jax.nn.silu(qkv).astype(jnp.float32)
    q = qkv[..., :GDN_QK].reshape(B_, T, GDN_HEADS, GDN_DK)
    k = qkv[..., GDN_QK:2 * GDN_QK].reshape(B_, T, GDN_HEADS, GDN_DK)
    v = qkv[..., 2 * GDN_QK:].reshape(B_, T, GDN_HEADS, GDN_DV)
    q = _l2norm(q) * (GDN_DK ** -0.5)
    k = _l2norm(k)

    def gates(b, a, A_log, dt_bias):
        beta = jax.nn.sigmoid(b.astype(jnp.float32))
        g = -jnp.exp(A_log.astype(jnp.float32)) * jax.nn.softplus(a.astype(jnp.float32) + dt_bias.astype(jnp.float32))
        return g, beta

    g_f, beta_f = gates(b_f, a_f, A_log_f, dt_bias_f)
    g_b, beta_b = gates(b_b, a_b, A_log_b, dt_bias_b)
    o_f = _gated_delta_chunked(q, k, v, g_f, beta_f)
    flip = lambda a: jnp.flip(a, axis=1)
    o_b = flip(_gated_delta_chunked(flip(q), flip(k), flip(v), flip(g_b), flip(beta_b)))
    o = o_f + o_b
    o = o * lax.rsqrt(jnp.mean(o * o, axis=-1, keepdims=True) + EPS) * gdn_norm_w.astype(jnp.float32)
    zz = z.astype(jnp.float32).reshape(B_, T, GDN_HEADS, GDN_DV)
    o = o * jax.nn.silu(zz)
    return o.reshape(B_, T, GDN_VW).astype(z.dtype)


def _rope_half(x, ang):
    m = ang.shape[-1]
    c = jnp.cos(ang)[:, None, :]
    s = jnp.sin(ang)[:, None, :]
    x1, x2 = x[..., :m], x[..., m:]
    return jnp.concatenate([x1 * c - x2 * s, x1 * s + x2 * c], axis=-1)


def _axial_rope(x, row_ang, col_ang):
    half = ATT_HD // 2
    xf = x.astype(jnp.float32)
    return jnp.concatenate([_rope_half(xf[..., :half], row_ang), _rope_half(xf[..., half:], col_ang)], axis=-1)


def _axial_gqa(qf, kf, vf, q_norm_w, k_norm_w):
    B_, T, _ = qf.shape
    rows = T // GRID_W
    row = jnp.repeat(jnp.arange(rows, dtype=jnp.float32), GRID_W)
    col = jnp.tile(jnp.arange(GRID_W, dtype=jnp.float32), rows)
    n_freq = ATT_HD // 4
    inv_freq = ROPE_THETA ** (-jnp.arange(n_freq, dtype=jnp.float32) / n_freq)
    row_ang = row[:, None] * inv_freq[None, :]
    col_ang = col[:, None] * inv_freq[None, :]

    q = _rmsnorm(qf.reshape(B_, T, ATT_HEADS, ATT_HD), q_norm_w)
    k = _rmsnorm(kf.reshape(B_, T, ATT_KV_HEADS, ATT_HD), k_norm_w)
    v = vf.reshape(B_, T, ATT_KV_HEADS, ATT_HD)
    q = _axial_rope(q, row_ang, col_ang).astype(vf.dtype)
    k = _axial_rope(k, row_ang, col_ang).astype(vf.dtype)
    scale = ATT_HD ** -0.5

    nb = T // Q_BLOCK
    qb = q.reshape(B_, nb, Q_BLOCK, ATT_KV_HEADS, ATT_GROUP, ATT_HD)
    qb = jnp.moveaxis(qb, 1, 0)

    def block(qi):
        s = jnp.einsum('bqkgd,btkd->bkgqt', qi, k).astype(jnp.float32) * scale
        p = jax.nn.softmax(s, axis=-1)
        return jnp.einsum('bkgqt,btkd->bqkgd', p.astype(v.dtype), v)

    o = lax.map(block, qb)
    return jnp.moveaxis(o, 0, 1).reshape(B_, T, ATT_Q)


def _layer(x, norm_mix_pre, w_in, conv_w, A_log_f, A_log_b, dt_bias_f, dt_bias_b, gdn_norm_w,
           q_norm_w, k_norm_w, w_out, norm_mix_post, norm_mlp_pre, w_up, w_down, norm_mlp_post):
    h = _rmsnorm(x, norm_mix_pre)
    p = h @ w_in
    qkv_a, z_a, b_f, b_b, a_f, a_b, q_b, k_b, v_b = _split_cols(p)
    o_a = _gated_deltanet(qkv_a, z_a, b_f, b_b, a_f, a_b, conv_w, A_log_f, A_log_b,
                          dt_bias_f, dt_bias_b, gdn_norm_w)
    o_b = _axial_gqa(q_b, k_b, v_b, q_norm_w, k_norm_w)
    mix = jnp.concatenate([o_a, o_b], axis=-1) @ w_out
    x = x + _rmsnorm(mix, norm_mix_post)
    hm = _rmsnorm(x, norm_mlp_pre)
    f = jnp.square(jax.nn.relu(hm @ w_up)) @ w_down
    return x + _rmsnorm(f, norm_mlp_post)


def setup_inputs(seed: int = 0) -> dict:
    key = jax.random.key(seed)
    ks = jax.random.split(key, 20)
    f32 = jnp.float32

    def gain(k, shape):
        return 1.0 + 0.1 * jax.random.normal(k, shape, f32)

    dt = jnp.exp(jax.random.uniform(ks[6], (DEPTH, 2, GDN_HEADS), f32, math.log(0.001), math.log(0.1)))
    dt_bias = dt + jnp.log(-jnp.expm1(-dt))
    A_log = jnp.log(jax.random.uniform(ks[7], (DEPTH, 2, GDN_HEADS), f32, 1.0, 16.0))
    return {
        "x_prompt": jax.random.normal(ks[0], (BATCH, SEQ, D_MODEL), f32),
        "x_sample": jax.random.normal(ks[1], (DEC_BATCH, DEC_SEQ, D_MODEL), f32),
        "norm_mix_pre": gain(ks[2], (DEPTH, D_MODEL)),
        "w_in": jax.random.normal(ks[3], (DEPTH, D_MODEL, IN_WIDTH), f32) * D_MODEL ** -0.5,
        "conv_w": jax.random.normal(ks[4], (DEPTH, CONV_K, GDN_CONV_W), f32) * CONV_K ** -0.5,
        "A_log_f": A_log[:, 0],
        "A_log_b": A_log[:, 1],
        "dt_bias_f": dt_bias[:, 0],
        "dt_bias_b": dt_bias[:, 1],
        "gdn_norm_w": gain(ks[8], (DEPTH, GDN_DV)),
        "q_norm_w": gain(ks[9], (DEPTH, ATT_HD)),
        "k_norm_w": gain(ks[10], (DEPTH, ATT_HD)),
        "w_out": jax.random.normal(ks[11], (DEPTH, MIX_WIDTH, D_MODEL), f32) * MIX_WIDTH ** -0.5,
        "norm_mix_post": gain(ks[12], (DEPTH, D_MODEL)),
        "norm_mlp_pre": gain(ks[13], (DEPTH, D_MODEL)),
        "w_up": jax.random.normal(ks[14], (DEPTH, D_MODEL, D_FF), f32) * D_MODEL ** -0.5,
        "w_down": jax.random.normal(ks[15], (DEPTH, D_FF, D_MODEL), f32) * D_FF ** -0.5,
        "norm_mlp_post": gain(ks[16], (DEPTH, D_MODEL)),
    }


def reference(x_prompt, x_sample, norm_mix_pre, w_in, conv_w, A_log_f, A_log_b, dt_bias_f, dt_bias_b,
              gdn_norm_w, q_norm_w, k_norm_w, w_out, norm_mix_post, norm_mlp_pre, w_up, w_down,
              norm_mlp_post):
    y_prompt = x_prompt
    y_sample = x_sample
    for l in range(DEPTH):
        args = (norm_mix_pre[l], w_in[l], conv_w[l], A_log_f[l], A_log_b[l], dt_bias_f[l], dt_bias_b[l],
                gdn_norm_w[l], q_norm_w[l], k_norm_w[l], w_out[l], norm_mix_post[l], norm_mlp_pre[l],
                w_up[l], w_down[l], norm_mlp_post[l])
        y_prompt = _layer(y_prompt, *args)
        y_sample = _layer(y_sample, *args)
    return (y_prompt, y_sample)
```

```python
import numpy as np
import concourse.bass as bass
import concourse.mybir as mybir
from concourse.bass_utils import run_bass_kernel_spmd

F32 = mybir.dt.float32
BF16 = mybir.dt.bfloat16
AF = mybir.ActivationFunctionType
ALU = mybir.AluOpType
AX = mybir.AxisListType

ENGS = ("pe", "act", "dve", "pool", "sp")
NS = 8


class Buf:
    __slots__ = ("w", "r", "rd", "name", "excl")

    def __init__(self, name=""):
        self.excl = False
        self.w = None
        self.r = {}
        self.rd = []
        self.name = name


class Prog:
    def __init__(self, nc):
        self.nc = nc
        self.sem = {e: nc.alloc_semaphore("s_" + e) for e in ENGS}
        self.dsem = {q: [nc.alloc_semaphore("d_%s_%d" % (q, i)) for i in range(NS)]
                     for q in ("sp", "act", "pool")}
        self.dcount = {q: 0 for q in self.dsem}
        self.base = {e: 0 for e in ENGS}
        self.rec = {e: [] for e in ENGS}
        self.off = {e: 0 for e in ENGS}
        self.msmap = {e: {} for e in ENGS}
        self.waited = {}
        self.limit = None
        self.nrec = 0

    def _need(self, eng, tok, waits):
        if tok is None:
            return
        if tok[0] == "c":
            _, se, gidx = tok
            if se == "pe" and eng == "pe":
                return
            key = (eng, se)
            if self.waited.get(key, -1) >= gidx:
                return
            self.waited[key] = gidx
            li = gidx - self.off[se]
            if li >= 0:
                self.rec[se][li][3] = True
            waits.append(tok)
        else:
            _, q, n = tok
            key = (eng, q, n % NS)
            if self.waited.get(key, -1) >= n:
                return
            self.waited[key] = n
            waits.append(tok)

    def _deps(self, eng, tok, r, w, waits):
        rx = [b for b in r if b.excl]
        if rx:
            r = [b for b in r if not b.excl]
            w = list(w) + [b for b in rx if b not in w]
        for b in r:
            self._need(eng, b.w, waits)
        for b in w:
            self._need(eng, b.w, waits)
            for se, gi in b.r.items():
                self._need(eng, ("c", se, gi), waits)
            for t in b.rd:
                self._need(eng, t, waits)
        for b in r:
            if tok[0] == "c":
                b.r[tok[1]] = tok[2]
            else:
                b.rd.append(tok)
                if len(b.rd) > 3 * NS:
                    b.rd = b.rd[-3 * NS:]
        for b in w:
            b.w = tok
            b.r = {}
            b.rd = []

    def op(self, eng, meth, r=(), w=(), **kw):
        fn = (meth, kw)
        self.nrec += 1
        if self.limit is not None and self.nrec > self.limit:
            return None
        waits = []
        gidx = self.off[eng] + len(self.rec[eng])
        tok = ("c", eng, gidx)
        self._deps(eng, tok, r, w, waits)
        self.rec[eng].append(["op", fn, waits, False])
        return tok

    def dma(self, q, out, in_, r=(), w=(), slow=False):
        self.nrec += 1
        if self.limit is not None and self.nrec > self.limit:
            return None
        waits = []
        n = self.dcount[q]
        self.dcount[q] += 1
        tok = ("d", q, n)
        if n >= NS:
            self._need(q, ("d", q, n - NS), waits)
        self._deps(q, tok, r, w, waits)
        self.rec[q].append(["dma", (out, in_, q, n, slow), waits, False])
        return tok

    def barrier(self, bufs=()):
        for e in ENGS:
            waits = []
            for se in ENGS:
                if se == e:
                    continue
                for li in range(len(self.rec[se]) - 1, -1, -1):
                    if self.rec[se][li][0] == "op":
                        self._need(e, ("c", se, self.off[se] + li), waits)
                        break
            for q in self.dsem:
                for k in range(max(0, self.dcount[q] - NS), self.dcount[q]):
                    self._need(e, ("d", q, k), waits)
            self.rec[e].append(["nop", None, waits, False])

    def emit(self):
        nc = self.nc
        for e in ENGS:
            c = self.base[e]
            for li, rcd in enumerate(self.rec[e]):
                if rcd[0] == "op" and rcd[3]:
                    c += 1
                    self.msmap[e][self.off[e] + li] = c
            self.base[e] = c

        def run(e, engobj):
            for rcd in self.rec[e]:
                kind, fn, waits, ms = rcd
                for t in waits:
                    if t[0] == "c":
                        engobj.wait_ge(self.sem[t[1]], self.msmap[t[1]][t[2]])
                    else:
                        engobj.wait_ge(self.dsem[t[1]][t[2] % NS], 16 * (t[2] // NS + 1))
                if kind == "op":
                    ins = getattr(engobj, fn[0])(**fn[1])
                    if ms:
                        ins.then_inc(self.sem[e], 1)
                elif kind == "dma":
                    out, in_, q, n, slow = fn
                    if slow:
                        ins = engobj.dma_start(out=out, in_=in_, allow_slow_non_contiguous=True)
                    else:
                        ins = engobj.dma_start(out=out, in_=in_)
                    ins.then_inc(self.dsem[q][n % NS], 16)

        with nc.Block() as block:
            @block.tensor
            def _(eng):
                run("pe", eng)

            @block.scalar
            def _(eng):
                run("act", eng)

            @block.vector
            def _(eng):
                run("dve", eng)

            @block.gpsimd
            def _(eng):
                run("pool", eng)

            @block.sync
            def _(eng):
                run("sp", eng)

        for e in ENGS:
            self.off[e] += len(self.rec[e])
            self.rec[e] = []

from contextlib import ExitStack

D = 1024
NKC = 8
DFF = 4096
EPS = 1e-6
TT = 512
FM_COLS = 3072
TM0 = 3072
WIN_COLS = 3856
GRID_W = 64


SBUF_LIMIT = 196608


def _chk(nc, handle):
    m = nc.lookup_mloc(handle)
    nbytes = 1
    for d in list(m.dims)[1:]:
        nbytes *= int(d)
    assert int(m.addr) + nbytes <= SBUF_LIMIT, ("SBUF overflow past physical limit", handle.name, int(m.addr), nbytes)
    return handle


class Tile(Buf):
    __slots__ = ("ap",)

    def __init__(self, ap, name=""):
        Buf.__init__(self, name)
        self.ap = ap


def rope_tables(T):
    t = np.arange(T)
    row = (t // GRID_W).astype(np.float32)
    col = (t % GRID_W).astype(np.float32)
    nf = 32
    inv = (np.float32(10000.0) ** (-np.arange(nf, dtype=np.float32) / np.float32(nf))).astype(np.float32)
    cosT = np.zeros((128, T), np.float32)
    sinT = np.zeros((128, T), np.float32)
    for d in range(128):
        half = d // 64
        idx = d % 64
        f = idx % 32
        pos = row if half == 0 else col
        ang = (pos * inv[f]).astype(np.float32)
        cosT[d] = np.cos(ang)
        sinT[d] = np.sin(ang) * (-1.0 if idx < 32 else 1.0)
    return cosT, sinT


def rope_perm():
    p = np.zeros(128, np.int64)
    for d in range(128):
        idx = d % 64
        p[d] = d + 32 if idx < 32 else d - 32
    return p


def gdn_masks():
    t = np.arange(128)[:, None]
    j = np.arange(128)[None, :]
    same = (t // 64) == (j // 64)
    m = {}
    m["cum_f"] = (same & (t <= j)).astype(np.float32)
    m["cum_b"] = (same & (t >= j)).astype(np.float32)
    m["same"] = same.astype(np.float32)
    i = t
    m["pos_f"] = np.where(same & (i >= j), 0.0, 30000.0).astype(np.float32)
    m["pos_b"] = np.where(same & (i <= j), 0.0, 30000.0).astype(np.float32)
    m["strict_f"] = (same & (i > j)).astype(np.float32)
    m["strict_b"] = (same & (i < j)).astype(np.float32)
    m["c0"] = np.tile((np.arange(128) < 64).astype(np.float32)[:, None], (1, 128))
    m["c1"] = np.tile((np.arange(128) >= 64).astype(np.float32)[:, None], (1, 128))
    return m


CONST_NAMES = ["ident", "cum_f", "cum_b", "same", "pos_f", "pos_b", "strict_f", "strict_b", "c0", "c1"]


def build(Tp, Ts, debug=False, stop_after=9):
    nc = bass.Bass("TRN2", target_bir_lowering=False)
    P = Prog(nc)
    seqs = [("p", Tp), ("s", Ts)]
    Tmax = max(Tp, Ts)

    def din(name, shape, dt=F32):
        return nc.dram_tensor(name, list(shape), dt, kind="ExternalInput").ap()

    def dscr(name, shape, dt):
        if debug:
            return nc.dram_tensor(name, list(shape), dt, kind="ExternalOutput").ap()
        return nc.dram_tensor(name, list(shape), dt).ap()

    X = {"p": din("x_p", [Tp, D]), "s": din("x_s", [Ts, D])}
    Y = {"p": nc.dram_tensor("y_p", [Tp, D], F32, kind="ExternalOutput").ap(),
         "s": nc.dram_tensor("y_s", [Ts, D], F32, kind="ExternalOutput").ap()}
    w_in = din("w_in_r", [D, WIN_COLS])
    w_out = din("w_out", [D, D])
    w_up = din("w_up", [D, DFF])
    w_down = din("w_down", [DFF, D])
    nrm = din("norms", [4, D])
    conv_w_r = din("conv_w_r", [128, 12, 5])
    nrm_r = din("nrm_r", [128, 4, NKC])
    hnorm_r = din("hnorm_r", [128, 5])
    gparam = din("gparam", [2, 8])
    hnorm = din("hnorm", [5, 128])
    cosT = din("cosT", [128, Tmax])
    sinT = din("sinT", [128, Tmax])
    cst = din("cst", [len(CONST_NAMES), 128, 128])

    S = {}
    for sn, T in seqs:
        S[sn] = dict(
            A=dscr("A_raw_" + sn, [1536, T], BF16),
            QK=dscr("QK_T_" + sn, [768, T], BF16),
            Z=dscr("Z_" + sn, [T, 512], BF16),
            G=dscr("G_" + sn, [T, 16], F32),
            V=dscr("V_" + sn, [T, 256], BF16),
            GQ=dscr("GQ_T_" + sn, [512, T], BF16),
            GK=dscr("GK_T_" + sn, [512, T], BF16),
            GKt=dscr("GKt_" + sn, [T, 512], BF16),
            GVt=dscr("GVt_" + sn, [T, 512], BF16),
            OF=dscr("OF_" + sn, [T, 512], F32),
            OB=dscr("OB_" + sn, [T, 512], F32),
            MIX=dscr("MIX_T_" + sn, [1024, T], BF16),
            X1=dscr("X1_" + sn, [T, 1024], F32),
        )
    dbufs = {}

    def DB(key):
        if key not in dbufs:
            dbufs[key] = Buf(str(key))
        return dbufs[key]

    ps = [Tile(nc.alloc_psum_tensor("ps%d" % i, [128, 512], F32).ap(), "ps%d" % i) for i in range(8)]
    for t in ps:
        t.excl = True

    with ExitStack() as g_es:
        def gtile(name, shape, dt):
            return Tile(_chk(nc, g_es.enter_context(nc.sbuf_tensor(name, list(shape), dt))).ap(), name)

        identf = gtile("identf", [128, 128], F32)
        P.dma("sp", identf.ap, cst[0], w=[identf])
        identb = gtile("identb", [128, 128], BF16)
        P.op("dve", "tensor_copy", out=identb.ap, in_=identf.ap, r=[identf], w=[identb])
        onesb = gtile("onesb", [128, 128], BF16)
        P.op("pool", "memset", ap=onesb.ap, constant=1.0, w=[onesb])
        onesf = gtile("onesf", [128, 128], F32)
        P.op("pool", "memset", ap=onesf.ap, constant=1.0, w=[onesf])
        epsc = gtile("epsc", [128, 1], F32)
        P.op("pool", "memset", ap=epsc.ap, constant=EPS, w=[epsc])
        nrm_t = gtile("nrm_t", [128, 4, NKC], F32)
        P.dma("sp", nrm_t.ap, nrm_r, w=[nrm_t])
        hn_t = gtile("hn_t", [128, 5], F32)
        P.dma("sp", hn_t.ap, hnorm_r, w=[hn_t])
        gp_b = gtile("gp_b", [128, 2, 8], F32)
        P.dma("sp", gp_b.ap[:, 0, :], gparam[0:1, :].partition_broadcast(128), w=[gp_b])
        P.dma("sp", gp_b.ap[:, 1, :], gparam[1:2, :].partition_broadcast(128), w=[gp_b])
        negA = gtile("negA", [128, 8], F32)
        P.op("act", "activation", out=negA.ap, in_=gp_b.ap[:, 0, :], func=AF.Exp, r=[gp_b], w=[negA])
        P.op("dve", "tensor_scalar", out=negA.ap, in0=negA.ap, scalar1=-1.0, scalar2=None, op0=ALU.mult,
             r=[negA], w=[negA])

        def load_weight_bf16(wt, st, src, nk, cols, scale_sel, stage_cols):
            n = 0
            for kc in range(nk):
                for c0 in range(0, cols, stage_cols):
                    cn = min(stage_cols, cols - c0)
                    s_ = st[n % 2]
                    P.dma("sp", s_.ap[:, 0:cn], src[kc * 128:(kc + 1) * 128, c0:c0 + cn], w=[s_])
                    dst = wt.ap[:, kc, c0:c0 + cn]
                    if scale_sel is None:
                        if n % 2 == 0:
                            P.op("act", "copy", out=dst, in_=s_.ap[:, 0:cn], r=[s_], w=[wt])
                        else:
                            P.op("dve", "tensor_copy", out=dst, in_=s_.ap[:, 0:cn], r=[s_], w=[wt])
                    else:
                        sc = nrm_t.ap[:, scale_sel, kc:kc + 1]
                        if n % 2 == 0:
                            P.op("act", "activation", out=dst, in_=s_.ap[:, 0:cn], func=AF.Copy, scale=sc,
                                 r=[s_, nrm_t], w=[wt])
                        else:
                            P.op("dve", "tensor_scalar", out=dst, in0=s_.ap[:, 0:cn], scalar1=sc, scalar2=None, op0=ALU.mult,
                                 r=[s_, nrm_t], w=[wt])
                    n += 1
            return wt

        def rms_rstd(ssq, rstd, n, inv_n):
            P.op("act", "activation", out=rstd.ap[:, 0:n], in_=ssq.ap[:, 0:n], func=AF.Ln, bias=epsc.ap[:, 0:1], scale=inv_n,
                 r=[ssq, epsc], w=[rstd])
            P.op("act", "activation", out=rstd.ap[:, 0:n], in_=rstd.ap[:, 0:n], func=AF.Exp, scale=-0.5,
                 r=[rstd], w=[rstd])

        with ExitStack() as es:
            def tile(name, shape, dt):
                return Tile(_chk(nc, es.enter_context(nc.sbuf_tensor(name, list(shape), dt))).ap(), name)

            wi = tile("wi", [128, NKC, WIN_COLS], BF16)
            with ExitStack() as ses:
                st = [Tile(_chk(nc, ses.enter_context(nc.sbuf_tensor("wst%d" % i, [128, 1928], F32))).ap()) for i in range(2)]
                load_weight_bf16(wi, st, w_in, NKC, WIN_COLS, 0, 1928)
                P.barrier()
                P.emit()
            xt = [tile("xt%d" % i, [128, 4, D], F32) for i in range(2)]
            hb = tile("hb", [128, 4, D], BF16)
            junk = tile("junk", [128, D], BF16)
            ssq = tile("ssq", [128, 4], F32)
            rstd = tile("rstd", [128, 4], F32)
            hT = [tile("hT%d" % i, [128, NKC, TT], BF16) for i in range(1)]
            a_st = [tile("a_st%d" % i, [128, 12, TT], BF16) for i in range(1)]
            qk_st = [tile("qk_st%d" % i, [128, 6, TT], BF16) for i in range(1)]
            z_st = [tile("z_st%d" % i, [128, 4, 512], BF16) for i in range(1)]
            v_st = [tile("v_st%d" % i, [128, 4, 256], BF16) for i in range(1)]
            g_st = [tile("g_st%d" % i, [128, 4, 16], F32) for i in range(1)]
            cs_t = [tile("cs_t%d" % i, [128, 2, TT], F32) for i in range(1)]
            sq_t = [tile("sq_t%d" % i, [128, TT], BF16) for i in range(2)]
            rs_t = [tile("rs_t%d" % i, [128, TT], F32) for i in range(2)]
            t1_t = [tile("t1_t%d" % i, [128, TT], F32) for i in range(2)]
            t2_t = [tile("t2_t%d" % i, [128, TT], F32) for i in range(2)]
            gt_t = [tile("gt_t%d" % i, [128, 16], F32) for i in range(2)]

            it = 0
            for sn, T in seqs:
                sc = S[sn]
                ntile = T // TT
                for ti in range(ntile):
                    t0 = ti * TT
                    x_ = xt[it % 2]
                    hT_ = hT[0]
                    P.dma("sp", x_.ap, X[sn][t0:t0 + TT, :].rearrange("(s p) d -> p s d", p=128), w=[x_])
                    cs_ = cs_t[0]
                    P.dma("sp", cs_.ap[:, 0, :], cosT[:, t0:t0 + TT], w=[cs_])
                    P.dma("sp", cs_.ap[:, 1, :], sinT[:, t0:t0 + TT], w=[cs_])
                    for s in range(4):
                        P.op("act", "activation", out=junk.ap, in_=x_.ap[:, s, :], func=AF.Square,
                                                                     accum_out=ssq.ap[:, s:s + 1],
                             r=[x_], w=[junk, ssq])
                    rms_rstd(ssq, rstd, 4, 1.0 / D)
                    for s in range(4):
                        P.op("dve", "tensor_scalar", out=hb.ap[:, s, :], in0=x_.ap[:, s, :],
                                                                        scalar1=rstd.ap[:, s:s + 1], scalar2=None, op0=ALU.mult,
                             r=[x_, rstd], w=[hb])
                    for s in range(4):
                        pb = ps[s % 2]
                        pv = pb.ap.bitcast(BF16).rearrange("p (k t) -> p k t", k=NKC)
                        for kc in range(NKC):
                            P.op("pe", "transpose", out=pv[:, kc, :], in_=hb.ap[:, s, kc * 128:(kc + 1) * 128],
                                                                               identity=identb.ap,
                                 r=[hb, identb], w=[pb])
                        eng = "act" if s % 2 == 0 else "dve"
                        if eng == "act":
                            P.op("act", "copy", out=hT_.ap[:, :, s * 128:(s + 1) * 128], in_=pv, r=[pb], w=[hT_])
                        else:
                            P.op("dve", "tensor_copy", out=hT_.ap[:, :, s * 128:(s + 1) * 128], in_=pv, r=[pb], w=[hT_])

                    def fm_mm(pb, c):
                        for kc in range(NKC):
                            P.op("pe", "matmul", out=pb.ap, lhsT=wi.ap[:, kc, c * 128:(c + 1) * 128], rhs=hT_.ap[:, kc, :],
                                                                            start=(kc == 0), stop=(kc == NKC - 1),
                                 r=[wi, hT_], w=[pb])

                    a_ = a_st[0]
                    for c in range(12):
                        pb = ps[2 + (c % 2)]
                        fm_mm(pb, c)
                        if c % 2 == 0:
                            P.op("act", "copy", out=a_.ap[:, c, :], in_=pb.ap, r=[pb], w=[a_])
                        else:
                            P.op("dve", "tensor_copy", out=a_.ap[:, c, :], in_=pb.ap, r=[pb], w=[a_])
                    P.dma("sp", sc["A"].rearrange("(c p) t -> p c t", p=128)[:, :, t0:t0 + TT], a_.ap, r=[a_], w=[DB(("A", sn))])
                    qk_ = qk_st[0]
                    for j in range(6):
                        pm = ps[2 + (j % 2)]
                        pp = ps[4 + (j % 2)]
                        pc = ps[6 + (j % 2)]
                        sq_, rs_, t1_, t2_ = sq_t[j % 2], rs_t[j % 2], t1_t[j % 2], t2_t[j % 2]
                        fm_mm(pm, 12 + j)
                        fm_mm(pp, 18 + j)
                        wsel = 1 if j < 4 else 3
                        P.op("act", "activation", out=sq_.ap, in_=pm.ap, func=AF.Square, r=[pm], w=[sq_])
                        P.op("pe", "matmul", out=pc.ap, lhsT=onesb.ap, rhs=sq_.ap, start=True, stop=True,
                             r=[onesb, sq_], w=[pc])
                        P.op("act", "activation", out=rs_.ap, in_=pc.ap, func=AF.Ln, bias=epsc.ap[:, 0:1], scale=1.0 / 128,
                             r=[pc, epsc], w=[rs_])
                        P.op("act", "activation", out=rs_.ap, in_=rs_.ap, func=AF.Exp, scale=-0.5, r=[rs_], w=[rs_])
                        P.op("dve", "scalar_tensor_tensor", out=t1_.ap, in0=pm.ap, scalar=hn_t.ap[:, wsel:wsel + 1],
                                                                                               in1=cs_.ap[:, 0, :], op0=ALU.mult, op1=ALU.mult,
                             r=[pm, hn_t, cs_], w=[t1_])
                        P.op("dve", "scalar_tensor_tensor", out=t2_.ap, in0=pp.ap, scalar=hn_t.ap[:, wsel + 1:wsel + 2],
                                                                                               in1=cs_.ap[:, 1, :], op0=ALU.mult, op1=ALU.mult,
                             r=[pp, hn_t, cs_], w=[t2_])
                        P.op("pool", "tensor_tensor", out=t1_.ap, in0=t1_.ap, in1=t2_.ap, op=ALU.add, r=[t1_, t2_], w=[t1_])
                        P.op("pool", "tensor_tensor", out=qk_.ap[:, j, :], in0=t1_.ap, in1=rs_.ap, op=ALU.mult,
                             r=[t1_, rs_], w=[qk_])
                    P.dma("sp", sc["QK"].rearrange("(c p) t -> p c t", p=128)[:, :, t0:t0 + TT], qk_.ap, r=[qk_], w=[DB(("QK", sn))])
                    z_, v_, g_ = z_st[0], v_st[0], g_st[0]
                    for s in range(4):
                        pz = ps[s % 2]
                        pg = ps[6 + (s % 2)]
                        gt_ = gt_t[s % 2]
                        for kc in range(NKC):
                            P.op("pe", "matmul", out=pz.ap, lhsT=hT_.ap[:, kc, s * 128:(s + 1) * 128], rhs=wi.ap[:, kc, TM0:TM0 + 512],
                                                                            start=(kc == 0), stop=(kc == NKC - 1), r=[hT_, wi], w=[pz])
                        for kc in range(NKC):
                            P.op("pe", "matmul", out=pg.ap[:, 0:272], lhsT=hT_.ap[:, kc, s * 128:(s + 1) * 128],
                                                                            rhs=wi.ap[:, kc, TM0 + 512:TM0 + 784],
                                                                            start=(kc == 0), stop=(kc == NKC - 1), r=[hT_, wi], w=[pg])
                        P.op("act", "copy", out=z_.ap[:, s, :], in_=pz.ap, r=[pz], w=[z_])
                        P.op("dve", "tensor_copy", out=v_.ap[:, s, :], in_=pg.ap[:, 16:272], r=[pg], w=[v_])
                        P.op("act", "activation", out=gt_.ap[:, 0:8], in_=pg.ap[:, 0:8], func=AF.Exp, scale=-1.0, r=[pg], w=[gt_])
                        P.op("dve", "tensor_scalar", out=gt_.ap[:, 0:8], in0=gt_.ap[:, 0:8], scalar1=1.0, scalar2=None, op0=ALU.add,
                             r=[gt_], w=[gt_])
                        P.op("dve", "reciprocal", out=g_.ap[:, s, 0:8], in_=gt_.ap[:, 0:8], r=[gt_], w=[g_])
                        P.op("dve", "tensor_tensor", out=gt_.ap[:, 8:16], in0=pg.ap[:, 8:16], in1=gp_b.ap[:, 1, :], op=ALU.add,
                             r=[pg, gp_b], w=[gt_])
                        P.op("act", "activation", out=gt_.ap[:, 8:16], in_=gt_.ap[:, 8:16], func=AF.Exp, r=[gt_], w=[gt_])
                        P.op("dve", "tensor_scalar", out=gt_.ap[:, 8:16], in0=gt_.ap[:, 8:16], scalar1=1.0, scalar2=None, op0=ALU.add,
                             r=[gt_], w=[gt_])
                        P.op("act", "activation", out=gt_.ap[:, 8:16], in_=gt_.ap[:, 8:16], func=AF.Ln, r=[gt_], w=[gt_])
                        P.op("dve", "tensor_tensor", out=g_.ap[:, s, 8:16], in0=gt_.ap[:, 8:16], in1=negA.ap, op=ALU.mult,
                             r=[gt_, negA], w=[g_])
                    P.dma("sp", sc["Z"][t0:t0 + TT, :].rearrange("(s p) c -> p s c", p=128), z_.ap, r=[z_], w=[DB(("Z", sn))])
                    P.dma("sp", sc["V"][t0:t0 + TT, :].rearrange("(s p) c -> p s c", p=128), v_.ap, r=[v_], w=[DB(("V", sn))])
                    P.dma("sp", sc["G"][t0:t0 + TT, :].rearrange("(s p) c -> p s c", p=128), g_.ap, r=[g_], w=[DB(("G", sn))])
                    it += 1
            P.barrier()
            P.emit()

        if stop_after >= 2:
            phases_rest(locals())
    return nc


def host_inputs(inp, Tmax):
    f = np.float32
    w_in = np.asarray(inp["w_in"][0], f)
    perm = rope_perm()
    qB = w_in[:, 2064:2576]
    kB = w_in[:, 2576:2832]
    qBp = qB.reshape(D, 4, 128)[:, :, perm].reshape(D, 512)
    kBp = kB.reshape(D, 2, 128)[:, :, perm].reshape(D, 256)
    w_in_r = np.ascontiguousarray(np.concatenate(
        [w_in[:, 0:1536], qB, kB, qBp, kBp, w_in[:, 1536:2048], w_in[:, 2048:2064], w_in[:, 2832:3088]], axis=1))
    norms = np.stack([np.asarray(inp[k][0], f) for k in ("norm_mix_pre", "norm_mix_post", "norm_mlp_pre", "norm_mlp_post")])
    gparam = np.stack([np.concatenate([np.asarray(inp["A_log_f"][0], f), np.asarray(inp["A_log_b"][0], f)]),
                       np.concatenate([np.asarray(inp["dt_bias_f"][0], f), np.asarray(inp["dt_bias_b"][0], f)])])
    qn = np.asarray(inp["q_norm_w"][0], f)
    kn = np.asarray(inp["k_norm_w"][0], f)
    hnorm = np.stack([np.asarray(inp["gdn_norm_w"][0], f), qn, qn[perm], kn, kn[perm]])
    cosT, sinT = rope_tables(Tmax)
    m = gdn_masks()
    m["ident"] = np.eye(128, dtype=f)
    cst = np.stack([m[n] for n in CONST_NAMES]).astype(f)
    return dict(w_in_r=w_in_r, w_out=np.ascontiguousarray(np.asarray(inp["w_out"][0], f)),
                w_up=np.ascontiguousarray(np.asarray(inp["w_up"][0], f)),
                w_down=np.ascontiguousarray(np.asarray(inp["w_down"][0], f)),
                norms=np.ascontiguousarray(norms),
                nrm_r=np.ascontiguousarray(norms.reshape(4, NKC, 128).transpose(2, 0, 1)),
                hnorm_r=np.ascontiguousarray(hnorm.T),
                conv_w_r=np.ascontiguousarray(np.asarray(inp["conv_w"][0], f).reshape(5, 12, 128).transpose(2, 1, 0)),
                gparam=np.ascontiguousarray(gparam), hnorm=np.ascontiguousarray(hnorm),
                cosT=cosT, sinT=sinT, cst=cst)


_NC_CACHE = {}


def run(inp, Tp, Ts, ncores, debug=False, stop_after=9):
    key = (Tp, Ts, debug, stop_after)
    if key not in _NC_CACHE:
        _NC_CACHE[key] = build(Tp, Ts, debug, stop_after)
    nc = _NC_CACHE[key]
    shared = host_inputs(inp, max(Tp, Ts))
    xp = np.asarray(inp["x_prompt"], np.float32)
    xs = np.asarray(inp["x_sample"], np.float32)
    in_maps = []
    for c in range(ncores):
        m = dict(shared)
        m["x_p"] = np.ascontiguousarray(xp[c])
        m["x_s"] = np.ascontiguousarray(xs[c])
        in_maps.append(m)
    res = run_bass_kernel_spmd(nc, in_maps, core_ids=list(range(ncores)))
    return res.results


def kernel(**inputs):
    res = run(inputs, 8192, 2048, 8)
    yp = np.stack([np.asarray(r["y_p"], np.float32) for r in res])
    ys = np.stack([np.asarray(r["y_s"], np.float32) for r in res])
    return (yp, ys)


def phase_attn(L):
    nc, P, ps, S, seqs, hnorm, identb = L["nc"], L["P"], L["ps"], L["S"], L["seqs"], L["hnorm"], L["identb"]
    DB = L["DB"]
    with ExitStack() as es:
        def tile(name, shape, dt):
            return Tile(_chk(nc, es.enter_context(nc.sbuf_tensor(name, list(shape), dt))).ap(), name)
        Tmax = max(T for _, T in seqs)
        NBmax = Tmax // 128
        wqk_b = tile("wqk_b", [128, 2, 128], F32)
        P.dma("sp", wqk_b.ap[:, 0, :], hnorm[1:2, :].partition_broadcast(128), w=[wqk_b])
        P.dma("sp", wqk_b.ap[:, 1, :], hnorm[3:4, :].partition_broadcast(128), w=[wqk_b])
        mx = tile("mx", [128, 2], F32)
        P.op("dve", "tensor_reduce", out=mx.ap, in_=wqk_b.ap, axis=AX.X, op=ALU.max, apply_absolute_value=True, r=[wqk_b], w=[mx])
        negM = tile("negM", [128, 1], F32)
        P.op("dve", "tensor_tensor", out=negM.ap, in0=mx.ap[:, 0:1], in1=mx.ap[:, 1:2], op=ALU.mult, r=[mx], w=[negM])
        P.op("dve", "tensor_scalar", out=negM.ap, in0=negM.ap, scalar1=-(128.0 ** 0.5), scalar2=None, op0=ALU.mult, r=[negM], w=[negM])
        kT = tile("kT", [128, Tmax], BF16)
        va = tile("va", [128, NBmax, 129], BF16)
        P.op("pool", "memset", ap=va.ap, constant=1.0, w=[va])
        qT = [tile("qT%d" % i, [128, TT], BF16) for i in range(2)]
        pT = [tile("pT%d" % i, [128, TT], BF16) for i in range(3)]
        osb = tile("osb", [128, 4, 129], F32)
        rcp = tile("rcp", [128, 4], F32)
        onb = tile("onb", [128, 4, 128], BF16)
        mst = [tile("mst%d" % i, [128, TT], BF16) for i in range(2)]
        scale = 128.0 ** -0.5
        qi = 0
        npt = 0
        for sn, T in seqs:
            sc = S[sn]
            NB = T // 128
            for g in range(2):
                P.dma("sp", kT.ap[:, 0:T], sc["QK"][(4 + g) * 128:(5 + g) * 128, :], r=[DB(("QK", sn))], w=[kT])
                P.dma("sp", va.ap[:, 0:NB, 0:128], sc["V"][:, g * 128:(g + 1) * 128].rearrange("(b p) c -> p b c", p=128),
                      r=[DB(("V", sn))], w=[va])
                for h in (2 * g, 2 * g + 1):
                    for qt in range(T // TT):
                        q0 = qt * TT
                        q_ = qT[qi % 2]
                        P.dma("sp", q_.ap, sc["QK"][h * 128:(h + 1) * 128, q0:q0 + TT], r=[DB(("QK", sn))], w=[q_])
                        accb = [ps[2 + 2 * (qi % 2)], ps[3 + 2 * (qi % 2)]]
                        accv = [b.ap[:, 0:258].rearrange("p (a c) -> p a c", a=2) for b in accb]

                        def qk(kb):
                            sb = ps[kb % 2]
                            P.op("pe", "matmul", out=sb.ap, lhsT=kT.ap[:, kb * 128:(kb + 1) * 128], rhs=q_.ap, start=True, stop=True,
                                 r=[kT, q_], w=[sb])

                        qk(0)
                        for kb in range(NB):
                            if kb + 1 < NB:
                                qk(kb + 1)
                            sb = ps[kb % 2]
                            p_ = pT[npt % 3]
                            npt += 1
                            P.op("act", "activation", out=p_.ap, in_=sb.ap, func=AF.Exp, bias=negM.ap[:, 0:1], scale=scale,
                                 r=[sb, negM], w=[p_])
                            for qs in range(4):
                                P.op("pe", "matmul", out=accv[qs // 2][:, qs % 2, :], lhsT=p_.ap[:, qs * 128:(qs + 1) * 128], rhs=va.ap[:, kb, :],
                                     start=(kb == 0 and qs % 2 == 0), stop=(kb == NB - 1), skip_group_check=True,
                                     r=[p_, va], w=[accb[qs // 2]])
                        P.op("dve", "tensor_copy", out=osb.ap[:, 0:2, :], in_=accv[0], r=[accb[0]], w=[osb])
                        P.op("dve", "tensor_copy", out=osb.ap[:, 2:4, :], in_=accv[1], r=[accb[1]], w=[osb])
                        P.op("dve", "reciprocal", out=rcp.ap, in_=osb.ap[:, :, 128], r=[osb], w=[rcp])
                        P.op("dve", "tensor_tensor", out=onb.ap, in0=osb.ap[:, :, 0:128], in1=rcp.ap.unsqueeze(2).to_broadcast([128, 4, 128]),
                             op=ALU.mult, r=[osb, rcp], w=[onb])
                        tb = ps[6]
                        tv = tb.ap.bitcast(BF16)[:, 0:512].rearrange("p (a c) -> p a c", a=4)
                        for qs in range(4):
                            P.op("pe", "transpose", out=tv[:, qs, :], in_=onb.ap[:, qs, :], identity=identb.ap, r=[onb, identb], w=[tb])
                        m_ = mst[qi % 2]
                        P.op("dve", "tensor_copy", out=m_.ap.rearrange("p (a c) -> p a c", a=4), in_=tv, r=[tb], w=[m_])
                        P.dma("sp", sc["MIX"][512 + h * 128:512 + (h + 1) * 128, q0:q0 + TT], m_.ap, r=[m_], w=[DB(("MIX", sn))])
                        qi += 1
        P.barrier()
        P.emit()


def phase_mlp(L):
    nc, P, ps, S, seqs, identb = L["nc"], L["P"], L["ps"], L["S"], L["seqs"], L["identb"]
    DB, X, Y, rms_rstd, nrm = L["DB"], L["X"], L["Y"], L["rms_rstd"], L["nrm"]
    load_weight_bf16 = L["load_weight_bf16"]

    def make_norm_resid(mo, junk, ssq, rstd, nrm_b):
        def norm_resid(src_banks, resid_ap, out_ap, which, deps_resid, deps_out):
            P.op("act", "copy", out=mo.ap[:, 0:512], in_=src_banks[0].ap, r=[src_banks[0]], w=[mo])
            P.op("dve", "tensor_copy", out=mo.ap[:, 512:1024], in_=src_banks[1].ap, r=[src_banks[1]], w=[mo])
            P.op("act", "activation", out=junk.ap, in_=mo.ap, func=AF.Square, accum_out=ssq.ap[:, 0:1], r=[mo], w=[junk, ssq])
            rms_rstd(ssq, rstd, 1, 1.0 / D)
            P.op("dve", "scalar_tensor_tensor", out=mo.ap, in0=mo.ap, scalar=rstd.ap[:, 0:1], in1=nrm_b.ap[:, which, :],
                 op0=ALU.mult, op1=ALU.mult, r=[mo, rstd, nrm_b], w=[mo])
            P.op("pool", "tensor_tensor", out=out_ap, in0=mo.ap, in1=resid_ap, op=ALU.add, r=[mo] + deps_resid, w=deps_out)
        return norm_resid

    with ExitStack() as es:
        def tile(name, shape, dt):
            return Tile(_chk(nc, es.enter_context(nc.sbuf_tensor(name, list(shape), dt))).ap(), name)
        wo = tile("wo", [128, 8, D], BF16)
        with ExitStack() as ses:
            st = [Tile(_chk(nc, ses.enter_context(nc.sbuf_tensor("wst4a_%d" % i, [128, 1024], F32))).ap()) for i in range(2)]
            load_weight_bf16(wo, st, L["w_out"], 8, D, None, 1024)
            P.barrier()
            P.emit()
        nrm_b = tile("nrm_b", [128, 2, D], F32)
        P.dma("sp", nrm_b.ap[:, 0, :], nrm[1:2, :].partition_broadcast(128), w=[nrm_b])
        P.dma("sp", nrm_b.ap[:, 1, :], nrm[3:4, :].partition_broadcast(128), w=[nrm_b])
        xt = tile("x4", [128, 4, D], F32)
        mT = [tile("mT%d" % i, [128, 8, TT], BF16) for i in range(2)]
        mo = tile("mo", [128, D], F32)
        junk = tile("junk4", [128, D], BF16)
        ssq = tile("ssq4", [128, 1], F32)
        rstd = tile("rstd4", [128, 1], F32)
        norm_resid = make_norm_resid(mo, junk, ssq, rstd, nrm_b)
        it = 0
        for sn, T in seqs:
            sc = S[sn]
            for ti in range(T // TT):
                t0 = ti * TT
                m_ = mT[it % 2]
                P.dma("sp", xt.ap, X[sn][t0:t0 + TT, :].rearrange("(s p) d -> p s d", p=128), w=[xt])
                P.dma("sp", m_.ap, sc["MIX"].rearrange("(k p) t -> p k t", p=128)[:, :, t0:t0 + TT], r=[DB(("MIX", sn))], w=[m_])
                for s in range(4):
                    banks = [ps[2 * (s % 2)], ps[2 * (s % 2) + 1]]
                    for cg in range(2):
                        for kc in range(8):
                            P.op("pe", "matmul", out=banks[cg].ap, lhsT=m_.ap[:, kc, s * 128:(s + 1) * 128], rhs=wo.ap[:, kc, cg * 512:(cg + 1) * 512],
                                 start=(kc == 0), stop=(kc == 7), r=[m_, wo], w=[banks[cg]])
                    norm_resid(banks, xt.ap[:, s, :], xt.ap[:, s, :], 0, [xt], [xt])
                P.dma("sp", sc["X1"][t0:t0 + TT, :].rearrange("(s p) d -> p s d", p=128), xt.ap, r=[xt], w=[DB(("X1", sn))])
                it += 1
        P.barrier()
        P.emit()
    T4 = 256
    with ExitStack() as es:
        def tile(name, shape, dt):
            return Tile(_chk(nc, es.enter_context(nc.sbuf_tensor(name, list(shape), dt))).ap(), name)
        wu = tile("wu", [128, 8, DFF], BF16)
        wd = tile("wd", [128, 32, D], BF16)
        with ExitStack() as ses:
            st = [Tile(_chk(nc, ses.enter_context(nc.sbuf_tensor("wst4_%d" % i, [128, 2048], F32))).ap()) for i in range(2)]
            load_weight_bf16(wu, st, L["w_up"], 8, DFF, 2, 2048)
            load_weight_bf16(wd, st, L["w_down"], 32, D, None, 1024)
            P.barrier()
            P.emit()
        nrm_b = tile("nrm_b2", [128, 2, D], F32)
        P.dma("sp", nrm_b.ap[:, 0, :], nrm[1:2, :].partition_broadcast(128), w=[nrm_b])
        P.dma("sp", nrm_b.ap[:, 1, :], nrm[3:4, :].partition_broadcast(128), w=[nrm_b])
        xt = tile("x4b", [128, D], F32)
        mo = tile("mo2", [128, D], F32)
        ssq = tile("ssq4b", [128, 1], F32)
        rstd = tile("rstd4b", [128, 1], F32)
        h2 = tile("h2", [128, 2, D], BF16)
        h2T = tile("h2T", [128, 8, T4], BF16)
        fT = tile("fT", [128, 32, T4], BF16)
        rl = tile("rl", [128, T4], F32)
        junk = Tile(h2.ap[:, 0, :], "junk_alias")
        for sn, T in seqs:
            sc = S[sn]
            for ti in range(T // T4):
                t0 = ti * T4
                for s in range(2):
                    P.dma("sp", xt.ap, sc["X1"][t0 + s * 128:t0 + (s + 1) * 128, :], r=[DB(("X1", sn))], w=[xt])
                    P.op("act", "activation", out=h2.ap[:, s, :], in_=xt.ap, func=AF.Square, accum_out=ssq.ap[:, 0:1], r=[xt], w=[h2, ssq])
                    rms_rstd(ssq, rstd, 1, 1.0 / D)
                    P.op("dve", "tensor_scalar", out=h2.ap[:, s, :], in0=xt.ap, scalar1=rstd.ap[:, 0:1], scalar2=None, op0=ALU.mult,
                         r=[xt, rstd], w=[h2])
                    pb = ps[2 + s]
                    pv = pb.ap.bitcast(BF16).rearrange("p (k t) -> p k t", k=8)
                    for kc in range(8):
                        P.op("pe", "transpose", out=pv[:, kc, :], in_=h2.ap[:, s, kc * 128:(kc + 1) * 128], identity=identb.ap,
                             r=[h2, identb], w=[pb])
                    P.op("act", "copy", out=h2T.ap[:, :, s * 128:(s + 1) * 128], in_=pv, r=[pb], w=[h2T])
                for fc in range(32):
                    pb = ps[4 + (fc % 4)]
                    for kc in range(8):
                        P.op("pe", "matmul", out=pb.ap[:, 0:T4], lhsT=wu.ap[:, kc, fc * 128:(fc + 1) * 128], rhs=h2T.ap[:, kc, :],
                             start=(kc == 0), stop=(kc == 7), r=[wu, h2T], w=[pb])
                    P.op("act", "activation", out=rl.ap, in_=pb.ap[:, 0:T4], func=AF.Relu, r=[pb], w=[rl])
                    P.op("dve" if fc % 2 == 0 else "pool", "tensor_tensor", out=fT.ap[:, fc, :], in0=rl.ap, in1=rl.ap, op=ALU.mult, r=[rl], w=[fT])
                for s in range(2):
                    banks = [ps[0], ps[1]]
                    for cg in range(2):
                        for fc in range(32):
                            P.op("pe", "matmul", out=banks[cg].ap, lhsT=fT.ap[:, fc, s * 128:(s + 1) * 128], rhs=wd.ap[:, fc, cg * 512:(cg + 1) * 512],
                                 start=(fc == 0), stop=(fc == 31), r=[fT, wd], w=[banks[cg]])
                    P.dma("sp", xt.ap, sc["X1"][t0 + s * 128:t0 + (s + 1) * 128, :], r=[DB(("X1", sn))], w=[xt])
                    P.op("act", "copy", out=mo.ap[:, 0:512], in_=banks[0].ap, r=[banks[0]], w=[mo])
                    P.op("dve", "tensor_copy", out=mo.ap[:, 512:1024], in_=banks[1].ap, r=[banks[1]], w=[mo])
                    P.op("act", "activation", out=h2.ap[:, s, :], in_=mo.ap, func=AF.Square, accum_out=ssq.ap[:, 0:1], r=[mo], w=[h2, ssq])
                    rms_rstd(ssq, rstd, 1, 1.0 / D)
                    P.op("dve", "scalar_tensor_tensor", out=mo.ap, in0=mo.ap, scalar=rstd.ap[:, 0:1], in1=nrm_b.ap[:, 1, :],
                         op0=ALU.mult, op1=ALU.mult, r=[mo, rstd, nrm_b], w=[mo])
                    P.op("pool", "tensor_tensor", out=xt.ap, in0=mo.ap, in1=xt.ap, op=ALU.add, r=[mo, xt], w=[xt])
                    P.dma("sp", Y[sn][t0 + s * 128:t0 + (s + 1) * 128, :], xt.ap, r=[xt], w=[DB(("Y", sn))])
        P.barrier()
        P.emit()


def phases_rest(L):
    sa = L["stop_after"]
    if sa >= 2:
        phase_gdn(L)
    if sa >= 3:
        phase_attn(L)
    if sa >= 4:
        phase_mlp(L)


def phase_gdn(L):
    nc, P, ps, S, seqs, identb, onesb, onesf = L["nc"], L["P"], L["ps"], L["S"], L["seqs"], L["identb"], L["onesb"], L["onesf"]
    DB, epsc, rms_rstd = L["DB"], L["epsc"], L["rms_rstd"]
    with ExitStack() as es:
        def tile(name, shape, dt):
            return Tile(_chk(nc, es.enter_context(nc.sbuf_tensor(name, list(shape), dt))).ap(), name)
        cw_t = tile("cw_t", [128, 12, 5], F32)
        P.dma("sp", cw_t.ap, L["conv_w_r"], w=[cw_t])
        raw = [tile("raw%d" % i, [128, 12, TT + 4], BF16) for i in range(2)]
        acc = [tile("acc%d" % i, [128, TT], F32) for i in range(2)]
        sl = [tile("sl%d" % i, [128, TT], F32) for i in range(2)]
        sq = [tile("sq%d" % i, [128, TT], BF16) for i in range(2)]
        rs = [tile("rs%d" % i, [128, TT], F32) for i in range(2)]
        fm = [tile("fm%d" % i, [128, TT], BF16) for i in range(3)]
        tk = [tile("tk%d" % i, [128, 4, 128], BF16) for i in range(2)]
        it = 0
        n = 0
        for sn, T in seqs:
            sc = S[sn]
            for ti in range(T // TT):
                t0 = ti * TT
                r_ = raw[it % 2]
                lo, hi = max(t0 - 2, 0), min(t0 + TT + 2, T)
                if t0 == 0:
                    P.op("pool", "memset", ap=r_.ap[:, :, 0:2], constant=0.0, w=[r_])
                if t0 + TT == T:
                    P.op("pool", "memset", ap=r_.ap[:, :, TT + 2:TT + 4], constant=0.0, w=[r_])
                P.dma("sp", r_.ap[:, :, lo - (t0 - 2):hi - (t0 - 2)], sc["A"].rearrange("(c p) t -> p c t", p=128)[:, :, lo:hi],
                      r=[DB(("A", sn))], w=[r_])
                for c in range(12):
                    kind, h = c // 4, c % 4
                    a_, s_ = acc[n % 2], sl[n % 2]
                    ce = "dve"
                    P.op(ce, "tensor_scalar", out=a_.ap, in0=r_.ap[:, c, 0:TT], scalar1=cw_t.ap[:, c, 0:1], scalar2=None, op0=ALU.mult,
                         r=[r_, cw_t], w=[a_])
                    for k in range(1, 5):
                        P.op(ce, "scalar_tensor_tensor", out=a_.ap, in0=r_.ap[:, c, k:k + TT], scalar=cw_t.ap[:, c, k:k + 1], in1=a_.ap,
                             op0=ALU.mult, op1=ALU.add, r=[r_, cw_t, a_], w=[a_])
                    P.op("act", "activation", out=s_.ap, in_=a_.ap, func=AF.Silu, r=[a_], w=[s_])
                    f_ = fm[n % 3]
                    if kind == 2:
                        P.op("pool", "tensor_copy", out=f_.ap, in_=s_.ap, r=[s_], w=[f_])
                    else:
                        q_, rs_ = sq[n % 2], rs[n % 2]
                        pb = ps[n % 2]
                        P.op("act", "activation", out=q_.ap, in_=s_.ap, func=AF.Square, r=[s_], w=[q_])
                        P.op("pe", "matmul", out=pb.ap, lhsT=onesb.ap, rhs=q_.ap, start=True, stop=True, r=[onesb, q_], w=[pb])
                        P.op("act", "activation", out=rs_.ap, in_=pb.ap, func=AF.Ln, bias=epsc.ap[:, 0:1], scale=1.0, r=[pb, epsc], w=[rs_])
                        P.op("act", "activation", out=rs_.ap, in_=rs_.ap, func=AF.Exp, scale=-0.5, r=[rs_], w=[rs_])
                        P.op("dve", "scalar_tensor_tensor", out=f_.ap, in0=s_.ap, scalar=(128.0 ** -0.5 if kind == 0 else 1.0), in1=rs_.ap,
                             op0=ALU.mult, op1=ALU.mult, r=[s_, rs_], w=[f_])
                        dst = sc["GQ"] if kind == 0 else sc["GK"]
                        P.dma("sp", dst[h * 128:(h + 1) * 128, t0:t0 + TT], f_.ap, r=[f_], w=[DB(("GQK", sn))])
                    if kind >= 1:
                        tb = ps[2 + (n % 2)]
                        tv = tb.ap.bitcast(BF16)[:, 0:512].rearrange("p (a c) -> p a c", a=4)
                        for s in range(4):
                            P.op("pe", "transpose", out=tv[:, s, :], in_=f_.ap[:, s * 128:(s + 1) * 128], identity=identb.ap,
                                 r=[f_, identb], w=[tb])
                        t_ = tk[n % 2]
                        P.op("act", "copy", out=t_.ap, in_=tv, r=[tb], w=[t_])
                        dst = sc["GKt"] if kind == 1 else sc["GVt"]
                        P.dma("sp", dst[t0:t0 + TT, h * 128:(h + 1) * 128].rearrange("(s p) d -> p s d", p=128), t_.ap,
                              r=[t_], w=[DB(("GKV", sn))])
                    n += 1
                it += 1
        P.barrier()
        P.emit()
    if L["stop_after"] == 2 and L.get("gdn_pre_only"):
        return
    with ExitStack() as es:
        def tile(name, shape, dt):
            return Tile(_chk(nc, es.enter_context(nc.sbuf_tensor(name, list(shape), dt))).ap(), name)

        cf = tile("cf", [128, len(CONST_NAMES), 128], F32)
        P.dma("sp", cf.ap, L["cst"].rearrange("n p j -> p n j"), w=[cf])
        C = {n: cf.ap[:, i, :] for i, n in enumerate(CONST_NAMES)}
        gnw_b = tile("gnw_b", [128, 128], F32)
        P.dma("sp", gnw_b.ap, L["hnorm"][0:1, :].partition_broadcast(128), w=[gnw_b])

        def b3(ap2):
            return ap2.unsqueeze(1).to_broadcast([128, 4, 128])

        def bj(ap2):
            return ap2.unsqueeze(2).to_broadcast([128, 4, 128])

        def v4(t):
            return t.ap.rearrange("p (h j) -> p h j", h=4)

        R = []
        for d in range(2):
            r = {}
            for nm in ("KT", "QT", "Kt", "Vt"):
                r[nm] = [tile("%s%d_%d" % (nm, d, i), [128, 512], BF16) for i in range(2)]
            r["G"] = [tile("G%d_%d" % (d, i), [128, 16], F32) for i in range(2)]
            for nm in ("sm", "ex"):
                r[nm] = tile("%s%d" % (nm, d), [128, 16], F32)
            r["bgc"] = tile("bgc%d" % d, [128, 4], F32)
            for nm in ("Gm", "Ep", "D4", "Dsb", "PT", "u4", "Oq", "O4", "S4", "sz", "o2"):
                r[nm] = tile("%s%d" % (nm, d), [128, 512], F32)
            for nm in ("A0", "A1", "B0", "B1", "Aqk", "AqkT", "PTb", "Vb", "kbg", "ke", "wT", "vn", "S4b", "zb", "onb", "mxs"):
                r[nm] = tile("%s%d" % (nm, d), [128, 512], BF16)
            r["ssq"] = tile("gssq%d" % d, [128, 4], F32)
            r["rstd"] = tile("grstd%d" % d, [128, 4], F32)
            r["banks"] = [ps[4 * d + i] for i in range(4)]
            r["nb"] = 0
            R.append(r)

        def nbank(r):
            b = r["banks"][r["nb"] % 4]
            r["nb"] += 1
            return b

        def block_step(sn, T, d, b, step, NB):
            sc = S[sn]
            r = R[d]
            t0 = b * 128
            par = step % 2
            KT, QT, Kt, Vt, G = r["KT"][par], r["QT"][par], r["Kt"][par], r["Vt"][par], r["G"][par]
            P.dma("sp", v4(KT), sc["GK"].rearrange("(h p) t -> p h t", p=128)[:, :, t0:t0 + 128], r=[DB(("GQK", sn))], w=[KT])
            P.dma("sp", v4(QT), sc["GQ"].rearrange("(h p) t -> p h t", p=128)[:, :, t0:t0 + 128], r=[DB(("GQK", sn))], w=[QT])
            P.dma("sp", Kt.ap, sc["GKt"][t0:t0 + 128, :], r=[DB(("GKV", sn))], w=[Kt])
            P.dma("sp", Vt.ap, sc["GVt"][t0:t0 + 128, :], r=[DB(("GKV", sn))], w=[Vt])
            P.dma("sp", G.ap, sc["G"][t0:t0 + 128, :], r=[DB(("G", sn))], w=[G])
            beta = G.ap[:, 4 * d:4 * d + 4]
            gg = G.ap[:, 8 + 4 * d:12 + 4 * d]
            cum = C["cum_f"] if d == 0 else C["cum_b"]
            pos = C["pos_f"] if d == 0 else C["pos_b"]
            strict = C["strict_f"] if d == 0 else C["strict_b"]
            sm, ex, bgc = r["sm"], r["ex"], r["bgc"]
            pb = nbank(r)
            P.op("pe", "matmul", out=pb.ap[:, 0:4], lhsT=cum, rhs=gg, start=True, stop=True, r=[cf, G], w=[pb])
            P.op("pe", "matmul", out=pb.ap[:, 4:8], lhsT=C["same"], rhs=gg, start=True, stop=True, r=[cf, G], w=[pb])
            P.op("pe", "matmul", out=pb.ap[:, 8:12], lhsT=C["c0"], rhs=gg, start=True, stop=True, r=[cf, G], w=[pb])
            P.op("pe", "matmul", out=pb.ap[:, 12:16], lhsT=C["c1"], rhs=gg, start=True, stop=True, r=[cf, G], w=[pb])
            P.op("dve", "tensor_copy", out=sm.ap, in_=pb.ap[:, 0:16], r=[pb], w=[sm])
            P.op("dve", "tensor_tensor", out=sm.ap[:, 4:8], in0=sm.ap[:, 4:8], in1=sm.ap[:, 0:4], op=ALU.subtract, r=[sm], w=[sm])
            P.op("act", "activation", out=ex.ap, in_=sm.ap, func=AF.Exp, r=[sm], w=[ex])
            P.op("dve", "tensor_tensor", out=bgc.ap, in0=ex.ap[:, 0:4], in1=beta, op=ALU.mult, r=[ex, G], w=[bgc])
            Gm, Ep, D4, Dsb = r["Gm"], r["Ep"], r["D4"], r["Dsb"]
            P.op("pool", "tensor_tensor", out=v4(Gm), in0=b3(cum), in1=bj(gg), op=ALU.mult, r=[cf, G], w=[Gm])
            pe_ = nbank(r)
            P.op("pe", "matmul", out=pe_.ap, lhsT=onesf.ap, rhs=Gm.ap, start=True, stop=True, r=[onesf, Gm], w=[pe_])
            P.op("dve", "tensor_tensor", out=v4(Ep), in0=pe_.ap.rearrange("p (h j) -> p h j", h=4), in1=b3(pos), op=ALU.add,
                 r=[pe_, cf], w=[Ep])
            for h in range(4):
                P.op("act", "activation", out=v4(D4)[:, h, :], in_=v4(Ep)[:, h, :], func=AF.Exp, bias=sm.ap[:, h:h + 1], scale=-1.0,
                     r=[Ep, sm], w=[D4])
            P.op("pool", "tensor_tensor", out=v4(Dsb), in0=v4(D4), in1=b3(strict), op=ALU.mult, r=[D4, cf], w=[Dsb])
            P.op("pool", "tensor_tensor", out=v4(Dsb), in0=v4(Dsb), in1=bj(beta), op=ALU.mult, r=[Dsb, G], w=[Dsb])
            pk, pq = nbank(r), nbank(r)
            for h in range(4):
                P.op("pe", "matmul", out=pk.ap[:, h * 128:(h + 1) * 128], lhsT=v4(KT)[:, h, :], rhs=v4(KT)[:, h, :], start=True, stop=True,
                     r=[KT], w=[pk])
            for h in range(4):
                P.op("pe", "matmul", out=pq.ap[:, h * 128:(h + 1) * 128], lhsT=v4(QT)[:, h, :], rhs=v4(KT)[:, h, :], start=True, stop=True,
                     r=[QT, KT], w=[pq])
            A, B = [r["A0"], r["A1"]], [r["B0"], r["B1"]]
            Aqk, AqkT, PT, PTb = r["Aqk"], r["AqkT"], r["PT"], r["PTb"]
            P.op("dve", "tensor_tensor", out=A[0].ap, in0=pk.ap, in1=Dsb.ap, op=ALU.mult, r=[pk, Dsb], w=[A[0]])
            P.op("dve", "tensor_tensor", out=Aqk.ap, in0=pq.ap, in1=D4.ap, op=ALU.mult, r=[pq, D4], w=[Aqk])
            tb = nbank(r)
            tvA = tb.ap.bitcast(BF16)[:, 0:512].rearrange("p (a c) -> p a c", a=4)
            tvQ = tb.ap.bitcast(BF16)[:, 512:1024].rearrange("p (a c) -> p a c", a=4)
            for h in range(4):
                P.op("pe", "transpose", out=tvA[:, h, :], in_=v4(A[0])[:, h, :], identity=identb.ap, r=[A[0], identb], w=[tb])
            for h in range(4):
                P.op("pe", "transpose", out=tvQ[:, h, :], in_=v4(Aqk)[:, h, :], identity=identb.ap, r=[Aqk, identb], w=[tb])
            P.op("act", "copy", out=v4(B[0]), in_=tvA, r=[tb], w=[B[0]])
            P.op("act", "copy", out=v4(AqkT), in_=tvQ, r=[tb], w=[AqkT])
            P.op("pool", "tensor_tensor", out=v4(PT), in0=b3(C["ident"]), in1=v4(B[0]), op=ALU.subtract, r=[cf, B[0]], w=[PT])
            P.op("pool", "tensor_copy", out=PTb.ap, in_=PT.ap, r=[PT], w=[PTb])
            for k in range(5):
                Ak, Bk, An, Bn = A[k % 2], B[k % 2], A[(k + 1) % 2], B[(k + 1) % 2]
                pa = nbank(r)
                for h in range(4):
                    P.op("pe", "matmul", out=pa.ap[:, h * 128:(h + 1) * 128], lhsT=v4(Bk)[:, h, :], rhs=v4(Ak)[:, h, :], start=True, stop=True,
                         r=[Ak, Bk], w=[pa])
                if k < 4:
                    pbb = nbank(r)
                    for h in range(4):
                        P.op("pe", "matmul", out=pbb.ap[:, h * 128:(h + 1) * 128], lhsT=v4(Ak)[:, h, :], rhs=v4(Bk)[:, h, :], start=True, stop=True,
                             r=[Ak, Bk], w=[pbb])
                P.op("act", "copy", out=An.ap, in_=pa.ap, r=[pa], w=[An])
                if k < 4:
                    P.op("dve", "tensor_copy", out=Bn.ap, in_=pbb.ap, r=[pbb], w=[Bn])
                pd = nbank(r)
                for h in range(4):
                    P.op("pe", "matmul", out=pd.ap[:, h * 128:(h + 1) * 128], lhsT=v4(An)[:, h, :], rhs=v4(PTb)[:, h, :], start=True, stop=True,
                         r=[An, PTb], w=[pd])
                P.op("dve", "tensor_tensor", out=PT.ap, in0=pd.ap, in1=PT.ap, op=ALU.add, r=[pd, PT], w=[PT])
                P.op("pool", "tensor_copy", out=PTb.ap, in_=PT.ap, r=[PT], w=[PTb])
            Vb, kbg, ke, u4, wT = r["Vb"], r["kbg"], r["ke"], r["u4"], r["wT"]
            P.op("pool", "tensor_tensor", out=v4(Vb), in0=v4(Vt), in1=bj(beta), op=ALU.mult, r=[Vt, G], w=[Vb])
            P.op("pool", "tensor_tensor", out=v4(kbg), in0=v4(Kt), in1=bj(bgc.ap), op=ALU.mult, r=[Kt, bgc], w=[kbg])
            P.op("pool", "tensor_tensor", out=v4(ke), in0=v4(Kt), in1=bj(ex.ap[:, 4:8]), op=ALU.mult, r=[Kt, ex], w=[ke])
            pu, pw = nbank(r), nbank(r)
            for h in range(4):
                P.op("pe", "matmul", out=pu.ap[:, h * 128:(h + 1) * 128], lhsT=v4(PTb)[:, h, :], rhs=v4(Vb)[:, h, :], start=True, stop=True,
                     r=[PTb, Vb], w=[pu])
            for h in range(4):
                P.op("pe", "matmul", out=pw.ap[:, h * 128:(h + 1) * 128], lhsT=v4(kbg)[:, h, :], rhs=v4(PTb)[:, h, :], start=True, stop=True,
                     r=[PTb, kbg], w=[pw])
            P.op("act", "copy", out=u4.ap, in_=pu.ap, r=[pu], w=[u4])
            P.op("dve", "tensor_copy", out=wT.ap, in_=pw.ap, r=[pw], w=[wT])
            S4, S4b, vn, Oq, O4 = r["S4"], r["S4b"], r["vn"], r["Oq"], r["O4"]
            for c in ((0, 1) if d == 0 else (1, 0)):
                pc = slice(c * 64, (c + 1) * 64)
                p1, po1 = nbank(r), nbank(r)
                for h in range(4):
                    P.op("pe", "matmul", out=p1.ap[pc, h * 128:(h + 1) * 128], lhsT=v4(wT)[:, h, pc], rhs=v4(S4b)[:, h, :], start=True, stop=True,
                         r=[wT, S4b], w=[p1])
                for h in range(4):
                    P.op("pe", "matmul", out=po1.ap[pc, h * 128:(h + 1) * 128], lhsT=v4(QT)[:, h, pc], rhs=v4(S4b)[:, h, :], start=True, stop=True,
                         r=[QT, S4b], w=[po1])
                P.op("dve", "tensor_tensor", out=vn.ap[pc, :], in0=u4.ap[pc, :], in1=p1.ap[pc, :], op=ALU.subtract, r=[u4, p1], w=[vn])
                P.op("dve", "tensor_tensor", out=v4(Oq)[pc], in0=po1.ap[pc, :].rearrange("p (h j) -> p h j", h=4),
                     in1=ex.ap[pc, 0:4].unsqueeze(2).to_broadcast([64, 4, 128]), op=ALU.mult, r=[po1, ex], w=[Oq])
                po2, pds = nbank(r), nbank(r)
                for h in range(4):
                    P.op("pe", "matmul", out=po2.ap[pc, h * 128:(h + 1) * 128], lhsT=v4(AqkT)[pc, h, pc], rhs=v4(vn)[pc, h, :], start=True, stop=True,
                         r=[AqkT, vn], w=[po2])
                for h in range(4):
                    P.op("pe", "matmul", out=pds.ap[:, h * 128:(h + 1) * 128], lhsT=v4(ke)[pc, h, :], rhs=v4(vn)[pc, h, :], start=True, stop=True,
                         r=[ke, vn], w=[pds])
                P.op("dve", "tensor_tensor", out=O4.ap[pc, :], in0=po2.ap[pc, :], in1=Oq.ap[pc, :], op=ALU.add, r=[po2, Oq], w=[O4])
                P.op("pool", "tensor_tensor", out=v4(S4), in0=v4(S4), in1=bj(ex.ap[:, 8 + 4 * c:12 + 4 * c]), op=ALU.mult, r=[S4, ex], w=[S4])
                P.op("dve", "tensor_tensor", out=S4.ap, in0=pds.ap, in1=S4.ap, op=ALU.add, r=[pds, S4], w=[S4])
                P.op("act", "copy", out=S4b.ap, in_=S4.ap, r=[S4], w=[S4b])
            mine, other = ("OF", "OB") if d == 0 else ("OB", "OF")
            if step < NB // 2:
                P.dma("sp", sc[mine][t0:t0 + 128, :], O4.ap, r=[O4], w=[DB((mine, sn, b))])
            else:
                o2, sz, zb, onb, mxs = r["o2"], r["sz"], r["zb"], r["onb"], r["mxs"]
                ssq, rstd = r["ssq"], r["rstd"]
                P.dma("sp", o2.ap, sc[other][t0:t0 + 128, :], r=[DB((other, sn, b))], w=[o2])
                P.dma("sp", zb.ap, sc["Z"][t0:t0 + 128, :], r=[DB(("Z", sn))], w=[zb])
                P.op("pool", "tensor_tensor", out=o2.ap, in0=o2.ap, in1=O4.ap, op=ALU.add, r=[o2, O4], w=[o2])
                P.op("pool", "tensor_tensor", out=sz.ap, in0=o2.ap, in1=o2.ap, op=ALU.mult, r=[o2], w=[sz])
                P.op("dve", "tensor_reduce", out=ssq.ap, in_=v4(sz), axis=AX.X, op=ALU.add, r=[sz], w=[ssq])
                rms_rstd(ssq, rstd, 4, 1.0 / 128)
                P.op("act", "activation", out=sz.ap, in_=zb.ap, func=AF.Silu, r=[zb], w=[sz])
                P.op("dve", "tensor_tensor", out=v4(o2), in0=v4(o2), in1=bj(rstd.ap), op=ALU.mult, r=[o2, rstd], w=[o2])
                P.op("pool", "tensor_tensor", out=v4(o2), in0=v4(o2), in1=b3(gnw_b.ap), op=ALU.mult, r=[o2, gnw_b], w=[o2])
                P.op("dve", "tensor_tensor", out=onb.ap, in0=o2.ap, in1=sz.ap, op=ALU.mult, r=[o2, sz], w=[onb])
                tb2 = nbank(r)
                tv2 = tb2.ap.bitcast(BF16)[:, 0:512].rearrange("p (a c) -> p a c", a=4)
                for h in range(4):
                    P.op("pe", "transpose", out=tv2[:, h, :], in_=v4(onb)[:, h, :], identity=identb.ap, r=[onb, identb], w=[tb2])
                P.op("act", "copy", out=v4(mxs), in_=tv2, r=[tb2], w=[mxs])
                P.dma("sp", sc["MIX"][0:512, :].rearrange("(h p) t -> p h t", p=128)[:, :, t0:t0 + 128], v4(mxs), r=[mxs], w=[DB(("MIX", sn))])

        for sn, T in seqs:
            NB = T // 128
            for d in range(2):
                P.op("pool", "memset", ap=R[d]["S4"].ap, constant=0.0, w=[R[d]["S4"]])
                P.op("pool", "memset", ap=R[d]["S4b"].ap, constant=0.0, w=[R[d]["S4b"]])
            for step in range(NB):
                block_step(sn, T, 0, step, step, NB)
                block_step(sn, T, 1, NB - 1 - step, step, NB)
        P.barrier()
        P.emit()
```

```python
import numpy as np
import concourse.bass as bass
import concourse.mybir as mybir
from concourse.bass_utils import run_bass_kernel_spmd

F32 = mybir.dt.float32
BF16 = mybir.dt.bfloat16
AF = mybir.ActivationFunctionType
ALU = mybir.AluOpType
AX = mybir.AxisListType

ENGS = ("pe", "act", "dve", "pool", "sp")
NS = 8


class Buf:
    __slots__ = ("w", "r", "rd", "name", "excl")

    def __init__(self, name=""):
        self.excl = False
        self.w = None
        self.r = {}
        self.rd = []
        self.name = name


class Prog:
    def __init__(self, nc):
        self.nc = nc
        self.sem = {e: nc.alloc_semaphore("s_" + e) for e in ENGS}
        self.dsem = {q: [nc.alloc_semaphore("d_%s_%d" % (q, i)) for i in range(NS)]
                     for q in ("sp", "act", "pool")}
        self.dcount = {q: 0 for q in self.dsem}
        self.base = {e: 0 for e in ENGS}
        self.rec = {e: [] for e in ENGS}
        self.off = {e: 0 for e in ENGS}
        self.msmap = {e: {} for e in ENGS}
        self.waited = {}
        self.limit = None
        self.nrec = 0

    def _need(self, eng, tok, waits):
        if tok is None:
            return
        if tok[0] == "c":
            _, se, gidx = tok
            if se == "pe" and eng == "pe":
                return
            key = (eng, se)
            if self.waited.get(key, -1) >= gidx:
                return
            self.waited[key] = gidx
            li = gidx - self.off[se]
            if li >= 0:
                self.rec[se][li][3] = True
            waits.append(tok)
        else:
            _, q, n = tok
            key = (eng, q, n % NS)
            if self.waited.get(key, -1) >= n:
                return
            self.waited[key] = n
            waits.append(tok)

    def _deps(self, eng, tok, r, w, waits):
        rx = [b for b in r if b.excl]
        if rx:
            r = [b for b in r if not b.excl]
            w = list(w) + [b for b in rx if b not in w]
        for b in r:
            self._need(eng, b.w, waits)
        for b in w:
            self._need(eng, b.w, waits)
            for se, gi in b.r.items():
                self._need(eng, ("c", se, gi), waits)
            for t in b.rd:
                self._need(eng, t, waits)
        for b in r:
            if tok[0] == "c":
                b.r[tok[1]] = tok[2]
            else:
                b.rd.append(tok)
                if len(b.rd) > 3 * NS:
                    b.rd = b.rd[-3 * NS:]
        for b in w:
            b.w = tok
            b.r = {}
            b.rd = []

    def op(self, eng, meth, r=(), w=(), **kw):
        fn = (meth, kw)
        self.nrec += 1
        if self.limit is not None and self.nrec > self.limit:
            return None
        waits = []
        gidx = self.off[eng] + len(self.rec[eng])
        tok = ("c", eng, gidx)
        self._deps(eng, tok, r, w, waits)
        self.rec[eng].append(["op", fn, waits, False])
        return tok

    def dma(self, q, out, in_, r=(), w=(), slow=False):
        self.nrec += 1
        if self.limit is not None and self.nrec > self.limit:
            return None
        waits = []
        n = self.dcount[q]
        self.dcount[q] += 1
        tok = ("d", q, n)
        if n >= NS:
            self._need(q, ("d", q, n - NS), waits)
        self._deps(q, tok, r, w, waits)
        self.rec[q].append(["dma", (out, in_, q, n, slow), waits, False])
        return tok

    def barrier(self, bufs=()):
        for e in ENGS:
            waits = []
            for se in ENGS:
                if se == e:
                    continue
                for li in range(len(self.rec[se]) - 1, -1, -1):
                    if self.rec[se][li][0] == "op":
                        self._need(e, ("c", se, self.off[se] + li), waits)
                        break
            for q in self.dsem:
                for k in range(max(0, self.dcount[q] - NS), self.dcount[q]):
                    self._need(e, ("d", q, k), waits)
            self.rec[e].append(["nop", None, waits, False])

    def emit(self):
        nc = self.nc
        for e in ENGS:
            c = self.base[e]
            for li, rcd in enumerate(self.rec[e]):
                if rcd[0] == "op" and rcd[3]:
                    c += 1
                    self.msmap[e][self.off[e] + li] = c
            self.base[e] = c

        def run(e, engobj):
            for rcd in self.rec[e]:
                kind, fn, waits, ms = rcd
                for t in waits:
                    if t[0] == "c":
                        engobj.wait_ge(self.sem[t[1]], self.msmap[t[1]][t[2]])
                    else:
                        engobj.wait_ge(self.dsem[t[1]][t[2] % NS], 16 * (t[2] // NS + 1))
                if kind == "op":
                    ins = getattr(engobj, fn[0])(**fn[1])
                    if ms:
                        ins.then_inc(self.sem[e], 1)
                elif kind == "dma":
                    out, in_, q, n, slow = fn
                    if slow:
                        ins = engobj.dma_start(out=out, in_=in_, allow_slow_non_contiguous=True)
                    else:
                        ins = engobj.dma_start(out=out, in_=in_)
                    ins.then_inc(self.dsem[q][n % NS], 16)

        with nc.Block() as block:
            @block.tensor
            def _(eng):
                run("pe", eng)

            @block.scalar
            def _(eng):
                run("act", eng)

            @block.vector
            def _(eng):
                run("dve", eng)

            @block.gpsimd
            def _(eng):
                run("pool", eng)

            @block.sync
            def _(eng):
                run("sp", eng)

        for e in ENGS:
            self.off[e] += len(self.rec[e])
            self.rec[e] = []

from contextlib import ExitStack

D = 1024
NKC = 8
DFF = 4096
EPS = 1e-6
TT = 512
FM_COLS = 3072
TM0 = 3072
WIN_COLS = 3856
GRID_W = 64


SBUF_LIMIT = 196608


def _chk(nc, handle):
    m = nc.lookup_mloc(handle)
    nbytes = 1
    for d in list(m.dims)[1:]:
        nbytes *= int(d)
    assert int(m.addr) + nbytes <= SBUF_LIMIT, ("SBUF overflow past physical limit", handle.name, int(m.addr), nbytes)
    return handle


class Tile(Buf):
    __slots__ = ("ap",)

    def __init__(self, ap, name=""):
        Buf.__init__(self, name)
        self.ap = ap


def rope_tables(T):
    t = np.arange(T)
    row = (t // GRID_W).astype(np.float32)
    col = (t % GRID_W).astype(np.float32)
    nf = 32
    inv = (np.float32(10000.0) ** (-np.arange(nf, dtype=np.float32) / np.float32(nf))).astype(np.float32)
    cosT = np.zeros((128, T), np.float32)
    sinT = np.zeros((128, T), np.float32)
    for d in range(128):
        half = d // 64
        idx = d % 64
        f = idx % 32
        pos = row if half == 0 else col
        ang = (pos * inv[f]).astype(np.float32)
        cosT[d] = np.cos(ang)
        sinT[d] = np.sin(ang) * (-1.0 if idx < 32 else 1.0)
    return cosT, sinT


def rope_perm():
    p = np.zeros(128, np.int64)
    for d in range(128):
        idx = d % 64
        p[d] = d + 32 if idx < 32 else d - 32
    return p


def gdn_masks():
    t = np.arange(128)[:, None]
    j = np.arange(128)[None, :]
    same = (t // 64) == (j // 64)
    m = {}
    m["cum_f"] = (same & (t <= j)).astype(np.float32)
    m["cum_b"] = (same & (t >= j)).astype(np.float32)
    m["same"] = same.astype(np.float32)
    i = t
    m["pos_f"] = np.where(same & (i >= j), 0.0, 30000.0).astype(np.float32)
    m["pos_b"] = np.where(same & (i <= j), 0.0, 30000.0).astype(np.float32)
    m["strict_f"] = (same & (i > j)).astype(np.float32)
    m["strict_b"] = (same & (i < j)).astype(np.float32)
    m["c0"] = np.tile((np.arange(128) < 64).astype(np.float32)[:, None], (1, 128))
    m["c1"] = np.tile((np.arange(128) >= 64).astype(np.float32)[:, None], (1, 128))
    return m


CONST_NAMES = ["ident", "cum_f", "cum_b", "same", "pos_f", "pos_b", "strict_f", "strict_b", "c0", "c1"]


def build(Tp, Ts, debug=False, stop_after=9):
    nc = bass.Bass("TRN2", target_bir_lowering=False)
    P = Prog(nc)
    seqs = [("p", Tp), ("s", Ts)]
    Tmax = max(Tp, Ts)

    def din(name, shape, dt=F32):
        return nc.dram_tensor(name, list(shape), dt, kind="ExternalInput").ap()

    def dscr(name, shape, dt):
        if debug:
            return nc.dram_tensor(name, list(shape), dt, kind="ExternalOutput").ap()
        return nc.dram_tensor(name, list(shape), dt).ap()

    X = {"p": din("x_p", [Tp, D]), "s": din("x_s", [Ts, D])}
    Y = {"p": nc.dram_tensor("y_p", [Tp, D], F32, kind="ExternalOutput").ap(),
         "s": nc.dram_tensor("y_s", [Ts, D], F32, kind="ExternalOutput").ap()}
    w_in = din("w_in_r", [D, WIN_COLS])
    w_out = din("w_out", [D, D])
    w_up = din("w_up", [D, DFF])
    w_down = din("w_down", [DFF, D])
    nrm = din("norms", [4, D])
    conv_w_r = din("conv_w_r", [128, 12, 5])
    nrm_r = din("nrm_r", [128, 4, NKC])
    hnorm_r = din("hnorm_r", [128, 5])
    gparam = din("gparam", [2, 8])
    hnorm = din("hnorm", [5, 128])
    cosT = din("cosT", [128, Tmax])
    sinT = din("sinT", [128, Tmax])
    cst = din("cst", [len(CONST_NAMES), 128, 128])

    S = {}
    for sn, T in seqs:
        S[sn] = dict(
            A=dscr("A_raw_" + sn, [1536, T], BF16),
            QK=dscr("QK_T_" + sn, [768, T], BF16),
            Z=dscr("Z_" + sn, [T, 512], BF16),
            G=dscr("G_" + sn, [T, 16], F32),
            V=dscr("V_" + sn, [T, 256], BF16),
            GQ=dscr("GQ_T_" + sn, [512, T], BF16),
            GK=dscr("GK_T_" + sn, [512, T], BF16),
            GKt=dscr("GKt_" + sn, [T, 512], BF16),
            GVt=dscr("GVt_" + sn, [T, 512], BF16),
            OF=dscr("OF_" + sn, [T, 512], F32),
            OB=dscr("OB_" + sn, [T, 512], F32),
            MIX=dscr("MIX_T_" + sn, [1024, T], BF16),
            X1=dscr("X1_" + sn, [T, 1024], F32),
        )
    dbufs = {}

    def DB(key):
        if key not in dbufs:
            dbufs[key] = Buf(str(key))
        return dbufs[key]

    ps = [Tile(nc.alloc_psum_tensor("ps%d" % i, [128, 512], F32).ap(), "ps%d" % i) for i in range(8)]
    for t in ps:
        t.excl = True

    with ExitStack() as g_es:
        def gtile(name, shape, dt):
            return Tile(_chk(nc, g_es.enter_context(nc.sbuf_tensor(name, list(shape), dt))).ap(), name)

        identf = gtile("identf", [128, 128], F32)
        P.dma("sp", identf.ap, cst[0], w=[identf])
        identb = gtile("identb", [128, 128], BF16)
        P.op("dve", "tensor_copy", out=identb.ap, in_=identf.ap, r=[identf], w=[identb])
        onesb = gtile("onesb", [128, 128], BF16)
        P.op("pool", "memset", ap=onesb.ap, constant=1.0, w=[onesb])
        onesf = gtile("onesf", [128, 128], F32)
        P.op("pool", "memset", ap=onesf.ap, constant=1.0, w=[onesf])
        epsc = gtile("epsc", [128, 1], F32)
        P.op("pool", "memset", ap=epsc.ap, constant=EPS, w=[epsc])
        nrm_t = gtile("nrm_t", [128, 4, NKC], F32)
        P.dma("sp", nrm_t.ap, nrm_r, w=[nrm_t])
        hn_t = gtile("hn_t", [128, 5], F32)
        P.dma("sp", hn_t.ap, hnorm_r, w=[hn_t])
        gp_b = gtile("gp_b", [128, 2, 8], F32)
        P.dma("sp", gp_b.ap[:, 0, :], gparam[0:1, :].partition_broadcast(128), w=[gp_b])
        P.dma("sp", gp_b.ap[:, 1, :], gparam[1:2, :].partition_broadcast(128), w=[gp_b])
        negA = gtile("negA", [128, 8], F32)
        P.op("act", "activation", out=negA.ap, in_=gp_b.ap[:, 0, :], func=AF.Exp, r=[gp_b], w=[negA])
        P.op("dve", "tensor_scalar", out=negA.ap, in0=negA.ap, scalar1=-1.0, scalar2=None, op0=ALU.mult,
             r=[negA], w=[negA])

        def load_weight_bf16(wt, st, src, nk, cols, scale_sel, stage_cols):
            n = 0
            for kc in range(nk):
                for c0 in range(0, cols, stage_cols):
                    cn = min(stage_cols, cols - c0)
                    s_ = st[n % 2]
                    P.dma("sp", s_.ap[:, 0:cn], src[kc * 128:(kc + 1) * 128, c0:c0 + cn], w=[s_])
                    dst = wt.ap[:, kc, c0:c0 + cn]
                    if scale_sel is None:
                        if n % 2 == 0:
                            P.op("act", "copy", out=dst, in_=s_.ap[:, 0:cn], r=[s_], w=[wt])
                        else:
                            P.op("dve", "tensor_copy", out=dst, in_=s_.ap[:, 0:cn], r=[s_], w=[wt])
                    else:
                        sc = nrm_t.ap[:, scale_sel, kc:kc + 1]
                        if n % 2 == 0:
                            P.op("act", "activation", out=dst, in_=s_.ap[:, 0:cn], func=AF.Copy, scale=sc,
                                 r=[s_, nrm_t], w=[wt])
                        else:
                            P.op("dve", "tensor_scalar", out=dst, in0=s_.ap[:, 0:cn], scalar1=sc, scalar2=None, op0=ALU.mult,
                                 r=[s_, nrm_t], w=[wt])
                    n += 1
            return wt

        def rms_rstd(ssq, rstd, n, inv_n):
            P.op("act", "activation", out=rstd.ap[:, 0:n], in_=ssq.ap[:, 0:n], func=AF.Ln, bias=epsc.ap[:, 0:1], scale=inv_n,
                 r=[ssq, epsc], w=[rstd])
            P.op("act", "activation", out=rstd.ap[:, 0:n], in_=rstd.ap[:, 0:n], func=AF.Exp, scale=-0.5,
                 r=[rstd], w=[rstd])

        with ExitStack() as es:
            def tile(name, shape, dt):
                return Tile(_chk(nc, es.enter_context(nc.sbuf_tensor(name, list(shape), dt))).ap(), name)

            wi = tile("wi", [128, NKC, WIN_COLS], BF16)
            with ExitStack() as ses:
                st = [Tile(_chk(nc, ses.enter_context(nc.sbuf_tensor("wst%d" % i, [128, 1928], F32))).ap()) for i in range(2)]
                load_weight_bf16(wi, st, w_in, NKC, WIN_COLS, 0, 1928)
                P.barrier()
                P.emit()
            xt = [tile("xt%d" % i, [128, 4, D], F32) for i in range(2)]
            hb = tile("hb", [128, 4, D], BF16)
            junk = tile("junk", [128, D], BF16)
            ssq = tile("ssq", [128, 4], F32)
            rstd = tile("rstd", [128, 4], F32)
            hT = [tile("hT%d" % i, [128, NKC, TT], BF16) for i in range(1)]
            a_st = [tile("a_st%d" % i, [128, 12, TT], BF16) for i in range(1)]
            qk_st = [tile("qk_st%d" % i, [128, 6, TT], BF16) for i in range(1)]
            z_st = [tile("z_st%d" % i, [128, 4, 512], BF16) for i in range(1)]
            v_st = [tile("v_st%d" % i, [128, 4, 256], BF16) for i in range(1)]
            g_st = [tile("g_st%d" % i, [128, 4, 16], F32) for i in range(1)]
            cs_t = [tile("cs_t%d" % i, [128, 2, TT], F32) for i in range(1)]
            sq_t = [tile("sq_t%d" % i, [128, TT], BF16) for i in range(2)]
            rs_t = [tile("rs_t%d" % i, [128, TT], F32) for i in range(2)]
            t1_t = [tile("t1_t%d" % i, [128, TT], F32) for i in range(2)]
            t2_t = [tile("t2_t%d" % i, [128, TT], F32) for i in range(2)]
            gt_t = [tile("gt_t%d" % i, [128, 16], F32) for i in range(2)]

            it = 0
            for sn, T in seqs:
                sc = S[sn]
                ntile = T // TT
                for ti in range(ntile):
                    t0 = ti * TT
                    x_ = xt[it % 2]
                    hT_ = hT[0]
                    P.dma("sp", x_.ap, X[sn][t0:t0 + TT, :].rearrange("(s p) d -> p s d", p=128), w=[x_])
                    cs_ = cs_t[0]
                    P.dma("sp", cs_.ap[:, 0, :], cosT[:, t0:t0 + TT], w=[cs_])
                    P.dma("sp", cs_.ap[:, 1, :], sinT[:, t0:t0 + TT], w=[cs_])
                    for s in range(4):
                        P.op("act", "activation", out=junk.ap, in_=x_.ap[:, s, :], func=AF.Square,
                                                                     accum_out=ssq.ap[:, s:s + 1],
                             r=[x_], w=[junk, ssq])
                    rms_rstd(ssq, rstd, 4, 1.0 / D)
                    for s in range(4):
                        P.op("dve", "tensor_scalar", out=hb.ap[:, s, :], in0=x_.ap[:, s, :],
                                                                        scalar1=rstd.ap[:, s:s + 1], scalar2=None, op0=ALU.mult,
                             r=[x_, rstd], w=[hb])
                    for s in range(4):
                        pb = ps[s % 2]
                        pv = pb.ap.bitcast(BF16).rearrange("p (k t) -> p k t", k=NKC)
                        for kc in range(NKC):
                            P.op("pe", "transpose", out=pv[:, kc, :], in_=hb.ap[:, s, kc * 128:(kc + 1) * 128],
                                                                               identity=identb.ap,
                                 r=[hb, identb], w=[pb])
                        eng = "act" if s % 2 == 0 else "dve"
                        if eng == "act":
                            P.op("act", "copy", out=hT_.ap[:, :, s * 128:(s + 1) * 128], in_=pv, r=[pb], w=[hT_])
                        else:
                            P.op("dve", "tensor_copy", out=hT_.ap[:, :, s * 128:(s + 1) * 128], in_=pv, r=[pb], w=[hT_])

                    def fm_mm(pb, c):
                        for kc in range(NKC):
                            P.op("pe", "matmul", out=pb.ap, lhsT=wi.ap[:, kc, c * 128:(c + 1) * 128], rhs=hT_.ap[:, kc, :],
                                                                            start=(kc == 0), stop=(kc == NKC - 1),
                                 r=[wi, hT_], w=[pb])

                    a_ = a_st[0]
                    for c in range(12):
                        pb = ps[2 + (c % 2)]
                        fm_mm(pb, c)
                        if c % 2 == 0:
                            P.op("act", "copy", out=a_.ap[:, c, :], in_=pb.ap, r=[pb], w=[a_])
                        else:
                            P.op("dve", "tensor_copy", out=a_.ap[:, c, :], in_=pb.ap, r=[pb], w=[a_])
                    P.dma("sp", sc["A"].rearrange("(c p) t -> p c t", p=128)[:, :, t0:t0 + TT], a_.ap, r=[a_], w=[DB(("A", sn))])
                    qk_ = qk_st[0]
                    for j in range(6):
                        pm = ps[2 + (j % 2)]
                        pp = ps[4 + (j % 2)]
                        pc = ps[6 + (j % 2)]
                        sq_, rs_, t1_, t2_ = sq_t[j % 2], rs_t[j % 2], t1_t[j % 2], t2_t[j % 2]
                        fm_mm(pm, 12 + j)
                        fm_mm(pp, 18 + j)
                        wsel = 1 if j < 4 else 3
                        P.op("act", "activation", out=sq_.ap, in_=pm.ap, func=AF.Square, r=[pm], w=[sq_])
                        P.op("pe", "matmul", out=pc.ap, lhsT=onesb.ap, rhs=sq_.ap, start=True, stop=True,
                             r=[onesb, sq_], w=[pc])
                        P.op("act", "activation", out=rs_.ap, in_=pc.ap, func=AF.Ln, bias=epsc.ap[:, 0:1], scale=1.0 / 128,
                             r=[pc, epsc], w=[rs_])
                        P.op("act", "activation", out=rs_.ap, in_=rs_.ap, func=AF.Exp, scale=-0.5, r=[rs_], w=[rs_])
                        P.op("dve", "scalar_tensor_tensor", out=t1_.ap, in0=pm.ap, scalar=hn_t.ap[:, wsel:wsel + 1],
                                                                                               in1=cs_.ap[:, 0, :], op0=ALU.mult, op1=ALU.mult,
                             r=[pm, hn_t, cs_], w=[t1_])
                        P.op("dve", "scalar_tensor_tensor", out=t2_.ap, in0=pp.ap, scalar=hn_t.ap[:, wsel + 1:wsel + 2],
                                                                                               in1=cs_.ap[:, 1, :], op0=ALU.mult, op1=ALU.mult,
                             r=[pp, hn_t, cs_], w=[t2_])
                        P.op("pool", "tensor_tensor", out=t1_.ap, in0=t1_.ap, in1=t2_.ap, op=ALU.add, r=[t1_, t2_], w=[t1_])
                        P.op("pool", "tensor_tensor", out=qk_.ap[:, j, :], in0=t1_.ap, in1=rs_.ap, op=ALU.mult,
                             r=[t1_, rs_], w=[qk_])
                    P.dma("sp", sc["QK"].rearrange("(c p) t -> p c t", p=128)[:, :, t0:t0 + TT], qk_.ap, r=[qk_], w=[DB(("QK", sn))])
                    z_, v_, g_ = z_st[0], v_st[0], g_st[0]
                    for s in range(4):
                        pz = ps[s % 2]
                        pg = ps[6 + (s % 2)]
                        gt_ = gt_t[s % 2]
                        for kc in range(NKC):
                            P.op("pe", "matmul", out=pz.ap, lhsT=hT_.ap[:, kc, s * 128:(s + 1) * 128], rhs=wi.ap[:, kc, TM0:TM0 + 512],
                                                                            start=(kc == 0), stop=(kc == NKC - 1), r=[hT_, wi], w=[pz])
                        for kc in range(NKC):
                            P.op("pe", "matmul", out=pg.ap[:, 0:272], lhsT=hT_.ap[:, kc, s * 128:(s + 1) * 128],
                                                                            rhs=wi.ap[:, kc, TM0 + 512:TM0 + 784],
                                                                            start=(kc == 0), stop=(kc == NKC - 1), r=[hT_, wi], w=[pg])
                        P.op("act", "copy", out=z_.ap[:, s, :], in_=pz.ap, r=[pz], w=[z_])
                        P.op("dve", "tensor_copy", out=v_.ap[:, s, :], in_=pg.ap[:, 16:272], r=[pg], w=[v_])
                        P.op("act", "activation", out=gt_.ap[:, 0:8], in_=pg.ap[:, 0:8], func=AF.Exp, scale=-1.0, r=[pg], w=[gt_])
                        P.op("dve", "tensor_scalar", out=gt_.ap[:, 0:8], in0=gt_.ap[:, 0:8], scalar1=1.0, scalar2=None, op0=ALU.add,
                             r=[gt_], w=[gt_])
                        P.op("dve", "reciprocal", out=g_.ap[:, s, 0:8], in_=gt_.ap[:, 0:8], r=[gt_], w=[g_])
                        P.op("dve", "tensor_tensor", out=gt_.ap[:, 8:16], in0=pg.ap[:, 8:16], in1=gp_b.ap[:, 1, :], op=ALU.add,
                             r=[pg, gp_b], w=[gt_])
                        P.op("act", "activation", out=gt_.ap[:, 8:16], in_=gt_.ap[:, 8:16], func=AF.Exp, r=[gt_], w=[gt_])
                        P.op("dve", "tensor_scalar", out=gt_.ap[:, 8:16], in0=gt_.ap[:, 8:16], scalar1=1.0, scalar2=None, op0=ALU.add,
                             r=[gt_], w=[gt_])
                        P.op("act", "activation", out=gt_.ap[:, 8:16], in_=gt_.ap[:, 8:16], func=AF.Ln, r=[gt_], w=[gt_])
                        P.op("dve", "tensor_tensor", out=g_.ap[:, s, 8:16], in0=gt_.ap[:, 8:16], in1=negA.ap, op=ALU.mult,
                             r=[gt_, negA], w=[g_])
                    P.dma("sp", sc["Z"][t0:t0 + TT, :].rearrange("(s p) c -> p s c", p=128), z_.ap, r=[z_], w=[DB(("Z", sn))])
                    P.dma("sp", sc["V"][t0:t0 + TT, :].rearrange("(s p) c -> p s c", p=128), v_.ap, r=[v_], w=[DB(("V", sn))])
                    P.dma("sp", sc["G"][t0:t0 + TT, :].rearrange("(s p) c -> p s c", p=128), g_.ap, r=[g_], w=[DB(("G", sn))])
                    it += 1
            P.barrier()
            P.emit()

        if stop_after >= 2:
            phases_rest(locals())
    return nc


def host_inputs(inp, Tmax):
    f = np.float32
    w_in = np.asarray(inp["w_in"][0], f)
    perm = rope_perm()
    qB = w_in[:, 2064:2576]
    kB = w_in[:, 2576:2832]
    qBp = qB.reshape(D, 4, 128)[:, :, perm].reshape(D, 512)
    kBp = kB.reshape(D, 2, 128)[:, :, perm].reshape(D, 256)
    w_in_r = np.ascontiguousarray(np.concatenate(
        [w_in[:, 0:1536], qB, kB, qBp, kBp, w_in[:, 1536:2048], w_in[:, 2048:2064], w_in[:, 2832:3088]], axis=1))
    norms = np.stack([np.asarray(inp[k][0], f) for k in ("norm_mix_pre", "norm_mix_post", "norm_mlp_pre", "norm_mlp_post")])
    gparam = np.stack([np.concatenate([np.asarray(inp["A_log_f"][0], f), np.asarray(inp["A_log_b"][0], f)]),
                       np.concatenate([np.asarray(inp["dt_bias_f"][0], f), np.asarray(inp["dt_bias_b"][0], f)])])
    qn = np.asarray(inp["q_norm_w"][0], f)
    kn = np.asarray(inp["k_norm_w"][0], f)
    hnorm = np.stack([np.asarray(inp["gdn_norm_w"][0], f), qn, qn[perm], kn, kn[perm]])
    cosT, sinT = rope_tables(Tmax)
    m = gdn_masks()
    m["ident"] = np.eye(128, dtype=f)
    cst = np.stack([m[n] for n in CONST_NAMES]).astype(f)
    return dict(w_in_r=w_in_r, w_out=np.ascontiguousarray(np.asarray(inp["w_out"][0], f)),
                w_up=np.ascontiguousarray(np.asarray(inp["w_up"][0], f)),
                w_down=np.ascontiguousarray(np.asarray(inp["w_down"][0], f)),
                norms=np.ascontiguousarray(norms),
                nrm_r=np.ascontiguousarray(norms.reshape(4, NKC, 128).transpose(2, 0, 1)),
                hnorm_r=np.ascontiguousarray(hnorm.T),
                conv_w_r=np.ascontiguousarray(np.asarray(inp["conv_w"][0], f).reshape(5, 12, 128).transpose(2, 1, 0)),
                gparam=np.ascontiguousarray(gparam), hnorm=np.ascontiguousarray(hnorm),
                cosT=cosT, sinT=sinT, cst=cst)


_NC_CACHE = {}


def run(inp, Tp, Ts, ncores, debug=False, stop_after=9):
    key = (Tp, Ts, debug, stop_after)
    if key not in _NC_CACHE:
        _NC_CACHE[key] = build(Tp, Ts, debug, stop_after)
    nc = _NC_CACHE[key]
    shared = host_inputs(inp, max(Tp, Ts))
    xp = np.asarray(inp["x_prompt"], np.float32)
    xs = np.asarray(inp["x_sample"], np.float32)
    in_maps = []
    for c in range(ncores):
        m = dict(shared)
        m["x_p"] = np.ascontiguousarray(xp[c])
        m["x_s"] = np.ascontiguousarray(xs[c])
        in_maps.append(m)
    res = run_bass_kernel_spmd(nc, in_maps, core_ids=list(range(ncores)))
    return res.results


def kernel(**inputs):
    res = run(inputs, 8192, 2048, 8)
    yp = np.stack([np.asarray(r["y_p"], np.float32) for r in res])
    ys = np.stack([np.asarray(r["y_s"], np.float32) for r in res])
    return (yp, ys)


def phase_attn(L):
    nc, P, ps, S, seqs, hnorm, identb = L["nc"], L["P"], L["ps"], L["S"], L["seqs"], L["hnorm"], L["identb"]
    DB = L["DB"]
    with ExitStack() as es:
        def tile(name, shape, dt):
            return Tile(_chk(nc, es.enter_context(nc.sbuf_tensor(name, list(shape), dt))).ap(), name)
        Tmax = max(T for _, T in seqs)
        NBmax = Tmax // 128
        wqk_b = tile("wqk_b", [128, 2, 128], F32)
        P.dma("sp", wqk_b.ap[:, 0, :], hnorm[1:2, :].partition_broadcast(128), w=[wqk_b])
        P.dma("sp", wqk_b.ap[:, 1, :], hnorm[3:4, :].partition_broadcast(128), w=[wqk_b])
        mx = tile("mx", [128, 2], F32)
        P.op("dve", "tensor_reduce", out=mx.ap, in_=wqk_b.ap, axis=AX.X, op=ALU.max, apply_absolute_value=True, r=[wqk_b], w=[mx])
        negM = tile("negM", [128, 1], F32)
        P.op("dve", "tensor_tensor", out=negM.ap, in0=mx.ap[:, 0:1], in1=mx.ap[:, 1:2], op=ALU.mult, r=[mx], w=[negM])
        P.op("dve", "tensor_scalar", out=negM.ap, in0=negM.ap, scalar1=-(128.0 ** 0.5), scalar2=None, op0=ALU.mult, r=[negM], w=[negM])
        kT = tile("kT", [128, Tmax], BF16)
        va = tile("va", [128, NBmax, 129], BF16)
        P.op("pool", "memset", ap=va.ap, constant=1.0, w=[va])
        qT = [tile("qT%d" % i, [128, TT], BF16) for i in range(2)]
        pT = [tile("pT%d" % i, [128, TT], BF16) for i in range(3)]
        osb = tile("osb", [128, 4, 129], F32)
        rcp = tile("rcp", [128, 4], F32)
        onb = tile("onb", [128, 4, 128], BF16)
        mst = [tile("mst%d" % i, [128, TT], BF16) for i in range(2)]
        scale = 128.0 ** -0.5
        qi = 0
        npt = 0
        for sn, T in seqs:
            sc = S[sn]
            NB = T // 128
            for g in range(2):
                P.dma("sp", kT.ap[:, 0:T], sc["QK"][(4 + g) * 128:(5 + g) * 128, :], r=[DB(("QK", sn))], w=[kT])
                P.dma("sp", va.ap[:, 0:NB, 0:128], sc["V"][:, g * 128:(g + 1) * 128].rearrange("(b p) c -> p b c", p=128),
                      r=[DB(("V", sn))], w=[va])
                for h in (2 * g, 2 * g + 1):
                    for qt in range(T // TT):
                        q0 = qt * TT
                        q_ = qT[qi % 2]
                        P.dma("sp", q_.ap, sc["QK"][h * 128:(h + 1) * 128, q0:q0 + TT], r=[DB(("QK", sn))], w=[q_])
                        accb = [ps[2 + 2 * (qi % 2)], ps[3 + 2 * (qi % 2)]]
                        accv = [b.ap[:, 0:258].rearrange("p (a c) -> p a c", a=2) for b in accb]

                        def qk(kb):
                            sb = ps[kb % 2]
                            P.op("pe", "matmul", out=sb.ap, lhsT=kT.ap[:, kb * 128:(kb + 1) * 128], rhs=q_.ap, start=True, stop=True,
                                 r=[kT, q_], w=[sb])

                        qk(0)
                        for kb in range(NB):
                            if kb + 1 < NB:
                                qk(kb + 1)
                            sb = ps[kb % 2]
                            p_ = pT[npt % 3]
                            npt += 1
                            P.op("act", "activation", out=p_.ap, in_=sb.ap, func=AF.Exp, bias=negM.ap[:, 0:1], scale=scale,
                                 r=[sb, negM], w=[p_])
                            for qs in range(4):
                                P.op("pe", "matmul", out=accv[qs // 2][:, qs % 2, :], lhsT=p_.ap[:, qs * 128:(qs + 1) * 128], rhs=va.ap[:, kb, :],
                                     start=(kb == 0 and qs % 2 == 0), stop=(kb == NB - 1), skip_group_check=True,
                                     r=[p_, va], w=[accb[qs // 2]])
                        P.op("dve", "tensor_copy", out=osb.ap[:, 0:2, :], in_=accv[0], r=[accb[0]], w=[osb])
                        P.op("dve", "tensor_copy", out=osb.ap[:, 2:4, :], in_=accv[1], r=[accb[1]], w=[osb])
                        P.op("dve", "reciprocal", out=rcp.ap, in_=osb.ap[:, :, 128], r=[osb], w=[rcp])
                        P.op("dve", "tensor_tensor", out=onb.ap, in0=osb.ap[:, :, 0:128], in1=rcp.ap.unsqueeze(2).to_broadcast([128, 4, 128]),
                             op=ALU.mult, r=[osb, rcp], w=[onb])
                        tb = ps[6]
                        tv = tb.ap.bitcast(BF16)[:, 0:512].rearrange("p (a c) -> p a c", a=4)
                        for qs in range(4):
                            P.op("pe", "transpose", out=tv[:, qs, :], in_=onb.ap[:, qs, :], identity=identb.ap, r=[onb, identb], w=[tb])
                        m_ = mst[qi % 2]
                        P.op("dve", "tensor_copy", out=m_.ap.rearrange("p (a c) -> p a c", a=4), in_=tv, r=[tb], w=[m_])
                        P.dma("sp", sc["MIX"][512 + h * 128:512 + (h + 1) * 128, q0:q0 + TT], m_.ap, r=[m_], w=[DB(("MIX", sn))])
                        qi += 1
        P.barrier()
        P.emit()


def phase_mlp(L):
    nc, P, ps, S, seqs, identb = L["nc"], L["P"], L["ps"], L["S"], L["seqs"], L["identb"]
    DB, X, Y, rms_rstd, nrm = L["DB"], L["X"], L["Y"], L["rms_rstd"], L["nrm"]
    load_weight_bf16 = L["load_weight_bf16"]

    def make_norm_resid(mo, junk, ssq, rstd, nrm_b):
        def norm_resid(src_banks, resid_ap, out_ap, which, deps_resid, deps_out):
            P.op("act", "copy", out=mo.ap[:, 0:512], in_=src_banks[0].ap, r=[src_banks[0]], w=[mo])
            P.op("dve", "tensor_copy", out=mo.ap[:, 512:1024], in_=src_banks[1].ap, r=[src_banks[1]], w=[mo])
            P.op("act", "activation", out=junk.ap, in_=mo.ap, func=AF.Square, accum_out=ssq.ap[:, 0:1], r=[mo], w=[junk, ssq])
            rms_rstd(ssq, rstd, 1, 1.0 / D)
            P.op("dve", "scalar_tensor_tensor", out=mo.ap, in0=mo.ap, scalar=rstd.ap[:, 0:1], in1=nrm_b.ap[:, which, :],
                 op0=ALU.mult, op1=ALU.mult, r=[mo, rstd, nrm_b], w=[mo])
            P.op("pool", "tensor_tensor", out=out_ap, in0=mo.ap, in1=resid_ap, op=ALU.add, r=[mo] + deps_resid, w=deps_out)
        return norm_resid

    with ExitStack() as es:
        def tile(name, shape, dt):
            return Tile(_chk(nc, es.enter_context(nc.sbuf_tensor(name, list(shape), dt))).ap(), name)
        wo = tile("wo", [128, 8, D], BF16)
        with ExitStack() as ses:
            st = [Tile(_chk(nc, ses.enter_context(nc.sbuf_tensor("wst4a_%d" % i, [128, 1024], F32))).ap()) for i in range(2)]
            load_weight_bf16(wo, st, L["w_out"], 8, D, None, 1024)
            P.barrier()
            P.emit()
        nrm_b = tile("nrm_b", [128, 2, D], F32)
        P.dma("sp", nrm_b.ap[:, 0, :], nrm[1:2, :].partition_broadcast(128), w=[nrm_b])
        P.dma("sp", nrm_b.ap[:, 1, :], nrm[3:4, :].partition_broadcast(128), w=[nrm_b])
        xt = tile("x4", [128, 4, D], F32)
        mT = [tile("mT%d" % i, [128, 8, TT], BF16) for i in range(2)]
        mo = tile("mo", [128, D], F32)
        junk = tile("junk4", [128, D], BF16)
        ssq = tile("ssq4", [128, 1], F32)
        rstd = tile("rstd4", [128, 1], F32)
        norm_resid = make_norm_resid(mo, junk, ssq, rstd, nrm_b)
        it = 0
        for sn, T in seqs:
            sc = S[sn]
            for ti in range(T // TT):
                t0 = ti * TT
                m_ = mT[it % 2]
                P.dma("sp", xt.ap, X[sn][t0:t0 + TT, :].rearrange("(s p) d -> p s d", p=128), w=[xt])
                P.dma("sp", m_.ap, sc["MIX"].rearrange("(k p) t -> p k t", p=128)[:, :, t0:t0 + TT], r=[DB(("MIX", sn))], w=[m_])
                for s in range(4):
                    banks = [ps[2 * (s % 2)], ps[2 * (s % 2) + 1]]
                    for cg in range(2):
                        for kc in range(8):
                            P.op("pe", "matmul", out=banks[cg].ap, lhsT=m_.ap[:, kc, s * 128:(s + 1) * 128], rhs=wo.ap[:, kc, cg * 512:(cg + 1) * 512],
                                 start=(kc == 0), stop=(kc == 7), r=[m_, wo], w=[banks[cg]])
                    norm_resid(banks, xt.ap[:, s, :], xt.ap[:, s, :], 0, [xt], [xt])
                P.dma("sp", sc["X1"][t0:t0 + TT, :].rearrange("(s p) d -> p s d", p=128), xt.ap, r=[xt], w=[DB(("X1", sn))])
                it += 1
        P.barrier()
        P.emit()
    T4 = 256
    with ExitStack() as es:
        def tile(name, shape, dt):
            return Tile(_chk(nc, es.enter_context(nc.sbuf_tensor(name, list(shape), dt))).ap(), name)
        wu = tile("wu", [128, 8, DFF], BF16)
        wd = tile("wd", [128, 32, D], BF16)
        with ExitStack() as ses:
            st = [Tile(_chk(nc, ses.enter_context(nc.sbuf_tensor("wst4_%d" % i, [128, 2048], F32))).ap()) for i in range(2)]
            load_weight_bf16(wu, st, L["w_up"], 8, DFF, 2, 2048)
            load_weight_bf16(wd, st, L["w_down"], 32, D, None, 1024)
            P.barrier()
            P.emit()
        nrm_b = tile("nrm_b2", [128, 2, D], F32)
        P.dma("sp", nrm_b.ap[:, 0, :], nrm[1:2, :].partition_broadcast(128), w=[nrm_b])
        P.dma("sp", nrm_b.ap[:, 1, :], nrm[3:4, :].partition_broadcast(128), w=[nrm_b])
        xt = tile("x4b", [128, D], F32)
        mo = tile("mo2", [128, D], F32)
        ssq = tile("ssq4b", [128, 1], F32)
        rstd = tile("rstd4b", [128, 1], F32)
        h2 = tile("h2", [128, 2, D], BF16)
        h2T = tile("h2T", [128, 8, T4], BF16)
        fT = tile("fT", [128, 32, T4], BF16)
        rl = tile("rl", [128, T4], F32)
        junk = Tile(h2.ap[:, 0, :], "junk_alias")
        for sn, T in seqs:
            sc = S[sn]
            for ti in range(T // T4):
                t0 = ti * T4
                for s in range(2):
                    P.dma("sp", xt.ap, sc["X1"][t0 + s * 128:t0 + (s + 1) * 128, :], r=[DB(("X1", sn))], w=[xt])
                    P.op("act", "activation", out=h2.ap[:, s, :], in_=xt.ap, func=AF.Square, accum_out=ssq.ap[:, 0:1], r=[xt], w=[h2, ssq])
                    rms_rstd(ssq, rstd, 1, 1.0 / D)
                    P.op("dve", "tensor_scalar", out=h2.ap[:, s, :], in0=xt.ap, scalar1=rstd.ap[:, 0:1], scalar2=None, op0=ALU.mult,
                         r=[xt, rstd], w=[h2])
                    pb = ps[2 + s]
                    pv = pb.ap.bitcast(BF16).rearrange("p (k t) -> p k t", k=8)
                    for kc in range(8):
                        P.op("pe", "transpose", out=pv[:, kc, :], in_=h2.ap[:, s, kc * 128:(kc + 1) * 128], identity=identb.ap,
                             r=[h2, identb], w=[pb])
                    P.op("act", "copy", out=h2T.ap[:, :, s * 128:(s + 1) * 128], in_=pv, r=[pb], w=[h2T])
                for fc in range(32):
                    pb = ps[4 + (fc % 4)]
                    for kc in range(8):
                        P.op("pe", "matmul", out=pb.ap[:, 0:T4], lhsT=wu.ap[:, kc, fc * 128:(fc + 1) * 128], rhs=h2T.ap[:, kc, :],
                             start=(kc == 0), stop=(kc == 7), r=[wu, h2T], w=[pb])
                    P.op("act", "activation", out=rl.ap, in_=pb.ap[:, 0:T4], func=AF.Relu, r=[pb], w=[rl])
                    P.op("dve" if fc % 2 == 0 else "pool", "tensor_tensor", out=fT.ap[:, fc, :], in0=rl.ap, in1=rl.ap, op=ALU.mult, r=[rl], w=[fT])
                for s in range(2):
                    banks = [ps[0], ps[1]]
                    for cg in range(2):
                        for fc in range(32):
                            P.op("pe", "matmul", out=banks[cg].ap, lhsT=fT.ap[:, fc, s * 128:(s + 1) * 128], rhs=wd.ap[:, fc, cg * 512:(cg + 1) * 512],
                                 start=(fc == 0), stop=(fc == 31), r=[fT, wd], w=[banks[cg]])
                    P.dma("sp", xt.ap, sc["X1"][t0 + s * 128:t0 + (s + 1) * 128, :], r=[DB(("X1", sn))], w=[xt])
                    P.op("act", "copy", out=mo.ap[:, 0:512], in_=banks[0].ap, r=[banks[0]], w=[mo])
                    P.op("dve", "tensor_copy", out=mo.ap[:, 512:1024], in_=banks[1].ap, r=[banks[1]], w=[mo])
                    P.op("act", "activation", out=h2.ap[:, s, :], in_=mo.ap, func=AF.Square, accum_out=ssq.ap[:, 0:1], r=[mo], w=[h2, ssq])
                    rms_rstd(ssq, rstd, 1, 1.0 / D)
                    P.op("dve", "scalar_tensor_tensor", out=mo.ap, in0=mo.ap, scalar=rstd.ap[:, 0:1], in1=nrm_b.ap[:, 1, :],
                         op0=ALU.mult, op1=ALU.mult, r=[mo, rstd, nrm_b], w=[mo])
                    P.op("pool", "tensor_tensor", out=xt.ap, in0=mo.ap, in1=xt.ap, op=ALU.add, r=[mo, xt], w=[xt])
                    P.dma("sp", Y[sn][t0 + s * 128:t0 + (s + 1) * 128, :], xt.ap, r=[xt], w=[DB(("Y", sn))])
        P.barrier()
        P.emit()


def phases_rest(L):
    sa = L["stop_after"]
    if sa >= 2:
        phase_gdn(L)
    if sa >= 3:
        phase_attn(L)
    if sa >= 4:
        phase_mlp(L)


def phase_gdn(L):
    nc, P, ps, S, seqs, identb, onesb, onesf = L["nc"], L["P"], L["ps"], L["S"], L["seqs"], L["identb"], L["onesb"], L["onesf"]
    DB, epsc, rms_rstd = L["DB"], L["epsc"], L["rms_rstd"]
    with ExitStack() as es:
        def tile(name, shape, dt):
            return Tile(_chk(nc, es.enter_context(nc.sbuf_tensor(name, list(shape), dt))).ap(), name)
        cw_t = tile("cw_t", [128, 12, 5], F32)
        P.dma("sp", cw_t.ap, L["conv_w_r"], w=[cw_t])
        raw = [tile("raw%d" % i, [128, 12, TT + 4], BF16) for i in range(2)]
        acc = [tile("acc%d" % i, [128, TT], F32) for i in range(2)]
        sl = [tile("sl%d" % i, [128, TT], F32) for i in range(2)]
        sq = [tile("sq%d" % i, [128, TT], BF16) for i in range(2)]
        rs = [tile("rs%d" % i, [128, TT], F32) for i in range(2)]
        fm = [tile("fm%d" % i, [128, TT], BF16) for i in range(3)]
        tk = [tile("tk%d" % i, [128, 4, 128], BF16) for i in range(2)]
        it = 0
        n = 0
        for sn, T in seqs:
            sc = S[sn]
            for ti in range(T // TT):
                t0 = ti * TT
                r_ = raw[it % 2]
                lo, hi = max(t0 - 2, 0), min(t0 + TT + 2, T)
                if t0 == 0:
                    P.op("pool", "memset", ap=r_.ap[:, :, 0:2], constant=0.0, w=[r_])
                if t0 + TT == T:
                    P.op("pool", "memset", ap=r_.ap[:, :, TT + 2:TT + 4], constant=0.0, w=[r_])
                P.dma("sp", r_.ap[:, :, lo - (t0 - 2):hi - (t0 - 2)], sc["A"].rearrange("(c p) t -> p c t", p=128)[:, :, lo:hi],
                      r=[DB(("A", sn))], w=[r_])
                for c in range(12):
                    kind, h = c // 4, c % 4
                    a_, s_ = acc[n % 2], sl[n % 2]
                    ce = "dve"
                    P.op(ce, "tensor_scalar", out=a_.ap, in0=r_.ap[:, c, 0:TT], scalar1=cw_t.ap[:, c, 0:1], scalar2=None, op0=ALU.mult,
                         r=[r_, cw_t], w=[a_])
                    for k in range(1, 5):
                        P.op(ce, "scalar_tensor_tensor", out=a_.ap, in0=r_.ap[:, c, k:k + TT], scalar=cw_t.ap[:, c, k:k + 1], in1=a_.ap,
                             op0=ALU.mult, op1=ALU.add, r=[r_, cw_t, a_], w=[a_])
                    P.op("act", "activation", out=s_.ap, in_=a_.ap, func=AF.Silu, r=[a_], w=[s_])
                    f_ = fm[n % 3]
                    if kind == 2:
                        P.op("pool", "tensor_copy", out=f_.ap, in_=s_.ap, r=[s_], w=[f_])
                    else:
                        q_, rs_ = sq[n % 2], rs[n % 2]
                        pb = ps[n % 2]
                        P.op("act", "activation", out=q_.ap, in_=s_.ap, func=AF.Square, r=[s_], w=[q_])
                        P.op("pe", "matmul", out=pb.ap, lhsT=onesb.ap, rhs=q_.ap, start=True, stop=True, r=[onesb, q_], w=[pb])
                        P.op("act", "activation", out=rs_.ap, in_=pb.ap, func=AF.Ln, bias=epsc.ap[:, 0:1], scale=1.0, r=[pb, epsc], w=[rs_])
                        P.op("act", "activation", out=rs_.ap, in_=rs_.ap, func=AF.Exp, scale=-0.5, r=[rs_], w=[rs_])
                        P.op("dve", "scalar_tensor_tensor", out=f_.ap, in0=s_.ap, scalar=(128.0 ** -0.5 if kind == 0 else 1.0), in1=rs_.ap,
                             op0=ALU.mult, op1=ALU.mult, r=[s_, rs_], w=[f_])
                        dst = sc["GQ"] if kind == 0 else sc["GK"]
                        P.dma("sp", dst[h * 128:(h + 1) * 128, t0:t0 + TT], f_.ap, r=[f_], w=[DB(("GQK", sn))])
                    if kind >= 1:
                        tb = ps[2 + (n % 2)]
                        tv = tb.ap.bitcast(BF16)[:, 0:512].rearrange("p (a c) -> p a c", a=4)
                        for s in range(4):
                            P.op("pe", "transpose", out=tv[:, s, :], in_=f_.ap[:, s * 128:(s + 1) * 128], identity=identb.ap,
                                 r=[f_, identb], w=[tb])
                        t_ = tk[n % 2]
                        P.op("act", "copy", out=t_.ap, in_=tv, r=[tb], w=[t_])
                        dst = sc["GKt"] if kind == 1 else sc["GVt"]
                        P.dma("sp", dst[t0:t0 + TT, h * 128:(h + 1) * 128].rearrange("(s p) d -> p s d", p=128), t_.ap,
                              r=[t_], w=[DB(("GKV", sn))])
                    n += 1
                it += 1
        P.barrier()
        P.emit()
    if L["stop_after"] == 2 and L.get("gdn_pre_only"):
        return
    with ExitStack() as es:
        def tile(name, shape, dt):
            return Tile(_chk(nc, es.enter_context(nc.sbuf_tensor(name, list(shape), dt))).ap(), name)

        cf = tile("cf", [128, len(CONST_NAMES), 128], F32)
        P.dma("sp", cf.ap, L["cst"].rearrange("n p j -> p n j"), w=[cf])
        C = {n: cf.ap[:, i, :] for i, n in enumerate(CONST_NAMES)}
        gnw_b = tile("gnw_b", [128, 128], F32)
        P.dma("sp", gnw_b.ap, L["hnorm"][0:1, :].partition_broadcast(128), w=[gnw_b])

        def b3(ap2):
            return ap2.unsqueeze(1).to_broadcast([128, 4, 128])

        def bj(ap2):
            return ap2.unsqueeze(2).to_broadcast([128, 4, 128])

        def v4(t):
            return t.ap.rearrange("p (h j) -> p h j", h=4)

        NU = 2
        Rd = []
        R = []
        for d in range(2):
            rd = {"S4": tile("S4_%d" % d, [128, 512], F32), "S4b": tile("S4b_%d" % d, [128, 512], BF16),
                  }
            Rd.append(rd)
            Ru = []
            for u in range(NU):
                r = {}
                sfx = "%d_%d" % (d, u)
                for nm in ("KT", "QT", "Kt", "Vt", "A0", "A1", "B0", "B1", "Aqk", "AqkT", "PTb", "Vb", "kbg", "ke", "wT", "vn"):
                    r[nm] = tile(nm + sfx, [128, 512], BF16)
                for nm in ("Gm", "Ep", "D4", "Dsb", "PT", "u4"):
                    r[nm] = tile(nm + sfx, [128, 512], F32)
                r["Oq"], r["O4"], r["sz"], r["o2"] = r["Gm"], r["Ep"], r["D4"], r["Dsb"]
                r["zb"], r["onb"], r["mxs"] = r["A0"], r["A1"], r["B0"]
                r["G"] = tile("G" + sfx, [128, 16], F32)
                r["sm"] = tile("sm" + sfx, [128, 16], F32)
                r["ex"] = tile("ex" + sfx, [128, 16], F32)
                r["bgc"] = tile("bgc" + sfx, [128, 4], F32)
                r["ssq"] = tile("gssq" + sfx, [128, 4], F32)
                r["rstd"] = tile("grstd" + sfx, [128, 4], F32)
                r["banks"] = [ps[2 * (NU * d + u)], ps[2 * (NU * d + u) + 1]]
                r["nb"] = 0
                Ru.append(r)
            R.append(Ru)

        def nbank(r):
            b = r["banks"][r["nb"] % 2]
            r["nb"] += 1
            return b

        def unit(sn, T, d, b, step, NB):
            sc = S[sn]
            r = R[d][step % NU]
            rd = Rd[d]
            t0 = b * 128
            KT, QT, Kt, Vt, G = r["KT"], r["QT"], r["Kt"], r["Vt"], r["G"]
            P.dma("sp", v4(KT), sc["GK"].rearrange("(h p) t -> p h t", p=128)[:, :, t0:t0 + 128], r=[DB(("GQK", sn))], w=[KT])
            P.dma("sp", v4(QT), sc["GQ"].rearrange("(h p) t -> p h t", p=128)[:, :, t0:t0 + 128], r=[DB(("GQK", sn))], w=[QT])
            P.dma("sp", Kt.ap, sc["GKt"][t0:t0 + 128, :], r=[DB(("GKV", sn))], w=[Kt])
            P.dma("sp", Vt.ap, sc["GVt"][t0:t0 + 128, :], r=[DB(("GKV", sn))], w=[Vt])
            P.dma("sp", G.ap, sc["G"][t0:t0 + 128, :], r=[DB(("G", sn))], w=[G])
            yield
            beta = G.ap[:, 4 * d:4 * d + 4]
            gg = G.ap[:, 8 + 4 * d:12 + 4 * d]
            cum = C["cum_f"] if d == 0 else C["cum_b"]
            pos = C["pos_f"] if d == 0 else C["pos_b"]
            strict = C["strict_f"] if d == 0 else C["strict_b"]
            sm, ex, bgc = r["sm"], r["ex"], r["bgc"]
            Gm, Ep, D4, Dsb = r["Gm"], r["Ep"], r["D4"], r["Dsb"]
            pb = nbank(r)
            P.op("pe", "matmul", out=pb.ap[:, 0:4], lhsT=cum, rhs=gg, start=True, stop=True, r=[cf, G], w=[pb])
            P.op("pe", "matmul", out=pb.ap[:, 4:8], lhsT=C["same"], rhs=gg, start=True, stop=True, r=[cf, G], w=[pb])
            P.op("pe", "matmul", out=pb.ap[:, 8:12], lhsT=C["c0"], rhs=gg, start=True, stop=True, r=[cf, G], w=[pb])
            P.op("pe", "matmul", out=pb.ap[:, 12:16], lhsT=C["c1"], rhs=gg, start=True, stop=True, r=[cf, G], w=[pb])
            P.op("pool", "tensor_tensor", out=v4(Gm), in0=b3(cum), in1=bj(gg), op=ALU.mult, r=[cf, G], w=[Gm])
            yield
            P.op("dve", "tensor_copy", out=sm.ap, in_=pb.ap[:, 0:16], r=[pb], w=[sm])
            P.op("dve", "tensor_tensor", out=sm.ap[:, 4:8], in0=sm.ap[:, 4:8], in1=sm.ap[:, 0:4], op=ALU.subtract, r=[sm], w=[sm])
            pe_ = nbank(r)
            P.op("pe", "matmul", out=pe_.ap, lhsT=onesf.ap, rhs=Gm.ap, start=True, stop=True, r=[onesf, Gm], w=[pe_])
            yield
            P.op("act", "activation", out=ex.ap, in_=sm.ap, func=AF.Exp, r=[sm], w=[ex])
            P.op("dve", "tensor_tensor", out=v4(Ep), in0=pe_.ap.rearrange("p (h j) -> p h j", h=4), in1=b3(pos), op=ALU.add,
                 r=[pe_, cf], w=[Ep])
            yield
            P.op("dve", "tensor_tensor", out=bgc.ap, in0=ex.ap[:, 0:4], in1=beta, op=ALU.mult, r=[ex, G], w=[bgc])
            for h in range(4):
                P.op("act", "activation", out=v4(D4)[:, h, :], in_=v4(Ep)[:, h, :], func=AF.Exp, bias=sm.ap[:, h:h + 1], scale=-1.0,
                     r=[Ep, sm], w=[D4])
            pk, pq = nbank(r), nbank(r)
            for h in range(4):
                P.op("pe", "matmul", out=pk.ap[:, h * 128:(h + 1) * 128], lhsT=v4(KT)[:, h, :], rhs=v4(KT)[:, h, :], start=True, stop=True,
                     r=[KT], w=[pk])
            for h in range(4):
                P.op("pe", "matmul", out=pq.ap[:, h * 128:(h + 1) * 128], lhsT=v4(QT)[:, h, :], rhs=v4(KT)[:, h, :], start=True, stop=True,
                     r=[QT, KT], w=[pq])
            yield
            Vb, kbg, ke, u4, wT = r["Vb"], r["kbg"], r["ke"], r["u4"], r["wT"]
            P.op("pool", "tensor_tensor", out=v4(Dsb), in0=v4(D4), in1=b3(strict), op=ALU.mult, r=[D4, cf], w=[Dsb])
            P.op("pool", "tensor_tensor", out=v4(Dsb), in0=v4(Dsb), in1=bj(beta), op=ALU.mult, r=[Dsb, G], w=[Dsb])
            A, B = [r["A0"], r["A1"]], [r["B0"], r["B1"]]
            Aqk, AqkT, PT, PTb = r["Aqk"], r["AqkT"], r["PT"], r["PTb"]
            P.op("dve", "tensor_tensor", out=Aqk.ap, in0=pq.ap, in1=D4.ap, op=ALU.mult, r=[pq, D4], w=[Aqk])
            yield
            P.op("dve", "tensor_tensor", out=A[0].ap, in0=pk.ap, in1=Dsb.ap, op=ALU.mult, r=[pk, Dsb], w=[A[0]])
            P.op("pool", "tensor_tensor", out=v4(Vb), in0=v4(Vt), in1=bj(beta), op=ALU.mult, r=[Vt, G], w=[Vb])
            P.op("pool", "tensor_tensor", out=v4(kbg), in0=v4(Kt), in1=bj(bgc.ap), op=ALU.mult, r=[Kt, bgc], w=[kbg])
            P.op("pool", "tensor_tensor", out=v4(ke), in0=v4(Kt), in1=bj(ex.ap[:, 4:8]), op=ALU.mult, r=[Kt, ex], w=[ke])
            yield
            tb = nbank(r)
            tvA = tb.ap.bitcast(BF16)[:, 0:512].rearrange("p (a c) -> p a c", a=4)
            tvQ = tb.ap.bitcast(BF16)[:, 512:1024].rearrange("p (a c) -> p a c", a=4)
            for h in range(4):
                P.op("pe", "transpose", out=tvA[:, h, :], in_=v4(A[0])[:, h, :], identity=identb.ap, r=[A[0], identb], w=[tb])
            for h in range(4):
                P.op("pe", "transpose", out=tvQ[:, h, :], in_=v4(Aqk)[:, h, :], identity=identb.ap, r=[Aqk, identb], w=[tb])
            yield
            P.op("act", "copy", out=v4(B[0]), in_=tvA, r=[tb], w=[B[0]])
            P.op("act", "copy", out=v4(AqkT), in_=tvQ, r=[tb], w=[AqkT])
            yield
            P.op("pool", "tensor_tensor", out=v4(PT), in0=b3(C["ident"]), in1=v4(B[0]), op=ALU.subtract, r=[cf, B[0]], w=[PT])
            P.op("pool", "tensor_copy", out=PTb.ap, in_=PT.ap, r=[PT], w=[PTb])
            for k in range(5):
                Ak, Bk, An, Bn = A[k % 2], B[k % 2], A[(k + 1) % 2], B[(k + 1) % 2]
                pa = nbank(r)
                for h in range(4):
                    P.op("pe", "matmul", out=pa.ap[:, h * 128:(h + 1) * 128], lhsT=v4(Bk)[:, h, :], rhs=v4(Ak)[:, h, :], start=True, stop=True,
                         r=[Ak, Bk], w=[pa])
                if k < 4:
                    pbb = nbank(r)
                    for h in range(4):
                        P.op("pe", "matmul", out=pbb.ap[:, h * 128:(h + 1) * 128], lhsT=v4(Ak)[:, h, :], rhs=v4(Bk)[:, h, :], start=True, stop=True,
                             r=[Ak, Bk], w=[pbb])
                yield
                P.op("act", "copy", out=An.ap, in_=pa.ap, r=[pa], w=[An])
                if k < 4:
                    P.op("dve", "tensor_copy", out=Bn.ap, in_=pbb.ap, r=[pbb], w=[Bn])
                yield
                pd = nbank(r)
                for h in range(4):
                    P.op("pe", "matmul", out=pd.ap[:, h * 128:(h + 1) * 128], lhsT=v4(An)[:, h, :], rhs=v4(PTb)[:, h, :], start=True, stop=True,
                         r=[An, PTb], w=[pd])
                yield
                P.op("dve", "tensor_tensor", out=PT.ap, in0=pd.ap, in1=PT.ap, op=ALU.add, r=[pd, PT], w=[PT])
                P.op("pool", "tensor_copy", out=PTb.ap, in_=PT.ap, r=[PT], w=[PTb])
                yield
            pu, pw = nbank(r), nbank(r)
            for h in range(4):
                P.op("pe", "matmul", out=pu.ap[:, h * 128:(h + 1) * 128], lhsT=v4(PTb)[:, h, :], rhs=v4(Vb)[:, h, :], start=True, stop=True,
                     r=[PTb, Vb], w=[pu])
            for h in range(4):
                P.op("pe", "matmul", out=pw.ap[:, h * 128:(h + 1) * 128], lhsT=v4(kbg)[:, h, :], rhs=v4(PTb)[:, h, :], start=True, stop=True,
                     r=[PTb, kbg], w=[pw])
            yield
            P.op("act", "copy", out=u4.ap, in_=pu.ap, r=[pu], w=[u4])
            P.op("dve", "tensor_copy", out=wT.ap, in_=pw.ap, r=[pw], w=[wT])
            yield
            S4, S4b, vn, Oq, O4 = rd["S4"], rd["S4b"], r["vn"], r["Oq"], r["O4"]
            for c in ((0, 1) if d == 0 else (1, 0)):
                pc = slice(c * 64, (c + 1) * 64)
                p1, po1 = nbank(r), nbank(r)
                for h in range(4):
                    P.op("pe", "matmul", out=p1.ap[pc, h * 128:(h + 1) * 128], lhsT=v4(wT)[:, h, pc], rhs=v4(S4b)[:, h, :], start=True, stop=True,
                         r=[wT, S4b], w=[p1])
                for h in range(4):
                    P.op("pe", "matmul", out=po1.ap[pc, h * 128:(h + 1) * 128], lhsT=v4(QT)[:, h, pc], rhs=v4(S4b)[:, h, :], start=True, stop=True,
                         r=[QT, S4b], w=[po1])
                yield
                P.op("dve", "tensor_tensor", out=vn.ap[pc, :], in0=u4.ap[pc, :], in1=p1.ap[pc, :], op=ALU.subtract, r=[u4, p1], w=[vn])
                P.op("dve", "tensor_tensor", out=v4(Oq)[pc], in0=po1.ap[pc, :].rearrange("p (h j) -> p h j", h=4),
                     in1=ex.ap[pc, 0:4].unsqueeze(2).to_broadcast([64, 4, 128]), op=ALU.mult, r=[po1, ex], w=[Oq])
                yield
                po2, pds = nbank(r), nbank(r)
                for h in range(4):
                    P.op("pe", "matmul", out=pds.ap[:, h * 128:(h + 1) * 128], lhsT=v4(ke)[pc, h, :], rhs=v4(vn)[pc, h, :], start=True, stop=True,
                         r=[ke, vn], w=[pds])
                for h in range(4):
                    P.op("pe", "matmul", out=po2.ap[pc, h * 128:(h + 1) * 128], lhsT=v4(AqkT)[pc, h, pc], rhs=v4(vn)[pc, h, :], start=True, stop=True,
                         r=[AqkT, vn], w=[po2])
                P.op("pool", "tensor_tensor", out=v4(S4), in0=v4(S4), in1=bj(ex.ap[:, 8 + 4 * c:12 + 4 * c]), op=ALU.mult, r=[S4, ex], w=[S4])
                yield
                P.op("dve", "tensor_tensor", out=S4.ap, in0=pds.ap, in1=S4.ap, op=ALU.add, r=[pds, S4], w=[S4])
                P.op("act", "copy", out=S4b.ap, in_=S4.ap, r=[S4], w=[S4b])
                P.op("dve", "tensor_tensor", out=O4.ap[pc, :], in0=po2.ap[pc, :], in1=Oq.ap[pc, :], op=ALU.add, r=[po2, Oq], w=[O4])
                yield
            mine, other = ("OF", "OB") if d == 0 else ("OB", "OF")
            if step < NB // 2:
                P.dma("sp", sc[mine][t0:t0 + 128, :], O4.ap, r=[O4], w=[DB((mine, sn, b))])
            else:
                o2, sz, zb, onb, mxs = r["o2"], r["sz"], r["zb"], r["onb"], r["mxs"]
                ssq, rstd = r["ssq"], r["rstd"]
                P.dma("sp", o2.ap, sc[other][t0:t0 + 128, :], r=[DB((other, sn, b))], w=[o2])
                P.dma("sp", zb.ap, sc["Z"][t0:t0 + 128, :], r=[DB(("Z", sn))], w=[zb])
                yield
                P.op("pool", "tensor_tensor", out=o2.ap, in0=o2.ap, in1=O4.ap, op=ALU.add, r=[o2, O4], w=[o2])
                P.op("pool", "tensor_tensor", out=sz.ap, in0=o2.ap, in1=o2.ap, op=ALU.mult, r=[o2], w=[sz])
                yield
                P.op("dve", "tensor_reduce", out=ssq.ap, in_=v4(sz), axis=AX.X, op=ALU.add, r=[sz], w=[ssq])
                rms_rstd(ssq, rstd, 4, 1.0 / 128)
                P.op("act", "activation", out=sz.ap, in_=zb.ap, func=AF.Silu, r=[zb], w=[sz])
                yield
                P.op("dve", "tensor_tensor", out=v4(o2), in0=v4(o2), in1=bj(rstd.ap), op=ALU.mult, r=[o2, rstd], w=[o2])
                P.op("pool", "tensor_tensor", out=v4(o2), in0=v4(o2), in1=b3(gnw_b.ap), op=ALU.mult, r=[o2, gnw_b], w=[o2])
                yield
                P.op("dve", "tensor_tensor", out=onb.ap, in0=o2.ap, in1=sz.ap, op=ALU.mult, r=[o2, sz], w=[onb])
                yield
                tb2 = nbank(r)
                tv2 = tb2.ap.bitcast(BF16)[:, 0:512].rearrange("p (a c) -> p a c", a=4)
                for h in range(4):
                    P.op("pe", "transpose", out=tv2[:, h, :], in_=v4(onb)[:, h, :], identity=identb.ap, r=[onb, identb], w=[tb2])
                yield
                P.op("act", "copy", out=v4(mxs), in_=tv2, r=[tb2], w=[mxs])
                P.dma("sp", sc["MIX"][0:512, :].rearrange("(h p) t -> p h t", p=128)[:, :, t0:t0 + 128], v4(mxs), r=[mxs], w=[DB(("MIX", sn))])

        for sn, T in seqs:
            NB = T // 128
            for d in range(2):
                P.op("pool", "memset", ap=Rd[d]["S4"].ap, constant=0.0, w=[Rd[d]["S4"]])
                P.op("pool", "memset", ap=Rd[d]["S4b"].ap, constant=0.0, w=[Rd[d]["S4b"]])
            active = []
            nxt = 0
            STAG = 16
            while nxt < NB or active:
                if nxt < NB and len(active) < 2 * NU and (not active or active[-1][1] >= STAG):
                    active.append([unit(sn, T, 0, nxt, nxt, NB), 0])
                    active.append([unit(sn, T, 1, NB - 1 - nxt, nxt, NB), 0])
                    nxt += 1
                for a_ in list(active):
                    try:
                        next(a_[0])
                        a_[1] += 1
                    except StopIteration:
                        active.remove(a_)
        P.barrier()
        P.emit()
```

```python
import numpy as np
import concourse.bass as bass
import concourse.mybir as mybir
from concourse.bass_utils import run_bass_kernel_spmd

F32 = mybir.dt.float32
BF16 = mybir.dt.bfloat16
AF = mybir.ActivationFunctionType
ALU = mybir.AluOpType
AX = mybir.AxisListType

ENGS = ("pe", "act", "dve", "pool", "sp")
NS = 8


class Buf:
    __slots__ = ("w", "r", "rd", "name", "excl")

    def __init__(self, name=""):
        self.excl = False
        self.w = None
        self.r = {}
        self.rd = []
        self.name = name


class Prog:
    def __init__(self, nc):
        self.nc = nc
        self.sem = {e: nc.alloc_semaphore("s_" + e) for e in ENGS}
        self.dsem = {q: [nc.alloc_semaphore("d_%s_%d" % (q, i)) for i in range(NS)]
                     for q in ("sp", "act", "pool")}
        self.dcount = {q: 0 for q in self.dsem}
        self.base = {e: 0 for e in ENGS}
        self.rec = {e: [] for e in ENGS}
        self.off = {e: 0 for e in ENGS}
        self.msmap = {e: {} for e in ENGS}
        self.waited = {}
        self.limit = None
        self.nrec = 0

    def _need(self, eng, tok, waits):
        if tok is None:
            return
        if tok[0] == "c":
            _, se, gidx = tok
            if se == "pe" and eng == "pe":
                return
            key = (eng, se)
            if self.waited.get(key, -1) >= gidx:
                return
            self.waited[key] = gidx
            li = gidx - self.off[se]
            if li >= 0:
                self.rec[se][li][3] = True
            waits.append(tok)
        else:
            _, q, n = tok
            key = (eng, q, n % NS)
            if self.waited.get(key, -1) >= n:
                return
            self.waited[key] = n
            waits.append(tok)

    def _deps(self, eng, tok, r, w, waits):
        rx = [b for b in r if b.excl]
        if rx:
            r = [b for b in r if not b.excl]
            w = list(w) + [b for b in rx if b not in w]
        for b in r:
            self._need(eng, b.w, waits)
        for b in w:
            self._need(eng, b.w, waits)
            for se, gi in b.r.items():
                self._need(eng, ("c", se, gi), waits)
            for t in b.rd:
                self._need(eng, t, waits)
        for b in r:
            if tok[0] == "c":
                b.r[tok[1]] = tok[2]
            else:
                b.rd.append(tok)
                if len(b.rd) > 3 * NS:
                    b.rd = b.rd[-3 * NS:]
        for b in w:
            b.w = tok
            b.r = {}
            b.rd = []

    def op(self, eng, meth, r=(), w=(), **kw):
        fn = (meth, kw)
        self.nrec += 1
        if self.limit is not None and self.nrec > self.limit:
            return None
        waits = []
        gidx = self.off[eng] + len(self.rec[eng])
        tok = ("c", eng, gidx)
        self._deps(eng, tok, r, w, waits)
        self.rec[eng].append(["op", fn, waits, False])
        return tok

    def dma(self, q, out, in_, r=(), w=(), slow=False):
        self.nrec += 1
        if self.limit is not None and self.nrec > self.limit:
            return None
        waits = []
        n = self.dcount[q]
        self.dcount[q] += 1
        tok = ("d", q, n)
        if n >= NS:
            self._need(q, ("d", q, n - NS), waits)
        self._deps(q, tok, r, w, waits)
        self.rec[q].append(["dma", (out, in_, q, n, slow), waits, False])
        return tok

    def barrier(self, bufs=()):
        for e in ENGS:
            waits = []
            for se in ENGS:
                if se == e:
                    continue
                for li in range(len(self.rec[se]) - 1, -1, -1):
                    if self.rec[se][li][0] == "op":
                        self._need(e, ("c", se, self.off[se] + li), waits)
                        break
            for q in self.dsem:
                for k in range(max(0, self.dcount[q] - NS), self.dcount[q]):
                    self._need(e, ("d", q, k), waits)
            self.rec[e].append(["nop", None, waits, False])

    def emit(self):
        nc = self.nc
        for e in ENGS:
            c = self.base[e]
            for li, rcd in enumerate(self.rec[e]):
                if rcd[0] == "op" and rcd[3]:
                    c += 1
                    self.msmap[e][self.off[e] + li] = c
            self.base[e] = c

        def run(e, engobj):
            for rcd in self.rec[e]:
                kind, fn, waits, ms = rcd
                for t in waits:
                    if t[0] == "c":
                        engobj.wait_ge(self.sem[t[1]], self.msmap[t[1]][t[2]])
                    else:
                        engobj.wait_ge(self.dsem[t[1]][t[2] % NS], 16 * (t[2] // NS + 1))
                if kind == "op":
                    ins = getattr(engobj, fn[0])(**fn[1])
                    if ms:
                        ins.then_inc(self.sem[e], 1)
                elif kind == "dma":
                    out, in_, q, n, slow = fn
                    if slow:
                        ins = engobj.dma_start(out=out, in_=in_, allow_slow_non_contiguous=True)
                    else:
                        ins = engobj.dma_start(out=out, in_=in_)
                    ins.then_inc(self.dsem[q][n % NS], 16)

        with nc.Block() as block:
            @block.tensor
            def _(eng):
                run("pe", eng)

            @block.scalar
            def _(eng):
                run("act", eng)

            @block.vector
            def _(eng):
                run("dve", eng)

            @block.gpsimd
            def _(eng):
                run("pool", eng)

            @block.sync
            def _(eng):
                run("sp", eng)

        for e in ENGS:
            self.off[e] += len(self.rec[e])
            self.rec[e] = []

from contextlib import ExitStack

D = 1024
NKC = 8
DFF = 4096
EPS = 1e-6
TT = 512
FM_COLS = 3072
TM0 = 3072
WIN_COLS = 3856
GRID_W = 64


SBUF_LIMIT = 196608


def _chk(nc, handle):
    m = nc.lookup_mloc(handle)
    nbytes = 1
    for d in list(m.dims)[1:]:
        nbytes *= int(d)
    assert int(m.addr) + nbytes <= SBUF_LIMIT, ("SBUF overflow past physical limit", handle.name, int(m.addr), nbytes)
    return handle


class Tile(Buf):
    __slots__ = ("ap",)

    def __init__(self, ap, name=""):
        Buf.__init__(self, name)
        self.ap = ap


def rope_tables(T):
    t = np.arange(T)
    row = (t // GRID_W).astype(np.float32)
    col = (t % GRID_W).astype(np.float32)
    nf = 32
    inv = (np.float32(10000.0) ** (-np.arange(nf, dtype=np.float32) / np.float32(nf))).astype(np.float32)
    cosT = np.zeros((128, T), np.float32)
    sinT = np.zeros((128, T), np.float32)
    for d in range(128):
        half = d // 64
        idx = d % 64
        f = idx % 32
        pos = row if half == 0 else col
        ang = (pos * inv[f]).astype(np.float32)
        cosT[d] = np.cos(ang)
        sinT[d] = np.sin(ang) * (-1.0 if idx < 32 else 1.0)
    return cosT, sinT


def rope_perm():
    p = np.zeros(128, np.int64)
    for d in range(128):
        idx = d % 64
        p[d] = d + 32 if idx < 32 else d - 32
    return p


def gdn_masks():
    t = np.arange(128)[:, None]
    j = np.arange(128)[None, :]
    same = (t // 64) == (j // 64)
    m = {}
    m["cum_f"] = (same & (t <= j)).astype(np.float32)
    m["cum_b"] = (same & (t >= j)).astype(np.float32)
    m["same"] = same.astype(np.float32)
    i = t
    m["pos_f"] = np.where(same & (i >= j), 0.0, 30000.0).astype(np.float32)
    m["pos_b"] = np.where(same & (i <= j), 0.0, 30000.0).astype(np.float32)
    m["strict_f"] = (same & (i > j)).astype(np.float32)
    m["strict_b"] = (same & (i < j)).astype(np.float32)
    m["c0"] = np.tile((np.arange(128) < 64).astype(np.float32)[:, None], (1, 128))
    m["c1"] = np.tile((np.arange(128) >= 64).astype(np.float32)[:, None], (1, 128))
    return m


CONST_NAMES = ["ident", "cum_f", "cum_b", "same", "pos_f", "pos_b", "strict_f", "strict_b", "c0", "c1"]


def build(Tp, Ts, debug=False, stop_after=9):
    nc = bass.Bass("TRN2", target_bir_lowering=False)
    P = Prog(nc)
    seqs = [("p", Tp), ("s", Ts)]
    Tmax = max(Tp, Ts)

    def din(name, shape, dt=F32):
        return nc.dram_tensor(name, list(shape), dt, kind="ExternalInput").ap()

    def dscr(name, shape, dt):
        if debug:
            return nc.dram_tensor(name, list(shape), dt, kind="ExternalOutput").ap()
        return nc.dram_tensor(name, list(shape), dt).ap()

    X = {"p": din("x_p", [Tp, D]), "s": din("x_s", [Ts, D])}
    Y = {"p": nc.dram_tensor("y_p", [Tp, D], F32, kind="ExternalOutput").ap(),
         "s": nc.dram_tensor("y_s", [Ts, D], F32, kind="ExternalOutput").ap()}
    w_in = din("w_in_r", [D, WIN_COLS])
    w_out = din("w_out", [D, D])
    w_up = din("w_up", [D, DFF])
    w_down = din("w_down", [DFF, D])
    nrm = din("norms", [4, D])
    conv_w_r = din("conv_w_r", [128, 12, 5])
    nrm_r = din("nrm_r", [128, 4, NKC])
    hnorm_r = din("hnorm_r", [128, 5])
    gparam = din("gparam", [2, 8])
    hnorm = din("hnorm", [5, 128])
    cosT = din("cosT", [128, Tmax])
    sinT = din("sinT", [128, Tmax])
    cst = din("cst", [len(CONST_NAMES), 128, 128])

    S = {}
    for sn, T in seqs:
        S[sn] = dict(
            A=dscr("A_raw_" + sn, [1536, T], BF16),
            QK=dscr("QK_T_" + sn, [768, T], BF16),
            Z=dscr("Z_" + sn, [T, 512], BF16),
            G=dscr("G_" + sn, [T, 16], F32),
            V=dscr("V_" + sn, [T, 256], BF16),
            GQ=dscr("GQ_T_" + sn, [512, T], BF16),
            GK=dscr("GK_T_" + sn, [512, T], BF16),
            GKt=dscr("GKt_" + sn, [T, 512], BF16),
            GVt=dscr("GVt_" + sn, [T, 512], BF16),
            OF=dscr("OF_" + sn, [T, 512], F32),
            OB=dscr("OB_" + sn, [T, 512], F32),
            MIX=dscr("MIX_T_" + sn, [1024, T], BF16),
            X1=dscr("X1_" + sn, [T, 1024], F32),
        )
    dbufs = {}

    def DB(key):
        if key not in dbufs:
            dbufs[key] = Buf(str(key))
        return dbufs[key]

    ps = [Tile(nc.alloc_psum_tensor("ps%d" % i, [128, 512], F32).ap(), "ps%d" % i) for i in range(8)]
    for t in ps:
        t.excl = True

    with ExitStack() as g_es:
        def gtile(name, shape, dt):
            return Tile(_chk(nc, g_es.enter_context(nc.sbuf_tensor(name, list(shape), dt))).ap(), name)

        identf = gtile("identf", [128, 128], F32)
        P.dma("sp", identf.ap, cst[0], w=[identf])
        identb = gtile("identb", [128, 128], BF16)
        P.op("dve", "tensor_copy", out=identb.ap, in_=identf.ap, r=[identf], w=[identb])
        onesb = gtile("onesb", [128, 128], BF16)
        P.op("pool", "memset", ap=onesb.ap, constant=1.0, w=[onesb])
        onesf = gtile("onesf", [128, 128], F32)
        P.op("pool", "memset", ap=onesf.ap, constant=1.0, w=[onesf])
        epsc = gtile("epsc", [128, 1], F32)
        P.op("pool", "memset", ap=epsc.ap, constant=EPS, w=[epsc])
        nrm_t = gtile("nrm_t", [128, 4, NKC], F32)
        P.dma("sp", nrm_t.ap, nrm_r, w=[nrm_t])
        hn_t = gtile("hn_t", [128, 5], F32)
        P.dma("sp", hn_t.ap, hnorm_r, w=[hn_t])
        gp_b = gtile("gp_b", [128, 2, 8], F32)
        P.dma("sp", gp_b.ap[:, 0, :], gparam[0:1, :].partition_broadcast(128), w=[gp_b])
        P.dma("sp", gp_b.ap[:, 1, :], gparam[1:2, :].partition_broadcast(128), w=[gp_b])
        negA = gtile("negA", [128, 8], F32)
        P.op("act", "activation", out=negA.ap, in_=gp_b.ap[:, 0, :], func=AF.Exp, r=[gp_b], w=[negA])
        P.op("dve", "tensor_scalar", out=negA.ap, in0=negA.ap, scalar1=-1.0, scalar2=None, op0=ALU.mult,
             r=[negA], w=[negA])

        def load_weight_bf16(wt, st, src, nk, cols, scale_sel, stage_cols):
            n = 0
            for kc in range(nk):
                for c0 in range(0, cols, stage_cols):
                    cn = min(stage_cols, cols - c0)
                    s_ = st[n % 2]
                    P.dma("sp", s_.ap[:, 0:cn], src[kc * 128:(kc + 1) * 128, c0:c0 + cn], w=[s_])
                    dst = wt.ap[:, kc, c0:c0 + cn]
                    if scale_sel is None:
                        if n % 2 == 0:
                            P.op("act", "copy", out=dst, in_=s_.ap[:, 0:cn], r=[s_], w=[wt])
                        else:
                            P.op("dve", "tensor_copy", out=dst, in_=s_.ap[:, 0:cn], r=[s_], w=[wt])
                    else:
                        sc = nrm_t.ap[:, scale_sel, kc:kc + 1]
                        if n % 2 == 0:
                            P.op("act", "activation", out=dst, in_=s_.ap[:, 0:cn], func=AF.Copy, scale=sc,
                                 r=[s_, nrm_t], w=[wt])
                        else:
                            P.op("dve", "tensor_scalar", out=dst, in0=s_.ap[:, 0:cn], scalar1=sc, scalar2=None, op0=ALU.mult,
                                 r=[s_, nrm_t], w=[wt])
                    n += 1
            return wt

        def rms_rstd(ssq, rstd, n, inv_n):
            P.op("act", "activation", out=rstd.ap[:, 0:n], in_=ssq.ap[:, 0:n], func=AF.Ln, bias=epsc.ap[:, 0:1], scale=inv_n,
                 r=[ssq, epsc], w=[rstd])
            P.op("act", "activation", out=rstd.ap[:, 0:n], in_=rstd.ap[:, 0:n], func=AF.Exp, scale=-0.5,
                 r=[rstd], w=[rstd])

        with ExitStack() as es:
            def tile(name, shape, dt):
                return Tile(_chk(nc, es.enter_context(nc.sbuf_tensor(name, list(shape), dt))).ap(), name)

            wi = tile("wi", [128, NKC, WIN_COLS], BF16)
            with ExitStack() as ses:
                st = [Tile(_chk(nc, ses.enter_context(nc.sbuf_tensor("wst%d" % i, [128, 1928], F32))).ap()) for i in range(2)]
                load_weight_bf16(wi, st, w_in, NKC, WIN_COLS, 0, 1928)
                P.barrier()
                P.emit()
            xt = [tile("xt%d" % i, [128, 4, D], F32) for i in range(2)]
            hb = tile("hb", [128, 4, D], BF16)
            junk = tile("junk", [128, D], BF16)
            ssq = tile("ssq", [128, 4], F32)
            rstd = tile("rstd", [128, 4], F32)
            hT = [tile("hT%d" % i, [128, NKC, TT], BF16) for i in range(1)]
            a_st = [tile("a_st%d" % i, [128, 12, TT], BF16) for i in range(1)]
            qk_st = [tile("qk_st%d" % i, [128, 6, TT], BF16) for i in range(1)]
            z_st = [tile("z_st%d" % i, [128, 4, 512], BF16) for i in range(1)]
            v_st = [tile("v_st%d" % i, [128, 4, 256], BF16) for i in range(1)]
            g_st = [tile("g_st%d" % i, [128, 4, 16], F32) for i in range(1)]
            cs_t = [tile("cs_t%d" % i, [128, 2, TT], F32) for i in range(1)]
            sq_t = [tile("sq_t%d" % i, [128, TT], BF16) for i in range(2)]
            rs_t = [tile("rs_t%d" % i, [128, TT], F32) for i in range(2)]
            t1_t = [tile("t1_t%d" % i, [128, TT], F32) for i in range(2)]
            t2_t = [tile("t2_t%d" % i, [128, TT], F32) for i in range(2)]
            gt_t = [tile("gt_t%d" % i, [128, 16], F32) for i in range(2)]

            it = 0
            for sn, T in seqs:
                sc = S[sn]
                ntile = T // TT
                for ti in range(ntile):
                    t0 = ti * TT
                    x_ = xt[it % 2]
                    hT_ = hT[0]
                    P.dma("sp", x_.ap, X[sn][t0:t0 + TT, :].rearrange("(s p) d -> p s d", p=128), w=[x_])
                    cs_ = cs_t[0]
                    P.dma("sp", cs_.ap[:, 0, :], cosT[:, t0:t0 + TT], w=[cs_])
                    P.dma("sp", cs_.ap[:, 1, :], sinT[:, t0:t0 + TT], w=[cs_])
                    for s in range(4):
                        P.op("act", "activation", out=junk.ap, in_=x_.ap[:, s, :], func=AF.Square,
                                                                     accum_out=ssq.ap[:, s:s + 1],
                             r=[x_], w=[junk, ssq])
                    rms_rstd(ssq, rstd, 4, 1.0 / D)
                    for s in range(4):
                        P.op("dve", "tensor_scalar", out=hb.ap[:, s, :], in0=x_.ap[:, s, :],
                                                                        scalar1=rstd.ap[:, s:s + 1], scalar2=None, op0=ALU.mult,
                             r=[x_, rstd], w=[hb])
                    for s in range(4):
                        pb = ps[s % 2]
                        pv = pb.ap.bitcast(BF16).rearrange("p (k t) -> p k t", k=NKC)
                        for kc in range(NKC):
                            P.op("pe", "transpose", out=pv[:, kc, :], in_=hb.ap[:, s, kc * 128:(kc + 1) * 128],
                                                                               identity=identb.ap,
                                 r=[hb, identb], w=[pb])
                        eng = "act" if s % 2 == 0 else "dve"
                        if eng == "act":
                            P.op("act", "copy", out=hT_.ap[:, :, s * 128:(s + 1) * 128], in_=pv, r=[pb], w=[hT_])
                        else:
                            P.op("dve", "tensor_copy", out=hT_.ap[:, :, s * 128:(s + 1) * 128], in_=pv, r=[pb], w=[hT_])

                    def fm_mm(pb, c):
                        for kc in range(NKC):
                            P.op("pe", "matmul", out=pb.ap, lhsT=wi.ap[:, kc, c * 128:(c + 1) * 128], rhs=hT_.ap[:, kc, :],
                                                                            start=(kc == 0), stop=(kc == NKC - 1),
                                 r=[wi, hT_], w=[pb])

                    a_ = a_st[0]
                    for c in range(12):
                        pb = ps[2 + (c % 2)]
                        fm_mm(pb, c)
                        if c % 2 == 0:
                            P.op("act", "copy", out=a_.ap[:, c, :], in_=pb.ap, r=[pb], w=[a_])
                        else:
                            P.op("dve", "tensor_copy", out=a_.ap[:, c, :], in_=pb.ap, r=[pb], w=[a_])
                    P.dma("sp", sc["A"].rearrange("(c p) t -> p c t", p=128)[:, :, t0:t0 + TT], a_.ap, r=[a_], w=[DB(("A", sn))])
                    qk_ = qk_st[0]
                    for j in range(6):
                        pm = ps[2 + (j % 2)]
                        pp = ps[4 + (j % 2)]
                        pc = ps[6 + (j % 2)]
                        sq_, rs_, t1_, t2_ = sq_t[j % 2], rs_t[j % 2], t1_t[j % 2], t2_t[j % 2]
                        fm_mm(pm, 12 + j)
                        fm_mm(pp, 18 + j)
                        wsel = 1 if j < 4 else 3
                        P.op("act", "activation", out=sq_.ap, in_=pm.ap, func=AF.Square, r=[pm], w=[sq_])
                        P.op("pe", "matmul", out=pc.ap, lhsT=onesb.ap, rhs=sq_.ap, start=True, stop=True,
                             r=[onesb, sq_], w=[pc])
                        P.op("act", "activation", out=rs_.ap, in_=pc.ap, func=AF.Ln, bias=epsc.ap[:, 0:1], scale=1.0 / 128,
                             r=[pc, epsc], w=[rs_])
                        P.op("act", "activation", out=rs_.ap, in_=rs_.ap, func=AF.Exp, scale=-0.5, r=[rs_], w=[rs_])
                        P.op("dve", "scalar_tensor_tensor", out=t1_.ap, in0=pm.ap, scalar=hn_t.ap[:, wsel:wsel + 1],
                                                                                               in1=cs_.ap[:, 0, :], op0=ALU.mult, op1=ALU.mult,
                             r=[pm, hn_t, cs_], w=[t1_])
                        P.op("dve", "scalar_tensor_tensor", out=t2_.ap, in0=pp.ap, scalar=hn_t.ap[:, wsel + 1:wsel + 2],
                                                                                               in1=cs_.ap[:, 1, :], op0=ALU.mult, op1=ALU.mult,
                             r=[pp, hn_t, cs_], w=[t2_])
                        P.op("pool", "tensor_tensor", out=t1_.ap, in0=t1_.ap, in1=t2_.ap, op=ALU.add, r=[t1_, t2_], w=[t1_])
                        P.op("pool", "tensor_tensor", out=qk_.ap[:, j, :], in0=t1_.ap, in1=rs_.ap, op=ALU.mult,
                             r=[t1_, rs_], w=[qk_])
                    P.dma("sp", sc["QK"].rearrange("(c p) t -> p c t", p=128)[:, :, t0:t0 + TT], qk_.ap, r=[qk_], w=[DB(("QK", sn))])
                    z_, v_, g_ = z_st[0], v_st[0], g_st[0]
                    for s in range(4):
                        pz = ps[s % 2]
                        pg = ps[6 + (s % 2)]
                        gt_ = gt_t[s % 2]
                        for kc in range(NKC):
                            P.op("pe", "matmul", out=pz.ap, lhsT=hT_.ap[:, kc, s * 128:(s + 1) * 128], rhs=wi.ap[:, kc, TM0:TM0 + 512],
                                                                            start=(kc == 0), stop=(kc == NKC - 1), r=[hT_, wi], w=[pz])
                        for kc in range(NKC):
                            P.op("pe", "matmul", out=pg.ap[:, 0:272], lhsT=hT_.ap[:, kc, s * 128:(s + 1) * 128],
                                                                            rhs=wi.ap[:, kc, TM0 + 512:TM0 + 784],
                                                                            start=(kc == 0), stop=(kc == NKC - 1), r=[hT_, wi], w=[pg])
                        P.op("act", "copy", out=z_.ap[:, s, :], in_=pz.ap, r=[pz], w=[z_])
                        P.op("dve", "tensor_copy", out=v_.ap[:, s, :], in_=pg.ap[:, 16:272], r=[pg], w=[v_])
                        P.op("act", "activation", out=gt_.ap[:, 0:8], in_=pg.ap[:, 0:8], func=AF.Exp, scale=-1.0, r=[pg], w=[gt_])
                        P.op("dve", "tensor_scalar", out=gt_.ap[:, 0:8], in0=gt_.ap[:, 0:8], scalar1=1.0, scalar2=None, op0=ALU.add,
                             r=[gt_], w=[gt_])
                        P.op("dve", "reciprocal", out=g_.ap[:, s, 0:8], in_=gt_.ap[:, 0:8], r=[gt_], w=[g_])
                        P.op("dve", "tensor_tensor", out=gt_.ap[:, 8:16], in0=pg.ap[:, 8:16], in1=gp_b.ap[:, 1, :], op=ALU.add,
                             r=[pg, gp_b], w=[gt_])
                        P.op("act", "activation", out=gt_.ap[:, 8:16], in_=gt_.ap[:, 8:16], func=AF.Exp, r=[gt_], w=[gt_])
                        P.op("dve", "tensor_scalar", out=gt_.ap[:, 8:16], in0=gt_.ap[:, 8:16], scalar1=1.0, scalar2=None, op0=ALU.add,
                             r=[gt_], w=[gt_])
                        P.op("act", "activation", out=gt_.ap[:, 8:16], in_=gt_.ap[:, 8:16], func=AF.Ln, r=[gt_], w=[gt_])
                        P.op("dve", "tensor_tensor", out=g_.ap[:, s, 8:16], in0=gt_.ap[:, 8:16], in1=negA.ap, op=ALU.mult,
                             r=[gt_, negA], w=[g_])
                    P.dma("sp", sc["Z"][t0:t0 + TT, :].rearrange("(s p) c -> p s c", p=128), z_.ap, r=[z_], w=[DB(("Z", sn))])
                    P.dma("sp", sc["V"][t0:t0 + TT, :].rearrange("(s p) c -> p s c", p=128), v_.ap, r=[v_], w=[DB(("V", sn))])
                    P.dma("sp", sc["G"][t0:t0 + TT, :].rearrange("(s p) c -> p s c", p=128), g_.ap, r=[g_], w=[DB(("G", sn))])
                    it += 1
            P.barrier()
            P.emit()

        if stop_after >= 2:
            phases_rest(locals())
    return nc


def host_inputs(inp, Tmax):
    f = np.float32
    w_in = np.asarray(inp["w_in"][0], f)
    perm = rope_perm()
    qB = w_in[:, 2064:2576]
    kB = w_in[:, 2576:2832]
    qBp = qB.reshape(D, 4, 128)[:, :, perm].reshape(D, 512)
    kBp = kB.reshape(D, 2, 128)[:, :, perm].reshape(D, 256)
    w_in_r = np.ascontiguousarray(np.concatenate(
        [w_in[:, 0:1536], qB, kB, qBp, kBp, w_in[:, 1536:2048], w_in[:, 2048:2064], w_in[:, 2832:3088]], axis=1))
    norms = np.stack([np.asarray(inp[k][0], f) for k in ("norm_mix_pre", "norm_mix_post", "norm_mlp_pre", "norm_mlp_post")])
    gparam = np.stack([np.concatenate([np.asarray(inp["A_log_f"][0], f), np.asarray(inp["A_log_b"][0], f)]),
                       np.concatenate([np.asarray(inp["dt_bias_f"][0], f), np.asarray(inp["dt_bias_b"][0], f)])])
    qn = np.asarray(inp["q_norm_w"][0], f)
    kn = np.asarray(inp["k_norm_w"][0], f)
    hnorm = np.stack([np.asarray(inp["gdn_norm_w"][0], f), qn, qn[perm], kn, kn[perm]])
    cosT, sinT = rope_tables(Tmax)
    m = gdn_masks()
    m["ident"] = np.eye(128, dtype=f)
    cst = np.stack([m[n] for n in CONST_NAMES]).astype(f)
    return dict(w_in_r=w_in_r, w_out=np.ascontiguousarray(np.asarray(inp["w_out"][0], f)),
                w_up=np.ascontiguousarray(np.asarray(inp["w_up"][0], f)),
                w_down=np.ascontiguousarray(np.asarray(inp["w_down"][0], f)),
                norms=np.ascontiguousarray(norms),
                nrm_r=np.ascontiguousarray(norms.reshape(4, NKC, 128).transpose(2, 0, 1)),
                hnorm_r=np.ascontiguousarray(hnorm.T),
                conv_w_r=np.ascontiguousarray(np.asarray(inp["conv_w"][0], f).reshape(5, 12, 128).transpose(2, 1, 0)),
                gparam=np.ascontiguousarray(gparam), hnorm=np.ascontiguousarray(hnorm),
                cosT=cosT, sinT=sinT, cst=cst)


_NC_CACHE = {}


def run(inp, Tp, Ts, ncores, debug=False, stop_after=9):
    key = (Tp, Ts, debug, stop_after)
    if key not in _NC_CACHE:
        _NC_CACHE[key] = build(Tp, Ts, debug, stop_after)
    nc = _NC_CACHE[key]
    shared = host_inputs(inp, max(Tp, Ts))
    xp = np.asarray(inp["x_prompt"], np.float32)
    xs = np.asarray(inp["x_sample"], np.float32)
    in_maps = []
    for c in range(ncores):
        m = dict(shared)
        m["x_p"] = np.ascontiguousarray(xp[c])
        m["x_s"] = np.ascontiguousarray(xs[c])
        in_maps.append(m)
    res = run_bass_kernel_spmd(nc, in_maps, core_ids=list(range(ncores)))
    return res.results


def kernel(**inputs):
    res = run(inputs, 8192, 2048, 8)
    yp = np.stack([np.asarray(r["y_p"], np.float32) for r in res])
    ys = np.stack([np.asarray(r["y_s"], np.float32) for r in res])
    return (yp, ys)


def phase_attn(L):
    nc, P, ps, S, seqs, hnorm, identb = L["nc"], L["P"], L["ps"], L["S"], L["seqs"], L["hnorm"], L["identb"]
    DB = L["DB"]
    with ExitStack() as es:
        def tile(name, shape, dt):
            return Tile(_chk(nc, es.enter_context(nc.sbuf_tensor(name, list(shape), dt))).ap(), name)
        Tmax = max(T for _, T in seqs)
        NBmax = Tmax // 128
        wqk_b = tile("wqk_b", [128, 2, 128], F32)
        P.dma("sp", wqk_b.ap[:, 0, :], hnorm[1:2, :].partition_broadcast(128), w=[wqk_b])
        P.dma("sp", wqk_b.ap[:, 1, :], hnorm[3:4, :].partition_broadcast(128), w=[wqk_b])
        mx = tile("mx", [128, 2], F32)
        P.op("dve", "tensor_reduce", out=mx.ap, in_=wqk_b.ap, axis=AX.X, op=ALU.max, apply_absolute_value=True, r=[wqk_b], w=[mx])
        negM = tile("negM", [128, 1], F32)
        P.op("dve", "tensor_tensor", out=negM.ap, in0=mx.ap[:, 0:1], in1=mx.ap[:, 1:2], op=ALU.mult, r=[mx], w=[negM])
        P.op("dve", "tensor_scalar", out=negM.ap, in0=negM.ap, scalar1=-(128.0 ** 0.5), scalar2=None, op0=ALU.mult, r=[negM], w=[negM])
        kT = tile("kT", [128, Tmax], BF16)
        va = tile("va", [128, NBmax, 129], BF16)
        P.op("pool", "memset", ap=va.ap, constant=1.0, w=[va])
        qT = [tile("qT%d" % i, [128, TT], BF16) for i in range(4)]
        pT = [tile("pT%d" % i, [128, TT], BF16) for i in range(6)]
        osb = [tile("osb%d" % i, [128, 4, 129], F32) for i in range(2)]
        rcp = [tile("rcp%d" % i, [128, 4], F32) for i in range(2)]
        onb = [tile("onb%d" % i, [128, 4, 128], BF16) for i in range(2)]
        mst = [tile("mst%d" % i, [128, TT], BF16) for i in range(2)]
        scale = 128.0 ** -0.5
        qi = 0
        npt = 0
        for sn, T in seqs:
            sc = S[sn]
            NB = T // 128
            for g in range(2):
                P.dma("sp", kT.ap[:, 0:T], sc["QK"][(4 + g) * 128:(5 + g) * 128, :], r=[DB(("QK", sn))], w=[kT])
                P.dma("sp", va.ap[:, 0:NB, 0:128], sc["V"][:, g * 128:(g + 1) * 128].rearrange("(b p) c -> p b c", p=128),
                      r=[DB(("V", sn))], w=[va])
                for h in (2 * g, 2 * g + 1):
                    for qp in range(T // (2 * TT)):
                        q0s = [(2 * qp + s) * TT for s in range(2)]
                        q_s = [qT[(2 * qi + s) % 4] for s in range(2)]
                        for s in range(2):
                            P.dma("sp", q_s[s].ap, sc["QK"][h * 128:(h + 1) * 128, q0s[s]:q0s[s] + TT], r=[DB(("QK", sn))], w=[q_s[s]])
                        sbank = [[ps[2 * s], ps[2 * s + 1]] for s in range(2)]
                        accb = [[ps[4 + 2 * s], ps[5 + 2 * s]] for s in range(2)]
                        accv = [[b.ap[:, 0:258].rearrange("p (a c) -> p a c", a=2) for b in accb[s]] for s in range(2)]

                        def qk(s, kb):
                            sb = sbank[s][kb % 2]
                            P.op("pe", "matmul", out=sb.ap, lhsT=kT.ap[:, kb * 128:(kb + 1) * 128], rhs=q_s[s].ap, start=True, stop=True,
                                 r=[kT, q_s[s]], w=[sb])

                        qk(0, 0)
                        qk(1, 0)
                        for kb in range(NB):
                            p_s = []
                            for s in range(2):
                                if kb + 1 < NB:
                                    qk(s, kb + 1)
                                sb = sbank[s][kb % 2]
                                p_ = pT[npt % 6]
                                npt += 1
                                P.op("act", "activation", out=p_.ap, in_=sb.ap, func=AF.Exp, bias=negM.ap[:, 0:1], scale=scale,
                                     r=[sb, negM], w=[p_])
                                p_s.append(p_)
                            for s in range(2):
                                for qs in range(4):
                                    P.op("pe", "matmul", out=accv[s][qs // 2][:, qs % 2, :], lhsT=p_s[s].ap[:, qs * 128:(qs + 1) * 128], rhs=va.ap[:, kb, :],
                                         start=(kb == 0 and qs % 2 == 0), stop=(kb == NB - 1), skip_group_check=True,
                                         r=[p_s[s], va], w=[accb[s][qs // 2]])
                        for s in range(2):
                            o_, r_, n_, m_ = osb[s], rcp[s], onb[s], mst[s]
                            P.op("dve", "tensor_copy", out=o_.ap[:, 0:2, :], in_=accv[s][0], r=[accb[s][0]], w=[o_])
                            P.op("dve", "tensor_copy", out=o_.ap[:, 2:4, :], in_=accv[s][1], r=[accb[s][1]], w=[o_])
                            P.op("dve", "reciprocal", out=r_.ap, in_=o_.ap[:, :, 128], r=[o_], w=[r_])
                            P.op("dve", "tensor_tensor", out=n_.ap, in0=o_.ap[:, :, 0:128], in1=r_.ap.unsqueeze(2).to_broadcast([128, 4, 128]),
                                 op=ALU.mult, r=[o_, r_], w=[n_])
                            tb = sbank[s][0]
                            tv = tb.ap.bitcast(BF16)[:, 0:512].rearrange("p (a c) -> p a c", a=4)
                            for qs in range(4):
                                P.op("pe", "transpose", out=tv[:, qs, :], in_=n_.ap[:, qs, :], identity=identb.ap, r=[n_, identb], w=[tb])
                            P.op("dve", "tensor_copy", out=m_.ap.rearrange("p (a c) -> p a c", a=4), in_=tv, r=[tb], w=[m_])
                            P.dma("sp", sc["MIX"][512 + h * 128:512 + (h + 1) * 128, q0s[s]:q0s[s] + TT], m_.ap, r=[m_], w=[DB(("MIX", sn))])
                        qi += 1
        P.barrier()
        P.emit()


def phase_mlp(L):
    nc, P, ps, S, seqs, identb = L["nc"], L["P"], L["ps"], L["S"], L["seqs"], L["identb"]
    DB, X, Y, rms_rstd, nrm = L["DB"], L["X"], L["Y"], L["rms_rstd"], L["nrm"]
    load_weight_bf16 = L["load_weight_bf16"]

    def make_norm_resid(mo, junk, ssq, rstd, nrm_b):
        def norm_resid(src_banks, resid_ap, out_ap, which, deps_resid, deps_out):
            P.op("act", "copy", out=mo.ap[:, 0:512], in_=src_banks[0].ap, r=[src_banks[0]], w=[mo])
            P.op("dve", "tensor_copy", out=mo.ap[:, 512:1024], in_=src_banks[1].ap, r=[src_banks[1]], w=[mo])
            P.op("act", "activation", out=junk.ap, in_=mo.ap, func=AF.Square, accum_out=ssq.ap[:, 0:1], r=[mo], w=[junk, ssq])
            rms_rstd(ssq, rstd, 1, 1.0 / D)
            P.op("dve", "scalar_tensor_tensor", out=mo.ap, in0=mo.ap, scalar=rstd.ap[:, 0:1], in1=nrm_b.ap[:, which, :],
                 op0=ALU.mult, op1=ALU.mult, r=[mo, rstd, nrm_b], w=[mo])
            P.op("pool", "tensor_tensor", out=out_ap, in0=mo.ap, in1=resid_ap, op=ALU.add, r=[mo] + deps_resid, w=deps_out)
        return norm_resid

    with ExitStack() as es:
        def tile(name, shape, dt):
            return Tile(_chk(nc, es.enter_context(nc.sbuf_tensor(name, list(shape), dt))).ap(), name)
        wo = tile("wo", [128, 8, D], BF16)
        with ExitStack() as ses:
            st = [Tile(_chk(nc, ses.enter_context(nc.sbuf_tensor("wst4a_%d" % i, [128, 1024], F32))).ap()) for i in range(2)]
            load_weight_bf16(wo, st, L["w_out"], 8, D, None, 1024)
            P.barrier()
            P.emit()
        nrm_b = tile("nrm_b", [128, 2, D], F32)
        P.dma("sp", nrm_b.ap[:, 0, :], nrm[1:2, :].partition_broadcast(128), w=[nrm_b])
        P.dma("sp", nrm_b.ap[:, 1, :], nrm[3:4, :].partition_broadcast(128), w=[nrm_b])
        xt = tile("x4", [128, 4, D], F32)
        mT = [tile("mT%d" % i, [128, 8, TT], BF16) for i in range(2)]
        mo = tile("mo", [128, D], F32)
        junk = tile("junk4", [128, D], BF16)
        ssq = tile("ssq4", [128, 1], F32)
        rstd = tile("rstd4", [128, 1], F32)
        norm_resid = make_norm_resid(mo, junk, ssq, rstd, nrm_b)
        it = 0
        for sn, T in seqs:
            sc = S[sn]
            for ti in range(T // TT):
                t0 = ti * TT
                m_ = mT[it % 2]
                P.dma("sp", xt.ap, X[sn][t0:t0 + TT, :].rearrange("(s p) d -> p s d", p=128), w=[xt])
                P.dma("sp", m_.ap, sc["MIX"].rearrange("(k p) t -> p k t", p=128)[:, :, t0:t0 + TT], r=[DB(("MIX", sn))], w=[m_])
                for s in range(4):
                    banks = [ps[2 * (s % 2)], ps[2 * (s % 2) + 1]]
                    for cg in range(2):
                        for kc in range(8):
                            P.op("pe", "matmul", out=banks[cg].ap, lhsT=m_.ap[:, kc, s * 128:(s + 1) * 128], rhs=wo.ap[:, kc, cg * 512:(cg + 1) * 512],
                                 start=(kc == 0), stop=(kc == 7), r=[m_, wo], w=[banks[cg]])
                    norm_resid(banks, xt.ap[:, s, :], xt.ap[:, s, :], 0, [xt], [xt])
                P.dma("sp", sc["X1"][t0:t0 + TT, :].rearrange("(s p) d -> p s d", p=128), xt.ap, r=[xt], w=[DB(("X1", sn))])
                it += 1
        P.barrier()
        P.emit()
    T4 = 256
    with ExitStack() as es:
        def tile(name, shape, dt):
            return Tile(_chk(nc, es.enter_context(nc.sbuf_tensor(name, list(shape), dt))).ap(), name)
        wu = tile("wu", [128, 8, DFF], BF16)
        wd = tile("wd", [128, 32, D], BF16)
        with ExitStack() as ses:
            st = [Tile(_chk(nc, ses.enter_context(nc.sbuf_tensor("wst4_%d" % i, [128, 2048], F32))).ap()) for i in range(2)]
            load_weight_bf16(wu, st, L["w_up"], 8, DFF, 2, 2048)
            load_weight_bf16(wd, st, L["w_down"], 32, D, None, 1024)
            P.barrier()
            P.emit()
        nrm_b = tile("nrm_b2", [128, D], F32)
        P.dma("sp", nrm_b.ap, nrm[3:4, :].partition_broadcast(128), w=[nrm_b])
        xt = tile("x4b", [128, D], F32)
        mo = tile("mo2", [128, D], F32)
        ssq = tile("ssq4b", [128, 1], F32)
        rstd = tile("rstd4b", [128, 1], F32)
        ssq2 = tile("ssq4c", [128, 1], F32)
        rstd2 = tile("rstd4c", [128, 1], F32)
        h2 = tile("h2", [128, D], BF16)
        h2T = [tile("h2T%d" % i, [128, 8, T4], BF16) for i in range(2)]
        fT = tile("fT", [128, 32, T4], BF16)
        rl = tile("rl", [128, T4], F32)
        jk = tile("jk4", [128, D], BF16)
        tiles = [(sn, T, ti) for sn, T in seqs for ti in range(T // T4)]

        def prologue(i):
            sn, T, ti = tiles[i]
            sc = S[sn]
            t0 = ti * T4
            hT_ = h2T[i % 2]
            for s in range(2):
                P.dma("sp", xt.ap, sc["X1"][t0 + s * 128:t0 + (s + 1) * 128, :], r=[DB(("X1", sn))], w=[xt])
                P.op("act", "activation", out=h2.ap, in_=xt.ap, func=AF.Square, accum_out=ssq.ap[:, 0:1], r=[xt], w=[h2, ssq])
                rms_rstd(ssq, rstd, 1, 1.0 / D)
                P.op("dve", "tensor_scalar", out=h2.ap, in0=xt.ap, scalar1=rstd.ap[:, 0:1], scalar2=None, op0=ALU.mult,
                     r=[xt, rstd], w=[h2])
                pb = ps[2 + s]
                pv = pb.ap.bitcast(BF16).rearrange("p (k t) -> p k t", k=8)
                for kc in range(8):
                    P.op("pe", "transpose", out=pv[:, kc, :], in_=h2.ap[:, kc * 128:(kc + 1) * 128], identity=identb.ap,
                         r=[h2, identb], w=[pb])
                P.op("act", "copy", out=hT_.ap[:, :, s * 128:(s + 1) * 128], in_=pv, r=[pb], w=[hT_])

        prologue(0)
        for i, (sn, T, ti) in enumerate(tiles):
            sc = S[sn]
            t0 = ti * T4
            hT_ = h2T[i % 2]
            for fc in range(32):
                pb = ps[4 + (fc % 4)]
                for kc in range(8):
                    P.op("pe", "matmul", out=pb.ap[:, 0:T4], lhsT=wu.ap[:, kc, fc * 128:(fc + 1) * 128], rhs=hT_.ap[:, kc, :],
                         start=(kc == 0), stop=(kc == 7), r=[wu, hT_], w=[pb])
                P.op("act", "activation", out=rl.ap, in_=pb.ap[:, 0:T4], func=AF.Relu, r=[pb], w=[rl])
                P.op("dve" if fc % 2 == 0 else "pool", "tensor_tensor", out=fT.ap[:, fc, :], in0=rl.ap, in1=rl.ap, op=ALU.mult, r=[rl], w=[fT])
            if i + 1 < len(tiles):
                prologue(i + 1)
            for s in range(2):
                banks = [ps[0], ps[1]]
                for cg in range(2):
                    for fc in range(32):
                        P.op("pe", "matmul", out=banks[cg].ap, lhsT=fT.ap[:, fc, s * 128:(s + 1) * 128], rhs=wd.ap[:, fc, cg * 512:(cg + 1) * 512],
                             start=(fc == 0), stop=(fc == 31), r=[fT, wd], w=[banks[cg]])
                P.op("act", "copy", out=mo.ap[:, 0:512], in_=banks[0].ap, r=[banks[0]], w=[mo])
                P.op("dve", "tensor_copy", out=mo.ap[:, 512:1024], in_=banks[1].ap, r=[banks[1]], w=[mo])
                P.op("act", "activation", out=jk.ap, in_=mo.ap, func=AF.Square, accum_out=ssq2.ap[:, 0:1], r=[mo], w=[jk, ssq2])
                rms_rstd(ssq2, rstd2, 1, 1.0 / D)
                P.op("dve", "scalar_tensor_tensor", out=mo.ap, in0=mo.ap, scalar=rstd2.ap[:, 0:1], in1=nrm_b.ap,
                     op0=ALU.mult, op1=ALU.mult, r=[mo, rstd2, nrm_b], w=[mo])
                P.dma("sp", xt.ap, sc["X1"][t0 + s * 128:t0 + (s + 1) * 128, :], r=[DB(("X1", sn))], w=[xt])
                P.op("pool", "tensor_tensor", out=xt.ap, in0=mo.ap, in1=xt.ap, op=ALU.add, r=[mo, xt], w=[xt])
                P.dma("sp", Y[sn][t0 + s * 128:t0 + (s + 1) * 128, :], xt.ap, r=[xt], w=[DB(("Y", sn))])
        P.barrier()
        P.emit()


def phases_rest(L):
    sa = L["stop_after"]
    if sa >= 2:
        phase_gdn(L)
    if sa >= 3:
        phase_attn(L)
    if sa >= 4:
        phase_mlp(L)


def phase_gdn(L):
    nc, P, ps, S, seqs, identb, onesb, onesf = L["nc"], L["P"], L["ps"], L["S"], L["seqs"], L["identb"], L["onesb"], L["onesf"]
    DB, epsc, rms_rstd = L["DB"], L["epsc"], L["rms_rstd"]
    with ExitStack() as es:
        def tile(name, shape, dt):
            return Tile(_chk(nc, es.enter_context(nc.sbuf_tensor(name, list(shape), dt))).ap(), name)
        cw_t = tile("cw_t", [128, 12, 5], F32)
        P.dma("sp", cw_t.ap, L["conv_w_r"], w=[cw_t])
        raw = [tile("raw%d" % i, [128, 12, TT + 4], BF16) for i in range(2)]
        acc = [tile("acc%d" % i, [128, TT], F32) for i in range(2)]
        sl = [tile("sl%d" % i, [128, TT], F32) for i in range(2)]
        sq = [tile("sq%d" % i, [128, TT], BF16) for i in range(2)]
        rs = [tile("rs%d" % i, [128, TT], F32) for i in range(2)]
        fm = [tile("fm%d" % i, [128, TT], BF16) for i in range(3)]
        tk = [tile("tk%d" % i, [128, 4, 128], BF16) for i in range(2)]
        it = 0
        n = 0
        for sn, T in seqs:
            sc = S[sn]
            for ti in range(T // TT):
                t0 = ti * TT
                r_ = raw[it % 2]
                lo, hi = max(t0 - 2, 0), min(t0 + TT + 2, T)
                if t0 == 0:
                    P.op("pool", "memset", ap=r_.ap[:, :, 0:2], constant=0.0, w=[r_])
                if t0 + TT == T:
                    P.op("pool", "memset", ap=r_.ap[:, :, TT + 2:TT + 4], constant=0.0, w=[r_])
                P.dma("sp", r_.ap[:, :, lo - (t0 - 2):hi - (t0 - 2)], sc["A"].rearrange("(c p) t -> p c t", p=128)[:, :, lo:hi],
                      r=[DB(("A", sn))], w=[r_])
                for c in range(12):
                    kind, h = c // 4, c % 4
                    a_, s_ = acc[n % 2], sl[n % 2]
                    ce = "dve"
                    P.op(ce, "tensor_scalar", out=a_.ap, in0=r_.ap[:, c, 0:TT], scalar1=cw_t.ap[:, c, 0:1], scalar2=None, op0=ALU.mult,
                         r=[r_, cw_t], w=[a_])
                    for k in range(1, 5):
                        P.op(ce, "scalar_tensor_tensor", out=a_.ap, in0=r_.ap[:, c, k:k + TT], scalar=cw_t.ap[:, c, k:k + 1], in1=a_.ap,
                             op0=ALU.mult, op1=ALU.add, r=[r_, cw_t, a_], w=[a_])
                    P.op("act", "activation", out=s_.ap, in_=a_.ap, func=AF.Silu, r=[a_], w=[s_])
                    f_ = fm[n % 3]
                    if kind == 2:
                        P.op("pool", "tensor_copy", out=f_.ap, in_=s_.ap, r=[s_], w=[f_])
                    else:
                        q_, rs_ = sq[n % 2], rs[n % 2]
                        pb = ps[n % 2]
                        P.op("act", "activation", out=q_.ap, in_=s_.ap, func=AF.Square, r=[s_], w=[q_])
                        P.op("pe", "matmul", out=pb.ap, lhsT=onesb.ap, rhs=q_.ap, start=True, stop=True, r=[onesb, q_], w=[pb])
                        P.op("act", "activation", out=rs_.ap, in_=pb.ap, func=AF.Ln, bias=epsc.ap[:, 0:1], scale=1.0, r=[pb, epsc], w=[rs_])
                        P.op("act", "activation", out=rs_.ap, in_=rs_.ap, func=AF.Exp, scale=-0.5, r=[rs_], w=[rs_])
                        P.op("dve", "scalar_tensor_tensor", out=f_.ap, in0=s_.ap, scalar=(128.0 ** -0.5 if kind == 0 else 1.0), in1=rs_.ap,
                             op0=ALU.mult, op1=ALU.mult, r=[s_, rs_], w=[f_])
                        dst = sc["GQ"] if kind == 0 else sc["GK"]
                        P.dma("sp", dst[h * 128:(h + 1) * 128, t0:t0 + TT], f_.ap, r=[f_], w=[DB(("GQK", sn))])
                    if kind >= 1:
                        tb = ps[2 + (n % 2)]
                        tv = tb.ap.bitcast(BF16)[:, 0:512].rearrange("p (a c) -> p a c", a=4)
                        for s in range(4):
                            P.op("pe", "transpose", out=tv[:, s, :], in_=f_.ap[:, s * 128:(s + 1) * 128], identity=identb.ap,
                                 r=[f_, identb], w=[tb])
                        t_ = tk[n % 2]
                        P.op("act", "copy", out=t_.ap, in_=tv, r=[tb], w=[t_])
                        dst = sc["GKt"] if kind == 1 else sc["GVt"]
                        P.dma("sp", dst[t0:t0 + TT, h * 128:(h + 1) * 128].rearrange("(s p) d -> p s d", p=128), t_.ap,
                              r=[t_], w=[DB(("GKV", sn))])
                    n += 1
                it += 1
        P.barrier()
        P.emit()
    if L["stop_after"] == 2 and L.get("gdn_pre_only"):
        return
    with ExitStack() as es:
        def tile(name, shape, dt):
            return Tile(_chk(nc, es.enter_context(nc.sbuf_tensor(name, list(shape), dt))).ap(), name)

        cf = tile("cf", [128, len(CONST_NAMES), 128], F32)
        P.dma("sp", cf.ap, L["cst"].rearrange("n p j -> p n j"), w=[cf])
        C = {n: cf.ap[:, i, :] for i, n in enumerate(CONST_NAMES)}
        gnw_b = tile("gnw_b", [128, 128], F32)
        P.dma("sp", gnw_b.ap, L["hnorm"][0:1, :].partition_broadcast(128), w=[gnw_b])

        def b3(ap2):
            return ap2.unsqueeze(1).to_broadcast([128, 4, 128])

        def bj(ap2):
            return ap2.unsqueeze(2).to_broadcast([128, 4, 128])

        def v4(t):
            return t.ap.rearrange("p (h j) -> p h j", h=4)

        NU = 2
        Rd = []
        R = []
        for d in range(2):
            rd = {"S4": tile("S4_%d" % d, [128, 512], F32), "S4b": tile("S4b_%d" % d, [128, 512], BF16),
                  }
            Rd.append(rd)
            Ru = []
            for u in range(NU):
                r = {}
                sfx = "%d_%d" % (d, u)
                for nm in ("KT", "QT", "Kt", "Vt", "A0", "A1", "B0", "B1", "Aqk", "AqkT", "PTb", "Vb", "kbg", "ke", "wT", "vn"):
                    r[nm] = tile(nm + sfx, [128, 512], BF16)
                for nm in ("Gm", "Ep", "D4", "Dsb", "PT", "u4"):
                    r[nm] = tile(nm + sfx, [128, 512], F32)
                r["Oq"], r["O4"], r["sz"], r["o2"] = r["Gm"], r["Ep"], r["D4"], r["Dsb"]
                r["zb"], r["onb"], r["mxs"] = r["A0"], r["A1"], r["B0"]
                r["G"] = tile("G" + sfx, [128, 16], F32)
                r["sm"] = tile("sm" + sfx, [128, 16], F32)
                r["ex"] = tile("ex" + sfx, [128, 16], F32)
                r["bgc"] = tile("bgc" + sfx, [128, 4], F32)
                r["ssq"] = tile("gssq" + sfx, [128, 4], F32)
                r["rstd"] = tile("grstd" + sfx, [128, 4], F32)
                r["banks"] = [ps[2 * (NU * d + u)], ps[2 * (NU * d + u) + 1]]
                r["nb"] = 0
                Ru.append(r)
            R.append(Ru)

        def nbank(r):
            b = r["banks"][r["nb"] % 2]
            r["nb"] += 1
            return b

        def unit(sn, T, d, b, step, NB):
            sc = S[sn]
            r = R[d][step % NU]
            rd = Rd[d]
            t0 = b * 128
            KT, QT, Kt, Vt, G = r["KT"], r["QT"], r["Kt"], r["Vt"], r["G"]
            P.dma("sp", v4(KT), sc["GK"].rearrange("(h p) t -> p h t", p=128)[:, :, t0:t0 + 128], r=[DB(("GQK", sn))], w=[KT])
            P.dma("sp", v4(QT), sc["GQ"].rearrange("(h p) t -> p h t", p=128)[:, :, t0:t0 + 128], r=[DB(("GQK", sn))], w=[QT])
            P.dma("sp", Kt.ap, sc["GKt"][t0:t0 + 128, :], r=[DB(("GKV", sn))], w=[Kt])
            P.dma("sp", Vt.ap, sc["GVt"][t0:t0 + 128, :], r=[DB(("GKV", sn))], w=[Vt])
            P.dma("sp", G.ap, sc["G"][t0:t0 + 128, :], r=[DB(("G", sn))], w=[G])
            yield
            beta = G.ap[:, 4 * d:4 * d + 4]
            gg = G.ap[:, 8 + 4 * d:12 + 4 * d]
            cum = C["cum_f"] if d == 0 else C["cum_b"]
            pos = C["pos_f"] if d == 0 else C["pos_b"]
            strict = C["strict_f"] if d == 0 else C["strict_b"]
            sm, ex, bgc = r["sm"], r["ex"], r["bgc"]
            Gm, Ep, D4, Dsb = r["Gm"], r["Ep"], r["D4"], r["Dsb"]
            pb = nbank(r)
            P.op("pe", "matmul", out=pb.ap[:, 0:4], lhsT=cum, rhs=gg, start=True, stop=True, r=[cf, G], w=[pb])
            P.op("pe", "matmul", out=pb.ap[:, 4:8], lhsT=C["same"], rhs=gg, start=True, stop=True, r=[cf, G], w=[pb])
            P.op("pe", "matmul", out=pb.ap[:, 8:12], lhsT=C["c0"], rhs=gg, start=True, stop=True, r=[cf, G], w=[pb])
            P.op("pe", "matmul", out=pb.ap[:, 12:16], lhsT=C["c1"], rhs=gg, start=True, stop=True, r=[cf, G], w=[pb])
            P.op("pool", "tensor_tensor", out=v4(Gm), in0=b3(cum), in1=bj(gg), op=ALU.mult, r=[cf, G], w=[Gm])
            yield
            P.op("dve", "tensor_copy", out=sm.ap, in_=pb.ap[:, 0:16], r=[pb], w=[sm])
            P.op("dve", "tensor_tensor", out=sm.ap[:, 4:8], in0=sm.ap[:, 4:8], in1=sm.ap[:, 0:4], op=ALU.subtract, r=[sm], w=[sm])
            pe_ = nbank(r)
            P.op("pe", "matmul", out=pe_.ap, lhsT=onesf.ap, rhs=Gm.ap, start=True, stop=True, r=[onesf, Gm], w=[pe_])
            yield
            P.op("act", "activation", out=ex.ap, in_=sm.ap, func=AF.Exp, r=[sm], w=[ex])
            P.op("dve", "tensor_tensor", out=v4(Ep), in0=pe_.ap.rearrange("p (h j) -> p h j", h=4), in1=b3(pos), op=ALU.add,
                 r=[pe_, cf], w=[Ep])
            yield
            P.op("dve", "tensor_tensor", out=bgc.ap, in0=ex.ap[:, 0:4], in1=beta, op=ALU.mult, r=[ex, G], w=[bgc])
            for h in range(4):
                P.op("act", "activation", out=v4(D4)[:, h, :], in_=v4(Ep)[:, h, :], func=AF.Exp, bias=sm.ap[:, h:h + 1], scale=-1.0,
                     r=[Ep, sm], w=[D4])
            pk, pq = nbank(r), nbank(r)
            for h in range(4):
                P.op("pe", "matmul", out=pk.ap[:, h * 128:(h + 1) * 128], lhsT=v4(KT)[:, h, :], rhs=v4(KT)[:, h, :], start=True, stop=True,
                     r=[KT], w=[pk])
            for h in range(4):
                P.op("pe", "matmul", out=pq.ap[:, h * 128:(h + 1) * 128], lhsT=v4(QT)[:, h, :], rhs=v4(KT)[:, h, :], start=True, stop=True,
                     r=[QT, KT], w=[pq])
            yield
            Vb, kbg, ke, u4, wT = r["Vb"], r["kbg"], r["ke"], r["u4"], r["wT"]
            P.op("pool", "tensor_tensor", out=v4(Dsb), in0=v4(D4), in1=b3(strict), op=ALU.mult, r=[D4, cf], w=[Dsb])
            P.op("pool", "tensor_tensor", out=v4(Dsb), in0=v4(Dsb), in1=bj(beta), op=ALU.mult, r=[Dsb, G], w=[Dsb])
            A, B = [r["A0"], r["A1"]], [r["B0"], r["B1"]]
            Aqk, AqkT, PT, PTb = r["Aqk"], r["AqkT"], r["PT"], r["PTb"]
            P.op("dve", "tensor_tensor", out=Aqk.ap, in0=pq.ap, in1=D4.ap, op=ALU.mult, r=[pq, D4], w=[Aqk])
            yield
            P.op("dve", "tensor_tensor", out=A[0].ap, in0=pk.ap, in1=Dsb.ap, op=ALU.mult, r=[pk, Dsb], w=[A[0]])
            P.op("pool", "tensor_tensor", out=v4(Vb), in0=v4(Vt), in1=bj(beta), op=ALU.mult, r=[Vt, G], w=[Vb])
            P.op("pool", "tensor_tensor", out=v4(kbg), in0=v4(Kt), in1=bj(bgc.ap), op=ALU.mult, r=[Kt, bgc], w=[kbg])
            P.op("pool", "tensor_tensor", out=v4(ke), in0=v4(Kt), in1=bj(ex.ap[:, 4:8]), op=ALU.mult, r=[Kt, ex], w=[ke])
            yield
            tb = nbank(r)
            tvA = tb.ap.bitcast(BF16)[:, 0:512].rearrange("p (a c) -> p a c", a=4)
            tvQ = tb.ap.bitcast(BF16)[:, 512:1024].rearrange("p (a c) -> p a c", a=4)
            for h in range(4):
                P.op("pe", "transpose", out=tvA[:, h, :], in_=v4(A[0])[:, h, :], identity=identb.ap, r=[A[0], identb], w=[tb])
            for h in range(4):
                P.op("pe", "transpose", out=tvQ[:, h, :], in_=v4(Aqk)[:, h, :], identity=identb.ap, r=[Aqk, identb], w=[tb])
            yield
            P.op("act", "copy", out=v4(B[0]), in_=tvA, r=[tb], w=[B[0]])
            P.op("act", "copy", out=v4(AqkT), in_=tvQ, r=[tb], w=[AqkT])
            yield
            P.op("pool", "tensor_tensor", out=v4(PT), in0=b3(C["ident"]), in1=v4(B[0]), op=ALU.subtract, r=[cf, B[0]], w=[PT])
            P.op("pool", "tensor_copy", out=PTb.ap, in_=PT.ap, r=[PT], w=[PTb])
            for k in range(5):
                Ak, Bk, An, Bn = A[k % 2], B[k % 2], A[(k + 1) % 2], B[(k + 1) % 2]
                pa = nbank(r)
                for h in range(4):
                    P.op("pe", "matmul", out=pa.ap[:, h * 128:(h + 1) * 128], lhsT=v4(Bk)[:, h, :], rhs=v4(Ak)[:, h, :], start=True, stop=True,
                         r=[Ak, Bk], w=[pa])
                if k < 4:
                    pbb = nbank(r)
                    for h in range(4):
                        P.op("pe", "matmul", out=pbb.ap[:, h * 128:(h + 1) * 128], lhsT=v4(Ak)[:, h, :], rhs=v4(Bk)[:, h, :], start=True, stop=True,
                             r=[Ak, Bk], w=[pbb])
                yield
                P.op("act", "copy", out=An.ap, in_=pa.ap, r=[pa], w=[An])
                if k < 4:
                    P.op("dve", "tensor_copy", out=Bn.ap, in_=pbb.ap, r=[pbb], w=[Bn])
                yield
                pd = nbank(r)
                for h in range(4):
                    P.op("pe", "matmul", out=pd.ap[:, h * 128:(h + 1) * 128], lhsT=v4(An)[:, h, :], rhs=v4(PTb)[:, h, :], start=True, stop=True,
                         r=[An, PTb], w=[pd])
                yield
                P.op("dve", "tensor_tensor", out=PT.ap, in0=pd.ap, in1=PT.ap, op=ALU.add, r=[pd, PT], w=[PT])
                P.op("pool", "tensor_copy", out=PTb.ap, in_=PT.ap, r=[PT], w=[PTb])
                yield
            pu, pw = nbank(r), nbank(r)
            for h in range(4):
                P.op("pe", "matmul", out=pu.ap[:, h * 128:(h + 1) * 128], lhsT=v4(PTb)[:, h, :], rhs=v4(Vb)[:, h, :], start=True, stop=True,
                     r=[PTb, Vb], w=[pu])
            for h in range(4):
                P.op("pe", "matmul", out=pw.ap[:, h * 128:(h + 1) * 128], lhsT=v4(kbg)[:, h, :], rhs=v4(PTb)[:, h, :], start=True, stop=True,
                     r=[PTb, kbg], w=[pw])
            yield
            P.op("act", "copy", out=u4.ap, in_=pu.ap, r=[pu], w=[u4])
            P.op("dve", "tensor_copy", out=wT.ap, in_=pw.ap, r=[pw], w=[wT])
            yield
            S4, S4b, vn, Oq, O4 = rd["S4"], rd["S4b"], r["vn"], r["Oq"], r["O4"]
            for c in ((0, 1) if d == 0 else (1, 0)):
                pc = slice(c * 64, (c + 1) * 64)
                p1, po1 = nbank(r), nbank(r)
                for h in range(4):
                    P.op("pe", "matmul", out=p1.ap[pc, h * 128:(h + 1) * 128], lhsT=v4(wT)[:, h, pc], rhs=v4(S4b)[:, h, :], start=True, stop=True,
                         r=[wT, S4b], w=[p1])
                for h in range(4):
                    P.op("pe", "matmul", out=po1.ap[pc, h * 128:(h + 1) * 128], lhsT=v4(QT)[:, h, pc], rhs=v4(S4b)[:, h, :], start=True, stop=True,
                         r=[QT, S4b], w=[po1])
                yield
                P.op("dve", "tensor_tensor", out=vn.ap[pc, :], in0=u4.ap[pc, :], in1=p1.ap[pc, :], op=ALU.subtract, r=[u4, p1], w=[vn])
                P.op("dve", "tensor_tensor", out=v4(Oq)[pc], in0=po1.ap[pc, :].rearrange("p (h j) -> p h j", h=4),
                     in1=ex.ap[pc, 0:4].unsqueeze(2).to_broadcast([64, 4, 128]), op=ALU.mult, r=[po1, ex], w=[Oq])
                yield
                po2, pds = nbank(r), nbank(r)
                for h in range(4):
                    P.op("pe", "matmul", out=pds.ap[:, h * 128:(h + 1) * 128], lhsT=v4(ke)[pc, h, :], rhs=v4(vn)[pc, h, :], start=True, stop=True,
                         r=[ke, vn], w=[pds])
                for h in range(4):
                    P.op("pe", "matmul", out=po2.ap[pc, h * 128:(h + 1) * 128], lhsT=v4(AqkT)[pc, h, pc], rhs=v4(vn)[pc, h, :], start=True, stop=True,
                         r=[AqkT, vn], w=[po2])
                P.op("pool", "tensor_tensor", out=v4(S4), in0=v4(S4), in1=bj(ex.ap[:, 8 + 4 * c:12 + 4 * c]), op=ALU.mult, r=[S4, ex], w=[S4])
                yield
                P.op("dve", "tensor_tensor", out=S4.ap, in0=pds.ap, in1=S4.ap, op=ALU.add, r=[pds, S4], w=[S4])
                P.op("act", "copy", out=S4b.ap, in_=S4.ap, r=[S4], w=[S4b])
                P.op("dve", "tensor_tensor", out=O4.ap[pc, :], in0=po2.ap[pc, :], in1=Oq.ap[pc, :], op=ALU.add, r=[po2, Oq], w=[O4])
                yield
            mine, other = ("OF", "OB") if d == 0 else ("OB", "OF")
            if step < NB // 2:
                P.dma("sp", sc[mine][t0:t0 + 128, :], O4.ap, r=[O4], w=[DB((mine, sn, b))])
            else:
                o2, sz, zb, onb, mxs = r["o2"], r["sz"], r["zb"], r["onb"], r["mxs"]
                ssq, rstd = r["ssq"], r["rstd"]
                P.dma("sp", o2.ap, sc[other][t0:t0 + 128, :], r=[DB((other, sn, b))], w=[o2])
                P.dma("sp", zb.ap, sc["Z"][t0:t0 + 128, :], r=[DB(("Z", sn))], w=[zb])
                yield
                P.op("pool", "tensor_tensor", out=o2.ap, in0=o2.ap, in1=O4.ap, op=ALU.add, r=[o2, O4], w=[o2])
                P.op("pool", "tensor_tensor", out=sz.ap, in0=o2.ap, in1=o2.ap, op=ALU.mult, r=[o2], w=[sz])
                yield
                P.op("dve", "tensor_reduce", out=ssq.ap, in_=v4(sz), axis=AX.X, op=ALU.add, r=[sz], w=[ssq])
                rms_rstd(ssq, rstd, 4, 1.0 / 128)
                P.op("act", "activation", out=sz.ap, in_=zb.ap, func=AF.Silu, r=[zb], w=[sz])
                yield
                P.op("dve", "tensor_tensor", out=v4(o2), in0=v4(o2), in1=bj(rstd.ap), op=ALU.mult, r=[o2, rstd], w=[o2])
                P.op("pool", "tensor_tensor", out=v4(o2), in0=v4(o2), in1=b3(gnw_b.ap), op=ALU.mult, r=[o2, gnw_b], w=[o2])
                yield
                P.op("dve", "tensor_tensor", out=onb.ap, in0=o2.ap, in1=sz.ap, op=ALU.mult, r=[o2, sz], w=[onb])
                yield
                tb2 = nbank(r)
                tv2 = tb2.ap.bitcast(BF16)[:, 0:512].rearrange("p (a c) -> p a c", a=4)
                for h in range(4):
                    P.op("pe", "transpose", out=tv2[:, h, :], in_=v4(onb)[:, h, :], identity=identb.ap, r=[onb, identb], w=[tb2])
                yield
                P.op("act", "copy", out=v4(mxs), in_=tv2, r=[tb2], w=[mxs])
                P.dma("sp", sc["MIX"][0:512, :].rearrange("(h p) t -> p h t", p=128)[:, :, t0:t0 + 128], v4(mxs), r=[mxs], w=[DB(("MIX", sn))])

        for sn, T in seqs:
            NB = T // 128
            for d in range(2):
                P.op("pool", "memset", ap=Rd[d]["S4"].ap, constant=0.0, w=[Rd[d]["S4"]])
                P.op("pool", "memset", ap=Rd[d]["S4b"].ap, constant=0.0, w=[Rd[d]["S4b"]])
            active = []
            nxt = 0
            STAG = 16
            while nxt < NB or active:
                if nxt < NB and len(active) < 2 * NU and (not active or active[-1][1] >= STAG):
                    active.append([unit(sn, T, 0, nxt, nxt, NB), 0])
                    active.append([unit(sn, T, 1, NB - 1 - nxt, nxt, NB), 0])
                    nxt += 1
                for a_ in list(active):
                    try:
                        next(a_[0])
                        a_[1] += 1
                    except StopIteration:
                        active.remove(a_)
        P.barrier()
        P.emit()
```

```python
import numpy as np
import concourse.bass as bass
import concourse.mybir as mybir
from concourse.bass_utils import run_bass_kernel_spmd

F32 = mybir.dt.float32
BF16 = mybir.dt.bfloat16
AF = mybir.ActivationFunctionType
ALU = mybir.AluOpType
AX = mybir.AxisListType

ENGS = ("pe", "act", "dve", "pool", "sp")
NS = 8


class Buf:
    __slots__ = ("w", "r", "rd", "name", "excl")

    def __init__(self, name=""):
        self.excl = False
        self.w = None
        self.r = {}
        self.rd = []
        self.name = name


class Prog:
    def __init__(self, nc):
        self.nc = nc
        self.sem = {e: nc.alloc_semaphore("s_" + e) for e in ENGS}
        self.dsem = {q: [nc.alloc_semaphore("d_%s_%d" % (q, i)) for i in range(NS)]
                     for q in ("sp", "act", "pool")}
        self.dcount = {q: 0 for q in self.dsem}
        self.base = {e: 0 for e in ENGS}
        self.rec = {e: [] for e in ENGS}
        self.off = {e: 0 for e in ENGS}
        self.msmap = {e: {} for e in ENGS}
        self.waited = {}
        self.limit = None
        self.nrec = 0

    def _need(self, eng, tok, waits):
        if tok is None:
            return
        if tok[0] == "c":
            _, se, gidx = tok
            if se == "pe" and eng == "pe":
                return
            key = (eng, se)
            if self.waited.get(key, -1) >= gidx:
                return
            self.waited[key] = gidx
            li = gidx - self.off[se]
            if li >= 0:
                self.rec[se][li][3] = True
            waits.append(tok)
        else:
            _, q, n = tok
            key = (eng, q, n % NS)
            if self.waited.get(key, -1) >= n:
                return
            self.waited[key] = n
            waits.append(tok)

    def _deps(self, eng, tok, r, w, waits):
        rx = [b for b in r if b.excl]
        if rx:
            r = [b for b in r if not b.excl]
            w = list(w) + [b for b in rx if b not in w]
        for b in r:
            self._need(eng, b.w, waits)
        for b in w:
            self._need(eng, b.w, waits)
            for se, gi in b.r.items():
                self._need(eng, ("c", se, gi), waits)
            for t in b.rd:
                self._need(eng, t, waits)
        for b in r:
            if tok[0] == "c":
                b.r[tok[1]] = tok[2]
            else:
                b.rd.append(tok)
                if len(b.rd) > 3 * NS:
                    b.rd = b.rd[-3 * NS:]
        for b in w:
            b.w = tok
            b.r = {}
            b.rd = []

    def op(self, eng, meth, r=(), w=(), **kw):
        fn = (meth, kw)
        self.nrec += 1
        if self.limit is not None and self.nrec > self.limit:
            return None
        waits = []
        gidx = self.off[eng] + len(self.rec[eng])
        tok = ("c", eng, gidx)
        self._deps(eng, tok, r, w, waits)
        self.rec[eng].append(["op", fn, waits, False])
        return tok

    def dma(self, q, out, in_, r=(), w=(), slow=False):
        self.nrec += 1
        if self.limit is not None and self.nrec > self.limit:
            return None
        waits = []
        n = self.dcount[q]
        self.dcount[q] += 1
        tok = ("d", q, n)
        if n >= NS:
            self._need(q, ("d", q, n - NS), waits)
        self._deps(q, tok, r, w, waits)
        self.rec[q].append(["dma", (out, in_, q, n, slow), waits, False])
        return tok

    def barrier(self, bufs=()):
        for e in ENGS:
            waits = []
            for se in ENGS:
                if se == e:
                    continue
                for li in range(len(self.rec[se]) - 1, -1, -1):
                    if self.rec[se][li][0] == "op":
                        self._need(e, ("c", se, self.off[se] + li), waits)
                        break
            for q in self.dsem:
                for k in range(max(0, self.dcount[q] - NS), self.dcount[q]):
                    self._need(e, ("d", q, k), waits)
            self.rec[e].append(["nop", None, waits, False])

    def emit(self):
        nc = self.nc
        for e in ENGS:
            c = self.base[e]
            for li, rcd in enumerate(self.rec[e]):
                if rcd[0] == "op" and rcd[3]:
                    c += 1
                    self.msmap[e][self.off[e] + li] = c
            self.base[e] = c

        def run(e, engobj):
            for rcd in self.rec[e]:
                kind, fn, waits, ms = rcd
                for t in waits:
                    if t[0] == "c":
                        engobj.wait_ge(self.sem[t[1]], self.msmap[t[1]][t[2]])
                    else:
                        engobj.wait_ge(self.dsem[t[1]][t[2] % NS], 16 * (t[2] // NS + 1))
                if kind == "op":
                    ins = getattr(engobj, fn[0])(**fn[1])
                    if ms:
                        ins.then_inc(self.sem[e], 1)
                elif kind == "dma":
                    out, in_, q, n, slow = fn
                    if slow:
                        ins = engobj.dma_start(out=out, in_=in_, allow_slow_non_contiguous=True)
                    else:
                        ins = engobj.dma_start(out=out, in_=in_)
                    ins.then_inc(self.dsem[q][n % NS], 16)

        with nc.Block() as block:
            @block.tensor
            def _(eng):
                run("pe", eng)

            @block.scalar
            def _(eng):
                run("act", eng)

            @block.vector
            def _(eng):
                run("dve", eng)

            @block.gpsimd
            def _(eng):
                run("pool", eng)

            @block.sync
            def _(eng):
                run("sp", eng)

        for e in ENGS:
            self.off[e] += len(self.rec[e])
            self.rec[e] = []

from contextlib import ExitStack

D = 1024
NKC = 8
DFF = 4096
EPS = 1e-6
TT = 512
FM_COLS = 3072
TM0 = 3072
WIN_COLS = 3856
GRID_W = 64


SBUF_LIMIT = 196608


def _chk(nc, handle):
    m = nc.lookup_mloc(handle)
    nbytes = 1
    for d in list(m.dims)[1:]:
        nbytes *= int(d)
    assert int(m.addr) + nbytes <= SBUF_LIMIT, ("SBUF overflow past physical limit", handle.name, int(m.addr), nbytes)
    return handle


class Tile(Buf):
    __slots__ = ("ap",)

    def __init__(self, ap, name=""):
        Buf.__init__(self, name)
        self.ap = ap


def rope_tables(T):
    t = np.arange(T)
    row = (t // GRID_W).astype(np.float32)
    col = (t % GRID_W).astype(np.float32)
    nf = 32
    inv = (np.float32(10000.0) ** (-np.arange(nf, dtype=np.float32) / np.float32(nf))).astype(np.float32)
    cosT = np.zeros((128, T), np.float32)
    sinT = np.zeros((128, T), np.float32)
    for d in range(128):
        half = d // 64
        idx = d % 64
        f = idx % 32
        pos = row if half == 0 else col
        ang = (pos * inv[f]).astype(np.float32)
        cosT[d] = np.cos(ang)
        sinT[d] = np.sin(ang) * (-1.0 if idx < 32 else 1.0)
    return cosT, sinT


def rope_perm():
    p = np.zeros(128, np.int64)
    for d in range(128):
        idx = d % 64
        p[d] = d + 32 if idx < 32 else d - 32
    return p


def gdn_masks():
    t = np.arange(128)[:, None]
    j = np.arange(128)[None, :]
    same = (t // 64) == (j // 64)
    m = {}
    m["cum_f"] = (same & (t <= j)).astype(np.float32)
    m["cum_b"] = (same & (t >= j)).astype(np.float32)
    m["same"] = same.astype(np.float32)
    i = t
    m["pos_f"] = np.where(same & (i >= j), 0.0, 30000.0).astype(np.float32)
    m["pos_b"] = np.where(same & (i <= j), 0.0, 30000.0).astype(np.float32)
    m["strict_f"] = (same & (i > j)).astype(np.float32)
    m["strict_b"] = (same & (i < j)).astype(np.float32)
    m["c0"] = np.tile((np.arange(128) < 64).astype(np.float32)[:, None], (1, 128))
    m["c1"] = np.tile((np.arange(128) >= 64).astype(np.float32)[:, None], (1, 128))
    return m


CONST_NAMES = ["ident", "cum_f", "cum_b", "same", "pos_f", "pos_b", "strict_f", "strict_b", "c0", "c1"]


def build(Tp, Ts, debug=False, stop_after=9):
    nc = bass.Bass("TRN2", target_bir_lowering=False)
    P = Prog(nc)
    seqs = [("p", Tp), ("s", Ts)]
    Tmax = max(Tp, Ts)

    def din(name, shape, dt=F32):
        return nc.dram_tensor(name, list(shape), dt, kind="ExternalInput").ap()

    def dscr(name, shape, dt):
        if debug:
            return nc.dram_tensor(name, list(shape), dt, kind="ExternalOutput").ap()
        return nc.dram_tensor(name, list(shape), dt).ap()

    X = {"p": din("x_p", [Tp, D]), "s": din("x_s", [Ts, D])}
    Y = {"p": nc.dram_tensor("y_p", [Tp, D], F32, kind="ExternalOutput").ap(),
         "s": nc.dram_tensor("y_s", [Ts, D], F32, kind="ExternalOutput").ap()}
    w_in = din("w_in_r", [D, WIN_COLS])
    w_out = din("w_out", [D, D])
    w_up = din("w_up", [D, DFF])
    w_down = din("w_down", [DFF, D])
    nrm = din("norms", [4, D])
    conv_w_r = din("conv_w_r", [128, 12, 5])
    nrm_r = din("nrm_r", [128, 4, NKC])
    hnorm_r = din("hnorm_r", [128, 5])
    gparam = din("gparam", [2, 8])
    hnorm = din("hnorm", [5, 128])
    cosT = din("cosT", [128, Tmax])
    sinT = din("sinT", [128, Tmax])
    cst = din("cst", [len(CONST_NAMES), 128, 128])

    S = {}
    for sn, T in seqs:
        S[sn] = dict(
            A=dscr("A_raw_" + sn, [1536, T], BF16),
            QK=dscr("QK_T_" + sn, [768, T], BF16),
            Z=dscr("Z_" + sn, [T, 512], BF16),
            G=dscr("G_" + sn, [T, 16], F32),
            V=dscr("V_" + sn, [T, 256], BF16),
            GQ=dscr("GQ_T_" + sn, [512, T], BF16),
            GK=dscr("GK_T_" + sn, [512, T], BF16),
            GKt=dscr("GKt_" + sn, [T, 512], BF16),
            GVt=dscr("GVt_" + sn, [T, 512], BF16),
            OF=dscr("OF_" + sn, [T, 512], F32),
            OB=dscr("OB_" + sn, [T, 512], F32),
            MIX=dscr("MIX_T_" + sn, [1024, T], BF16),
            X1=dscr("X1_" + sn, [T, 1024], F32),
        )
    dbufs = {}

    def DB(key):
        if key not in dbufs:
            dbufs[key] = Buf(str(key))
        return dbufs[key]

    ps = [Tile(nc.alloc_psum_tensor("ps%d" % i, [128, 512], F32).ap(), "ps%d" % i) for i in range(8)]
    for t in ps:
        t.excl = True

    with ExitStack() as g_es:
        def gtile(name, shape, dt):
            return Tile(_chk(nc, g_es.enter_context(nc.sbuf_tensor(name, list(shape), dt))).ap(), name)

        identb = gtile("identb", [128, 128], BF16)
        onesb = gtile("onesb", [128, 128], BF16)
        P.op("pool", "memset", ap=onesb.ap, constant=1.0, w=[onesb])
        epsc = gtile("epsc", [128, 1], F32)
        P.op("pool", "memset", ap=epsc.ap, constant=EPS, w=[epsc])
        nrm_t = gtile("nrm_t", [128, 4, NKC], F32)
        P.dma("sp", nrm_t.ap, nrm_r, w=[nrm_t])
        hn_t = gtile("hn_t", [128, 5], F32)
        P.dma("sp", hn_t.ap, hnorm_r, w=[hn_t])
        gp_b = gtile("gp_b", [128, 2, 8], F32)
        P.dma("sp", gp_b.ap[:, 0, :], gparam[0:1, :].partition_broadcast(128), w=[gp_b])
        P.dma("sp", gp_b.ap[:, 1, :], gparam[1:2, :].partition_broadcast(128), w=[gp_b])
        negA = gtile("negA", [128, 8], F32)
        P.op("act", "activation", out=negA.ap, in_=gp_b.ap[:, 0, :], func=AF.Exp, r=[gp_b], w=[negA])
        P.op("dve", "tensor_scalar", out=negA.ap, in0=negA.ap, scalar1=-1.0, scalar2=None, op0=ALU.mult,
             r=[negA], w=[negA])

        with ExitStack() as t_es:
            identf = Tile(_chk(nc, t_es.enter_context(nc.sbuf_tensor("identf", [128, 128], F32))).ap(), "identf")
            P.dma("sp", identf.ap, cst[0], w=[identf])
            P.op("dve", "tensor_copy", out=identb.ap, in_=identf.ap, r=[identf], w=[identb])
            P.barrier()
            P.emit()

        def load_weight_bf16(wt, st, src, nk, cols, scale_sel, stage_cols):
            n = 0
            for kc in range(nk):
                for c0 in range(0, cols, stage_cols):
                    cn = min(stage_cols, cols - c0)
                    s_ = st[n % 2]
                    P.dma("sp", s_.ap[:, 0:cn], src[kc * 128:(kc + 1) * 128, c0:c0 + cn], w=[s_])
                    dst = wt.ap[:, kc, c0:c0 + cn]
                    if scale_sel is None:
                        if n % 2 == 0:
                            P.op("act", "copy", out=dst, in_=s_.ap[:, 0:cn], r=[s_], w=[wt])
                        else:
                            P.op("dve", "tensor_copy", out=dst, in_=s_.ap[:, 0:cn], r=[s_], w=[wt])
                    else:
                        sc = nrm_t.ap[:, scale_sel, kc:kc + 1]
                        if n % 2 == 0:
                            P.op("act", "activation", out=dst, in_=s_.ap[:, 0:cn], func=AF.Copy, scale=sc,
                                 r=[s_, nrm_t], w=[wt])
                        else:
                            P.op("dve", "tensor_scalar", out=dst, in0=s_.ap[:, 0:cn], scalar1=sc, scalar2=None, op0=ALU.mult,
                                 r=[s_, nrm_t], w=[wt])
                    n += 1
            return wt

        def rms_rstd(ssq, rstd, n, inv_n):
            P.op("act", "activation", out=rstd.ap[:, 0:n], in_=ssq.ap[:, 0:n], func=AF.Ln, bias=epsc.ap[:, 0:1], scale=inv_n,
                 r=[ssq, epsc], w=[rstd])
            P.op("act", "activation", out=rstd.ap[:, 0:n], in_=rstd.ap[:, 0:n], func=AF.Exp, scale=-0.5,
                 r=[rstd], w=[rstd])

        with ExitStack() as es:
            def tile(name, shape, dt):
                return Tile(_chk(nc, es.enter_context(nc.sbuf_tensor(name, list(shape), dt))).ap(), name)

            wi = tile("wi", [128, NKC, WIN_COLS], BF16)
            with ExitStack() as ses:
                st = [Tile(_chk(nc, ses.enter_context(nc.sbuf_tensor("wst%d" % i, [128, 1928], F32))).ap()) for i in range(2)]
                load_weight_bf16(wi, st, w_in, NKC, WIN_COLS, 0, 1928)
                P.barrier()
                P.emit()
            xt = [tile("xt%d" % i, [128, 4, D], F32) for i in range(2)]
            hb = tile("hb", [128, 4, D], BF16)
            junk = tile("junk", [128, D], BF16)
            ssq = tile("ssq", [128, 4], F32)
            rstd = tile("rstd", [128, 4], F32)
            hT = [tile("hT%d" % i, [128, NKC, TT], BF16) for i in range(1)]
            a_st = [tile("a_st%d" % i, [128, 12, TT], BF16) for i in range(1)]
            qk_st = [tile("qk_st%d" % i, [128, 6, TT], BF16) for i in range(1)]
            z_st = [tile("z_st%d" % i, [128, 4, 512], BF16) for i in range(1)]
            v_st = [tile("v_st%d" % i, [128, 4, 256], BF16) for i in range(1)]
            g_st = [tile("g_st%d" % i, [128, 4, 16], F32) for i in range(1)]
            cs_t = [tile("cs_t%d" % i, [128, 2, TT], F32) for i in range(1)]
            sq_t = [tile("sq_t%d" % i, [128, TT], BF16) for i in range(2)]
            rs_t = [tile("rs_t%d" % i, [128, TT], F32) for i in range(2)]
            t1_t = [tile("t1_t%d" % i, [128, TT], F32) for i in range(2)]
            t2_t = [tile("t2_t%d" % i, [128, TT], F32) for i in range(2)]
            gt_t = [tile("gt_t%d" % i, [128, 16], F32) for i in range(2)]

            it = 0
            for sn, T in seqs:
                sc = S[sn]
                ntile = T // TT
                for ti in range(ntile):
                    t0 = ti * TT
                    x_ = xt[it % 2]
                    hT_ = hT[0]
                    P.dma("sp", x_.ap, X[sn][t0:t0 + TT, :].rearrange("(s p) d -> p s d", p=128), w=[x_])
                    cs_ = cs_t[0]
                    P.dma("sp", cs_.ap[:, 0, :], cosT[:, t0:t0 + TT], w=[cs_])
                    P.dma("sp", cs_.ap[:, 1, :], sinT[:, t0:t0 + TT], w=[cs_])
                    for s in range(4):
                        P.op("act", "activation", out=junk.ap, in_=x_.ap[:, s, :], func=AF.Square,
                                                                     accum_out=ssq.ap[:, s:s + 1],
                             r=[x_], w=[junk, ssq])
                    rms_rstd(ssq, rstd, 4, 1.0 / D)
                    for s in range(4):
                        P.op("dve", "tensor_scalar", out=hb.ap[:, s, :], in0=x_.ap[:, s, :],
                                                                        scalar1=rstd.ap[:, s:s + 1], scalar2=None, op0=ALU.mult,
                             r=[x_, rstd], w=[hb])
                    for s in range(4):
                        pb = ps[s % 2]
                        pv = pb.ap.bitcast(BF16).rearrange("p (k t) -> p k t", k=NKC)
                        for kc in range(NKC):
                            P.op("pe", "transpose", out=pv[:, kc, :], in_=hb.ap[:, s, kc * 128:(kc + 1) * 128],
                                                                               identity=identb.ap,
                                 r=[hb, identb], w=[pb])
                        eng = "act" if s % 2 == 0 else "dve"
                        if eng == "act":
                            P.op("act", "copy", out=hT_.ap[:, :, s * 128:(s + 1) * 128], in_=pv, r=[pb], w=[hT_])
                        else:
                            P.op("dve", "tensor_copy", out=hT_.ap[:, :, s * 128:(s + 1) * 128], in_=pv, r=[pb], w=[hT_])

                    def fm_mm(pb, c):
                        for kc in range(NKC):
                            P.op("pe", "matmul", out=pb.ap, lhsT=wi.ap[:, kc, c * 128:(c + 1) * 128], rhs=hT_.ap[:, kc, :],
                                                                            start=(kc == 0), stop=(kc == NKC - 1),
                                 r=[wi, hT_], w=[pb])

                    a_ = a_st[0]
                    for c in range(12):
                        pb = ps[2 + (c % 2)]
                        fm_mm(pb, c)
                        if c % 2 == 0:
                            P.op("act", "copy", out=a_.ap[:, c, :], in_=pb.ap, r=[pb], w=[a_])
                        else:
                            P.op("dve", "tensor_copy", out=a_.ap[:, c, :], in_=pb.ap, r=[pb], w=[a_])
                    P.dma("sp", sc["A"].rearrange("(c p) t -> p c t", p=128)[:, :, t0:t0 + TT], a_.ap, r=[a_], w=[DB(("A", sn))])
                    qk_ = qk_st[0]
                    for j in range(6):
                        pm = ps[2 + (j % 2)]
                        pp = ps[4 + (j % 2)]
                        pc = ps[6 + (j % 2)]
                        sq_, rs_, t1_, t2_ = sq_t[j % 2], rs_t[j % 2], t1_t[j % 2], t2_t[j % 2]
                        fm_mm(pm, 12 + j)
                        fm_mm(pp, 18 + j)
                        wsel = 1 if j < 4 else 3
                        P.op("act", "activation", out=sq_.ap, in_=pm.ap, func=AF.Square, r=[pm], w=[sq_])
                        P.op("pe", "matmul", out=pc.ap, lhsT=onesb.ap, rhs=sq_.ap, start=True, stop=True,
                             r=[onesb, sq_], w=[pc])
                        P.op("act", "activation", out=rs_.ap, in_=pc.ap, func=AF.Ln, bias=epsc.ap[:, 0:1], scale=1.0 / 128,
                             r=[pc, epsc], w=[rs_])
                        P.op("act", "activation", out=rs_.ap, in_=rs_.ap, func=AF.Exp, scale=-0.5, r=[rs_], w=[rs_])
                        P.op("dve", "scalar_tensor_tensor", out=t1_.ap, in0=pm.ap, scalar=hn_t.ap[:, wsel:wsel + 1],
                                                                                               in1=cs_.ap[:, 0, :], op0=ALU.mult, op1=ALU.mult,
                             r=[pm, hn_t, cs_], w=[t1_])
                        P.op("dve", "scalar_tensor_tensor", out=t2_.ap, in0=pp.ap, scalar=hn_t.ap[:, wsel + 1:wsel + 2],
                                                                                               in1=cs_.ap[:, 1, :], op0=ALU.mult, op1=ALU.mult,
                             r=[pp, hn_t, cs_], w=[t2_])
                        P.op("pool", "tensor_tensor", out=t1_.ap, in0=t1_.ap, in1=t2_.ap, op=ALU.add, r=[t1_, t2_], w=[t1_])
                        P.op("pool", "tensor_tensor", out=qk_.ap[:, j, :], in0=t1_.ap, in1=rs_.ap, op=ALU.mult,
                             r=[t1_, rs_], w=[qk_])
                    P.dma("sp", sc["QK"].rearrange("(c p) t -> p c t", p=128)[:, :, t0:t0 + TT], qk_.ap, r=[qk_], w=[DB(("QK", sn))])
                    z_, v_, g_ = z_st[0], v_st[0], g_st[0]
                    for s in range(4):
                        pz = ps[s % 2]
                        pg = ps[6 + (s % 2)]
                        gt_ = gt_t[s % 2]
                        for kc in range(NKC):
                            P.op("pe", "matmul", out=pz.ap, lhsT=hT_.ap[:, kc, s * 128:(s + 1) * 128], rhs=wi.ap[:, kc, TM0:TM0 + 512],
                                                                            start=(kc == 0), stop=(kc == NKC - 1), r=[hT_, wi], w=[pz])
                        for kc in range(NKC):
                            P.op("pe", "matmul", out=pg.ap[:, 0:272], lhsT=hT_.ap[:, kc, s * 128:(s + 1) * 128],
                                                                            rhs=wi.ap[:, kc, TM0 + 512:TM0 + 784],
                                                                            start=(kc == 0), stop=(kc == NKC - 1), r=[hT_, wi], w=[pg])
                        P.op("act", "copy", out=z_.ap[:, s, :], in_=pz.ap, r=[pz], w=[z_])
                        P.op("dve", "tensor_copy", out=v_.ap[:, s, :], in_=pg.ap[:, 16:272], r=[pg], w=[v_])
                        P.op("act", "activation", out=gt_.ap[:, 0:8], in_=pg.ap[:, 0:8], func=AF.Exp, scale=-1.0, r=[pg], w=[gt_])
                        P.op("dve", "tensor_scalar", out=gt_.ap[:, 0:8], in0=gt_.ap[:, 0:8], scalar1=1.0, scalar2=None, op0=ALU.add,
                             r=[gt_], w=[gt_])
                        P.op("dve", "reciprocal", out=g_.ap[:, s, 0:8], in_=gt_.ap[:, 0:8], r=[gt_], w=[g_])
                        P.op("dve", "tensor_tensor", out=gt_.ap[:, 8:16], in0=pg.ap[:, 8:16], in1=gp_b.ap[:, 1, :], op=ALU.add,
                             r=[pg, gp_b], w=[gt_])
                        P.op("act", "activation", out=gt_.ap[:, 8:16], in_=gt_.ap[:, 8:16], func=AF.Exp, r=[gt_], w=[gt_])
                        P.op("dve", "tensor_scalar", out=gt_.ap[:, 8:16], in0=gt_.ap[:, 8:16], scalar1=1.0, scalar2=None, op0=ALU.add,
                             r=[gt_], w=[gt_])
                        P.op("act", "activation", out=gt_.ap[:, 8:16], in_=gt_.ap[:, 8:16], func=AF.Ln, r=[gt_], w=[gt_])
                        P.op("dve", "tensor_tensor", out=g_.ap[:, s, 8:16], in0=gt_.ap[:, 8:16], in1=negA.ap, op=ALU.mult,
                             r=[gt_, negA], w=[g_])
                    P.dma("sp", sc["Z"][t0:t0 + TT, :].rearrange("(s p) c -> p s c", p=128), z_.ap, r=[z_], w=[DB(("Z", sn))])
                    P.dma("sp", sc["V"][t0:t0 + TT, :].rearrange("(s p) c -> p s c", p=128), v_.ap, r=[v_], w=[DB(("V", sn))])
                    P.dma("sp", sc["G"][t0:t0 + TT, :].rearrange("(s p) c -> p s c", p=128), g_.ap, r=[g_], w=[DB(("G", sn))])
                    it += 1
            P.barrier()
            P.emit()

        if stop_after >= 2:
            phases_rest(locals())
    return nc


def host_inputs(inp, Tmax):
    f = np.float32
    w_in = np.asarray(inp["w_in"][0], f)
    perm = rope_perm()
    qB = w_in[:, 2064:2576]
    kB = w_in[:, 2576:2832]
    qBp = qB.reshape(D, 4, 128)[:, :, perm].reshape(D, 512)
    kBp = kB.reshape(D, 2, 128)[:, :, perm].reshape(D, 256)
    w_in_r = np.ascontiguousarray(np.concatenate(
        [w_in[:, 0:1536], qB, kB, qBp, kBp, w_in[:, 1536:2048], w_in[:, 2048:2064], w_in[:, 2832:3088]], axis=1))
    norms = np.stack([np.asarray(inp[k][0], f) for k in ("norm_mix_pre", "norm_mix_post", "norm_mlp_pre", "norm_mlp_post")])
    gparam = np.stack([np.concatenate([np.asarray(inp["A_log_f"][0], f), np.asarray(inp["A_log_b"][0], f)]),
                       np.concatenate([np.asarray(inp["dt_bias_f"][0], f), np.asarray(inp["dt_bias_b"][0], f)])])
    qn = np.asarray(inp["q_norm_w"][0], f)
    kn = np.asarray(inp["k_norm_w"][0], f)
    hnorm = np.stack([np.asarray(inp["gdn_norm_w"][0], f), qn, qn[perm], kn, kn[perm]])
    cosT, sinT = rope_tables(Tmax)
    m = gdn_masks()
    m["ident"] = np.eye(128, dtype=f)
    cst = np.stack([m[n] for n in CONST_NAMES]).astype(f)
    return dict(w_in_r=w_in_r, w_out=np.ascontiguousarray(np.asarray(inp["w_out"][0], f)),
                w_up=np.ascontiguousarray(np.asarray(inp["w_up"][0], f)),
                w_down=np.ascontiguousarray(np.asarray(inp["w_down"][0], f)),
                norms=np.ascontiguousarray(norms),
                nrm_r=np.ascontiguousarray(norms.reshape(4, NKC, 128).transpose(2, 0, 1)),
                hnorm_r=np.ascontiguousarray(hnorm.T),
                conv_w_r=np.ascontiguousarray(np.asarray(inp["conv_w"][0], f).reshape(5, 12, 128).transpose(2, 1, 0)),
                gparam=np.ascontiguousarray(gparam), hnorm=np.ascontiguousarray(hnorm),
                cosT=cosT, sinT=sinT, cst=cst)


_NC_CACHE = {}


def run(inp, Tp, Ts, ncores, debug=False, stop_after=9):
    key = (Tp, Ts, debug, stop_after)
    if key not in _NC_CACHE:
        _NC_CACHE[key] = build(Tp, Ts, debug, stop_after)
    nc = _NC_CACHE[key]
    shared = host_inputs(inp, max(Tp, Ts))
    xp = np.asarray(inp["x_prompt"], np.float32)
    xs = np.asarray(inp["x_sample"], np.float32)
    in_maps = []
    for c in range(ncores):
        m = dict(shared)
        m["x_p"] = np.ascontiguousarray(xp[c])
        m["x_s"] = np.ascontiguousarray(xs[c])
        in_maps.append(m)
    res = run_bass_kernel_spmd(nc, in_maps, core_ids=list(range(ncores)))
    return res.results


def kernel(**inputs):
    res = run(inputs, 8192, 2048, 8)
    yp = np.stack([np.asarray(r["y_p"], np.float32) for r in res])
    ys = np.stack([np.asarray(r["y_s"], np.float32) for r in res])
    return (yp, ys)


def phase_attn(L):
    nc, P, ps, S, seqs, hnorm, identb = L["nc"], L["P"], L["ps"], L["S"], L["seqs"], L["hnorm"], L["identb"]
    DB = L["DB"]
    with ExitStack() as es:
        def tile(name, shape, dt):
            return Tile(_chk(nc, es.enter_context(nc.sbuf_tensor(name, list(shape), dt))).ap(), name)
        Tmax = max(T for _, T in seqs)
        NBmax = Tmax // 128
        wqk_b = tile("wqk_b", [128, 2, 128], F32)
        P.dma("sp", wqk_b.ap[:, 0, :], hnorm[1:2, :].partition_broadcast(128), w=[wqk_b])
        P.dma("sp", wqk_b.ap[:, 1, :], hnorm[3:4, :].partition_broadcast(128), w=[wqk_b])
        mx = tile("mx", [128, 2], F32)
        P.op("dve", "tensor_reduce", out=mx.ap, in_=wqk_b.ap, axis=AX.X, op=ALU.max, apply_absolute_value=True, r=[wqk_b], w=[mx])
        negM = tile("negM", [128, 1], F32)
        P.op("dve", "tensor_tensor", out=negM.ap, in0=mx.ap[:, 0:1], in1=mx.ap[:, 1:2], op=ALU.mult, r=[mx], w=[negM])
        P.op("dve", "tensor_scalar", out=negM.ap, in0=negM.ap, scalar1=-(128.0 ** 0.5), scalar2=None, op0=ALU.mult, r=[negM], w=[negM])
        kT = tile("kT", [128, Tmax], BF16)
        va = tile("va", [128, NBmax, 129], BF16)
        P.op("pool", "memset", ap=va.ap, constant=1.0, w=[va])
        qT = [tile("qT%d" % i, [128, TT], BF16) for i in range(4)]
        pT = [tile("pT%d" % i, [128, TT], BF16) for i in range(6)]
        osb = [tile("osb%d" % i, [128, 4, 129], F32) for i in range(2)]
        rcp = [tile("rcp%d" % i, [128, 4], F32) for i in range(2)]
        onb = [tile("onb%d" % i, [128, 4, 128], BF16) for i in range(2)]
        mst = [tile("mst%d" % i, [128, TT], BF16) for i in range(2)]
        scale = 128.0 ** -0.5
        qi = 0
        npt = 0
        for sn, T in seqs:
            sc = S[sn]
            NB = T // 128
            for g in range(2):
                P.dma("sp", kT.ap[:, 0:T], sc["QK"][(4 + g) * 128:(5 + g) * 128, :], r=[DB(("QK", sn))], w=[kT])
                P.dma("sp", va.ap[:, 0:NB, 0:128], sc["V"][:, g * 128:(g + 1) * 128].rearrange("(b p) c -> p b c", p=128),
                      r=[DB(("V", sn))], w=[va])
                for h in (2 * g, 2 * g + 1):
                    for qp in range(T // (2 * TT)):
                        q0s = [(2 * qp + s) * TT for s in range(2)]
                        q_s = [qT[(2 * qi + s) % 4] for s in range(2)]
                        for s in range(2):
                            P.dma("sp", q_s[s].ap, sc["QK"][h * 128:(h + 1) * 128, q0s[s]:q0s[s] + TT], r=[DB(("QK", sn))], w=[q_s[s]])
                        sbank = [[ps[2 * s], ps[2 * s + 1]] for s in range(2)]
                        accb = [[ps[4 + 2 * s], ps[5 + 2 * s]] for s in range(2)]
                        accv = [[b.ap[:, 0:258].rearrange("p (a c) -> p a c", a=2) for b in accb[s]] for s in range(2)]

                        def qk(s, kb):
                            sb = sbank[s][kb % 2]
                            P.op("pe", "matmul", out=sb.ap, lhsT=kT.ap[:, kb * 128:(kb + 1) * 128], rhs=q_s[s].ap, start=True, stop=True,
                                 r=[kT, q_s[s]], w=[sb])

                        qk(0, 0)
                        qk(1, 0)
                        for kb in range(NB):
                            p_s = []
                            for s in range(2):
                                if kb + 1 < NB:
                                    qk(s, kb + 1)
                                sb = sbank[s][kb % 2]
                                p_ = pT[npt % 6]
                                npt += 1
                                P.op("act", "activation", out=p_.ap, in_=sb.ap, func=AF.Exp, bias=negM.ap[:, 0:1], scale=scale,
                                     r=[sb, negM], w=[p_])
                                p_s.append(p_)
                            for s in range(2):
                                for qs in range(4):
                                    P.op("pe", "matmul", out=accv[s][qs // 2][:, qs % 2, :], lhsT=p_s[s].ap[:, qs * 128:(qs + 1) * 128], rhs=va.ap[:, kb, :],
                                         start=(kb == 0 and qs % 2 == 0), stop=(kb == NB - 1), skip_group_check=True,
                                         r=[p_s[s], va], w=[accb[s][qs // 2]])
                        for s in range(2):
                            o_, r_, n_, m_ = osb[s], rcp[s], onb[s], mst[s]
                            P.op("dve", "tensor_copy", out=o_.ap[:, 0:2, :], in_=accv[s][0], r=[accb[s][0]], w=[o_])
                            P.op("dve", "tensor_copy", out=o_.ap[:, 2:4, :], in_=accv[s][1], r=[accb[s][1]], w=[o_])
                            P.op("dve", "reciprocal", out=r_.ap, in_=o_.ap[:, :, 128], r=[o_], w=[r_])
                            P.op("dve", "tensor_tensor", out=n_.ap, in0=o_.ap[:, :, 0:128], in1=r_.ap.unsqueeze(2).to_broadcast([128, 4, 128]),
                                 op=ALU.mult, r=[o_, r_], w=[n_])
                            tb = sbank[s][0]
                            tv = tb.ap.bitcast(BF16)[:, 0:512].rearrange("p (a c) -> p a c", a=4)
                            for qs in range(4):
                                P.op("pe", "transpose", out=tv[:, qs, :], in_=n_.ap[:, qs, :], identity=identb.ap, r=[n_, identb], w=[tb])
                            P.op("dve", "tensor_copy", out=m_.ap.rearrange("p (a c) -> p a c", a=4), in_=tv, r=[tb], w=[m_])
                            P.dma("sp", sc["MIX"][512 + h * 128:512 + (h + 1) * 128, q0s[s]:q0s[s] + TT], m_.ap, r=[m_], w=[DB(("MIX", sn))])
                        qi += 1
        P.barrier()
        P.emit()


def phase_mlp(L):
    nc, P, ps, S, seqs, identb = L["nc"], L["P"], L["ps"], L["S"], L["seqs"], L["identb"]
    DB, X, Y, rms_rstd, nrm = L["DB"], L["X"], L["Y"], L["rms_rstd"], L["nrm"]
    load_weight_bf16 = L["load_weight_bf16"]

    def make_norm_resid(mo, junk, ssq, rstd, nrm_b):
        def norm_resid(src_banks, resid_ap, out_ap, which, deps_resid, deps_out):
            P.op("act", "copy", out=mo.ap[:, 0:512], in_=src_banks[0].ap, r=[src_banks[0]], w=[mo])
            P.op("dve", "tensor_copy", out=mo.ap[:, 512:1024], in_=src_banks[1].ap, r=[src_banks[1]], w=[mo])
            P.op("act", "activation", out=junk.ap, in_=mo.ap, func=AF.Square, accum_out=ssq.ap[:, 0:1], r=[mo], w=[junk, ssq])
            rms_rstd(ssq, rstd, 1, 1.0 / D)
            P.op("dve", "scalar_tensor_tensor", out=mo.ap, in0=mo.ap, scalar=rstd.ap[:, 0:1], in1=nrm_b.ap[:, which, :],
                 op0=ALU.mult, op1=ALU.mult, r=[mo, rstd, nrm_b], w=[mo])
            P.op("pool", "tensor_tensor", out=out_ap, in0=mo.ap, in1=resid_ap, op=ALU.add, r=[mo] + deps_resid, w=deps_out)
        return norm_resid

    with ExitStack() as es:
        def tile(name, shape, dt):
            return Tile(_chk(nc, es.enter_context(nc.sbuf_tensor(name, list(shape), dt))).ap(), name)
        wo = tile("wo", [128, 8, D], BF16)
        with ExitStack() as ses:
            st = [Tile(_chk(nc, ses.enter_context(nc.sbuf_tensor("wst4a_%d" % i, [128, 1024], F32))).ap()) for i in range(2)]
            load_weight_bf16(wo, st, L["w_out"], 8, D, None, 1024)
            P.barrier()
            P.emit()
        nrm_b = tile("nrm_b", [128, 2, D], F32)
        P.dma("sp", nrm_b.ap[:, 0, :], nrm[1:2, :].partition_broadcast(128), w=[nrm_b])
        P.dma("sp", nrm_b.ap[:, 1, :], nrm[3:4, :].partition_broadcast(128), w=[nrm_b])
        xt = tile("x4", [128, 4, D], F32)
        mT = [tile("mT%d" % i, [128, 8, TT], BF16) for i in range(2)]
        mo = tile("mo", [128, D], F32)
        junk = tile("junk4", [128, D], BF16)
        ssq = tile("ssq4", [128, 1], F32)
        rstd = tile("rstd4", [128, 1], F32)
        norm_resid = make_norm_resid(mo, junk, ssq, rstd, nrm_b)
        it = 0
        for sn, T in seqs:
            sc = S[sn]
            for ti in range(T // TT):
                t0 = ti * TT
                m_ = mT[it % 2]
                P.dma("sp", xt.ap, X[sn][t0:t0 + TT, :].rearrange("(s p) d -> p s d", p=128), w=[xt])
                P.dma("sp", m_.ap, sc["MIX"].rearrange("(k p) t -> p k t", p=128)[:, :, t0:t0 + TT], r=[DB(("MIX", sn))], w=[m_])
                for s in range(4):
                    banks = [ps[2 * (s % 2)], ps[2 * (s % 2) + 1]]
                    for cg in range(2):
                        for kc in range(8):
                            P.op("pe", "matmul", out=banks[cg].ap, lhsT=m_.ap[:, kc, s * 128:(s + 1) * 128], rhs=wo.ap[:, kc, cg * 512:(cg + 1) * 512],
                                 start=(kc == 0), stop=(kc == 7), r=[m_, wo], w=[banks[cg]])
                    norm_resid(banks, xt.ap[:, s, :], xt.ap[:, s, :], 0, [xt], [xt])
                P.dma("sp", sc["X1"][t0:t0 + TT, :].rearrange("(s p) d -> p s d", p=128), xt.ap, r=[xt], w=[DB(("X1", sn))])
                it += 1
        P.barrier()
        P.emit()
    T4 = 256
    with ExitStack() as es:
        def tile(name, shape, dt):
            return Tile(_chk(nc, es.enter_context(nc.sbuf_tensor(name, list(shape), dt))).ap(), name)
        wu = tile("wu", [128, 8, DFF], BF16)
        wd = tile("wd", [128, 32, D], BF16)
        with ExitStack() as ses:
            st = [Tile(_chk(nc, ses.enter_context(nc.sbuf_tensor("wst4_%d" % i, [128, 2048], F32))).ap()) for i in range(2)]
            load_weight_bf16(wu, st, L["w_up"], 8, DFF, 2, 2048)
            load_weight_bf16(wd, st, L["w_down"], 32, D, None, 1024)
            P.barrier()
            P.emit()
        nrm_b = tile("nrm_b2", [128, D], F32)
        P.dma("sp", nrm_b.ap, nrm[3:4, :].partition_broadcast(128), w=[nrm_b])
        xt = tile("x4b", [128, D], F32)
        mo = tile("mo2", [128, D], F32)
        ssq = tile("ssq4b", [128, 1], F32)
        rstd = tile("rstd4b", [128, 1], F32)
        ssq2 = tile("ssq4c", [128, 1], F32)
        rstd2 = tile("rstd4c", [128, 1], F32)
        h2 = tile("h2", [128, D], BF16)
        h2T = [tile("h2T%d" % i, [128, 8, T4], BF16) for i in range(2)]
        fT = tile("fT", [128, 32, T4], BF16)
        rl = [tile("rl%d" % i, [128, T4], F32) for i in range(2)]
        jk = tile("jk4", [128, D], BF16)
        tiles = [(sn, T, ti) for sn, T in seqs for ti in range(T // T4)]

        def prologue(i):
            sn, T, ti = tiles[i]
            sc = S[sn]
            t0 = ti * T4
            hT_ = h2T[i % 2]
            for s in range(2):
                P.dma("sp", xt.ap, sc["X1"][t0 + s * 128:t0 + (s + 1) * 128, :], r=[DB(("X1", sn))], w=[xt])
                P.op("act", "activation", out=h2.ap, in_=xt.ap, func=AF.Square, accum_out=ssq.ap[:, 0:1], r=[xt], w=[h2, ssq])
                rms_rstd(ssq, rstd, 1, 1.0 / D)
                P.op("dve", "tensor_scalar", out=h2.ap, in0=xt.ap, scalar1=rstd.ap[:, 0:1], scalar2=None, op0=ALU.mult,
                     r=[xt, rstd], w=[h2])
                pb = ps[2 + s]
                pv = pb.ap.bitcast(BF16).rearrange("p (k t) -> p k t", k=8)
                for kc in range(8):
                    P.op("pe", "transpose", out=pv[:, kc, :], in_=h2.ap[:, kc * 128:(kc + 1) * 128], identity=identb.ap,
                         r=[h2, identb], w=[pb])
                P.op("act", "copy", out=hT_.ap[:, :, s * 128:(s + 1) * 128], in_=pv, r=[pb], w=[hT_])

        prologue(0)
        for i, (sn, T, ti) in enumerate(tiles):
            sc = S[sn]
            t0 = ti * T4
            hT_ = h2T[i % 2]
            for fc in range(32):
                pb = ps[4 + (fc % 4)]
                for kc in range(8):
                    P.op("pe", "matmul", out=pb.ap[:, 0:T4], lhsT=wu.ap[:, kc, fc * 128:(fc + 1) * 128], rhs=hT_.ap[:, kc, :],
                         start=(kc == 0), stop=(kc == 7), r=[wu, hT_], w=[pb])
                rl_ = rl[fc % 2]
                P.op("act", "activation", out=rl_.ap, in_=pb.ap[:, 0:T4], func=AF.Relu, r=[pb], w=[rl_])
                P.op("dve" if fc % 2 == 0 else "pool", "tensor_tensor", out=fT.ap[:, fc, :], in0=rl_.ap, in1=rl_.ap, op=ALU.mult, r=[rl_], w=[fT])
            if i + 1 < len(tiles):
                prologue(i + 1)
            for s in range(2):
                banks = [ps[0], ps[1]]
                for cg in range(2):
                    for fc in range(32):
                        P.op("pe", "matmul", out=banks[cg].ap, lhsT=fT.ap[:, fc, s * 128:(s + 1) * 128], rhs=wd.ap[:, fc, cg * 512:(cg + 1) * 512],
                             start=(fc == 0), stop=(fc == 31), r=[fT, wd], w=[banks[cg]])
                P.op("act", "copy", out=mo.ap[:, 0:512], in_=banks[0].ap, r=[banks[0]], w=[mo])
                P.op("dve", "tensor_copy", out=mo.ap[:, 512:1024], in_=banks[1].ap, r=[banks[1]], w=[mo])
                P.op("act", "activation", out=jk.ap, in_=mo.ap, func=AF.Square, accum_out=ssq2.ap[:, 0:1], r=[mo], w=[jk, ssq2])
                rms_rstd(ssq2, rstd2, 1, 1.0 / D)
                P.op("dve", "scalar_tensor_tensor", out=mo.ap, in0=mo.ap, scalar=rstd2.ap[:, 0:1], in1=nrm_b.ap,
                     op0=ALU.mult, op1=ALU.mult, r=[mo, rstd2, nrm_b], w=[mo])
                P.dma("sp", xt.ap, sc["X1"][t0 + s * 128:t0 + (s + 1) * 128, :], r=[DB(("X1", sn))], w=[xt])
                P.op("pool", "tensor_tensor", out=xt.ap, in0=mo.ap, in1=xt.ap, op=ALU.add, r=[mo, xt], w=[xt])
                P.dma("sp", Y[sn][t0 + s * 128:t0 + (s + 1) * 128, :], xt.ap, r=[xt], w=[DB(("Y", sn))])
        P.barrier()
        P.emit()


def phases_rest(L):
    sa = L["stop_after"]
    if sa >= 2:
        phase_gdn(L)
    if sa >= 3:
        phase_attn(L)
    if sa >= 4:
        phase_mlp(L)


def phase_gdn(L):
    nc, P, ps, S, seqs, identb, onesb = L["nc"], L["P"], L["ps"], L["S"], L["seqs"], L["identb"], L["onesb"]
    DB, epsc, rms_rstd = L["DB"], L["epsc"], L["rms_rstd"]
    with ExitStack() as es:
        def tile(name, shape, dt):
            return Tile(_chk(nc, es.enter_context(nc.sbuf_tensor(name, list(shape), dt))).ap(), name)
        cw_t = tile("cw_t", [128, 12, 5], F32)
        P.dma("sp", cw_t.ap, L["conv_w_r"], w=[cw_t])
        dg = tile("dg", [128, 12, 5, 128], BF16)
        for c in range(12):
            for kk in range(5):
                P.op("dve" if (c * 5 + kk) % 2 == 0 else "pool", "tensor_scalar", out=dg.ap[:, c, kk, :], in0=identb.ap,
                     scalar1=cw_t.ap[:, c, kk:kk + 1], scalar2=None, op0=ALU.mult, r=[identb, cw_t], w=[dg])
        raw = [tile("raw%d" % i, [128, 12, TT + 4], BF16) for i in range(2)]
        sl = [tile("sl%d" % i, [128, TT], F32) for i in range(2)]
        sq = [tile("sq%d" % i, [128, TT], BF16) for i in range(2)]
        rs = [tile("rs%d" % i, [128, TT], F32) for i in range(2)]
        fm = [tile("fm%d" % i, [128, TT], BF16) for i in range(3)]
        tk = [tile("tk%d" % i, [128, 4, 128], BF16) for i in range(2)]
        it = 0
        n = 0
        for sn, T in seqs:
            sc = S[sn]
            for ti in range(T // TT):
                t0 = ti * TT
                r_ = raw[it % 2]
                lo, hi = max(t0 - 2, 0), min(t0 + TT + 2, T)
                if t0 == 0:
                    P.op("pool", "memset", ap=r_.ap[:, :, 0:2], constant=0.0, w=[r_])
                if t0 + TT == T:
                    P.op("pool", "memset", ap=r_.ap[:, :, TT + 2:TT + 4], constant=0.0, w=[r_])
                P.dma("sp", r_.ap[:, :, lo - (t0 - 2):hi - (t0 - 2)], sc["A"].rearrange("(c p) t -> p c t", p=128)[:, :, lo:hi],
                      r=[DB(("A", sn))], w=[r_])
                for c in range(12):
                    kind, h = c // 4, c % 4
                    s_ = sl[n % 2]
                    pcv = ps[4 + (n % 4)]
                    for kk in range(5):
                        P.op("pe", "matmul", out=pcv.ap, lhsT=dg.ap[:, c, kk, :], rhs=r_.ap[:, c, kk:kk + TT], start=(kk == 0), stop=(kk == 4),
                             r=[dg, r_], w=[pcv])
                    P.op("act", "activation", out=s_.ap, in_=pcv.ap, func=AF.Silu, r=[pcv], w=[s_])
                    f_ = fm[n % 3]
                    if kind == 2:
                        P.op("pool", "tensor_copy", out=f_.ap, in_=s_.ap, r=[s_], w=[f_])
                    else:
                        q_, rs_ = sq[n % 2], rs[n % 2]
                        pb = ps[n % 2]
                        P.op("act", "activation", out=q_.ap, in_=s_.ap, func=AF.Square, r=[s_], w=[q_])
                        P.op("pe", "matmul", out=pb.ap, lhsT=onesb.ap, rhs=q_.ap, start=True, stop=True, r=[onesb, q_], w=[pb])
                        P.op("act", "activation", out=rs_.ap, in_=pb.ap, func=AF.Ln, bias=epsc.ap[:, 0:1], scale=1.0, r=[pb, epsc], w=[rs_])
                        P.op("act", "activation", out=rs_.ap, in_=rs_.ap, func=AF.Exp, scale=-0.5, r=[rs_], w=[rs_])
                        P.op("dve", "scalar_tensor_tensor", out=f_.ap, in0=s_.ap, scalar=(128.0 ** -0.5 if kind == 0 else 1.0), in1=rs_.ap,
                             op0=ALU.mult, op1=ALU.mult, r=[s_, rs_], w=[f_])
                        dst = sc["GQ"] if kind == 0 else sc["GK"]
                        P.dma("sp", dst[h * 128:(h + 1) * 128, t0:t0 + TT], f_.ap, r=[f_], w=[DB(("GQK", sn))])
                    if kind >= 1:
                        tb = ps[2 + (n % 2)]
                        tv = tb.ap.bitcast(BF16)[:, 0:512].rearrange("p (a c) -> p a c", a=4)
                        for s in range(4):
                            P.op("pe", "transpose", out=tv[:, s, :], in_=f_.ap[:, s * 128:(s + 1) * 128], identity=identb.ap,
                                 r=[f_, identb], w=[tb])
                        t_ = tk[n % 2]
                        P.op("act", "copy", out=t_.ap, in_=tv, r=[tb], w=[t_])
                        dst = sc["GKt"] if kind == 1 else sc["GVt"]
                        P.dma("sp", dst[t0:t0 + TT, h * 128:(h + 1) * 128].rearrange("(s p) d -> p s d", p=128), t_.ap,
                              r=[t_], w=[DB(("GKV", sn))])
                    n += 1
                it += 1
        P.barrier()
        P.emit()
    if L["stop_after"] == 2 and L.get("gdn_pre_only"):
        return
    with ExitStack() as es:
        def tile(name, shape, dt):
            return Tile(_chk(nc, es.enter_context(nc.sbuf_tensor(name, list(shape), dt))).ap(), name)

        cf = tile("cf", [128, len(CONST_NAMES), 128], F32)
        P.dma("sp", cf.ap, L["cst"].rearrange("n p j -> p n j"), w=[cf])
        C = {n: cf.ap[:, i, :] for i, n in enumerate(CONST_NAMES)}
        onesf = tile("onesf", [128, 128], F32)
        P.op("pool", "memset", ap=onesf.ap, constant=1.0, w=[onesf])
        gnw_b = tile("gnw_b", [128, 128], F32)
        P.dma("sp", gnw_b.ap, L["hnorm"][0:1, :].partition_broadcast(128), w=[gnw_b])

        def b3(ap2):
            return ap2.unsqueeze(1).to_broadcast([128, 4, 128])

        def bj(ap2):
            return ap2.unsqueeze(2).to_broadcast([128, 4, 128])

        def v4(t):
            return t.ap.rearrange("p (h j) -> p h j", h=4)

        NU = 2
        Rd = []
        R = []
        for d in range(2):
            rd = {"S4": tile("S4_%d" % d, [128, 512], F32), "S4b": tile("S4b_%d" % d, [128, 512], BF16),
                  }
            Rd.append(rd)
            Ru = []
            for u in range(NU):
                r = {}
                sfx = "%d_%d" % (d, u)
                for nm in ("KT", "QT", "Kt", "Vt", "A0", "A1", "B0", "B1", "Aqk", "AqkT", "PTb", "Vb", "kbg", "ke", "wT", "vn"):
                    r[nm] = tile(nm + sfx, [128, 512], BF16)
                for nm in ("Gm", "Ep", "D4", "Dsb", "PT", "u4"):
                    r[nm] = tile(nm + sfx, [128, 512], F32)
                r["Oq"], r["O4"], r["sz"], r["o2"] = r["Gm"], r["Ep"], r["D4"], r["Dsb"]
                r["zb"], r["onb"], r["mxs"] = r["A0"], r["A1"], r["B0"]
                r["G"] = tile("G" + sfx, [128, 16], F32)
                r["sm"] = tile("sm" + sfx, [128, 16], F32)
                r["ex"] = tile("ex" + sfx, [128, 16], F32)
                r["bgc"] = tile("bgc" + sfx, [128, 4], F32)
                r["ssq"] = tile("gssq" + sfx, [128, 4], F32)
                r["rstd"] = tile("grstd" + sfx, [128, 4], F32)
                r["banks"] = [ps[2 * (NU * d + u)], ps[2 * (NU * d + u) + 1]]
                r["nb"] = 0
                Ru.append(r)
            R.append(Ru)

        def nbank(r):
            b = r["banks"][r["nb"] % 2]
            r["nb"] += 1
            return b

        def unit(sn, T, d, b, step, NB):
            sc = S[sn]
            r = R[d][step % NU]
            rd = Rd[d]
            t0 = b * 128
            KT, QT, Kt, Vt, G = r["KT"], r["QT"], r["Kt"], r["Vt"], r["G"]
            P.dma("sp", v4(KT), sc["GK"].rearrange("(h p) t -> p h t", p=128)[:, :, t0:t0 + 128], r=[DB(("GQK", sn))], w=[KT])
            P.dma("sp", v4(QT), sc["GQ"].rearrange("(h p) t -> p h t", p=128)[:, :, t0:t0 + 128], r=[DB(("GQK", sn))], w=[QT])
            P.dma("sp", Kt.ap, sc["GKt"][t0:t0 + 128, :], r=[DB(("GKV", sn))], w=[Kt])
            P.dma("sp", Vt.ap, sc["GVt"][t0:t0 + 128, :], r=[DB(("GKV", sn))], w=[Vt])
            P.dma("sp", G.ap, sc["G"][t0:t0 + 128, :], r=[DB(("G", sn))], w=[G])
            yield
            beta = G.ap[:, 4 * d:4 * d + 4]
            gg = G.ap[:, 8 + 4 * d:12 + 4 * d]
            cum = C["cum_f"] if d == 0 else C["cum_b"]
            pos = C["pos_f"] if d == 0 else C["pos_b"]
            strict = C["strict_f"] if d == 0 else C["strict_b"]
            sm, ex, bgc = r["sm"], r["ex"], r["bgc"]
            Gm, Ep, D4, Dsb = r["Gm"], r["Ep"], r["D4"], r["Dsb"]
            pb = nbank(r)
            P.op("pe", "matmul", out=pb.ap[:, 0:4], lhsT=cum, rhs=gg, start=True, stop=True, r=[cf, G], w=[pb])
            P.op("pe", "matmul", out=pb.ap[:, 4:8], lhsT=C["same"], rhs=gg, start=True, stop=True, r=[cf, G], w=[pb])
            P.op("pe", "matmul", out=pb.ap[:, 8:12], lhsT=C["c0"], rhs=gg, start=True, stop=True, r=[cf, G], w=[pb])
            P.op("pe", "matmul", out=pb.ap[:, 12:16], lhsT=C["c1"], rhs=gg, start=True, stop=True, r=[cf, G], w=[pb])
            P.op("pool", "tensor_tensor", out=v4(Gm), in0=b3(cum), in1=bj(gg), op=ALU.mult, r=[cf, G], w=[Gm])
            yield
            P.op("dve", "tensor_copy", out=sm.ap, in_=pb.ap[:, 0:16], r=[pb], w=[sm])
            P.op("dve", "tensor_tensor", out=sm.ap[:, 4:8], in0=sm.ap[:, 4:8], in1=sm.ap[:, 0:4], op=ALU.subtract, r=[sm], w=[sm])
            pe_ = nbank(r)
            P.op("pe", "matmul", out=pe_.ap, lhsT=onesf.ap, rhs=Gm.ap, start=True, stop=True, r=[onesf, Gm], w=[pe_])
            yield
            P.op("act", "activation", out=ex.ap, in_=sm.ap, func=AF.Exp, r=[sm], w=[ex])
            P.op("dve", "tensor_tensor", out=v4(Ep), in0=pe_.ap.rearrange("p (h j) -> p h j", h=4), in1=b3(pos), op=ALU.add,
                 r=[pe_, cf], w=[Ep])
            yield
            P.op("dve", "tensor_tensor", out=bgc.ap, in0=ex.ap[:, 0:4], in1=beta, op=ALU.mult, r=[ex, G], w=[bgc])
            for h in range(4):
                P.op("act", "activation", out=v4(D4)[:, h, :], in_=v4(Ep)[:, h, :], func=AF.Exp, bias=sm.ap[:, h:h + 1], scale=-1.0,
                     r=[Ep, sm], w=[D4])
            pk, pq = nbank(r), nbank(r)
            for h in range(4):
                P.op("pe", "matmul", out=pk.ap[:, h * 128:(h + 1) * 128], lhsT=v4(KT)[:, h, :], rhs=v4(KT)[:, h, :], start=True, stop=True,
                     r=[KT], w=[pk])
            for h in range(4):
                P.op("pe", "matmul", out=pq.ap[:, h * 128:(h + 1) * 128], lhsT=v4(QT)[:, h, :], rhs=v4(KT)[:, h, :], start=True, stop=True,
                     r=[QT, KT], w=[pq])
            yield
            Vb, kbg, ke, u4, wT = r["Vb"], r["kbg"], r["ke"], r["u4"], r["wT"]
            P.op("pool", "tensor_tensor", out=v4(Dsb), in0=v4(D4), in1=b3(strict), op=ALU.mult, r=[D4, cf], w=[Dsb])
            P.op("pool", "tensor_tensor", out=v4(Dsb), in0=v4(Dsb), in1=bj(beta), op=ALU.mult, r=[Dsb, G], w=[Dsb])
            A, B = [r["A0"], r["A1"]], [r["B0"], r["B1"]]
            Aqk, AqkT, PT, PTb = r["Aqk"], r["AqkT"], r["PT"], r["PTb"]
            P.op("dve", "tensor_tensor", out=Aqk.ap, in0=pq.ap, in1=D4.ap, op=ALU.mult, r=[pq, D4], w=[Aqk])
            yield
            P.op("dve", "tensor_tensor", out=A[0].ap, in0=pk.ap, in1=Dsb.ap, op=ALU.mult, r=[pk, Dsb], w=[A[0]])
            P.op("pool", "tensor_tensor", out=v4(Vb), in0=v4(Vt), in1=bj(beta), op=ALU.mult, r=[Vt, G], w=[Vb])
            P.op("pool", "tensor_tensor", out=v4(kbg), in0=v4(Kt), in1=bj(bgc.ap), op=ALU.mult, r=[Kt, bgc], w=[kbg])
            P.op("pool", "tensor_tensor", out=v4(ke), in0=v4(Kt), in1=bj(ex.ap[:, 4:8]), op=ALU.mult, r=[Kt, ex], w=[ke])
            yield
            tb = nbank(r)
            tvA = tb.ap.bitcast(BF16)[:, 0:512].rearrange("p (a c) -> p a c", a=4)
            tvQ = tb.ap.bitcast(BF16)[:, 512:1024].rearrange("p (a c) -> p a c", a=4)
            for h in range(4):
                P.op("pe", "transpose", out=tvA[:, h, :], in_=v4(A[0])[:, h, :], identity=identb.ap, r=[A[0], identb], w=[tb])
            for h in range(4):
                P.op("pe", "transpose", out=tvQ[:, h, :], in_=v4(Aqk)[:, h, :], identity=identb.ap, r=[Aqk, identb], w=[tb])
            yield
            P.op("act", "copy", out=v4(B[0]), in_=tvA, r=[tb], w=[B[0]])
            P.op("act", "copy", out=v4(AqkT), in_=tvQ, r=[tb], w=[AqkT])
            yield
            P.op("pool", "tensor_tensor", out=v4(PT), in0=b3(C["ident"]), in1=v4(B[0]), op=ALU.subtract, r=[cf, B[0]], w=[PT])
            P.op("act", "copy", out=PTb.ap, in_=PT.ap, r=[PT], w=[PTb])
            for k in range(5):
                Ak, Bk, An, Bn = A[k % 2], B[k % 2], A[(k + 1) % 2], B[(k + 1) % 2]
                pa = nbank(r)
                for h in range(4):
                    P.op("pe", "matmul", out=pa.ap[:, h * 128:(h + 1) * 128], lhsT=v4(Bk)[:, h, :], rhs=v4(Ak)[:, h, :], start=True, stop=True,
                         r=[Ak, Bk], w=[pa])
                if k < 4:
                    pbb = nbank(r)
                    for h in range(4):
                        P.op("pe", "matmul", out=pbb.ap[:, h * 128:(h + 1) * 128], lhsT=v4(Ak)[:, h, :], rhs=v4(Bk)[:, h, :], start=True, stop=True,
                             r=[Ak, Bk], w=[pbb])
                yield
                P.op("act", "copy", out=An.ap, in_=pa.ap, r=[pa], w=[An])
                if k < 4:
                    P.op("dve", "tensor_copy", out=Bn.ap, in_=pbb.ap, r=[pbb], w=[Bn])
                yield
                pd = nbank(r)
                for h in range(4):
                    P.op("pe", "matmul", out=pd.ap[:, h * 128:(h + 1) * 128], lhsT=v4(An)[:, h, :], rhs=v4(PTb)[:, h, :], start=True, stop=True,
                         r=[An, PTb], w=[pd])
                yield
                P.op("dve", "tensor_tensor", out=PT.ap, in0=pd.ap, in1=PT.ap, op=ALU.add, r=[pd, PT], w=[PT])
                P.op("act", "copy", out=PTb.ap, in_=PT.ap, r=[PT], w=[PTb])
                yield
            pu, pw = nbank(r), nbank(r)
            for h in range(4):
                P.op("pe", "matmul", out=pu.ap[:, h * 128:(h + 1) * 128], lhsT=v4(PTb)[:, h, :], rhs=v4(Vb)[:, h, :], start=True, stop=True,
                     r=[PTb, Vb], w=[pu])
            for h in range(4):
                P.op("pe", "matmul", out=pw.ap[:, h * 128:(h + 1) * 128], lhsT=v4(kbg)[:, h, :], rhs=v4(PTb)[:, h, :], start=True, stop=True,
                     r=[PTb, kbg], w=[pw])
            yield
            P.op("act", "copy", out=u4.ap, in_=pu.ap, r=[pu], w=[u4])
            P.op("dve", "tensor_copy", out=wT.ap, in_=pw.ap, r=[pw], w=[wT])
            yield
            S4, S4b, vn, Oq, O4 = rd["S4"], rd["S4b"], r["vn"], r["Oq"], r["O4"]
            for c in ((0, 1) if d == 0 else (1, 0)):
                pc = slice(c * 64, (c + 1) * 64)
                p1, po1 = nbank(r), nbank(r)
                for h in range(4):
                    P.op("pe", "matmul", out=p1.ap[pc, h * 128:(h + 1) * 128], lhsT=v4(wT)[:, h, pc], rhs=v4(S4b)[:, h, :], start=True, stop=True,
                         r=[wT, S4b], w=[p1])
                for h in range(4):
                    P.op("pe", "matmul", out=po1.ap[pc, h * 128:(h + 1) * 128], lhsT=v4(QT)[:, h, pc], rhs=v4(S4b)[:, h, :], start=True, stop=True,
                         r=[QT, S4b], w=[po1])
                yield
                P.op("dve", "tensor_tensor", out=vn.ap[pc, :], in0=u4.ap[pc, :], in1=p1.ap[pc, :], op=ALU.subtract, r=[u4, p1], w=[vn])
                P.op("dve", "tensor_tensor", out=v4(Oq)[pc], in0=po1.ap[pc, :].rearrange("p (h j) -> p h j", h=4),
                     in1=ex.ap[pc, 0:4].unsqueeze(2).to_broadcast([64, 4, 128]), op=ALU.mult, r=[po1, ex], w=[Oq])
                yield
                po2, pds = nbank(r), nbank(r)
                for h in range(4):
                    P.op("pe", "matmul", out=pds.ap[:, h * 128:(h + 1) * 128], lhsT=v4(ke)[pc, h, :], rhs=v4(vn)[pc, h, :], start=True, stop=True,
                         r=[ke, vn], w=[pds])
                for h in range(4):
                    P.op("pe", "matmul", out=po2.ap[pc, h * 128:(h + 1) * 128], lhsT=v4(AqkT)[pc, h, pc], rhs=v4(vn)[pc, h, :], start=True, stop=True,
                         r=[AqkT, vn], w=[po2])
                P.op("pool", "tensor_tensor", out=v4(S4), in0=v4(S4), in1=bj(ex.ap[:, 8 + 4 * c:12 + 4 * c]), op=ALU.mult, r=[S4, ex], w=[S4])
                yield
                P.op("dve", "tensor_tensor", out=S4.ap, in0=pds.ap, in1=S4.ap, op=ALU.add, r=[pds, S4], w=[S4])
                P.op("act", "copy", out=S4b.ap, in_=S4.ap, r=[S4], w=[S4b])
                P.op("dve", "tensor_tensor", out=O4.ap[pc, :], in0=po2.ap[pc, :], in1=Oq.ap[pc, :], op=ALU.add, r=[po2, Oq], w=[O4])
                yield
            mine, other = ("OF", "OB") if d == 0 else ("OB", "OF")
            if step < NB // 2:
                P.dma("sp", sc[mine][t0:t0 + 128, :], O4.ap, r=[O4], w=[DB((mine, sn, b))])
            else:
                o2, sz, zb, onb, mxs = r["o2"], r["sz"], r["zb"], r["onb"], r["mxs"]
                ssq, rstd = r["ssq"], r["rstd"]
                P.dma("sp", o2.ap, sc[other][t0:t0 + 128, :], r=[DB((other, sn, b))], w=[o2])
                P.dma("sp", zb.ap, sc["Z"][t0:t0 + 128, :], r=[DB(("Z", sn))], w=[zb])
                yield
                P.op("pool", "tensor_tensor", out=o2.ap, in0=o2.ap, in1=O4.ap, op=ALU.add, r=[o2, O4], w=[o2])
                P.op("pool", "tensor_tensor", out=sz.ap, in0=o2.ap, in1=o2.ap, op=ALU.mult, r=[o2], w=[sz])
                yield
                P.op("dve", "tensor_reduce", out=ssq.ap, in_=v4(sz), axis=AX.X, op=ALU.add, r=[sz], w=[ssq])
                rms_rstd(ssq, rstd, 4, 1.0 / 128)
                P.op("act", "activation", out=sz.ap, in_=zb.ap, func=AF.Silu, r=[zb], w=[sz])
                yield
                P.op("dve", "tensor_tensor", out=v4(o2), in0=v4(o2), in1=bj(rstd.ap), op=ALU.mult, r=[o2, rstd], w=[o2])
                P.op("pool", "tensor_tensor", out=v4(o2), in0=v4(o2), in1=b3(gnw_b.ap), op=ALU.mult, r=[o2, gnw_b], w=[o2])
                yield
                P.op("dve", "tensor_tensor", out=onb.ap, in0=o2.ap, in1=sz.ap, op=ALU.mult, r=[o2, sz], w=[onb])
                yield
                tb2 = nbank(r)
                tv2 = tb2.ap.bitcast(BF16)[:, 0:512].rearrange("p (a c) -> p a c", a=4)
                for h in range(4):
                    P.op("pe", "transpose", out=tv2[:, h, :], in_=v4(onb)[:, h, :], identity=identb.ap, r=[onb, identb], w=[tb2])
                yield
                P.op("act", "copy", out=v4(mxs), in_=tv2, r=[tb2], w=[mxs])
                P.dma("sp", sc["MIX"][0:512, :].rearrange("(h p) t -> p h t", p=128)[:, :, t0:t0 + 128], v4(mxs), r=[mxs], w=[DB(("MIX", sn))])

        for sn, T in seqs:
            NB = T // 128
            for d in range(2):
                P.op("pool", "memset", ap=Rd[d]["S4"].ap, constant=0.0, w=[Rd[d]["S4"]])
                P.op("pool", "memset", ap=Rd[d]["S4b"].ap, constant=0.0, w=[Rd[d]["S4b"]])
            active = []
            pending = []
            for st_ in range(NB):
                pending.append((0, st_, st_))
                pending.append((1, NB - 1 - st_, st_))
            STAG = 10
            while pending or active:
                if pending and len(active) < 2 * NU and (not active or active[-1][1] >= STAG):
                    d_, b_, st_ = pending.pop(0)
                    active.append([unit(sn, T, d_, b_, st_, NB), 0])
                for a_ in list(active):
                    try:
                        next(a_[0])
                        a_[1] += 1
                    except StopIteration:
                        active.remove(a_)
        P.barrier()
        P.emit()
```

```python
import numpy as np
import concourse.bass as bass
import concourse.mybir as mybir
from concourse.bass_utils import run_bass_kernel_spmd

F32 = mybir.dt.float32
BF16 = mybir.dt.bfloat16
AF = mybir.ActivationFunctionType
ALU = mybir.AluOpType
AX = mybir.AxisListType

ENGS = ("pe", "act", "dve", "pool", "sp")
NS = 8


class Buf:
    __slots__ = ("w", "r", "rd", "name", "excl")

    def __init__(self, name=""):
        self.excl = False
        self.w = None
        self.r = {}
        self.rd = []
        self.name = name


class Prog:
    def __init__(self, nc):
        self.nc = nc
        self.sem = {e: nc.alloc_semaphore("s_" + e) for e in ENGS}
        self.dsem = {q: [nc.alloc_semaphore("d_%s_%d" % (q, i)) for i in range(NS)]
                     for q in ("sp", "act", "pool")}
        self.dcount = {q: 0 for q in self.dsem}
        self.base = {e: 0 for e in ENGS}
        self.rec = {e: [] for e in ENGS}
        self.off = {e: 0 for e in ENGS}
        self.msmap = {e: {} for e in ENGS}
        self.waited = {}
        self.limit = None
        self.nrec = 0

    def _need(self, eng, tok, waits):
        if tok is None:
            return
        if tok[0] == "c":
            _, se, gidx = tok
            if se == "pe" and eng == "pe":
                return
            key = (eng, se)
            if self.waited.get(key, -1) >= gidx:
                return
            self.waited[key] = gidx
            li = gidx - self.off[se]
            if li >= 0:
                self.rec[se][li][3] = True
            waits.append(tok)
        else:
            _, q, n = tok
            key = (eng, q, n % NS)
            if self.waited.get(key, -1) >= n:
                return
            self.waited[key] = n
            waits.append(tok)

    def _deps(self, eng, tok, r, w, waits):
        rx = [b for b in r if b.excl]
        if rx:
            r = [b for b in r if not b.excl]
            w = list(w) + [b for b in rx if b not in w]
        for b in r:
            self._need(eng, b.w, waits)
        for b in w:
            self._need(eng, b.w, waits)
            for se, gi in b.r.items():
                self._need(eng, ("c", se, gi), waits)
            for t in b.rd:
                self._need(eng, t, waits)
        for b in r:
            if tok[0] == "c":
                b.r[tok[1]] = tok[2]
            else:
                b.rd.append(tok)
                if len(b.rd) > 3 * NS:
                    b.rd = b.rd[-3 * NS:]
        for b in w:
            b.w = tok
            b.r = {}
            b.rd = []

    def op(self, eng, meth, r=(), w=(), **kw):
        fn = (meth, kw)
        self.nrec += 1
        if self.limit is not None and self.nrec > self.limit:
            return None
        waits = []
        gidx = self.off[eng] + len(self.rec[eng])
        tok = ("c", eng, gidx)
        self._deps(eng, tok, r, w, waits)
        self.rec[eng].append(["op", fn, waits, False])
        return tok

    def dma(self, q, out, in_, r=(), w=(), slow=False):
        self.nrec += 1
        if self.limit is not None and self.nrec > self.limit:
            return None
        waits = []
        n = self.dcount[q]
        self.dcount[q] += 1
        tok = ("d", q, n)
        if n >= NS:
            self._need(q, ("d", q, n - NS), waits)
        self._deps(q, tok, r, w, waits)
        self.rec[q].append(["dma", (out, in_, q, n, slow), waits, False])
        return tok

    def barrier(self, bufs=()):
        for e in ENGS:
            waits = []
            for se in ENGS:
                if se == e:
                    continue
                for li in range(len(self.rec[se]) - 1, -1, -1):
                    if self.rec[se][li][0] == "op":
                        self._need(e, ("c", se, self.off[se] + li), waits)
                        break
            for q in self.dsem:
                for k in range(max(0, self.dcount[q] - NS), self.dcount[q]):
                    self._need(e, ("d", q, k), waits)
            self.rec[e].append(["nop", None, waits, False])

    def emit(self):
        nc = self.nc
        for e in ENGS:
            c = self.base[e]
            for li, rcd in enumerate(self.rec[e]):
                if rcd[0] == "op" and rcd[3]:
                    c += 1
                    self.msmap[e][self.off[e] + li] = c
            self.base[e] = c

        def run(e, engobj):
            for rcd in self.rec[e]:
                kind, fn, waits, ms = rcd
                for t in waits:
                    if t[0] == "c":
                        engobj.wait_ge(self.sem[t[1]], self.msmap[t[1]][t[2]])
                    else:
                        engobj.wait_ge(self.dsem[t[1]][t[2] % NS], 16 * (t[2] // NS + 1))
                if kind == "op":
                    ins = getattr(engobj, fn[0])(**fn[1])
                    if ms:
                        ins.then_inc(self.sem[e], 1)
                elif kind == "dma":
                    out, in_, q, n, slow = fn
                    if slow:
                        ins = engobj.dma_start(out=out, in_=in_, allow_slow_non_contiguous=True)
                    else:
                        ins = engobj.dma_start(out=out, in_=in_)
                    ins.then_inc(self.dsem[q][n % NS], 16)

        with nc.Block() as block:
            @block.tensor
            def _(eng):
                run("pe", eng)

            @block.scalar
            def _(eng):
                run("act", eng)

            @block.vector
            def _(eng):
                run("dve", eng)

            @block.gpsimd
            def _(eng):
                run("pool", eng)

            @block.sync
            def _(eng):
                run("sp", eng)

        for e in ENGS:
            self.off[e] += len(self.rec[e])
            self.rec[e] = []

from contextlib import ExitStack

D = 1024
NKC = 8
DFF = 4096
EPS = 1e-6
TT = 512
FM_COLS = 3072
TM0 = 3072
WIN_COLS = 3856
GRID_W = 64


SBUF_LIMIT = 196608


def _chk(nc, handle):
    m = nc.lookup_mloc(handle)
    nbytes = 1
    for d in list(m.dims)[1:]:
        nbytes *= int(d)
    assert int(m.addr) + nbytes <= SBUF_LIMIT, ("SBUF overflow past physical limit", handle.name, int(m.addr), nbytes)
    return handle


class Tile(Buf):
    __slots__ = ("ap",)

    def __init__(self, ap, name=""):
        Buf.__init__(self, name)
        self.ap = ap


def rope_tables(T):
    t = np.arange(T)
    row = (t // GRID_W).astype(np.float32)
    col = (t % GRID_W).astype(np.float32)
    nf = 32
    inv = (np.float32(10000.0) ** (-np.arange(nf, dtype=np.float32) / np.float32(nf))).astype(np.float32)
    cosT = np.zeros((128, T), np.float32)
    sinT = np.zeros((128, T), np.float32)
    for d in range(128):
        half = d // 64
        idx = d % 64
        f = idx % 32
        pos = row if half == 0 else col
        ang = (pos * inv[f]).astype(np.float32)
        cosT[d] = np.cos(ang)
        sinT[d] = np.sin(ang) * (-1.0 if idx < 32 else 1.0)
    return cosT, sinT


def rope_perm():
    p = np.zeros(128, np.int64)
    for d in range(128):
        idx = d % 64
        p[d] = d + 32 if idx < 32 else d - 32
    return p


def gdn_masks():
    t = np.arange(128)[:, None]
    j = np.arange(128)[None, :]
    same = (t // 64) == (j // 64)
    m = {}
    m["cum_f"] = (same & (t <= j)).astype(np.float32)
    m["cum_b"] = (same & (t >= j)).astype(np.float32)
    m["same"] = same.astype(np.float32)
    i = t
    m["pos_f"] = np.where(same & (i >= j), 0.0, 30000.0).astype(np.float32)
    m["pos_b"] = np.where(same & (i <= j), 0.0, 30000.0).astype(np.float32)
    m["strict_f"] = (same & (i > j)).astype(np.float32)
    m["strict_b"] = (same & (i < j)).astype(np.float32)
    m["c0"] = np.tile((np.arange(128) < 64).astype(np.float32)[:, None], (1, 128))
    m["c1"] = np.tile((np.arange(128) >= 64).astype(np.float32)[:, None], (1, 128))
    return m


CONST_NAMES = ["ident", "cum_f", "cum_b", "same", "pos_f", "pos_b", "strict_f", "strict_b", "c0", "c1"]


def build(Tp, Ts, debug=False, stop_after=9):
    nc = bass.Bass("TRN2", target_bir_lowering=False)
    P = Prog(nc)
    seqs = [("p", Tp), ("s", Ts)]
    Tmax = max(Tp, Ts)

    def din(name, shape, dt=F32):
        return nc.dram_tensor(name, list(shape), dt, kind="ExternalInput").ap()

    def dscr(name, shape, dt):
        if debug:
            return nc.dram_tensor(name, list(shape), dt, kind="ExternalOutput").ap()
        return nc.dram_tensor(name, list(shape), dt).ap()

    X = {"p": din("x_p", [Tp, D]), "s": din("x_s", [Ts, D])}
    Y = {"p": nc.dram_tensor("y_p", [Tp, D], F32, kind="ExternalOutput").ap(),
         "s": nc.dram_tensor("y_s", [Ts, D], F32, kind="ExternalOutput").ap()}
    w_in = din("w_in_r", [D, WIN_COLS])
    w_out = din("w_out", [D, D])
    w_up = din("w_up", [D, DFF])
    w_down = din("w_down", [DFF, D])
    nrm = din("norms", [4, D])
    conv_w_r = din("conv_w_r", [128, 12, 5])
    nrm_r = din("nrm_r", [128, 4, NKC])
    hnorm_r = din("hnorm_r", [128, 5])
    gparam = din("gparam", [2, 8])
    hnorm = din("hnorm", [5, 128])
    cosT = din("cosT", [128, Tmax])
    sinT = din("sinT", [128, Tmax])
    cst = din("cst", [len(CONST_NAMES), 128, 128])

    S = {}
    for sn, T in seqs:
        S[sn] = dict(
            A=dscr("A_raw_" + sn, [1536, T], BF16),
            QK=dscr("QK_T_" + sn, [768, T], BF16),
            Z=dscr("Z_" + sn, [T, 512], BF16),
            G=dscr("G_" + sn, [T, 16], F32),
            V=dscr("V_" + sn, [T, 256], BF16),
            GQ=dscr("GQ_T_" + sn, [512, T], BF16),
            GK=dscr("GK_T_" + sn, [512, T], BF16),
            GKt=dscr("GKt_" + sn, [T, 512], BF16),
            GVt=dscr("GVt_" + sn, [T, 512], BF16),
            OF=dscr("OF_" + sn, [T, 512], F32),
            OB=dscr("OB_" + sn, [T, 512], F32),
            MIX=dscr("MIX_T_" + sn, [1024, T], BF16),
            X1=dscr("X1_" + sn, [T, 1024], F32),
        )
    dbufs = {}

    def DB(key):
        if key not in dbufs:
            dbufs[key] = Buf(str(key))
        return dbufs[key]

    ps = [Tile(nc.alloc_psum_tensor("ps%d" % i, [128, 512], F32).ap(), "ps%d" % i) for i in range(8)]
    for t in ps:
        t.excl = True

    with ExitStack() as g_es:
        def gtile(name, shape, dt):
            return Tile(_chk(nc, g_es.enter_context(nc.sbuf_tensor(name, list(shape), dt))).ap(), name)

        identb = gtile("identb", [128, 128], BF16)
        onesb = gtile("onesb", [128, 128], BF16)
        P.op("pool", "memset", ap=onesb.ap, constant=1.0, w=[onesb])
        epsc = gtile("epsc", [128, 1], F32)
        P.op("pool", "memset", ap=epsc.ap, constant=EPS, w=[epsc])
        nrm_t = gtile("nrm_t", [128, 4, NKC], F32)
        P.dma("sp", nrm_t.ap, nrm_r, w=[nrm_t])
        hn_t = gtile("hn_t", [128, 5], F32)
        P.dma("sp", hn_t.ap, hnorm_r, w=[hn_t])
        gp_b = gtile("gp_b", [128, 2, 8], F32)
        P.dma("sp", gp_b.ap[:, 0, :], gparam[0:1, :].partition_broadcast(128), w=[gp_b])
        P.dma("sp", gp_b.ap[:, 1, :], gparam[1:2, :].partition_broadcast(128), w=[gp_b])
        negA = gtile("negA", [128, 8], F32)
        P.op("act", "activation", out=negA.ap, in_=gp_b.ap[:, 0, :], func=AF.Exp, r=[gp_b], w=[negA])
        P.op("dve", "tensor_scalar", out=negA.ap, in0=negA.ap, scalar1=-1.0, scalar2=None, op0=ALU.mult,
             r=[negA], w=[negA])

        with ExitStack() as t_es:
            identf = Tile(_chk(nc, t_es.enter_context(nc.sbuf_tensor("identf", [128, 128], F32))).ap(), "identf")
            P.dma("sp", identf.ap, cst[0], w=[identf])
            P.op("dve", "tensor_copy", out=identb.ap, in_=identf.ap, r=[identf], w=[identb])
            P.barrier()
            P.emit()

        def load_weight_bf16(wt, st, src, nk, cols, scale_sel, stage_cols):
            n = 0
            for kc in range(nk):
                for c0 in range(0, cols, stage_cols):
                    cn = min(stage_cols, cols - c0)
                    s_ = st[n % 2]
                    P.dma("sp", s_.ap[:, 0:cn], src[kc * 128:(kc + 1) * 128, c0:c0 + cn], w=[s_])
                    dst = wt.ap[:, kc, c0:c0 + cn]
                    if scale_sel is None:
                        if n % 2 == 0:
                            P.op("act", "copy", out=dst, in_=s_.ap[:, 0:cn], r=[s_], w=[wt])
                        else:
                            P.op("dve", "tensor_copy", out=dst, in_=s_.ap[:, 0:cn], r=[s_], w=[wt])
                    else:
                        sc = nrm_t.ap[:, scale_sel, kc:kc + 1]
                        if n % 2 == 0:
                            P.op("act", "activation", out=dst, in_=s_.ap[:, 0:cn], func=AF.Copy, scale=sc,
                                 r=[s_, nrm_t], w=[wt])
                        else:
                            P.op("dve", "tensor_scalar", out=dst, in0=s_.ap[:, 0:cn], scalar1=sc, scalar2=None, op0=ALU.mult,
                                 r=[s_, nrm_t], w=[wt])
                    n += 1
            return wt

        def rms_rstd(ssq, rstd, n, inv_n):
            P.op("act", "activation", out=rstd.ap[:, 0:n], in_=ssq.ap[:, 0:n], func=AF.Ln, bias=epsc.ap[:, 0:1], scale=inv_n,
                 r=[ssq, epsc], w=[rstd])
            P.op("act", "activation", out=rstd.ap[:, 0:n], in_=rstd.ap[:, 0:n], func=AF.Exp, scale=-0.5,
                 r=[rstd], w=[rstd])

        with ExitStack() as es:
            def tile(name, shape, dt):
                return Tile(_chk(nc, es.enter_context(nc.sbuf_tensor(name, list(shape), dt))).ap(), name)

            wi = tile("wi", [128, NKC, WIN_COLS], BF16)
            with ExitStack() as ses:
                st = [Tile(_chk(nc, ses.enter_context(nc.sbuf_tensor("wst%d" % i, [128, 1928], F32))).ap()) for i in range(2)]
                load_weight_bf16(wi, st, w_in, NKC, WIN_COLS, 0, 1928)
                P.barrier()
                P.emit()
            xt = [tile("xt%d" % i, [128, 4, D], F32) for i in range(2)]
            hb = tile("hb", [128, 4, D], BF16)
            junk = tile("junk", [128, D], BF16)
            ssq = tile("ssq", [128, 4], F32)
            rstd = tile("rstd", [128, 4], F32)
            hT = [tile("hT%d" % i, [128, NKC, TT], BF16) for i in range(1)]
            a_st = [tile("a_st%d" % i, [128, 12, TT], BF16) for i in range(1)]
            qk_st = [tile("qk_st%d" % i, [128, 6, TT], BF16) for i in range(1)]
            z_st = [tile("z_st%d" % i, [128, 4, 512], BF16) for i in range(1)]
            v_st = [tile("v_st%d" % i, [128, 4, 256], BF16) for i in range(1)]
            g_st = [tile("g_st%d" % i, [128, 4, 16], F32) for i in range(1)]
            cs_t = [tile("cs_t%d" % i, [128, 2, TT], F32) for i in range(1)]
            sq_t = [tile("sq_t%d" % i, [128, TT], BF16) for i in range(2)]
            rs_t = [tile("rs_t%d" % i, [128, TT], F32) for i in range(2)]
            t1_t = [tile("t1_t%d" % i, [128, TT], F32) for i in range(2)]
            t2_t = [tile("t2_t%d" % i, [128, TT], F32) for i in range(2)]
            gt_t = [tile("gt_t%d" % i, [128, 16], F32) for i in range(2)]

            it = 0
            for sn, T in seqs:
                sc = S[sn]
                ntile = T // TT
                for ti in range(ntile):
                    t0 = ti * TT
                    x_ = xt[it % 2]
                    hT_ = hT[0]
                    P.dma("sp", x_.ap, X[sn][t0:t0 + TT, :].rearrange("(s p) d -> p s d", p=128), w=[x_])
                    cs_ = cs_t[0]
                    P.dma("sp", cs_.ap[:, 0, :], cosT[:, t0:t0 + TT], w=[cs_])
                    P.dma("sp", cs_.ap[:, 1, :], sinT[:, t0:t0 + TT], w=[cs_])
                    for s in range(4):
                        P.op("act", "activation", out=junk.ap, in_=x_.ap[:, s, :], func=AF.Square,
                                                                     accum_out=ssq.ap[:, s:s + 1],
                             r=[x_], w=[junk, ssq])
                    rms_rstd(ssq, rstd, 4, 1.0 / D)
                    for s in range(4):
                        P.op("dve", "tensor_scalar", out=hb.ap[:, s, :], in0=x_.ap[:, s, :],
                                                                        scalar1=rstd.ap[:, s:s + 1], scalar2=None, op0=ALU.mult,
                             r=[x_, rstd], w=[hb])
                    for s in range(4):
                        pb = ps[s % 2]
                        pv = pb.ap.bitcast(BF16).rearrange("p (k t) -> p k t", k=NKC)
                        for kc in range(NKC):
                            P.op("pe", "transpose", out=pv[:, kc, :], in_=hb.ap[:, s, kc * 128:(kc + 1) * 128],
                                                                               identity=identb.ap,
                                 r=[hb, identb], w=[pb])
                        eng = "act" if s % 2 == 0 else "dve"
                        if eng == "act":
                            P.op("act", "copy", out=hT_.ap[:, :, s * 128:(s + 1) * 128], in_=pv, r=[pb], w=[hT_])
                        else:
                            P.op("dve", "tensor_copy", out=hT_.ap[:, :, s * 128:(s + 1) * 128], in_=pv, r=[pb], w=[hT_])

                    def fm_mm(pb, c):
                        for kc in range(NKC):
                            P.op("pe", "matmul", out=pb.ap, lhsT=wi.ap[:, kc, c * 128:(c + 1) * 128], rhs=hT_.ap[:, kc, :],
                                                                            start=(kc == 0), stop=(kc == NKC - 1),
                                 r=[wi, hT_], w=[pb])

                    a_ = a_st[0]
                    for c in range(12):
                        pb = ps[2 + (c % 2)]
                        fm_mm(pb, c)
                        if c % 2 == 0:
                            P.op("act", "copy", out=a_.ap[:, c, :], in_=pb.ap, r=[pb], w=[a_])
                        else:
                            P.op("dve", "tensor_copy", out=a_.ap[:, c, :], in_=pb.ap, r=[pb], w=[a_])
                    P.dma("sp", sc["A"].rearrange("(c p) t -> p c t", p=128)[:, :, t0:t0 + TT], a_.ap, r=[a_], w=[DB(("A", sn))])
                    qk_ = qk_st[0]
                    for j in range(6):
                        pm = ps[2 + (j % 2)]
                        pp = ps[4 + (j % 2)]
                        pc = ps[6 + (j % 2)]
                        sq_, rs_, t1_, t2_ = sq_t[j % 2], rs_t[j % 2], t1_t[j % 2], t2_t[j % 2]
                        fm_mm(pm, 12 + j)
                        fm_mm(pp, 18 + j)
                        wsel = 1 if j < 4 else 3
                        P.op("act", "activation", out=sq_.ap, in_=pm.ap, func=AF.Square, r=[pm], w=[sq_])
                        P.op("pe", "matmul", out=pc.ap, lhsT=onesb.ap, rhs=sq_.ap, start=True, stop=True,
                             r=[onesb, sq_], w=[pc])
                        P.op("act", "activation", out=rs_.ap, in_=pc.ap, func=AF.Ln, bias=epsc.ap[:, 0:1], scale=1.0 / 128,
                             r=[pc, epsc], w=[rs_])
                        P.op("act", "activation", out=rs_.ap, in_=rs_.ap, func=AF.Exp, scale=-0.5, r=[rs_], w=[rs_])
                        P.op("dve", "scalar_tensor_tensor", out=t1_.ap, in0=pm.ap, scalar=hn_t.ap[:, wsel:wsel + 1],
                                                                                               in1=cs_.ap[:, 0, :], op0=ALU.mult, op1=ALU.mult,
                             r=[pm, hn_t, cs_], w=[t1_])
                        P.op("dve", "scalar_tensor_tensor", out=t2_.ap, in0=pp.ap, scalar=hn_t.ap[:, wsel + 1:wsel + 2],
                                                                                               in1=cs_.ap[:, 1, :], op0=ALU.mult, op1=ALU.mult,
                             r=[pp, hn_t, cs_], w=[t2_])
                        P.op("pool", "tensor_tensor", out=t1_.ap, in0=t1_.ap, in1=t2_.ap, op=ALU.add, r=[t1_, t2_], w=[t1_])
                        P.op("pool", "tensor_tensor", out=qk_.ap[:, j, :], in0=t1_.ap, in1=rs_.ap, op=ALU.mult,
                             r=[t1_, rs_], w=[qk_])
                    P.dma("sp", sc["QK"].rearrange("(c p) t -> p c t", p=128)[:, :, t0:t0 + TT], qk_.ap, r=[qk_], w=[DB(("QK", sn))])
                    z_, v_, g_ = z_st[0], v_st[0], g_st[0]
                    for s in range(4):
                        pz = ps[s % 2]
                        pg = ps[6 + (s % 2)]
                        gt_ = gt_t[s % 2]
                        for kc in range(NKC):
                            P.op("pe", "matmul", out=pz.ap, lhsT=hT_.ap[:, kc, s * 128:(s + 1) * 128], rhs=wi.ap[:, kc, TM0:TM0 + 512],
                                                                            start=(kc == 0), stop=(kc == NKC - 1), r=[hT_, wi], w=[pz])
                        for kc in range(NKC):
                            P.op("pe", "matmul", out=pg.ap[:, 0:272], lhsT=hT_.ap[:, kc, s * 128:(s + 1) * 128],
                                                                            rhs=wi.ap[:, kc, TM0 + 512:TM0 + 784],
                                                                            start=(kc == 0), stop=(kc == NKC - 1), r=[hT_, wi], w=[pg])
                        P.op("act", "copy", out=z_.ap[:, s, :], in_=pz.ap, r=[pz], w=[z_])
                        P.op("dve", "tensor_copy", out=v_.ap[:, s, :], in_=pg.ap[:, 16:272], r=[pg], w=[v_])
                        P.op("act", "activation", out=gt_.ap[:, 0:8], in_=pg.ap[:, 0:8], func=AF.Exp, scale=-1.0, r=[pg], w=[gt_])
                        P.op("dve", "tensor_scalar", out=gt_.ap[:, 0:8], in0=gt_.ap[:, 0:8], scalar1=1.0, scalar2=None, op0=ALU.add,
                             r=[gt_], w=[gt_])
                        P.op("dve", "reciprocal", out=g_.ap[:, s, 0:8], in_=gt_.ap[:, 0:8], r=[gt_], w=[g_])
                        P.op("dve", "tensor_tensor", out=gt_.ap[:, 8:16], in0=pg.ap[:, 8:16], in1=gp_b.ap[:, 1, :], op=ALU.add,
                             r=[pg, gp_b], w=[gt_])
                        P.op("act", "activation", out=gt_.ap[:, 8:16], in_=gt_.ap[:, 8:16], func=AF.Exp, r=[gt_], w=[gt_])
                        P.op("dve", "tensor_scalar", out=gt_.ap[:, 8:16], in0=gt_.ap[:, 8:16], scalar1=1.0, scalar2=None, op0=ALU.add,
                             r=[gt_], w=[gt_])
                        P.op("act", "activation", out=gt_.ap[:, 8:16], in_=gt_.ap[:, 8:16], func=AF.Ln, r=[gt_], w=[gt_])
                        P.op("dve", "tensor_tensor", out=g_.ap[:, s, 8:16], in0=gt_.ap[:, 8:16], in1=negA.ap, op=ALU.mult,
                             r=[gt_, negA], w=[g_])
                    P.dma("sp", sc["Z"][t0:t0 + TT, :].rearrange("(s p) c -> p s c", p=128), z_.ap, r=[z_], w=[DB(("Z", sn))])
                    P.dma("sp", sc["V"][t0:t0 + TT, :].rearrange("(s p) c -> p s c", p=128), v_.ap, r=[v_], w=[DB(("V", sn))])
                    P.dma("sp", sc["G"][t0:t0 + TT, :].rearrange("(s p) c -> p s c", p=128), g_.ap, r=[g_], w=[DB(("G", sn))])
                    it += 1
            P.barrier()
            P.emit()

        if stop_after >= 2:
            phases_rest(locals())
    return nc


def host_inputs(inp, Tmax):
    f = np.float32
    w_in = np.asarray(inp["w_in"][0], f)
    perm = rope_perm()
    qB = w_in[:, 2064:2576]
    kB = w_in[:, 2576:2832]
    qBp = qB.reshape(D, 4, 128)[:, :, perm].reshape(D, 512)
    kBp = kB.reshape(D, 2, 128)[:, :, perm].reshape(D, 256)
    w_in_r = np.ascontiguousarray(np.concatenate(
        [w_in[:, 0:1536], qB, kB, qBp, kBp, w_in[:, 1536:2048], w_in[:, 2048:2064], w_in[:, 2832:3088]], axis=1))
    norms = np.stack([np.asarray(inp[k][0], f) for k in ("norm_mix_pre", "norm_mix_post", "norm_mlp_pre", "norm_mlp_post")])
    gparam = np.stack([np.concatenate([np.asarray(inp["A_log_f"][0], f), np.asarray(inp["A_log_b"][0], f)]),
                       np.concatenate([np.asarray(inp["dt_bias_f"][0], f), np.asarray(inp["dt_bias_b"][0], f)])])
    qn = np.asarray(inp["q_norm_w"][0], f)
    kn = np.asarray(inp["k_norm_w"][0], f)
    hnorm = np.stack([np.asarray(inp["gdn_norm_w"][0], f), qn, qn[perm], kn, kn[perm]])
    cosT, sinT = rope_tables(Tmax)
    m = gdn_masks()
    m["ident"] = np.eye(128, dtype=f)
    cst = np.stack([m[n] for n in CONST_NAMES]).astype(f)
    return dict(w_in_r=w_in_r, w_out=np.ascontiguousarray(np.asarray(inp["w_out"][0], f)),
                w_up=np.ascontiguousarray(np.asarray(inp["w_up"][0], f)),
                w_down=np.ascontiguousarray(np.asarray(inp["w_down"][0], f)),
                norms=np.ascontiguousarray(norms),
                nrm_r=np.ascontiguousarray(norms.reshape(4, NKC, 128).transpose(2, 0, 1)),
                hnorm_r=np.ascontiguousarray(hnorm.T),
                conv_w_r=np.ascontiguousarray(np.asarray(inp["conv_w"][0], f).reshape(5, 12, 128).transpose(2, 1, 0)),
                gparam=np.ascontiguousarray(gparam), hnorm=np.ascontiguousarray(hnorm),
                cosT=cosT, sinT=sinT, cst=cst)


_NC_CACHE = {}


def run(inp, Tp, Ts, ncores, debug=False, stop_after=9):
    key = (Tp, Ts, debug, stop_after)
    if key not in _NC_CACHE:
        _NC_CACHE[key] = build(Tp, Ts, debug, stop_after)
    nc = _NC_CACHE[key]
    shared = host_inputs(inp, max(Tp, Ts))
    xp = np.asarray(inp["x_prompt"], np.float32)
    xs = np.asarray(inp["x_sample"], np.float32)
    in_maps = []
    for c in range(ncores):
        m = dict(shared)
        m["x_p"] = np.ascontiguousarray(xp[c])
        m["x_s"] = np.ascontiguousarray(xs[c])
        in_maps.append(m)
    res = run_bass_kernel_spmd(nc, in_maps, core_ids=list(range(ncores)))
    return res.results


def kernel(**inputs):
    res = run(inputs, 8192, 2048, 8)
    yp = np.stack([np.asarray(r["y_p"], np.float32) for r in res])
    ys = np.stack([np.asarray(r["y_s"], np.float32) for r in res])
    return (yp, ys)


def phase_attn(L):
    nc, P, ps, S, seqs, hnorm, identb = L["nc"], L["P"], L["ps"], L["S"], L["seqs"], L["hnorm"], L["identb"]
    DB = L["DB"]
    with ExitStack() as es:
        def tile(name, shape, dt):
            return Tile(_chk(nc, es.enter_context(nc.sbuf_tensor(name, list(shape), dt))).ap(), name)
        Tmax = max(T for _, T in seqs)
        NBmax = Tmax // 128
        wqk_b = tile("wqk_b", [128, 2, 128], F32)
        P.dma("sp", wqk_b.ap[:, 0, :], hnorm[1:2, :].partition_broadcast(128), w=[wqk_b])
        P.dma("sp", wqk_b.ap[:, 1, :], hnorm[3:4, :].partition_broadcast(128), w=[wqk_b])
        mx = tile("mx", [128, 2], F32)
        P.op("dve", "tensor_reduce", out=mx.ap, in_=wqk_b.ap, axis=AX.X, op=ALU.max, apply_absolute_value=True, r=[wqk_b], w=[mx])
        negM = tile("negM", [128, 1], F32)
        P.op("dve", "tensor_tensor", out=negM.ap, in0=mx.ap[:, 0:1], in1=mx.ap[:, 1:2], op=ALU.mult, r=[mx], w=[negM])
        P.op("dve", "tensor_scalar", out=negM.ap, in0=negM.ap, scalar1=-(128.0 ** 0.5), scalar2=None, op0=ALU.mult, r=[negM], w=[negM])
        kT = tile("kT", [128, Tmax], BF16)
        va = tile("va", [128, NBmax, 129], BF16)
        P.op("pool", "memset", ap=va.ap, constant=1.0, w=[va])
        qT = [tile("qT%d" % i, [128, TT], BF16) for i in range(4)]
        pT = [tile("pT%d" % i, [128, TT], BF16) for i in range(6)]
        osb = [tile("osb%d" % i, [128, 4, 129], F32) for i in range(2)]
        rcp = [tile("rcp%d" % i, [128, 4], F32) for i in range(2)]
        onb = [tile("onb%d" % i, [128, 4, 128], BF16) for i in range(2)]
        mst = [tile("mst%d" % i, [128, TT], BF16) for i in range(2)]
        scale = 128.0 ** -0.5
        qi = 0
        npt = 0
        for sn, T in seqs:
            sc = S[sn]
            NB = T // 128
            for g in range(2):
                P.dma("sp", kT.ap[:, 0:T], sc["QK"][(4 + g) * 128:(5 + g) * 128, :], r=[DB(("QK", sn))], w=[kT])
                P.dma("sp", va.ap[:, 0:NB, 0:128], sc["V"][:, g * 128:(g + 1) * 128].rearrange("(b p) c -> p b c", p=128),
                      r=[DB(("V", sn))], w=[va])
                for h in (2 * g, 2 * g + 1):
                    for qp in range(T // (2 * TT)):
                        q0s = [(2 * qp + s) * TT for s in range(2)]
                        q_s = [qT[(2 * qi + s) % 4] for s in range(2)]
                        for s in range(2):
                            P.dma("sp", q_s[s].ap, sc["QK"][h * 128:(h + 1) * 128, q0s[s]:q0s[s] + TT], r=[DB(("QK", sn))], w=[q_s[s]])
                        sbank = [[ps[2 * s], ps[2 * s + 1]] for s in range(2)]
                        accb = [[ps[4 + 2 * s], ps[5 + 2 * s]] for s in range(2)]
                        accv = [[b.ap[:, 0:258].rearrange("p (a c) -> p a c", a=2) for b in accb[s]] for s in range(2)]

                        def qk(s, kb):
                            sb = sbank[s][kb % 2]
                            P.op("pe", "matmul", out=sb.ap, lhsT=kT.ap[:, kb * 128:(kb + 1) * 128], rhs=q_s[s].ap, start=True, stop=True,
                                 r=[kT, q_s[s]], w=[sb])

                        qk(0, 0)
                        qk(1, 0)
                        for kb in range(NB):
                            p_s = []
                            for s in range(2):
                                if kb + 1 < NB:
                                    qk(s, kb + 1)
                                sb = sbank[s][kb % 2]
                                p_ = pT[npt % 6]
                                npt += 1
                                P.op("act", "activation", out=p_.ap, in_=sb.ap, func=AF.Exp, bias=negM.ap[:, 0:1], scale=scale,
                                     r=[sb, negM], w=[p_])
                                p_s.append(p_)
                            for s in range(2):
                                for qs in range(4):
                                    P.op("pe", "matmul", out=accv[s][qs // 2][:, qs % 2, :], lhsT=p_s[s].ap[:, qs * 128:(qs + 1) * 128], rhs=va.ap[:, kb, :],
                                         start=(kb == 0 and qs % 2 == 0), stop=(kb == NB - 1), skip_group_check=True,
                                         r=[p_s[s], va], w=[accb[s][qs // 2]])
                        for s in range(2):
                            o_, r_, n_, m_ = osb[s], rcp[s], onb[s], mst[s]
                            P.op("dve", "tensor_copy", out=o_.ap[:, 0:2, :], in_=accv[s][0], r=[accb[s][0]], w=[o_])
                            P.op("dve", "tensor_copy", out=o_.ap[:, 2:4, :], in_=accv[s][1], r=[accb[s][1]], w=[o_])
                            P.op("dve", "reciprocal", out=r_.ap, in_=o_.ap[:, :, 128], r=[o_], w=[r_])
                            P.op("dve", "tensor_tensor", out=n_.ap, in0=o_.ap[:, :, 0:128], in1=r_.ap.unsqueeze(2).to_broadcast([128, 4, 128]),
                                 op=ALU.mult, r=[o_, r_], w=[n_])
                            tb = sbank[s][0]
                            tv = tb.ap.bitcast(BF16)[:, 0:512].rearrange("p (a c) -> p a c", a=4)
                            for qs in range(4):
                                P.op("pe", "transpose", out=tv[:, qs, :], in_=n_.ap[:, qs, :], identity=identb.ap, r=[n_, identb], w=[tb])
                            P.op("dve", "tensor_copy", out=m_.ap.rearrange("p (a c) -> p a c", a=4), in_=tv, r=[tb], w=[m_])
                            P.dma("sp", sc["MIX"][512 + h * 128:512 + (h + 1) * 128, q0s[s]:q0s[s] + TT], m_.ap, r=[m_], w=[DB(("MIX", sn))])
                        qi += 1
        P.barrier()
        P.emit()


def phase_mlp(L):
    nc, P, ps, S, seqs, identb = L["nc"], L["P"], L["ps"], L["S"], L["seqs"], L["identb"]
    DB, X, Y, rms_rstd, nrm = L["DB"], L["X"], L["Y"], L["rms_rstd"], L["nrm"]
    load_weight_bf16 = L["load_weight_bf16"]

    def make_norm_resid(mo, junk, ssq, rstd, nrm_b):
        def norm_resid(src_banks, resid_ap, out_ap, which, deps_resid, deps_out):
            P.op("act", "copy", out=mo.ap[:, 0:512], in_=src_banks[0].ap, r=[src_banks[0]], w=[mo])
            P.op("dve", "tensor_copy", out=mo.ap[:, 512:1024], in_=src_banks[1].ap, r=[src_banks[1]], w=[mo])
            P.op("act", "activation", out=junk.ap, in_=mo.ap, func=AF.Square, accum_out=ssq.ap[:, 0:1], r=[mo], w=[junk, ssq])
            rms_rstd(ssq, rstd, 1, 1.0 / D)
            P.op("dve", "scalar_tensor_tensor", out=mo.ap, in0=mo.ap, scalar=rstd.ap[:, 0:1], in1=nrm_b.ap[:, which, :],
                 op0=ALU.mult, op1=ALU.mult, r=[mo, rstd, nrm_b], w=[mo])
            P.op("pool", "tensor_tensor", out=out_ap, in0=mo.ap, in1=resid_ap, op=ALU.add, r=[mo] + deps_resid, w=deps_out)
        return norm_resid

    with ExitStack() as es:
        def tile(name, shape, dt):
            return Tile(_chk(nc, es.enter_context(nc.sbuf_tensor(name, list(shape), dt))).ap(), name)
        wo = tile("wo", [128, 8, D], BF16)
        with ExitStack() as ses:
            st = [Tile(_chk(nc, ses.enter_context(nc.sbuf_tensor("wst4a_%d" % i, [128, 1024], F32))).ap()) for i in range(2)]
            load_weight_bf16(wo, st, L["w_out"], 8, D, None, 1024)
            P.barrier()
            P.emit()
        nrm_b = tile("nrm_b", [128, 2, D], F32)
        P.dma("sp", nrm_b.ap[:, 0, :], nrm[1:2, :].partition_broadcast(128), w=[nrm_b])
        P.dma("sp", nrm_b.ap[:, 1, :], nrm[3:4, :].partition_broadcast(128), w=[nrm_b])
        xt = tile("x4", [128, 4, D], F32)
        mT = [tile("mT%d" % i, [128, 8, TT], BF16) for i in range(2)]
        mo = tile("mo", [128, D], F32)
        junk = tile("junk4", [128, D], BF16)
        ssq = tile("ssq4", [128, 1], F32)
        rstd = tile("rstd4", [128, 1], F32)
        norm_resid = make_norm_resid(mo, junk, ssq, rstd, nrm_b)
        it = 0
        for sn, T in seqs:
            sc = S[sn]
            for ti in range(T // TT):
                t0 = ti * TT
                m_ = mT[it % 2]
                P.dma("sp", xt.ap, X[sn][t0:t0 + TT, :].rearrange("(s p) d -> p s d", p=128), w=[xt])
                P.dma("sp", m_.ap, sc["MIX"].rearrange("(k p) t -> p k t", p=128)[:, :, t0:t0 + TT], r=[DB(("MIX", sn))], w=[m_])
                for s in range(4):
                    banks = [ps[2 * (s % 2)], ps[2 * (s % 2) + 1]]
                    for cg in range(2):
                        for kc in range(8):
                            P.op("pe", "matmul", out=banks[cg].ap, lhsT=m_.ap[:, kc, s * 128:(s + 1) * 128], rhs=wo.ap[:, kc, cg * 512:(cg + 1) * 512],
                                 start=(kc == 0), stop=(kc == 7), r=[m_, wo], w=[banks[cg]])
                    norm_resid(banks, xt.ap[:, s, :], xt.ap[:, s, :], 0, [xt], [xt])
                P.dma("sp", sc["X1"][t0:t0 + TT, :].rearrange("(s p) d -> p s d", p=128), xt.ap, r=[xt], w=[DB(("X1", sn))])
                it += 1
        P.barrier()
        P.emit()
    T4 = 256
    with ExitStack() as es:
        def tile(name, shape, dt):
            return Tile(_chk(nc, es.enter_context(nc.sbuf_tensor(name, list(shape), dt))).ap(), name)
        wu = tile("wu", [128, 8, DFF], BF16)
        wd = tile("wd", [128, 32, D], BF16)
        with ExitStack() as ses:
            st = [Tile(_chk(nc, ses.enter_context(nc.sbuf_tensor("wst4_%d" % i, [128, 2048], F32))).ap()) for i in range(2)]
            load_weight_bf16(wu, st, L["w_up"], 8, DFF, 2, 2048)
            load_weight_bf16(wd, st, L["w_down"], 32, D, None, 1024)
            P.barrier()
            P.emit()
        nrm_b = tile("nrm_b2", [128, D], F32)
        P.dma("sp", nrm_b.ap, nrm[3:4, :].partition_broadcast(128), w=[nrm_b])
        xt = tile("x4b", [128, D], F32)
        mo = tile("mo2", [128, D], F32)
        ssq = tile("ssq4b", [128, 1], F32)
        rstd = tile("rstd4b", [128, 1], F32)
        ssq2 = tile("ssq4c", [128, 1], F32)
        rstd2 = tile("rstd4c", [128, 1], F32)
        h2 = tile("h2", [128, D], BF16)
        h2T = [tile("h2T%d" % i, [128, 8, T4], BF16) for i in range(2)]
        fT = tile("fT", [128, 32, T4], BF16)
        rl = [tile("rl%d" % i, [128, T4], F32) for i in range(2)]
        jk = tile("jk4", [128, D], BF16)
        tiles = [(sn, T, ti) for sn, T in seqs for ti in range(T // T4)]

        def prologue(i):
            sn, T, ti = tiles[i]
            sc = S[sn]
            t0 = ti * T4
            hT_ = h2T[i % 2]
            for s in range(2):
                P.dma("sp", xt.ap, sc["X1"][t0 + s * 128:t0 + (s + 1) * 128, :], r=[DB(("X1", sn))], w=[xt])
                P.op("act", "activation", out=h2.ap, in_=xt.ap, func=AF.Square, accum_out=ssq.ap[:, 0:1], r=[xt], w=[h2, ssq])
                rms_rstd(ssq, rstd, 1, 1.0 / D)
                P.op("dve", "tensor_scalar", out=h2.ap, in0=xt.ap, scalar1=rstd.ap[:, 0:1], scalar2=None, op0=ALU.mult,
                     r=[xt, rstd], w=[h2])
                pb = ps[2 + s]
                pv = pb.ap.bitcast(BF16).rearrange("p (k t) -> p k t", k=8)
                for kc in range(8):
                    P.op("pe", "transpose", out=pv[:, kc, :], in_=h2.ap[:, kc * 128:(kc + 1) * 128], identity=identb.ap,
                         r=[h2, identb], w=[pb])
                P.op("act", "copy", out=hT_.ap[:, :, s * 128:(s + 1) * 128], in_=pv, r=[pb], w=[hT_])

        prologue(0)
        for i, (sn, T, ti) in enumerate(tiles):
            sc = S[sn]
            t0 = ti * T4
            hT_ = h2T[i % 2]
            for fc in range(32):
                pb = ps[4 + (fc % 4)]
                for kc in range(8):
                    P.op("pe", "matmul", out=pb.ap[:, 0:T4], lhsT=wu.ap[:, kc, fc * 128:(fc + 1) * 128], rhs=hT_.ap[:, kc, :],
                         start=(kc == 0), stop=(kc == 7), r=[wu, hT_], w=[pb])
                rl_ = rl[fc % 2]
                P.op("act", "activation", out=rl_.ap, in_=pb.ap[:, 0:T4], func=AF.Relu, r=[pb], w=[rl_])
                P.op("dve" if fc % 2 == 0 else "pool", "tensor_tensor", out=fT.ap[:, fc, :], in0=rl_.ap, in1=rl_.ap, op=ALU.mult, r=[rl_], w=[fT])
            if i + 1 < len(tiles):
                prologue(i + 1)
            for s in range(2):
                banks = [ps[0], ps[1]]
                for cg in range(2):
                    for fc in range(32):
                        P.op("pe", "matmul", out=banks[cg].ap, lhsT=fT.ap[:, fc, s * 128:(s + 1) * 128], rhs=wd.ap[:, fc, cg * 512:(cg + 1) * 512],
                             start=(fc == 0), stop=(fc == 31), r=[fT, wd], w=[banks[cg]])
                P.op("act", "copy", out=mo.ap[:, 0:512], in_=banks[0].ap, r=[banks[0]], w=[mo])
                P.op("dve", "tensor_copy", out=mo.ap[:, 512:1024], in_=banks[1].ap, r=[banks[1]], w=[mo])
                P.op("act", "activation", out=jk.ap, in_=mo.ap, func=AF.Square, accum_out=ssq2.ap[:, 0:1], r=[mo], w=[jk, ssq2])
                rms_rstd(ssq2, rstd2, 1, 1.0 / D)
                P.op("dve", "scalar_tensor_tensor", out=mo.ap, in0=mo.ap, scalar=rstd2.ap[:, 0:1], in1=nrm_b.ap,
                     op0=ALU.mult, op1=ALU.mult, r=[mo, rstd2, nrm_b], w=[mo])
                P.dma("sp", xt.ap, sc["X1"][t0 + s * 128:t0 + (s + 1) * 128, :], r=[DB(("X1", sn))], w=[xt])
                P.op("pool", "tensor_tensor", out=xt.ap, in0=mo.ap, in1=xt.ap, op=ALU.add, r=[mo, xt], w=[xt])
                P.dma("sp", Y[sn][t0 + s * 128:t0 + (s + 1) * 128, :], xt.ap, r=[xt], w=[DB(("Y", sn))])
        P.barrier()
        P.emit()


def phases_rest(L):
    sa = L["stop_after"]
    if sa >= 2:
        phase_gdn(L)
    if sa >= 3:
        phase_attn(L)
    if sa >= 4:
        phase_mlp(L)


def phase_gdn(L):
    nc, P, ps, S, seqs, identb, onesb = L["nc"], L["P"], L["ps"], L["S"], L["seqs"], L["identb"], L["onesb"]
    DB, epsc, rms_rstd = L["DB"], L["epsc"], L["rms_rstd"]
    with ExitStack() as es:
        def tile(name, shape, dt):
            return Tile(_chk(nc, es.enter_context(nc.sbuf_tensor(name, list(shape), dt))).ap(), name)
        cw_t = tile("cw_t", [128, 12, 5], F32)
        P.dma("sp", cw_t.ap, L["conv_w_r"], w=[cw_t])
        dg = tile("dg", [128, 12, 5, 128], BF16)
        for c in range(12):
            for kk in range(5):
                P.op("dve" if (c * 5 + kk) % 2 == 0 else "pool", "tensor_scalar", out=dg.ap[:, c, kk, :], in0=identb.ap,
                     scalar1=cw_t.ap[:, c, kk:kk + 1], scalar2=None, op0=ALU.mult, r=[identb, cw_t], w=[dg])
        raw = [tile("raw%d" % i, [128, 12, TT + 4], BF16) for i in range(2)]
        sl = [tile("sl%d" % i, [128, TT], F32) for i in range(2)]
        sq = [tile("sq%d" % i, [128, TT], BF16) for i in range(2)]
        rs = [tile("rs%d" % i, [128, TT], F32) for i in range(2)]
        fm = [tile("fm%d" % i, [128, TT], BF16) for i in range(3)]
        tk = [tile("tk%d" % i, [128, 4, 128], BF16) for i in range(2)]
        it = 0
        n = 0
        for sn, T in seqs:
            sc = S[sn]
            for ti in range(T // TT):
                t0 = ti * TT
                r_ = raw[it % 2]
                lo, hi = max(t0 - 2, 0), min(t0 + TT + 2, T)
                if t0 == 0:
                    P.op("pool", "memset", ap=r_.ap[:, :, 0:2], constant=0.0, w=[r_])
                if t0 + TT == T:
                    P.op("pool", "memset", ap=r_.ap[:, :, TT + 2:TT + 4], constant=0.0, w=[r_])
                P.dma("sp", r_.ap[:, :, lo - (t0 - 2):hi - (t0 - 2)], sc["A"].rearrange("(c p) t -> p c t", p=128)[:, :, lo:hi],
                      r=[DB(("A", sn))], w=[r_])
                for c in range(12):
                    kind, h = c // 4, c % 4
                    s_ = sl[n % 2]
                    pcv = ps[4 + (n % 4)]
                    for kk in range(5):
                        P.op("pe", "matmul", out=pcv.ap, lhsT=dg.ap[:, c, kk, :], rhs=r_.ap[:, c, kk:kk + TT], start=(kk == 0), stop=(kk == 4),
                             r=[dg, r_], w=[pcv])
                    P.op("act", "activation", out=s_.ap, in_=pcv.ap, func=AF.Silu, r=[pcv], w=[s_])
                    f_ = fm[n % 3]
                    if kind == 2:
                        P.op("pool", "tensor_copy", out=f_.ap, in_=s_.ap, r=[s_], w=[f_])
                    else:
                        q_, rs_ = sq[n % 2], rs[n % 2]
                        pb = ps[n % 2]
                        P.op("act", "activation", out=q_.ap, in_=s_.ap, func=AF.Square, r=[s_], w=[q_])
                        P.op("pe", "matmul", out=pb.ap, lhsT=onesb.ap, rhs=q_.ap, start=True, stop=True, r=[onesb, q_], w=[pb])
                        P.op("act", "activation", out=rs_.ap, in_=pb.ap, func=AF.Ln, bias=epsc.ap[:, 0:1], scale=1.0, r=[pb, epsc], w=[rs_])
                        P.op("act", "activation", out=rs_.ap, in_=rs_.ap, func=AF.Exp, scale=-0.5, r=[rs_], w=[rs_])
                        P.op("dve", "scalar_tensor_tensor", out=f_.ap, in0=s_.ap, scalar=(128.0 ** -0.5 if kind == 0 else 1.0), in1=rs_.ap,
                             op0=ALU.mult, op1=ALU.mult, r=[s_, rs_], w=[f_])
                        dst = sc["GQ"] if kind == 0 else sc["GK"]
                        P.dma("sp", dst[h * 128:(h + 1) * 128, t0:t0 + TT], f_.ap, r=[f_], w=[DB(("GQK", sn))])
                    if kind >= 1:
                        tb = ps[2 + (n % 2)]
                        tv = tb.ap.bitcast(BF16)[:, 0:512].rearrange("p (a c) -> p a c", a=4)
                        for s in range(4):
                            P.op("pe", "transpose", out=tv[:, s, :], in_=f_.ap[:, s * 128:(s + 1) * 128], identity=identb.ap,
                                 r=[f_, identb], w=[tb])
                        t_ = tk[n % 2]
                        P.op("act", "copy", out=t_.ap, in_=tv, r=[tb], w=[t_])
                        dst = sc["GKt"] if kind == 1 else sc["GVt"]
                        P.dma("sp", dst[t0:t0 + TT, h * 128:(h + 1) * 128].rearrange("(s p) d -> p s d", p=128), t_.ap,
                              r=[t_], w=[DB(("GKV", sn))])
                    n += 1
                it += 1
        P.barrier()
        P.emit()
    if L["stop_after"] == 2 and L.get("gdn_pre_only"):
        return
    with ExitStack() as es:
        def tile(name, shape, dt):
            return Tile(_chk(nc, es.enter_context(nc.sbuf_tensor(name, list(shape), dt))).ap(), name)

        cf = tile("cf", [128, len(CONST_NAMES), 128], F32)
        P.dma("sp", cf.ap, L["cst"].rearrange("n p j -> p n j"), w=[cf])
        C = {n: cf.ap[:, i, :] for i, n in enumerate(CONST_NAMES)}
        onesf = tile("onesf", [128, 128], F32)
        P.op("pool", "memset", ap=onesf.ap, constant=1.0, w=[onesf])
        gnw_b = tile("gnw_b", [128, 128], F32)
        P.dma("sp", gnw_b.ap, L["hnorm"][0:1, :].partition_broadcast(128), w=[gnw_b])

        def b3(ap2):
            return ap2.unsqueeze(1).to_broadcast([128, 4, 128])

        def bj(ap2):
            return ap2.unsqueeze(2).to_broadcast([128, 4, 128])

        def v4(t):
            return t.ap.rearrange("p (h j) -> p h j", h=4)

        NU = 2
        Rd = []
        R = []
        for d in range(2):
            rd = {"S4": tile("S4_%d" % d, [128, 512], F32), "S4b": tile("S4b_%d" % d, [128, 512], BF16),
                  }
            Rd.append(rd)
            Ru = []
            for u in range(NU):
                r = {}
                sfx = "%d_%d" % (d, u)
                for nm in ("KT", "QT", "Kt", "Vt", "A0", "A1", "B0", "B1", "Aqk", "AqkT", "PTb", "Vb", "kbg", "ke", "wT", "vn"):
                    r[nm] = tile(nm + sfx, [128, 512], BF16)
                for nm in ("Gm", "Ep", "D4", "Dsb", "PT", "u4"):
                    r[nm] = tile(nm + sfx, [128, 512], F32)
                r["Oq"], r["O4"], r["sz"], r["o2"] = r["Gm"], r["Ep"], r["D4"], r["Dsb"]
                r["zb"], r["onb"], r["mxs"] = r["A0"], r["A1"], r["B0"]
                r["G"] = tile("G" + sfx, [128, 16], F32)
                r["sm"] = tile("sm" + sfx, [128, 16], F32)
                r["ex"] = tile("ex" + sfx, [128, 16], F32)
                r["bgc"] = tile("bgc" + sfx, [128, 4], F32)
                r["ssq"] = tile("gssq" + sfx, [128, 4], F32)
                r["rstd"] = tile("grstd" + sfx, [128, 4], F32)
                r["banks"] = [ps[2 * (NU * d + u)], ps[2 * (NU * d + u) + 1]]
                r["nb"] = 0
                Ru.append(r)
            R.append(Ru)

        def nbank(r):
            b = r["banks"][r["nb"] % 2]
            r["nb"] += 1
            return b

        def unit(sn, T, d, b, step, NB):
            sc = S[sn]
            r = R[d][step % NU]
            rd = Rd[d]
            t0 = b * 128
            KT, QT, Kt, Vt, G = r["KT"], r["QT"], r["Kt"], r["Vt"], r["G"]
            P.dma("act", v4(KT), sc["GK"].rearrange("(h p) t -> p h t", p=128)[:, :, t0:t0 + 128], r=[DB(("GQK", sn))], w=[KT])
            P.dma("act", v4(QT), sc["GQ"].rearrange("(h p) t -> p h t", p=128)[:, :, t0:t0 + 128], r=[DB(("GQK", sn))], w=[QT])
            P.dma("act", Kt.ap, sc["GKt"][t0:t0 + 128, :], r=[DB(("GKV", sn))], w=[Kt])
            P.dma("act", Vt.ap, sc["GVt"][t0:t0 + 128, :], r=[DB(("GKV", sn))], w=[Vt])
            P.dma("act", G.ap, sc["G"][t0:t0 + 128, :], r=[DB(("G", sn))], w=[G])
            yield
            beta = G.ap[:, 4 * d:4 * d + 4]
            gg = G.ap[:, 8 + 4 * d:12 + 4 * d]
            cum = C["cum_f"] if d == 0 else C["cum_b"]
            pos = C["pos_f"] if d == 0 else C["pos_b"]
            strict = C["strict_f"] if d == 0 else C["strict_b"]
            sm, ex, bgc = r["sm"], r["ex"], r["bgc"]
            Gm, Ep, D4, Dsb = r["Gm"], r["Ep"], r["D4"], r["Dsb"]
            pb = nbank(r)
            P.op("pe", "matmul", out=pb.ap[:, 0:4], lhsT=cum, rhs=gg, start=True, stop=True, r=[cf, G], w=[pb])
            P.op("pe", "matmul", out=pb.ap[:, 4:8], lhsT=C["same"], rhs=gg, start=True, stop=True, r=[cf, G], w=[pb])
            P.op("pe", "matmul", out=pb.ap[:, 8:12], lhsT=C["c0"], rhs=gg, start=True, stop=True, r=[cf, G], w=[pb])
            P.op("pe", "matmul", out=pb.ap[:, 12:16], lhsT=C["c1"], rhs=gg, start=True, stop=True, r=[cf, G], w=[pb])
            P.op("pool", "tensor_tensor", out=v4(Gm), in0=b3(cum), in1=bj(gg), op=ALU.mult, r=[cf, G], w=[Gm])
            yield
            P.op("dve", "tensor_copy", out=sm.ap, in_=pb.ap[:, 0:16], r=[pb], w=[sm])
            P.op("dve", "tensor_tensor", out=sm.ap[:, 4:8], in0=sm.ap[:, 4:8], in1=sm.ap[:, 0:4], op=ALU.subtract, r=[sm], w=[sm])
            pe_ = nbank(r)
            P.op("pe", "matmul", out=pe_.ap, lhsT=onesf.ap, rhs=Gm.ap, start=True, stop=True, r=[onesf, Gm], w=[pe_])
            yield
            P.op("act", "activation", out=ex.ap, in_=sm.ap, func=AF.Exp, r=[sm], w=[ex])
            P.op("dve", "tensor_tensor", out=v4(Ep), in0=pe_.ap.rearrange("p (h j) -> p h j", h=4), in1=b3(pos), op=ALU.add,
                 r=[pe_, cf], w=[Ep])
            yield
            P.op("dve", "tensor_tensor", out=bgc.ap, in0=ex.ap[:, 0:4], in1=beta, op=ALU.mult, r=[ex, G], w=[bgc])
            for h in range(4):
                P.op("act", "activation", out=v4(D4)[:, h, :], in_=v4(Ep)[:, h, :], func=AF.Exp, bias=sm.ap[:, h:h + 1], scale=-1.0,
                     r=[Ep, sm], w=[D4])
            pk, pq = nbank(r), nbank(r)
            for h in range(4):
                P.op("pe", "matmul", out=pk.ap[:, h * 128:(h + 1) * 128], lhsT=v4(KT)[:, h, :], rhs=v4(KT)[:, h, :], start=True, stop=True,
                     r=[KT], w=[pk])
            for h in range(4):
                P.op("pe", "matmul", out=pq.ap[:, h * 128:(h + 1) * 128], lhsT=v4(QT)[:, h, :], rhs=v4(KT)[:, h, :], start=True, stop=True,
                     r=[QT, KT], w=[pq])
            yield
            Vb, kbg, ke, u4, wT = r["Vb"], r["kbg"], r["ke"], r["u4"], r["wT"]
            P.op("pool", "tensor_tensor", out=v4(Dsb), in0=v4(D4), in1=b3(strict), op=ALU.mult, r=[D4, cf], w=[Dsb])
            P.op("pool", "tensor_tensor", out=v4(Dsb), in0=v4(Dsb), in1=bj(beta), op=ALU.mult, r=[Dsb, G], w=[Dsb])
            A, B = [r["A0"], r["A1"]], [r["B0"], r["B1"]]
            Aqk, AqkT, PT, PTb = r["Aqk"], r["AqkT"], r["PT"], r["PTb"]
            P.op("dve", "tensor_tensor", out=Aqk.ap, in0=pq.ap, in1=D4.ap, op=ALU.mult, r=[pq, D4], w=[Aqk])
            yield
            P.op("dve", "tensor_tensor", out=A[0].ap, in0=pk.ap, in1=Dsb.ap, op=ALU.mult, r=[pk, Dsb], w=[A[0]])
            P.op("pool", "tensor_tensor", out=v4(Vb), in0=v4(Vt), in1=bj(beta), op=ALU.mult, r=[Vt, G], w=[Vb])
            P.op("pool", "tensor_tensor", out=v4(kbg), in0=v4(Kt), in1=bj(bgc.ap), op=ALU.mult, r=[Kt, bgc], w=[kbg])
            P.op("pool", "tensor_tensor", out=v4(ke), in0=v4(Kt), in1=bj(ex.ap[:, 4:8]), op=ALU.mult, r=[Kt, ex], w=[ke])
            yield
            tb = nbank(r)
            tvA = tb.ap.bitcast(BF16)[:, 0:512].rearrange("p (a c) -> p a c", a=4)
            tvQ = tb.ap.bitcast(BF16)[:, 512:1024].rearrange("p (a c) -> p a c", a=4)
            for h in range(4):
                P.op("pe", "transpose", out=tvA[:, h, :], in_=v4(A[0])[:, h, :], identity=identb.ap, r=[A[0], identb], w=[tb])
            for h in range(4):
                P.op("pe", "transpose", out=tvQ[:, h, :], in_=v4(Aqk)[:, h, :], identity=identb.ap, r=[Aqk, identb], w=[tb])
            yield
            P.op("act", "copy", out=v4(B[0]), in_=tvA, r=[tb], w=[B[0]])
            P.op("act", "copy", out=v4(AqkT), in_=tvQ, r=[tb], w=[AqkT])
            yield
            P.op("pool", "tensor_tensor", out=v4(PT), in0=b3(C["ident"]), in1=v4(B[0]), op=ALU.subtract, r=[cf, B[0]], w=[PT])
            P.op("act", "copy", out=PTb.ap, in_=PT.ap, r=[PT], w=[PTb])
            for k in range(5):
                Ak, Bk, An, Bn = A[k % 2], B[k % 2], A[(k + 1) % 2], B[(k + 1) % 2]
                pa = nbank(r)
                for h in range(4):
                    P.op("pe", "matmul", out=pa.ap[:, h * 128:(h + 1) * 128], lhsT=v4(Bk)[:, h, :], rhs=v4(Ak)[:, h, :], start=True, stop=True,
                         r=[Ak, Bk], w=[pa])
                if k < 4:
                    pbb = nbank(r)
                    for h in range(4):
                        P.op("pe", "matmul", out=pbb.ap[:, h * 128:(h + 1) * 128], lhsT=v4(Ak)[:, h, :], rhs=v4(Bk)[:, h, :], start=True, stop=True,
                             r=[Ak, Bk], w=[pbb])
                yield
                P.op("act", "copy", out=An.ap, in_=pa.ap, r=[pa], w=[An])
                if k < 4:
                    P.op("dve", "tensor_copy", out=Bn.ap, in_=pbb.ap, r=[pbb], w=[Bn])
                yield
                pd = nbank(r)
                for h in range(4):
                    P.op("pe", "matmul", out=pd.ap[:, h * 128:(h + 1) * 128], lhsT=v4(An)[:, h, :], rhs=v4(PTb)[:, h, :], start=True, stop=True,
                         r=[An, PTb], w=[pd])
                yield
                P.op("dve", "tensor_tensor", out=PT.ap, in0=pd.ap, in1=PT.ap, op=ALU.add, r=[pd, PT], w=[PT])
                P.op("act", "copy", out=PTb.ap, in_=PT.ap, r=[PT], w=[PTb])
                yield
            pu, pw = nbank(r), nbank(r)
            for h in range(4):
                P.op("pe", "matmul", out=pu.ap[:, h * 128:(h + 1) * 128], lhsT=v4(PTb)[:, h, :], rhs=v4(Vb)[:, h, :], start=True, stop=True,
                     r=[PTb, Vb], w=[pu])
            for h in range(4):
                P.op("pe", "matmul", out=pw.ap[:, h * 128:(h + 1) * 128], lhsT=v4(kbg)[:, h, :], rhs=v4(PTb)[:, h, :], start=True, stop=True,
                     r=[PTb, kbg], w=[pw])
            yield
            P.op("act", "copy", out=u4.ap, in_=pu.ap, r=[pu], w=[u4])
            P.op("dve", "tensor_copy", out=wT.ap, in_=pw.ap, r=[pw], w=[wT])
            yield
            S4, S4b, vn, Oq, O4 = rd["S4"], rd["S4b"], r["vn"], r["Oq"], r["O4"]
            for c in ((0, 1) if d == 0 else (1, 0)):
                pc = slice(c * 64, (c + 1) * 64)
                p1, po1 = nbank(r), nbank(r)
                for h in range(4):
                    P.op("pe", "matmul", out=p1.ap[pc, h * 128:(h + 1) * 128], lhsT=v4(wT)[:, h, pc], rhs=v4(S4b)[:, h, :], start=True, stop=True,
                         r=[wT, S4b], w=[p1])
                for h in range(4):
                    P.op("pe", "matmul", out=po1.ap[pc, h * 128:(h + 1) * 128], lhsT=v4(QT)[:, h, pc], rhs=v4(S4b)[:, h, :], start=True, stop=True,
                         r=[QT, S4b], w=[po1])
                yield
                P.op("dve", "tensor_tensor", out=vn.ap[pc, :], in0=u4.ap[pc, :], in1=p1.ap[pc, :], op=ALU.subtract, r=[u4, p1], w=[vn])
                P.op("dve", "tensor_tensor", out=v4(Oq)[pc], in0=po1.ap[pc, :].rearrange("p (h j) -> p h j", h=4),
                     in1=ex.ap[pc, 0:4].unsqueeze(2).to_broadcast([64, 4, 128]), op=ALU.mult, r=[po1, ex], w=[Oq])
                yield
                po2, pds = nbank(r), nbank(r)
                for h in range(4):
                    P.op("pe", "matmul", out=pds.ap[:, h * 128:(h + 1) * 128], lhsT=v4(ke)[pc, h, :], rhs=v4(vn)[pc, h, :], start=True, stop=True,
                         r=[ke, vn], w=[pds])
                for h in range(4):
                    P.op("pe", "matmul", out=po2.ap[pc, h * 128:(h + 1) * 128], lhsT=v4(AqkT)[pc, h, pc], rhs=v4(vn)[pc, h, :], start=True, stop=True,
                         r=[AqkT, vn], w=[po2])
                P.op("pool", "tensor_tensor", out=v4(S4), in0=v4(S4), in1=bj(ex.ap[:, 8 + 4 * c:12 + 4 * c]), op=ALU.mult, r=[S4, ex], w=[S4])
                yield
                P.op("dve", "tensor_tensor", out=S4.ap, in0=pds.ap, in1=S4.ap, op=ALU.add, r=[pds, S4], w=[S4])
                P.op("act", "copy", out=S4b.ap, in_=S4.ap, r=[S4], w=[S4b])
                P.op("dve", "tensor_tensor", out=O4.ap[pc, :], in0=po2.ap[pc, :], in1=Oq.ap[pc, :], op=ALU.add, r=[po2, Oq], w=[O4])
                yield
            mine, other = ("OF", "OB") if d == 0 else ("OB", "OF")
            if step < NB // 2:
                P.dma("sp", sc[mine][t0:t0 + 128, :], O4.ap, r=[O4], w=[DB((mine, sn, b))])
            else:
                o2, sz, zb, onb, mxs = r["o2"], r["sz"], r["zb"], r["onb"], r["mxs"]
                ssq, rstd = r["ssq"], r["rstd"]
                P.dma("sp", o2.ap, sc[other][t0:t0 + 128, :], r=[DB((other, sn, b))], w=[o2])
                P.dma("sp", zb.ap, sc["Z"][t0:t0 + 128, :], r=[DB(("Z", sn))], w=[zb])
                yield
                P.op("pool", "tensor_tensor", out=o2.ap, in0=o2.ap, in1=O4.ap, op=ALU.add, r=[o2, O4], w=[o2])
                P.op("pool", "tensor_tensor", out=sz.ap, in0=o2.ap, in1=o2.ap, op=ALU.mult, r=[o2], w=[sz])
                yield
                P.op("dve", "tensor_reduce", out=ssq.ap, in_=v4(sz), axis=AX.X, op=ALU.add, r=[sz], w=[ssq])
                rms_rstd(ssq, rstd, 4, 1.0 / 128)
                P.op("act", "activation", out=sz.ap, in_=zb.ap, func=AF.Silu, r=[zb], w=[sz])
                yield
                P.op("dve", "tensor_tensor", out=v4(o2), in0=v4(o2), in1=bj(rstd.ap), op=ALU.mult, r=[o2, rstd], w=[o2])
                P.op("pool", "tensor_tensor", out=v4(o2), in0=v4(o2), in1=b3(gnw_b.ap), op=ALU.mult, r=[o2, gnw_b], w=[o2])
                yield
                P.op("dve", "tensor_tensor", out=onb.ap, in0=o2.ap, in1=sz.ap, op=ALU.mult, r=[o2, sz], w=[onb])
                yield
                tb2 = nbank(r)
                tv2 = tb2.ap.bitcast(BF16)[:, 0:512].rearrange("p (a c) -> p a c", a=4)
                for h in range(4):
                    P.op("pe", "transpose", out=tv2[:, h, :], in_=v4(onb)[:, h, :], identity=identb.ap, r=[onb, identb], w=[tb2])
                yield
                P.op("act", "copy", out=v4(mxs), in_=tv2, r=[tb2], w=[mxs])
                P.dma("sp", sc["MIX"][0:512, :].rearrange("(h p) t -> p h t", p=128)[:, :, t0:t0 + 128], v4(mxs), r=[mxs], w=[DB(("MIX", sn))])

        for sn, T in seqs:
            NB = T // 128
            for d in range(2):
                P.op("pool", "memset", ap=Rd[d]["S4"].ap, constant=0.0, w=[Rd[d]["S4"]])
                P.op("pool", "memset", ap=Rd[d]["S4b"].ap, constant=0.0, w=[Rd[d]["S4b"]])
            active = []
            pending = []
            for st_ in range(NB):
                pending.append((0, st_, st_))
                pending.append((1, NB - 1 - st_, st_))
            STAG = 10
            while pending or active:
                if pending and len(active) < 2 * NU and (not active or active[-1][1] >= STAG):
                    d_, b_, st_ = pending.pop(0)
                    active.append([unit(sn, T, d_, b_, st_, NB), 0])
                for a_ in list(active):
                    try:
                        next(a_[0])
                        a_[1] += 1
                    except StopIteration:
                        active.remove(a_)
        P.barrier()
        P.emit()
```

```python
import numpy as np
import concourse.bass as bass
import concourse.mybir as mybir
from concourse.bass_utils import run_bass_kernel_spmd

F32 = mybir.dt.float32
BF16 = mybir.dt.bfloat16
AF = mybir.ActivationFunctionType
ALU = mybir.AluOpType
AX = mybir.AxisListType

ENGS = ("pe", "act", "dve", "pool", "sp")
NS = 8


class Buf:
    __slots__ = ("w", "r", "rd", "name", "excl")

    def __init__(self, name=""):
        self.excl = False
        self.w = None
        self.r = {}
        self.rd = []
        self.name = name


class Prog:
    def __init__(self, nc):
        self.nc = nc
        self.sem = {e: nc.alloc_semaphore("s_" + e) for e in ENGS}
        self.dsem = {q: [nc.alloc_semaphore("d_%s_%d" % (q, i)) for i in range(NS)]
                     for q in ("sp", "act", "pool")}
        self.dcount = {q: 0 for q in self.dsem}
        self.base = {e: 0 for e in ENGS}
        self.rec = {e: [] for e in ENGS}
        self.off = {e: 0 for e in ENGS}
        self.msmap = {e: {} for e in ENGS}
        self.waited = {}
        self.limit = None
        self.nrec = 0

    def _need(self, eng, tok, waits):
        if tok is None:
            return
        if tok[0] == "c":
            _, se, gidx = tok
            if se == "pe" and eng == "pe":
                return
            key = (eng, se)
            if self.waited.get(key, -1) >= gidx:
                return
            self.waited[key] = gidx
            li = gidx - self.off[se]
            if li >= 0:
                self.rec[se][li][3] = True
            waits.append(tok)
        else:
            _, q, n = tok
            key = (eng, q, n % NS)
            if self.waited.get(key, -1) >= n:
                return
            self.waited[key] = n
            waits.append(tok)

    def _deps(self, eng, tok, r, w, waits):
        rx = [b for b in r if b.excl]
        if rx:
            r = [b for b in r if not b.excl]
            w = list(w) + [b for b in rx if b not in w]
        for b in r:
            self._need(eng, b.w, waits)
        for b in w:
            self._need(eng, b.w, waits)
            for se, gi in b.r.items():
                self._need(eng, ("c", se, gi), waits)
            for t in b.rd:
                self._need(eng, t, waits)
        for b in r:
            if tok[0] == "c":
                b.r[tok[1]] = tok[2]
            else:
                b.rd.append(tok)
                if len(b.rd) > 3 * NS:
                    b.rd = b.rd[-3 * NS:]
        for b in w:
            b.w = tok
            b.r = {}
            b.rd = []

    def op(self, eng, meth, r=(), w=(), **kw):
        fn = (meth, kw)
        self.nrec += 1
        if self.limit is not None and self.nrec > self.limit:
            return None
        waits = []
        gidx = self.off[eng] + len(self.rec[eng])
        tok = ("c", eng, gidx)
        self._deps(eng, tok, r, w, waits)
        self.rec[eng].append(["op", fn, waits, False])
        return tok

    def dma(self, q, out, in_, r=(), w=(), slow=False):
        self.nrec += 1
        if self.limit is not None and self.nrec > self.limit:
            return None
        waits = []
        n = self.dcount[q]
        self.dcount[q] += 1
        tok = ("d", q, n)
        if n >= NS:
            self._need(q, ("d", q, n - NS), waits)
        self._deps(q, tok, r, w, waits)
        self.rec[q].append(["dma", (out, in_, q, n, slow), waits, False])
        return tok

    def barrier(self, bufs=()):
        for e in ENGS:
            waits = []
            for se in ENGS:
                if se == e:
                    continue
                for li in range(len(self.rec[se]) - 1, -1, -1):
                    if self.rec[se][li][0] == "op":
                        self._need(e, ("c", se, self.off[se] + li), waits)
                        break
            for q in self.dsem:
                for k in range(max(0, self.dcount[q] - NS), self.dcount[q]):
                    self._need(e, ("d", q, k), waits)
            self.rec[e].append(["nop", None, waits, False])

    def emit(self):
        nc = self.nc
        for e in ENGS:
            c = self.base[e]
            for li, rcd in enumerate(self.rec[e]):
                if rcd[0] == "op" and rcd[3]:
                    c += 1
                    self.msmap[e][self.off[e] + li] = c
            self.base[e] = c

        def run(e, engobj):
            for rcd in self.rec[e]:
                kind, fn, waits, ms = rcd
                for t in waits:
                    if t[0] == "c":
                        engobj.wait_ge(self.sem[t[1]], self.msmap[t[1]][t[2]])
                    else:
                        engobj.wait_ge(self.dsem[t[1]][t[2] % NS], 16 * (t[2] // NS + 1))
                if kind == "op":
                    ins = getattr(engobj, fn[0])(**fn[1])
                    if ms:
                        ins.then_inc(self.sem[e], 1)
                elif kind == "dma":
                    out, in_, q, n, slow = fn
                    if slow:
                        ins = engobj.dma_start(out=out, in_=in_, allow_slow_non_contiguous=True)
                    else:
                        ins = engobj.dma_start(out=out, in_=in_)
                    ins.then_inc(self.dsem[q][n % NS], 16)

        with nc.Block() as block:
            @block.tensor
            def _(eng):
                run("pe", eng)

            @block.scalar
            def _(eng):
                run("act", eng)

            @block.vector
            def _(eng):
                run("dve", eng)

            @block.gpsimd
            def _(eng):
                run("pool", eng)

            @block.sync
            def _(eng):
                run("sp", eng)

        for e in ENGS:
            self.off[e] += len(self.rec[e])
            self.rec[e] = []

from contextlib import ExitStack

D = 1024
NKC = 8
DFF = 4096
EPS = 1e-6
TT = 512
FM_COLS = 3072
TM0 = 3072
WIN_COLS = 3856
GRID_W = 64


SBUF_LIMIT = 196608


def _chk(nc, handle):
    m = nc.lookup_mloc(handle)
    nbytes = 1
    for d in list(m.dims)[1:]:
        nbytes *= int(d)
    assert int(m.addr) + nbytes <= SBUF_LIMIT, ("SBUF overflow past physical limit", handle.name, int(m.addr), nbytes)
    return handle


class Tile(Buf):
    __slots__ = ("ap",)

    def __init__(self, ap, name=""):
        Buf.__init__(self, name)
        self.ap = ap


def rope_tables(T):
    t = np.arange(T)
    row = (t // GRID_W).astype(np.float32)
    col = (t % GRID_W).astype(np.float32)
    nf = 32
    inv = (np.float32(10000.0) ** (-np.arange(nf, dtype=np.float32) / np.float32(nf))).astype(np.float32)
    cosT = np.zeros((128, T), np.float32)
    sinT = np.zeros((128, T), np.float32)
    for d in range(128):
        half = d // 64
        idx = d % 64
        f = idx % 32
        pos = row if half == 0 else col
        ang = (pos * inv[f]).astype(np.float32)
        cosT[d] = np.cos(ang)
        sinT[d] = np.sin(ang) * (-1.0 if idx < 32 else 1.0)
    return cosT, sinT


def rope_perm():
    p = np.zeros(128, np.int64)
    for d in range(128):
        idx = d % 64
        p[d] = d + 32 if idx < 32 else d - 32
    return p


def gdn_masks():
    t = np.arange(128)[:, None]
    j = np.arange(128)[None, :]
    same = (t // 64) == (j // 64)
    m = {}
    m["cum_f"] = (same & (t <= j)).astype(np.float32)
    m["cum_b"] = (same & (t >= j)).astype(np.float32)
    m["same"] = same.astype(np.float32)
    i = t
    m["pos_f"] = np.where(same & (i >= j), 0.0, 30000.0).astype(np.float32)
    m["pos_b"] = np.where(same & (i <= j), 0.0, 30000.0).astype(np.float32)
    m["strict_f"] = (same & (i > j)).astype(np.float32)
    m["strict_b"] = (same & (i < j)).astype(np.float32)
    m["c0"] = np.tile((np.arange(128) < 64).astype(np.float32)[:, None], (1, 128))
    m["c1"] = np.tile((np.arange(128) >= 64).astype(np.float32)[:, None], (1, 128))
    return m


CONST_NAMES = ["ident", "cum_f", "cum_b", "same", "pos_f", "pos_b", "strict_f", "strict_b", "c0", "c1"]


def build(Tp, Ts, debug=False, stop_after=9):
    nc = bass.Bass("TRN2", target_bir_lowering=False)
    P = Prog(nc)
    seqs = [("p", Tp), ("s", Ts)]
    Tmax = max(Tp, Ts)

    def din(name, shape, dt=F32):
        return nc.dram_tensor(name, list(shape), dt, kind="ExternalInput").ap()

    def dscr(name, shape, dt):
        if debug:
            return nc.dram_tensor(name, list(shape), dt, kind="ExternalOutput").ap()
        return nc.dram_tensor(name, list(shape), dt).ap()

    X = {"p": din("x_p", [Tp, D]), "s": din("x_s", [Ts, D])}
    Y = {"p": nc.dram_tensor("y_p", [Tp, D], F32, kind="ExternalOutput").ap(),
         "s": nc.dram_tensor("y_s", [Ts, D], F32, kind="ExternalOutput").ap()}
    w_in = din("w_in_r", [D, WIN_COLS])
    w_out = din("w_out", [D, D])
    w_up = din("w_up", [D, DFF])
    w_down = din("w_down", [DFF, D])
    nrm = din("norms", [4, D])
    conv_w_r = din("conv_w_r", [128, 12, 5])
    nrm_r = din("nrm_r", [128, 4, NKC])
    hnorm_r = din("hnorm_r", [128, 5])
    gparam = din("gparam", [2, 8])
    hnorm = din("hnorm", [5, 128])
    cosT = din("cosT", [128, Tmax])
    sinT = din("sinT", [128, Tmax])
    cst = din("cst", [len(CONST_NAMES), 128, 128])

    S = {}
    for sn, T in seqs:
        S[sn] = dict(
            A=dscr("A_raw_" + sn, [1536, T], BF16),
            QK=dscr("QK_T_" + sn, [768, T], BF16),
            Z=dscr("Z_" + sn, [T, 512], BF16),
            G=dscr("G_" + sn, [T, 16], F32),
            V=dscr("V_" + sn, [T, 256], BF16),
            GQ=dscr("GQ_T_" + sn, [512, T], BF16),
            GK=dscr("GK_T_" + sn, [512, T], BF16),
            GKt=dscr("GKt_" + sn, [T, 512], BF16),
            GVt=dscr("GVt_" + sn, [T, 512], BF16),
            OF=dscr("OF_" + sn, [T, 512], F32),
            OB=dscr("OB_" + sn, [T, 512], F32),
            MIX=dscr("MIX_T_" + sn, [1024, T], BF16),
            X1=dscr("X1_" + sn, [T, 1024], F32),
        )
    dbufs = {}

    def DB(key):
        if key not in dbufs:
            dbufs[key] = Buf(str(key))
        return dbufs[key]

    ps = [Tile(nc.alloc_psum_tensor("ps%d" % i, [128, 512], F32).ap(), "ps%d" % i) for i in range(8)]
    for t in ps:
        t.excl = True

    with ExitStack() as g_es:
        def gtile(name, shape, dt):
            return Tile(_chk(nc, g_es.enter_context(nc.sbuf_tensor(name, list(shape), dt))).ap(), name)

        identb = gtile("identb", [128, 128], BF16)
        onesb = gtile("onesb", [128, 128], BF16)
        P.op("pool", "memset", ap=onesb.ap, constant=1.0, w=[onesb])
        epsc = gtile("epsc", [128, 1], F32)
        P.op("pool", "memset", ap=epsc.ap, constant=EPS, w=[epsc])
        nrm_t = gtile("nrm_t", [128, 4, NKC], F32)
        P.dma("sp", nrm_t.ap, nrm_r, w=[nrm_t])
        hn_t = gtile("hn_t", [128, 5], F32)
        P.dma("sp", hn_t.ap, hnorm_r, w=[hn_t])
        gp_b = gtile("gp_b", [128, 2, 8], F32)
        P.dma("sp", gp_b.ap[:, 0, :], gparam[0:1, :].partition_broadcast(128), w=[gp_b])
        P.dma("sp", gp_b.ap[:, 1, :], gparam[1:2, :].partition_broadcast(128), w=[gp_b])
        negA = gtile("negA", [128, 8], F32)
        P.op("act", "activation", out=negA.ap, in_=gp_b.ap[:, 0, :], func=AF.Exp, r=[gp_b], w=[negA])
        P.op("dve", "tensor_scalar", out=negA.ap, in0=negA.ap, scalar1=-1.0, scalar2=None, op0=ALU.mult,
             r=[negA], w=[negA])

        with ExitStack() as t_es:
            identf = Tile(_chk(nc, t_es.enter_context(nc.sbuf_tensor("identf", [128, 128], F32))).ap(), "identf")
            P.dma("sp", identf.ap, cst[0], w=[identf])
            P.op("dve", "tensor_copy", out=identb.ap, in_=identf.ap, r=[identf], w=[identb])
            P.barrier()
            P.emit()

        def load_weight_bf16(wt, st, src, nk, cols, scale_sel, stage_cols):
            n = 0
            for kc in range(nk):
                for c0 in range(0, cols, stage_cols):
                    cn = min(stage_cols, cols - c0)
                    s_ = st[n % 2]
                    P.dma("sp", s_.ap[:, 0:cn], src[kc * 128:(kc + 1) * 128, c0:c0 + cn], w=[s_])
                    dst = wt.ap[:, kc, c0:c0 + cn]
                    if scale_sel is None:
                        if n % 2 == 0:
                            P.op("act", "copy", out=dst, in_=s_.ap[:, 0:cn], r=[s_], w=[wt])
                        else:
                            P.op("dve", "tensor_copy", out=dst, in_=s_.ap[:, 0:cn], r=[s_], w=[wt])
                    else:
                        sc = nrm_t.ap[:, scale_sel, kc:kc + 1]
                        if n % 2 == 0:
                            P.op("act", "activation", out=dst, in_=s_.ap[:, 0:cn], func=AF.Copy, scale=sc,
                                 r=[s_, nrm_t], w=[wt])
                        else:
                            P.op("dve", "tensor_scalar", out=dst, in0=s_.ap[:, 0:cn], scalar1=sc, scalar2=None, op0=ALU.mult,
                                 r=[s_, nrm_t], w=[wt])
                    n += 1
            return wt

        def rms_rstd(ssq, rstd, n, inv_n):
            P.op("act", "activation", out=rstd.ap[:, 0:n], in_=ssq.ap[:, 0:n], func=AF.Ln, bias=epsc.ap[:, 0:1], scale=inv_n,
                 r=[ssq, epsc], w=[rstd])
            P.op("act", "activation", out=rstd.ap[:, 0:n], in_=rstd.ap[:, 0:n], func=AF.Exp, scale=-0.5,
                 r=[rstd], w=[rstd])

        with ExitStack() as es:
            def tile(name, shape, dt):
                return Tile(_chk(nc, es.enter_context(nc.sbuf_tensor(name, list(shape), dt))).ap(), name)

            wi = tile("wi", [128, NKC, WIN_COLS], BF16)
            with ExitStack() as ses:
                st = [Tile(_chk(nc, ses.enter_context(nc.sbuf_tensor("wst%d" % i, [128, 1928], F32))).ap()) for i in range(2)]
                load_weight_bf16(wi, st, w_in, NKC, WIN_COLS, 0, 1928)
                P.barrier()
                P.emit()
            xt = [tile("xt%d" % i, [128, 4, D], F32) for i in range(2)]
            hb = tile("hb", [128, 4, D], BF16)
            junk = tile("junk", [128, D], BF16)
            ssq = tile("ssq", [128, 4], F32)
            rstd = tile("rstd", [128, 4], F32)
            hT = [tile("hT%d" % i, [128, NKC, TT], BF16) for i in range(1)]
            a_st = [tile("a_st%d" % i, [128, 12, TT], BF16) for i in range(1)]
            qk_st = [tile("qk_st%d" % i, [128, 6, TT], BF16) for i in range(1)]
            z_st = [tile("z_st%d" % i, [128, 4, 512], BF16) for i in range(1)]
            v_st = [tile("v_st%d" % i, [128, 4, 256], BF16) for i in range(1)]
            g_st = [tile("g_st%d" % i, [128, 4, 16], F32) for i in range(1)]
            cs_t = [tile("cs_t%d" % i, [128, 2, TT], F32) for i in range(1)]
            sq_t = [tile("sq_t%d" % i, [128, TT], BF16) for i in range(2)]
            rs_t = [tile("rs_t%d" % i, [128, TT], F32) for i in range(2)]
            t1_t = [tile("t1_t%d" % i, [128, TT], F32) for i in range(2)]
            t2_t = [tile("t2_t%d" % i, [128, TT], F32) for i in range(2)]
            gt_t = [tile("gt_t%d" % i, [128, 16], F32) for i in range(2)]

            it = 0
            for sn, T in seqs:
                sc = S[sn]
                ntile = T // TT
                for ti in range(ntile):
                    t0 = ti * TT
                    x_ = xt[it % 2]
                    hT_ = hT[0]
                    P.dma("sp", x_.ap, X[sn][t0:t0 + TT, :].rearrange("(s p) d -> p s d", p=128), w=[x_])
                    cs_ = cs_t[0]
                    P.dma("sp", cs_.ap[:, 0, :], cosT[:, t0:t0 + TT], w=[cs_])
                    P.dma("sp", cs_.ap[:, 1, :], sinT[:, t0:t0 + TT], w=[cs_])
                    for s in range(4):
                        P.op("act", "activation", out=junk.ap, in_=x_.ap[:, s, :], func=AF.Square,
                                                                     accum_out=ssq.ap[:, s:s + 1],
                             r=[x_], w=[junk, ssq])
                    rms_rstd(ssq, rstd, 4, 1.0 / D)
                    for s in range(4):
                        P.op("dve", "tensor_scalar", out=hb.ap[:, s, :], in0=x_.ap[:, s, :],
                                                                        scalar1=rstd.ap[:, s:s + 1], scalar2=None, op0=ALU.mult,
                             r=[x_, rstd], w=[hb])
                    for s in range(4):
                        pb = ps[s % 2]
                        pv = pb.ap.bitcast(BF16).rearrange("p (k t) -> p k t", k=NKC)
                        for kc in range(NKC):
                            P.op("pe", "transpose", out=pv[:, kc, :], in_=hb.ap[:, s, kc * 128:(kc + 1) * 128],
                                                                               identity=identb.ap,
                                 r=[hb, identb], w=[pb])
                        eng = "act" if s % 2 == 0 else "dve"
                        if eng == "act":
                            P.op("act", "copy", out=hT_.ap[:, :, s * 128:(s + 1) * 128], in_=pv, r=[pb], w=[hT_])
                        else:
                            P.op("dve", "tensor_copy", out=hT_.ap[:, :, s * 128:(s + 1) * 128], in_=pv, r=[pb], w=[hT_])

                    def fm_mm(pb, c):
                        for kc in range(NKC):
                            P.op("pe", "matmul", out=pb.ap, lhsT=wi.ap[:, kc, c * 128:(c + 1) * 128], rhs=hT_.ap[:, kc, :],
                                                                            start=(kc == 0), stop=(kc == NKC - 1),
                                 r=[wi, hT_], w=[pb])

                    a_ = a_st[0]
                    for c in range(12):
                        pb = ps[2 + (c % 2)]
                        fm_mm(pb, c)
                        if c % 2 == 0:
                            P.op("act", "copy", out=a_.ap[:, c, :], in_=pb.ap, r=[pb], w=[a_])
                        else:
                            P.op("dve", "tensor_copy", out=a_.ap[:, c, :], in_=pb.ap, r=[pb], w=[a_])
                    P.dma("sp", sc["A"].rearrange("(c p) t -> p c t", p=128)[:, :, t0:t0 + TT], a_.ap, r=[a_], w=[DB(("A", sn))])
                    qk_ = qk_st[0]
                    for j in range(6):
                        pm = ps[2 + (j % 2)]
                        pp = ps[4 + (j % 2)]
                        pc = ps[6 + (j % 2)]
                        sq_, rs_, t1_, t2_ = sq_t[j % 2], rs_t[j % 2], t1_t[j % 2], t2_t[j % 2]
                        fm_mm(pm, 12 + j)
                        fm_mm(pp, 18 + j)
                        wsel = 1 if j < 4 else 3
                        P.op("act", "activation", out=sq_.ap, in_=pm.ap, func=AF.Square, r=[pm], w=[sq_])
                        P.op("pe", "matmul", out=pc.ap, lhsT=onesb.ap, rhs=sq_.ap, start=True, stop=True,
                             r=[onesb, sq_], w=[pc])
                        P.op("act", "activation", out=rs_.ap, in_=pc.ap, func=AF.Ln, bias=epsc.ap[:, 0:1], scale=1.0 / 128,
                             r=[pc, epsc], w=[rs_])
                        P.op("act", "activation", out=rs_.ap, in_=rs_.ap, func=AF.Exp, scale=-0.5, r=[rs_], w=[rs_])
                        P.op("dve", "scalar_tensor_tensor", out=t1_.ap, in0=pm.ap, scalar=hn_t.ap[:, wsel:wsel + 1],
                                                                                               in1=cs_.ap[:, 0, :], op0=ALU.mult, op1=ALU.mult,
                             r=[pm, hn_t, cs_], w=[t1_])
                        P.op("dve", "scalar_tensor_tensor", out=t2_.ap, in0=pp.ap, scalar=hn_t.ap[:, wsel + 1:wsel + 2],
                                                                                               in1=cs_.ap[:, 1, :], op0=ALU.mult, op1=ALU.mult,
                             r=[pp, hn_t, cs_], w=[t2_])
                        P.op("pool", "tensor_tensor", out=t1_.ap, in0=t1_.ap, in1=t2_.ap, op=ALU.add, r=[t1_, t2_], w=[t1_])
                        P.op("pool", "tensor_tensor", out=qk_.ap[:, j, :], in0=t1_.ap, in1=rs_.ap, op=ALU.mult,
                             r=[t1_, rs_], w=[qk_])
                    P.dma("sp", sc["QK"].rearrange("(c p) t -> p c t", p=128)[:, :, t0:t0 + TT], qk_.ap, r=[qk_], w=[DB(("QK", sn))])
                    z_, v_, g_ = z_st[0], v_st[0], g_st[0]
                    for s in range(4):
                        pz = ps[s % 2]
                        pg = ps[6 + (s % 2)]
                        gt_ = gt_t[s % 2]
                        for kc in range(NKC):
                            P.op("pe", "matmul", out=pz.ap, lhsT=hT_.ap[:, kc, s * 128:(s + 1) * 128], rhs=wi.ap[:, kc, TM0:TM0 + 512],
                                                                            start=(kc == 0), stop=(kc == NKC - 1), r=[hT_, wi], w=[pz])
                        for kc in range(NKC):
                            P.op("pe", "matmul", out=pg.ap[:, 0:272], lhsT=hT_.ap[:, kc, s * 128:(s + 1) * 128],
                                                                            rhs=wi.ap[:, kc, TM0 + 512:TM0 + 784],
                                                                            start=(kc == 0), stop=(kc == NKC - 1), r=[hT_, wi], w=[pg])
                        P.op("act", "copy", out=z_.ap[:, s, :], in_=pz.ap, r=[pz], w=[z_])
                        P.op("dve", "tensor_copy", out=v_.ap[:, s, :], in_=pg.ap[:, 16:272], r=[pg], w=[v_])
                        P.op("act", "activation", out=gt_.ap[:, 0:8], in_=pg.ap[:, 0:8], func=AF.Exp, scale=-1.0, r=[pg], w=[gt_])
                        P.op("dve", "tensor_scalar", out=gt_.ap[:, 0:8], in0=gt_.ap[:, 0:8], scalar1=1.0, scalar2=None, op0=ALU.add,
                             r=[gt_], w=[gt_])
                        P.op("dve", "reciprocal", out=g_.ap[:, s, 0:8], in_=gt_.ap[:, 0:8], r=[gt_], w=[g_])
                        P.op("dve", "tensor_tensor", out=gt_.ap[:, 8:16], in0=pg.ap[:, 8:16], in1=gp_b.ap[:, 1, :], op=ALU.add,
                             r=[pg, gp_b], w=[gt_])
                        P.op("act", "activation", out=gt_.ap[:, 8:16], in_=gt_.ap[:, 8:16], func=AF.Exp, r=[gt_], w=[gt_])
                        P.op("dve", "tensor_scalar", out=gt_.ap[:, 8:16], in0=gt_.ap[:, 8:16], scalar1=1.0, scalar2=None, op0=ALU.add,
                             r=[gt_], w=[gt_])
                        P.op("act", "activation", out=gt_.ap[:, 8:16], in_=gt_.ap[:, 8:16], func=AF.Ln, r=[gt_], w=[gt_])
                        P.op("dve", "tensor_tensor", out=g_.ap[:, s, 8:16], in0=gt_.ap[:, 8:16], in1=negA.ap, op=ALU.mult,
                             r=[gt_, negA], w=[g_])
                    P.dma("sp", sc["Z"][t0:t0 + TT, :].rearrange("(s p) c -> p s c", p=128), z_.ap, r=[z_], w=[DB(("Z", sn))])
                    P.dma("sp", sc["V"][t0:t0 + TT, :].rearrange("(s p) c -> p s c", p=128), v_.ap, r=[v_], w=[DB(("V", sn))])
                    P.dma("sp", sc["G"][t0:t0 + TT, :].rearrange("(s p) c -> p s c", p=128), g_.ap, r=[g_], w=[DB(("G", sn))])
                    it += 1
            P.barrier()
            P.emit()

        if stop_after >= 2:
            phases_rest(locals())
    return nc


def host_inputs(inp, Tmax):
    f = np.float32
    w_in = np.asarray(inp["w_in"][0], f)
    perm = rope_perm()
    qB = w_in[:, 2064:2576]
    kB = w_in[:, 2576:2832]
    qBp = qB.reshape(D, 4, 128)[:, :, perm].reshape(D, 512)
    kBp = kB.reshape(D, 2, 128)[:, :, perm].reshape(D, 256)
    w_in_r = np.ascontiguousarray(np.concatenate(
        [w_in[:, 0:1536], qB, kB, qBp, kBp, w_in[:, 1536:2048], w_in[:, 2048:2064], w_in[:, 2832:3088]], axis=1))
    norms = np.stack([np.asarray(inp[k][0], f) for k in ("norm_mix_pre", "norm_mix_post", "norm_mlp_pre", "norm_mlp_post")])
    gparam = np.stack([np.concatenate([np.asarray(inp["A_log_f"][0], f), np.asarray(inp["A_log_b"][0], f)]),
                       np.concatenate([np.asarray(inp["dt_bias_f"][0], f), np.asarray(inp["dt_bias_b"][0], f)])])
    qn = np.asarray(inp["q_norm_w"][0], f)
    kn = np.asarray(inp["k_norm_w"][0], f)
    hnorm = np.stack([np.asarray(inp["gdn_norm_w"][0], f), qn, qn[perm], kn, kn[perm]])
    cosT, sinT = rope_tables(Tmax)
    m = gdn_masks()
    m["ident"] = np.eye(128, dtype=f)
    cst = np.stack([m[n] for n in CONST_NAMES]).astype(f)
    return dict(w_in_r=w_in_r, w_out=np.ascontiguousarray(np.asarray(inp["w_out"][0], f)),
                w_up=np.ascontiguousarray(np.asarray(inp["w_up"][0], f)),
                w_down=np.ascontiguousarray(np.asarray(inp["w_down"][0], f)),
                norms=np.ascontiguousarray(norms),
                nrm_r=np.ascontiguousarray(norms.reshape(4, NKC, 128).transpose(2, 0, 1)),
                hnorm_r=np.ascontiguousarray(hnorm.T),
                conv_w_r=np.ascontiguousarray(np.asarray(inp["conv_w"][0], f).reshape(5, 12, 128).transpose(2, 1, 0)),
                gparam=np.ascontiguousarray(gparam), hnorm=np.ascontiguousarray(hnorm),
                cosT=cosT, sinT=sinT, cst=cst)


_NC_CACHE = {}


def run(inp, Tp, Ts, ncores, debug=False, stop_after=9):
    key = (Tp, Ts, debug, stop_after)
    if key not in _NC_CACHE:
        _NC_CACHE[key] = build(Tp, Ts, debug, stop_after)
    nc = _NC_CACHE[key]
    shared = host_inputs(inp, max(Tp, Ts))
    xp = np.asarray(inp["x_prompt"], np.float32)
    xs = np.asarray(inp["x_sample"], np.float32)
    in_maps = []
    for c in range(ncores):
        m = dict(shared)
        m["x_p"] = np.ascontiguousarray(xp[c])
        m["x_s"] = np.ascontiguousarray(xs[c])
        in_maps.append(m)
    res = run_bass_kernel_spmd(nc, in_maps, core_ids=list(range(ncores)))
    return res.results


def kernel(**inputs):
    res = run(inputs, 8192, 2048, 8)
    yp = np.stack([np.asarray(r["y_p"], np.float32) for r in res])
    ys = np.stack([np.asarray(r["y_s"], np.float32) for r in res])
    return (yp, ys)


def phase_attn(L):
    nc, P, ps, S, seqs, hnorm, identb = L["nc"], L["P"], L["ps"], L["S"], L["seqs"], L["hnorm"], L["identb"]
    DB = L["DB"]
    with ExitStack() as es:
        def tile(name, shape, dt):
            return Tile(_chk(nc, es.enter_context(nc.sbuf_tensor(name, list(shape), dt))).ap(), name)
        Tmax = max(T for _, T in seqs)
        NBmax = Tmax // 128
        wqk_b = tile("wqk_b", [128, 2, 128], F32)
        P.dma("sp", wqk_b.ap[:, 0, :], hnorm[1:2, :].partition_broadcast(128), w=[wqk_b])
        P.dma("sp", wqk_b.ap[:, 1, :], hnorm[3:4, :].partition_broadcast(128), w=[wqk_b])
        mx = tile("mx", [128, 2], F32)
        P.op("dve", "tensor_reduce", out=mx.ap, in_=wqk_b.ap, axis=AX.X, op=ALU.max, apply_absolute_value=True, r=[wqk_b], w=[mx])
        negM = tile("negM", [128, 1], F32)
        P.op("dve", "tensor_tensor", out=negM.ap, in0=mx.ap[:, 0:1], in1=mx.ap[:, 1:2], op=ALU.mult, r=[mx], w=[negM])
        P.op("dve", "tensor_scalar", out=negM.ap, in0=negM.ap, scalar1=-(128.0 ** 0.5), scalar2=None, op0=ALU.mult, r=[negM], w=[negM])
        kT = tile("kT", [128, Tmax], BF16)
        va = tile("va", [128, NBmax, 129], BF16)
        P.op("pool", "memset", ap=va.ap, constant=1.0, w=[va])
        qT = [tile("qT%d" % i, [128, TT], BF16) for i in range(4)]
        pT = [tile("pT%d" % i, [128, TT], BF16) for i in range(6)]
        osb = [tile("osb%d" % i, [128, 4, 129], F32) for i in range(2)]
        rcp = [tile("rcp%d" % i, [128, 4], F32) for i in range(2)]
        onb = [tile("onb%d" % i, [128, 4, 128], BF16) for i in range(2)]
        mst = [tile("mst%d" % i, [128, TT], BF16) for i in range(2)]
        scale = 128.0 ** -0.5
        qi = 0
        npt = 0
        for sn, T in seqs:
            sc = S[sn]
            NB = T // 128
            for g in range(2):
                P.dma("sp", kT.ap[:, 0:T], sc["QK"][(4 + g) * 128:(5 + g) * 128, :], r=[DB(("QK", sn))], w=[kT])
                P.dma("sp", va.ap[:, 0:NB, 0:128], sc["V"][:, g * 128:(g + 1) * 128].rearrange("(b p) c -> p b c", p=128),
                      r=[DB(("V", sn))], w=[va])
                for h in (2 * g, 2 * g + 1):
                    for qp in range(T // (2 * TT)):
                        q0s = [(2 * qp + s) * TT for s in range(2)]
                        q_s = [qT[(2 * qi + s) % 4] for s in range(2)]
                        for s in range(2):
                            P.dma("sp", q_s[s].ap, sc["QK"][h * 128:(h + 1) * 128, q0s[s]:q0s[s] + TT], r=[DB(("QK", sn))], w=[q_s[s]])
                        sbank = [[ps[2 * s], ps[2 * s + 1]] for s in range(2)]
                        accb = [[ps[4 + 2 * s], ps[5 + 2 * s]] for s in range(2)]
                        accv = [[b.ap[:, 0:258].rearrange("p (a c) -> p a c", a=2) for b in accb[s]] for s in range(2)]

                        def qk(s, kb):
                            sb = sbank[s][kb % 2]
                            P.op("pe", "matmul", out=sb.ap, lhsT=kT.ap[:, kb * 128:(kb + 1) * 128], rhs=q_s[s].ap, start=True, stop=True,
                                 r=[kT, q_s[s]], w=[sb])

                        qk(0, 0)
                        qk(1, 0)
                        for kb in range(NB):
                            p_s = []
                            for s in range(2):
                                if kb + 1 < NB:
                                    qk(s, kb + 1)
                                sb = sbank[s][kb % 2]
                                p_ = pT[npt % 6]
                                npt += 1
                                P.op("act", "activation", out=p_.ap, in_=sb.ap, func=AF.Exp, bias=negM.ap[:, 0:1], scale=scale,
                                     r=[sb, negM], w=[p_])
                                p_s.append(p_)
                            for s in range(2):
                                for qs in range(4):
                                    P.op("pe", "matmul", out=accv[s][qs // 2][:, qs % 2, :], lhsT=p_s[s].ap[:, qs * 128:(qs + 1) * 128], rhs=va.ap[:, kb, :],
                                         start=(kb == 0 and qs % 2 == 0), stop=(kb == NB - 1), skip_group_check=True,
                                         r=[p_s[s], va], w=[accb[s][qs // 2]])
                        for s in range(2):
                            o_, r_, n_, m_ = osb[s], rcp[s], onb[s], mst[s]
                            P.op("dve", "tensor_copy", out=o_.ap[:, 0:2, :], in_=accv[s][0], r=[accb[s][0]], w=[o_])
                            P.op("dve", "tensor_copy", out=o_.ap[:, 2:4, :], in_=accv[s][1], r=[accb[s][1]], w=[o_])
                            P.op("dve", "reciprocal", out=r_.ap, in_=o_.ap[:, :, 128], r=[o_], w=[r_])
                            P.op("dve", "tensor_tensor", out=n_.ap, in0=o_.ap[:, :, 0:128], in1=r_.ap.unsqueeze(2).to_broadcast([128, 4, 128]),
                                 op=ALU.mult, r=[o_, r_], w=[n_])
                            tb = sbank[s][0]
                            tv = tb.ap.bitcast(BF16)[:, 0:512].rearrange("p (a c) -> p a c", a=4)
                            for qs in range(4):
                                P.op("pe", "transpose", out=tv[:, qs, :], in_=n_.ap[:, qs, :], identity=identb.ap, r=[n_, identb], w=[tb])
                            P.op("dve", "tensor_copy", out=m_.ap.rearrange("p (a c) -> p a c", a=4), in_=tv, r=[tb], w=[m_])
                            P.dma("sp", sc["MIX"][512 + h * 128:512 + (h + 1) * 128, q0s[s]:q0s[s] + TT], m_.ap, r=[m_], w=[DB(("MIX", sn))])
                        qi += 1
        P.barrier()
        P.emit()


def phase_mlp(L):
    nc, P, ps, S, seqs, identb = L["nc"], L["P"], L["ps"], L["S"], L["seqs"], L["identb"]
    DB, X, Y, rms_rstd, nrm = L["DB"], L["X"], L["Y"], L["rms_rstd"], L["nrm"]
    load_weight_bf16 = L["load_weight_bf16"]

    def make_norm_resid(mo, junk, ssq, rstd, nrm_b):
        def norm_resid(src_banks, resid_ap, out_ap, which, deps_resid, deps_out):
            P.op("act", "copy", out=mo.ap[:, 0:512], in_=src_banks[0].ap, r=[src_banks[0]], w=[mo])
            P.op("dve", "tensor_copy", out=mo.ap[:, 512:1024], in_=src_banks[1].ap, r=[src_banks[1]], w=[mo])
            P.op("act", "activation", out=junk.ap, in_=mo.ap, func=AF.Square, accum_out=ssq.ap[:, 0:1], r=[mo], w=[junk, ssq])
            rms_rstd(ssq, rstd, 1, 1.0 / D)
            P.op("dve", "scalar_tensor_tensor", out=mo.ap, in0=mo.ap, scalar=rstd.ap[:, 0:1], in1=nrm_b.ap[:, which, :],
                 op0=ALU.mult, op1=ALU.mult, r=[mo, rstd, nrm_b], w=[mo])
            P.op("pool", "tensor_tensor", out=out_ap, in0=mo.ap, in1=resid_ap, op=ALU.add, r=[mo] + deps_resid, w=deps_out)
        return norm_resid

    with ExitStack() as es:
        def tile(name, shape, dt):
            return Tile(_chk(nc, es.enter_context(nc.sbuf_tensor(name, list(shape), dt))).ap(), name)
        wo = tile("wo", [128, 8, D], BF16)
        with ExitStack() as ses:
            st = [Tile(_chk(nc, ses.enter_context(nc.sbuf_tensor("wst4a_%d" % i, [128, 1024], F32))).ap()) for i in range(2)]
            load_weight_bf16(wo, st, L["w_out"], 8, D, None, 1024)
            P.barrier()
            P.emit()
        nrm_b = tile("nrm_b", [128, 2, D], F32)
        P.dma("sp", nrm_b.ap[:, 0, :], nrm[1:2, :].partition_broadcast(128), w=[nrm_b])
        P.dma("sp", nrm_b.ap[:, 1, :], nrm[3:4, :].partition_broadcast(128), w=[nrm_b])
        xt = tile("x4", [128, 4, D], F32)
        mT = [tile("mT%d" % i, [128, 8, TT], BF16) for i in range(2)]
        mo = tile("mo", [128, D], F32)
        junk = tile("junk4", [128, D], BF16)
        ssq = tile("ssq4", [128, 1], F32)
        rstd = tile("rstd4", [128, 1], F32)
        norm_resid = make_norm_resid(mo, junk, ssq, rstd, nrm_b)
        it = 0
        for sn, T in seqs:
            sc = S[sn]
            for ti in range(T // TT):
                t0 = ti * TT
                m_ = mT[it % 2]
                P.dma("sp", xt.ap, X[sn][t0:t0 + TT, :].rearrange("(s p) d -> p s d", p=128), w=[xt])
                P.dma("sp", m_.ap, sc["MIX"].rearrange("(k p) t -> p k t", p=128)[:, :, t0:t0 + TT], r=[DB(("MIX", sn))], w=[m_])
                for s in range(4):
                    banks = [ps[2 * (s % 2)], ps[2 * (s % 2) + 1]]
                    for cg in range(2):
                        for kc in range(8):
                            P.op("pe", "matmul", out=banks[cg].ap, lhsT=m_.ap[:, kc, s * 128:(s + 1) * 128], rhs=wo.ap[:, kc, cg * 512:(cg + 1) * 512],
                                 start=(kc == 0), stop=(kc == 7), r=[m_, wo], w=[banks[cg]])
                    norm_resid(banks, xt.ap[:, s, :], xt.ap[:, s, :], 0, [xt], [xt])
                P.dma("sp", sc["X1"][t0:t0 + TT, :].rearrange("(s p) d -> p s d", p=128), xt.ap, r=[xt], w=[DB(("X1", sn))])
                it += 1
        P.barrier()
        P.emit()
    T4 = 256
    with ExitStack() as es:
        def tile(name, shape, dt):
            return Tile(_chk(nc, es.enter_context(nc.sbuf_tensor(name, list(shape), dt))).ap(), name)
        wu = tile("wu", [128, 8, DFF], BF16)
        wd = tile("wd", [128, 32, D], BF16)
        with ExitStack() as ses:
            st = [Tile(_chk(nc, ses.enter_context(nc.sbuf_tensor("wst4_%d" % i, [128, 2048], F32))).ap()) for i in range(2)]
            load_weight_bf16(wu, st, L["w_up"], 8, DFF, 2, 2048)
            load_weight_bf16(wd, st, L["w_down"], 32, D, None, 1024)
            P.barrier()
            P.emit()
        nrm_b = tile("nrm_b2", [128, D], F32)
        P.dma("sp", nrm_b.ap, nrm[3:4, :].partition_broadcast(128), w=[nrm_b])
        xt = tile("x4b", [128, D], F32)
        mo = tile("mo2", [128, D], F32)
        ssq = tile("ssq4b", [128, 1], F32)
        rstd = tile("rstd4b", [128, 1], F32)
        ssq2 = tile("ssq4c", [128, 1], F32)
        rstd2 = tile("rstd4c", [128, 1], F32)
        h2 = tile("h2", [128, D], BF16)
        h2T = [tile("h2T%d" % i, [128, 8, T4], BF16) for i in range(2)]
        fT = tile("fT", [128, 32, T4], BF16)
        rl = [tile("rl%d" % i, [128, T4], F32) for i in range(2)]
        jk = tile("jk4", [128, D], BF16)
        tiles = [(sn, T, ti) for sn, T in seqs for ti in range(T // T4)]

        def prologue(i):
            sn, T, ti = tiles[i]
            sc = S[sn]
            t0 = ti * T4
            hT_ = h2T[i % 2]
            for s in range(2):
                P.dma("sp", xt.ap, sc["X1"][t0 + s * 128:t0 + (s + 1) * 128, :], r=[DB(("X1", sn))], w=[xt])
                P.op("act", "activation", out=h2.ap, in_=xt.ap, func=AF.Square, accum_out=ssq.ap[:, 0:1], r=[xt], w=[h2, ssq])
                rms_rstd(ssq, rstd, 1, 1.0 / D)
                P.op("dve", "tensor_scalar", out=h2.ap, in0=xt.ap, scalar1=rstd.ap[:, 0:1], scalar2=None, op0=ALU.mult,
                     r=[xt, rstd], w=[h2])
                pb = ps[2 + s]
                pv = pb.ap.bitcast(BF16).rearrange("p (k t) -> p k t", k=8)
                for kc in range(8):
                    P.op("pe", "transpose", out=pv[:, kc, :], in_=h2.ap[:, kc * 128:(kc + 1) * 128], identity=identb.ap,
                         r=[h2, identb], w=[pb])
                P.op("act", "copy", out=hT_.ap[:, :, s * 128:(s + 1) * 128], in_=pv, r=[pb], w=[hT_])

        prologue(0)
        for i, (sn, T, ti) in enumerate(tiles):
            sc = S[sn]
            t0 = ti * T4
            hT_ = h2T[i % 2]
            for fc in range(32):
                pb = ps[4 + (fc % 4)]
                for kc in range(8):
                    P.op("pe", "matmul", out=pb.ap[:, 0:T4], lhsT=wu.ap[:, kc, fc * 128:(fc + 1) * 128], rhs=hT_.ap[:, kc, :],
                         start=(kc == 0), stop=(kc == 7), r=[wu, hT_], w=[pb])
                rl_ = rl[fc % 2]
                P.op("act", "activation", out=rl_.ap, in_=pb.ap[:, 0:T4], func=AF.Relu, r=[pb], w=[rl_])
                P.op("dve" if fc % 2 == 0 else "pool", "tensor_tensor", out=fT.ap[:, fc, :], in0=rl_.ap, in1=rl_.ap, op=ALU.mult, r=[rl_], w=[fT])
            if i + 1 < len(tiles):
                prologue(i + 1)
            for s in range(2):
                banks = [ps[0], ps[1]]
                for cg in range(2):
                    for fc in range(32):
                        P.op("pe", "matmul", out=banks[cg].ap, lhsT=fT.ap[:, fc, s * 128:(s + 1) * 128], rhs=wd.ap[:, fc, cg * 512:(cg + 1) * 512],
                             start=(fc == 0), stop=(fc == 31), r=[fT, wd], w=[banks[cg]])
                P.op("act", "copy", out=mo.ap[:, 0:512], in_=banks[0].ap, r=[banks[0]], w=[mo])
                P.op("dve", "tensor_copy", out=mo.ap[:, 512:1024], in_=banks[1].ap, r=[banks[1]], w=[mo])
                P.op("act", "activation", out=jk.ap, in_=mo.ap, func=AF.Square, accum_out=ssq2.ap[:, 0:1], r=[mo], w=[jk, ssq2])
                rms_rstd(ssq2, rstd2, 1, 1.0 / D)
                P.op("dve", "scalar_tensor_tensor", out=mo.ap, in0=mo.ap, scalar=rstd2.ap[:, 0:1], in1=nrm_b.ap,
                     op0=ALU.mult, op1=ALU.mult, r=[mo, rstd2, nrm_b], w=[mo])
                P.dma("sp", xt.ap, sc["X1"][t0 + s * 128:t0 + (s + 1) * 128, :], r=[DB(("X1", sn))], w=[xt])
                P.op("pool", "tensor_tensor", out=xt.ap, in0=mo.ap, in1=xt.ap, op=ALU.add, r=[mo, xt], w=[xt])
                P.dma("sp", Y[sn][t0 + s * 128:t0 + (s + 1) * 128, :], xt.ap, r=[xt], w=[DB(("Y", sn))])
        P.barrier()
        P.emit()


def phases_rest(L):
    sa = L["stop_after"]
    if sa >= 2:
        phase_gdn(L)
    if sa >= 3:
        phase_attn(L)
    if sa >= 4:
        phase_mlp(L)


def phase_gdn(L):
    nc, P, ps, S, seqs, identb, onesb = L["nc"], L["P"], L["ps"], L["S"], L["seqs"], L["identb"], L["onesb"]
    DB, epsc, rms_rstd = L["DB"], L["epsc"], L["rms_rstd"]
    with ExitStack() as es:
        def tile(name, shape, dt):
            return Tile(_chk(nc, es.enter_context(nc.sbuf_tensor(name, list(shape), dt))).ap(), name)
        cw_t = tile("cw_t", [128, 12, 5], F32)
        P.dma("sp", cw_t.ap, L["conv_w_r"], w=[cw_t])
        dg = tile("dg", [128, 12, 5, 128], BF16)
        for c in range(12):
            for kk in range(5):
                P.op("dve" if (c * 5 + kk) % 2 == 0 else "pool", "tensor_scalar", out=dg.ap[:, c, kk, :], in0=identb.ap,
                     scalar1=cw_t.ap[:, c, kk:kk + 1], scalar2=None, op0=ALU.mult, r=[identb, cw_t], w=[dg])
        raw = [tile("raw%d" % i, [128, 12, TT + 4], BF16) for i in range(2)]
        sl = [tile("sl%d" % i, [128, TT], F32) for i in range(2)]
        sq = [tile("sq%d" % i, [128, TT], BF16) for i in range(2)]
        rs = [tile("rs%d" % i, [128, TT], F32) for i in range(2)]
        fm = [tile("fm%d" % i, [128, TT], BF16) for i in range(3)]
        tk = [tile("tk%d" % i, [128, 4, 128], BF16) for i in range(2)]
        it = 0
        n = 0
        for sn, T in seqs:
            sc = S[sn]
            for ti in range(T // TT):
                t0 = ti * TT
                r_ = raw[it % 2]
                lo, hi = max(t0 - 2, 0), min(t0 + TT + 2, T)
                if t0 == 0:
                    P.op("pool", "memset", ap=r_.ap[:, :, 0:2], constant=0.0, w=[r_])
                if t0 + TT == T:
                    P.op("pool", "memset", ap=r_.ap[:, :, TT + 2:TT + 4], constant=0.0, w=[r_])
                P.dma("sp", r_.ap[:, :, lo - (t0 - 2):hi - (t0 - 2)], sc["A"].rearrange("(c p) t -> p c t", p=128)[:, :, lo:hi],
                      r=[DB(("A", sn))], w=[r_])
                for c in range(12):
                    kind, h = c // 4, c % 4
                    s_ = sl[n % 2]
                    pcv = ps[4 + (n % 4)]
                    for kk in range(5):
                        P.op("pe", "matmul", out=pcv.ap, lhsT=dg.ap[:, c, kk, :], rhs=r_.ap[:, c, kk:kk + TT], start=(kk == 0), stop=(kk == 4),
                             r=[dg, r_], w=[pcv])
                    P.op("act", "activation", out=s_.ap, in_=pcv.ap, func=AF.Silu, r=[pcv], w=[s_])
                    f_ = fm[n % 3]
                    if kind == 2:
                        P.op("pool", "tensor_copy", out=f_.ap, in_=s_.ap, r=[s_], w=[f_])
                    else:
                        q_, rs_ = sq[n % 2], rs[n % 2]
                        pb = ps[n % 2]
                        P.op("act", "activation", out=q_.ap, in_=s_.ap, func=AF.Square, r=[s_], w=[q_])
                        P.op("pe", "matmul", out=pb.ap, lhsT=onesb.ap, rhs=q_.ap, start=True, stop=True, r=[onesb, q_], w=[pb])
                        P.op("act", "activation", out=rs_.ap, in_=pb.ap, func=AF.Ln, bias=epsc.ap[:, 0:1], scale=1.0, r=[pb, epsc], w=[rs_])
                        P.op("act", "activation", out=rs_.ap, in_=rs_.ap, func=AF.Exp, scale=-0.5, r=[rs_], w=[rs_])
                        P.op("dve", "scalar_tensor_tensor", out=f_.ap, in0=s_.ap, scalar=(128.0 ** -0.5 if kind == 0 else 1.0), in1=rs_.ap,
                             op0=ALU.mult, op1=ALU.mult, r=[s_, rs_], w=[f_])
                        dst = sc["GQ"] if kind == 0 else sc["GK"]
                        P.dma("sp", dst[h * 128:(h + 1) * 128, t0:t0 + TT], f_.ap, r=[f_], w=[DB(("GQK", sn))])
                    if kind >= 1:
                        tb = ps[2 + (n % 2)]
                        tv = tb.ap.bitcast(BF16)[:, 0:512].rearrange("p (a c) -> p a c", a=4)
                        for s in range(4):
                            P.op("pe", "transpose", out=tv[:, s, :], in_=f_.ap[:, s * 128:(s + 1) * 128], identity=identb.ap,
                                 r=[f_, identb], w=[tb])
                        t_ = tk[n % 2]
                        P.op("act", "copy", out=t_.ap, in_=tv, r=[tb], w=[t_])
                        dst = sc["GKt"] if kind == 1 else sc["GVt"]
                        P.dma("sp", dst[t0:t0 + TT, h * 128:(h + 1) * 128].rearrange("(s p) d -> p s d", p=128), t_.ap,
                              r=[t_], w=[DB(("GKV", sn))])
                    n += 1
                it += 1
        P.barrier()
        P.emit()
    if L["stop_after"] == 2 and L.get("gdn_pre_only"):
        return
    with ExitStack() as es:
        def tile(name, shape, dt):
            return Tile(_chk(nc, es.enter_context(nc.sbuf_tensor(name, list(shape), dt))).ap(), name)

        cf = tile("cf", [128, len(CONST_NAMES), 128], F32)
        P.dma("sp", cf.ap, L["cst"].rearrange("n p j -> p n j"), w=[cf])
        C = {n: cf.ap[:, i, :] for i, n in enumerate(CONST_NAMES)}
        onesf = tile("onesf", [128, 128], F32)
        P.op("pool", "memset", ap=onesf.ap, constant=1.0, w=[onesf])
        gnw_b = tile("gnw_b", [128, 128], F32)
        P.dma("sp", gnw_b.ap, L["hnorm"][0:1, :].partition_broadcast(128), w=[gnw_b])

        def b3(ap2):
            return ap2.unsqueeze(1).to_broadcast([128, 4, 128])

        def bj(ap2):
            return ap2.unsqueeze(2).to_broadcast([128, 4, 128])

        def v4(t):
            return t.ap.rearrange("p (h j) -> p h j", h=4)

        NU = 2
        Rd = []
        R = []
        for d in range(2):
            rd = {"S4": tile("S4_%d" % d, [128, 512], F32), "S4b": tile("S4b_%d" % d, [128, 512], BF16),
                  }
            Rd.append(rd)
            Ru = []
            for u in range(NU):
                r = {}
                sfx = "%d_%d" % (d, u)
                for nm in ("KT", "QT", "Kt", "Vt", "A0", "A1", "B0", "B1", "Aqk", "AqkT", "PTb", "Vb", "kbg", "ke", "wT", "vn"):
                    r[nm] = tile(nm + sfx, [128, 512], BF16)
                for nm in ("Gm", "Ep", "D4", "Dsb", "PT", "u4"):
                    r[nm] = tile(nm + sfx, [128, 512], F32)
                r["Oq"], r["O4"], r["sz"], r["o2"] = r["Gm"], r["Ep"], r["D4"], r["Dsb"]
                r["zb"], r["onb"], r["mxs"] = r["A0"], r["A1"], r["B0"]
                r["G"] = tile("G" + sfx, [128, 16], F32)
                r["sm"] = tile("sm" + sfx, [128, 16], F32)
                r["ex"] = tile("ex" + sfx, [128, 16], F32)
                r["bgc"] = tile("bgc" + sfx, [128, 4], F32)
                r["ssq"] = tile("gssq" + sfx, [128, 4], F32)
                r["rstd"] = tile("grstd" + sfx, [128, 4], F32)
                r["banks"] = [ps[2 * (NU * d + u)], ps[2 * (NU * d + u) + 1]]
                r["nb"] = 0
                Ru.append(r)
            R.append(Ru)

        def nbank(r):
            b = r["banks"][r["nb"] % 2]
            r["nb"] += 1
            return b

        def unit(sn, T, d, b, step, NB):
            sc = S[sn]
            r = R[d][step % NU]
            rd = Rd[d]
            t0 = b * 128
            KT, QT, Kt, Vt, G = r["KT"], r["QT"], r["Kt"], r["Vt"], r["G"]
            P.dma("sp", v4(KT), sc["GK"].rearrange("(h p) t -> p h t", p=128)[:, :, t0:t0 + 128], r=[DB(("GQK", sn))], w=[KT])
            P.dma("sp", v4(QT), sc["GQ"].rearrange("(h p) t -> p h t", p=128)[:, :, t0:t0 + 128], r=[DB(("GQK", sn))], w=[QT])
            P.dma("sp", Kt.ap, sc["GKt"][t0:t0 + 128, :], r=[DB(("GKV", sn))], w=[Kt])
            P.dma("sp", Vt.ap, sc["GVt"][t0:t0 + 128, :], r=[DB(("GKV", sn))], w=[Vt])
            P.dma("sp", G.ap, sc["G"][t0:t0 + 128, :], r=[DB(("G", sn))], w=[G])
            yield
            beta = G.ap[:, 4 * d:4 * d + 4]
            gg = G.ap[:, 8 + 4 * d:12 + 4 * d]
            cum = C["cum_f"] if d == 0 else C["cum_b"]
            pos = C["pos_f"] if d == 0 else C["pos_b"]
            strict = C["strict_f"] if d == 0 else C["strict_b"]
            sm, ex, bgc = r["sm"], r["ex"], r["bgc"]
            Gm, Ep, D4, Dsb = r["Gm"], r["Ep"], r["D4"], r["Dsb"]
            pb = nbank(r)
            P.op("pe", "matmul", out=pb.ap[:, 0:4], lhsT=cum, rhs=gg, start=True, stop=True, r=[cf, G], w=[pb])
            P.op("pe", "matmul", out=pb.ap[:, 4:8], lhsT=C["same"], rhs=gg, start=True, stop=True, r=[cf, G], w=[pb])
            P.op("pe", "matmul", out=pb.ap[:, 8:12], lhsT=C["c0"], rhs=gg, start=True, stop=True, r=[cf, G], w=[pb])
            P.op("pe", "matmul", out=pb.ap[:, 12:16], lhsT=C["c1"], rhs=gg, start=True, stop=True, r=[cf, G], w=[pb])
            P.op("pool", "tensor_tensor", out=v4(Gm), in0=b3(cum), in1=bj(gg), op=ALU.mult, r=[cf, G], w=[Gm])
            yield
            P.op("dve", "tensor_copy", out=sm.ap, in_=pb.ap[:, 0:16], r=[pb], w=[sm])
            P.op("dve", "tensor_tensor", out=sm.ap[:, 4:8], in0=sm.ap[:, 4:8], in1=sm.ap[:, 0:4], op=ALU.subtract, r=[sm], w=[sm])
            pe_ = nbank(r)
            P.op("pe", "matmul", out=pe_.ap, lhsT=onesf.ap, rhs=Gm.ap, start=True, stop=True, r=[onesf, Gm], w=[pe_])
            yield
            P.op("act", "activation", out=ex.ap, in_=sm.ap, func=AF.Exp, r=[sm], w=[ex])
            P.op("dve", "tensor_tensor", out=v4(Ep), in0=pe_.ap.rearrange("p (h j) -> p h j", h=4), in1=b3(pos), op=ALU.add,
                 r=[pe_, cf], w=[Ep])
            yield
            P.op("dve", "tensor_tensor", out=bgc.ap, in0=ex.ap[:, 0:4], in1=beta, op=ALU.mult, r=[ex, G], w=[bgc])
            for h in range(4):
                P.op("act", "activation", out=v4(D4)[:, h, :], in_=v4(Ep)[:, h, :], func=AF.Exp, bias=sm.ap[:, h:h + 1], scale=-1.0,
                     r=[Ep, sm], w=[D4])
            pk, pq = nbank(r), nbank(r)
            for h in range(4):
                P.op("pe", "matmul", out=pk.ap[:, h * 128:(h + 1) * 128], lhsT=v4(KT)[:, h, :], rhs=v4(KT)[:, h, :], start=True, stop=True,
                     r=[KT], w=[pk])
            for h in range(4):
                P.op("pe", "matmul", out=pq.ap[:, h * 128:(h + 1) * 128], lhsT=v4(QT)[:, h, :], rhs=v4(KT)[:, h, :], start=True, stop=True,
                     r=[QT, KT], w=[pq])
            yield
            Vb, kbg, ke, u4, wT = r["Vb"], r["kbg"], r["ke"], r["u4"], r["wT"]
            P.op("pool", "tensor_tensor", out=v4(Dsb), in0=v4(D4), in1=b3(strict), op=ALU.mult, r=[D4, cf], w=[Dsb])
            P.op("pool", "tensor_tensor", out=v4(Dsb), in0=v4(Dsb), in1=bj(beta), op=ALU.mult, r=[Dsb, G], w=[Dsb])
            A, B = [r["A0"], r["A1"]], [r["B0"], r["B1"]]
            Aqk, AqkT, PT, PTb = r["Aqk"], r["AqkT"], r["PT"], r["PTb"]
            P.op("dve", "tensor_tensor", out=Aqk.ap, in0=pq.ap, in1=D4.ap, op=ALU.mult, r=[pq, D4], w=[Aqk])
            yield
            P.op("dve", "tensor_tensor", out=A[0].ap, in0=pk.ap, in1=Dsb.ap, op=ALU.mult, r=[pk, Dsb], w=[A[0]])
            P.op("pool", "tensor_tensor", out=v4(Vb), in0=v4(Vt), in1=bj(beta), op=ALU.mult, r=[Vt, G], w=[Vb])
            P.op("pool", "tensor_tensor", out=v4(kbg), in0=v4(Kt), in1=bj(bgc.ap), op=ALU.mult, r=[Kt, bgc], w=[kbg])
            P.op("pool", "tensor_tensor", out=v4(ke), in0=v4(Kt), in1=bj(ex.ap[:, 4:8]), op=ALU.mult, r=[Kt, ex], w=[ke])
            yield
            tb = nbank(r)
            tvA = tb.ap.bitcast(BF16)[:, 0:512].rearrange("p (a c) -> p a c", a=4)
            tvQ = tb.ap.bitcast(BF16)[:, 512:1024].rearrange("p (a c) -> p a c", a=4)
            for h in range(4):
                P.op("pe", "transpose", out=tvA[:, h, :], in_=v4(A[0])[:, h, :], identity=identb.ap, r=[A[0], identb], w=[tb])
            for h in range(4):
                P.op("pe", "transpose", out=tvQ[:, h, :], in_=v4(Aqk)[:, h, :], identity=identb.ap, r=[Aqk, identb], w=[tb])
            yield
            P.op("act", "copy", out=v4(B[0]), in_=tvA, r=[tb], w=[B[0]])
            P.op("act", "copy", out=v4(AqkT), in_=tvQ, r=[tb], w=[AqkT])
            yield
            P.op("pool", "tensor_tensor", out=v4(PT), in0=b3(C["ident"]), in1=v4(B[0]), op=ALU.subtract, r=[cf, B[0]], w=[PT])
            P.op("act", "copy", out=PTb.ap, in_=PT.ap, r=[PT], w=[PTb])
            for k in range(5):
                Ak, Bk, An, Bn = A[k % 2], B[k % 2], A[(k + 1) % 2], B[(k + 1) % 2]
                pa = nbank(r)
                for h in range(4):
                    P.op("pe", "matmul", out=pa.ap[:, h * 128:(h + 1) * 128], lhsT=v4(Bk)[:, h, :], rhs=v4(Ak)[:, h, :], start=True, stop=True,
                         r=[Ak, Bk], w=[pa])
                if k < 4:
                    pbb = nbank(r)
                    for h in range(4):
                        P.op("pe", "matmul", out=pbb.ap[:, h * 128:(h + 1) * 128], lhsT=v4(Ak)[:, h, :], rhs=v4(Bk)[:, h, :], start=True, stop=True,
                             r=[Ak, Bk], w=[pbb])
                yield
                P.op("act", "copy", out=An.ap, in_=pa.ap, r=[pa], w=[An])
                if k < 4:
                    P.op("dve", "tensor_copy", out=Bn.ap, in_=pbb.ap, r=[pbb], w=[Bn])
                yield
                pd = nbank(r)
                for h in range(4):
                    P.op("pe", "matmul", out=pd.ap[:, h * 128:(h + 1) * 128], lhsT=v4(An)[:, h, :], rhs=v4(PTb)[:, h, :], start=True, stop=True,
                         r=[An, PTb], w=[pd])
                yield
                P.op("dve", "tensor_tensor", out=PT.ap, in0=pd.ap, in1=PT.ap, op=ALU.add, r=[pd, PT], w=[PT])
                P.op("act", "copy", out=PTb.ap, in_=PT.ap, r=[PT], w=[PTb])
                yield
            pu, pw = nbank(r), nbank(r)
            for h in range(4):
                P.op("pe", "matmul", out=pu.ap[:, h * 128:(h + 1) * 128], lhsT=v4(PTb)[:, h, :], rhs=v4(Vb)[:, h, :], start=True, stop=True,
                     r=[PTb, Vb], w=[pu])
            for h in range(4):
                P.op("pe", "matmul", out=pw.ap[:, h * 128:(h + 1) * 128], lhsT=v4(kbg)[:, h, :], rhs=v4(PTb)[:, h, :], start=True, stop=True,
                     r=[PTb, kbg], w=[pw])
            yield
            P.op("act", "copy", out=u4.ap, in_=pu.ap, r=[pu], w=[u4])
            P.op("dve", "tensor_copy", out=wT.ap, in_=pw.ap, r=[pw], w=[wT])
            yield
            S4, S4b, vn, Oq, O4 = rd["S4"], rd["S4b"], r["vn"], r["Oq"], r["O4"]
            for c in ((0, 1) if d == 0 else (1, 0)):
                pc = slice(c * 64, (c + 1) * 64)
                p1, po1 = nbank(r), nbank(r)
                for h in range(4):
                    P.op("pe", "matmul", out=p1.ap[pc, h * 128:(h + 1) * 128], lhsT=v4(wT)[:, h, pc], rhs=v4(S4b)[:, h, :], start=True, stop=True,
                         r=[wT, S4b], w=[p1])
                for h in range(4):
                    P.op("pe", "matmul", out=po1.ap[pc, h * 128:(h + 1) * 128], lhsT=v4(QT)[:, h, pc], rhs=v4(S4b)[:, h, :], start=True, stop=True,
                         r=[QT, S4b], w=[po1])
                yield
                P.op("dve", "tensor_tensor", out=vn.ap[pc, :], in0=u4.ap[pc, :], in1=p1.ap[pc, :], op=ALU.subtract, r=[u4, p1], w=[vn])
                P.op("dve", "tensor_tensor", out=v4(Oq)[pc], in0=po1.ap[pc, :].rearrange("p (h j) -> p h j", h=4),
                     in1=ex.ap[pc, 0:4].unsqueeze(2).to_broadcast([64, 4, 128]), op=ALU.mult, r=[po1, ex], w=[Oq])
                yield
                po2, pds = nbank(r), nbank(r)
                for h in range(4):
                    P.op("pe", "matmul", out=pds.ap[:, h * 128:(h + 1) * 128], lhsT=v4(ke)[pc, h, :], rhs=v4(vn)[pc, h, :], start=True, stop=True,
                         r=[ke, vn], w=[pds])
                for h in range(4):
                    P.op("pe", "matmul", out=po2.ap[pc, h * 128:(h + 1) * 128], lhsT=v4(AqkT)[pc, h, pc], rhs=v4(vn)[pc, h, :], start=True, stop=True,
                         r=[AqkT, vn], w=[po2])
                P.op("pool", "tensor_tensor", out=v4(S4), in0=v4(S4), in1=bj(ex.ap[:, 8 + 4 * c:12 + 4 * c]), op=ALU.mult, r=[S4, ex], w=[S4])
                yield
                P.op("dve", "tensor_tensor", out=S4.ap, in0=pds.ap, in1=S4.ap, op=ALU.add, r=[pds, S4], w=[S4])
                P.op("act", "copy", out=S4b.ap, in_=S4.ap, r=[S4], w=[S4b])
                P.op("dve", "tensor_tensor", out=O4.ap[pc, :], in0=po2.ap[pc, :], in1=Oq.ap[pc, :], op=ALU.add, r=[po2, Oq], w=[O4])
                yield
            mine, other = ("OF", "OB") if d == 0 else ("OB", "OF")
            if step < NB // 2:
                P.dma("sp", sc[mine][t0:t0 + 128, :], O4.ap, r=[O4], w=[DB((mine, sn, b))])
            else:
                o2, sz, zb, onb, mxs = r["o2"], r["sz"], r["zb"], r["onb"], r["mxs"]
                ssq, rstd = r["ssq"], r["rstd"]
                P.dma("sp", o2.ap, sc[other][t0:t0 + 128, :], r=[DB((other, sn, b))], w=[o2])
                P.dma("sp", zb.ap, sc["Z"][t0:t0 + 128, :], r=[DB(("Z", sn))], w=[zb])
                yield
                P.op("pool", "tensor_tensor", out=o2.ap, in0=o2.ap, in1=O4.ap, op=ALU.add, r=[o2, O4], w=[o2])
                P.op("pool", "tensor_tensor", out=sz.ap, in0=o2.ap, in1=o2.ap, op=ALU.mult, r=[o2], w=[sz])
                yield
                P.op("dve", "tensor_reduce", out=ssq.ap, in_=v4(sz), axis=AX.X, op=ALU.add, r=[sz], w=[ssq])
                rms_rstd(ssq, rstd, 4, 1.0 / 128)
                P.op("act", "activation", out=sz.ap, in_=zb.ap, func=AF.Silu, r=[zb], w=[sz])
                yield
                P.op("dve", "tensor_tensor", out=v4(o2), in0=v4(o2), in1=bj(rstd.ap), op=ALU.mult, r=[o2, rstd], w=[o2])
                P.op("pool", "tensor_tensor", out=v4(o2), in0=v4(o2), in1=b3(gnw_b.ap), op=ALU.mult, r=[o2, gnw_b], w=[o2])
                yield
                P.op("dve", "tensor_tensor", out=onb.ap, in0=o2.ap, in1=sz.ap, op=ALU.mult, r=[o2, sz], w=[onb])
                yield
                tb2 = nbank(r)
                tv2 = tb2.ap.bitcast(BF16)[:, 0:512].rearrange("p (a c) -> p a c", a=4)
                for h in range(4):
                    P.op("pe", "transpose", out=tv2[:, h, :], in_=v4(onb)[:, h, :], identity=identb.ap, r=[onb, identb], w=[tb2])
                yield
                P.op("act", "copy", out=v4(mxs), in_=tv2, r=[tb2], w=[mxs])
                P.dma("sp", sc["MIX"][0:512, :].rearrange("(h p) t -> p h t", p=128)[:, :, t0:t0 + 128], v4(mxs), r=[mxs], w=[DB(("MIX", sn))])

        for sn, T in seqs:
            NB = T // 128
            for d in range(2):
                P.op("pool", "memset", ap=Rd[d]["S4"].ap, constant=0.0, w=[Rd[d]["S4"]])
                P.op("pool", "memset", ap=Rd[d]["S4b"].ap, constant=0.0, w=[Rd[d]["S4b"]])
            active = []
            pending = []
            for st_ in range(NB):
                pending.append((0, st_, st_))
                pending.append((1, NB - 1 - st_, st_))
            STAG = 6
            while pending or active:
                if pending and len(active) < 2 * NU and (not active or active[-1][1] >= STAG):
                    d_, b_, st_ = pending.pop(0)
                    active.append([unit(sn, T, d_, b_, st_, NB), 0])
                for a_ in list(active):
                    try:
                        next(a_[0])
                        a_[1] += 1
                    except StopIteration:
                        active.remove(a_)
        P.barrier()
        P.emit()
```

```python
import numpy as np
import concourse.bass as bass
import concourse.mybir as mybir
from concourse.bass_utils import run_bass_kernel_spmd

F32 = mybir.dt.float32
BF16 = mybir.dt.bfloat16
AF = mybir.ActivationFunctionType
ALU = mybir.AluOpType
AX = mybir.AxisListType

ENGS = ("pe", "act", "dve", "pool", "sp")
NS = 8


class Buf:
    __slots__ = ("w", "r", "rd", "name", "excl")

    def __init__(self, name=""):
        self.excl = False
        self.w = None
        self.r = {}
        self.rd = []
        self.name = name


class Prog:
    def __init__(self, nc):
        self.nc = nc
        self.sem = {e: nc.alloc_semaphore("s_" + e) for e in ENGS}
        self.dsem = {q: [nc.alloc_semaphore("d_%s_%d" % (q, i)) for i in range(NS)]
                     for q in ("sp", "act", "pool")}
        self.dcount = {q: 0 for q in self.dsem}
        self.base = {e: 0 for e in ENGS}
        self.rec = {e: [] for e in ENGS}
        self.off = {e: 0 for e in ENGS}
        self.msmap = {e: {} for e in ENGS}
        self.waited = {}
        self.limit = None
        self.nrec = 0

    def _need(self, eng, tok, waits):
        if tok is None:
            return
        if tok[0] == "c":
            _, se, gidx = tok
            if se == "pe" and eng == "pe":
                return
            key = (eng, se)
            if self.waited.get(key, -1) >= gidx:
                return
            self.waited[key] = gidx
            li = gidx - self.off[se]
            if li >= 0:
                self.rec[se][li][3] = True
            waits.append(tok)
        else:
            _, q, n = tok
            key = (eng, q, n % NS)
            if self.waited.get(key, -1) >= n:
                return
            self.waited[key] = n
            waits.append(tok)

    def _deps(self, eng, tok, r, w, waits):
        rx = [b for b in r if b.excl]
        if rx:
            r = [b for b in r if not b.excl]
            w = list(w) + [b for b in rx if b not in w]
        for b in r:
            self._need(eng, b.w, waits)
        for b in w:
            self._need(eng, b.w, waits)
            for se, gi in b.r.items():
                self._need(eng, ("c", se, gi), waits)
            for t in b.rd:
                self._need(eng, t, waits)
        for b in r:
            if tok[0] == "c":
                b.r[tok[1]] = tok[2]
            else:
                b.rd.append(tok)
                if len(b.rd) > 3 * NS:
                    b.rd = b.rd[-3 * NS:]
        for b in w:
            b.w = tok
            b.r = {}
            b.rd = []

    def op(self, eng, meth, r=(), w=(), **kw):
        fn = (meth, kw)
        self.nrec += 1
        if self.limit is not None and self.nrec > self.limit:
            return None
        waits = []
        gidx = self.off[eng] + len(self.rec[eng])
        tok = ("c", eng, gidx)
        self._deps(eng, tok, r, w, waits)
        self.rec[eng].append(["op", fn, waits, False])
        return tok

    def dma(self, q, out, in_, r=(), w=(), slow=False):
        self.nrec += 1
        if self.limit is not None and self.nrec > self.limit:
            return None
        waits = []
        n = self.dcount[q]
        self.dcount[q] += 1
        tok = ("d", q, n)
        if n >= NS:
            self._need(q, ("d", q, n - NS), waits)
        self._deps(q, tok, r, w, waits)
        self.rec[q].append(["dma", (out, in_, q, n, slow), waits, False])
        return tok

    def barrier(self, bufs=()):
        for e in ENGS:
            waits = []
            for se in ENGS:
                if se == e:
                    continue
                for li in range(len(self.rec[se]) - 1, -1, -1):
                    if self.rec[se][li][0] == "op":
                        self._need(e, ("c", se, self.off[se] + li), waits)
                        break
            for q in self.dsem:
                for k in range(max(0, self.dcount[q] - NS), self.dcount[q]):
                    self._need(e, ("d", q, k), waits)
            self.rec[e].append(["nop", None, waits, False])

    def emit(self):
        nc = self.nc
        for e in ENGS:
            c = self.base[e]
            for li, rcd in enumerate(self.rec[e]):
                if rcd[0] == "op" and rcd[3]:
                    c += 1
                    self.msmap[e][self.off[e] + li] = c
            self.base[e] = c

        def run(e, engobj):
            for rcd in self.rec[e]:
                kind, fn, waits, ms = rcd
                for t in waits:
                    if t[0] == "c":
                        engobj.wait_ge(self.sem[t[1]], self.msmap[t[1]][t[2]])
                    else:
                        engobj.wait_ge(self.dsem[t[1]][t[2] % NS], 16 * (t[2] // NS + 1))
                if kind == "op":
                    ins = getattr(engobj, fn[0])(**fn[1])
                    if ms:
                        ins.then_inc(self.sem[e], 1)
                elif kind == "dma":
                    out, in_, q, n, slow = fn
                    if slow:
                        ins = engobj.dma_start(out=out, in_=in_, allow_slow_non_contiguous=True)
                    else:
                        ins = engobj.dma_start(out=out, in_=in_)
                    ins.then_inc(self.dsem[q][n % NS], 16)

        with nc.Block() as block:
            @block.tensor
            def _(eng):
                run("pe", eng)

            @block.scalar
            def _(eng):
                run("act", eng)

            @block.vector
            def _(eng):
                run("dve", eng)

            @block.gpsimd
            def _(eng):
                run("pool", eng)

            @block.sync
            def _(eng):
                run("sp", eng)

        for e in ENGS:
            self.off[e] += len(self.rec[e])
            self.rec[e] = []

from contextlib import ExitStack

D = 1024
NKC = 8
DFF = 4096
EPS = 1e-6
TT = 512
FM_COLS = 3072
TM0 = 3072
WIN_COLS = 3856
GRID_W = 64


SBUF_LIMIT = 196608


def _chk(nc, handle):
    m = nc.lookup_mloc(handle)
    nbytes = 1
    for d in list(m.dims)[1:]:
        nbytes *= int(d)
    assert int(m.addr) + nbytes <= SBUF_LIMIT, ("SBUF overflow past physical limit", handle.name, int(m.addr), nbytes)
    return handle


class Tile(Buf):
    __slots__ = ("ap",)

    def __init__(self, ap, name=""):
        Buf.__init__(self, name)
        self.ap = ap


def rope_tables(T):
    t = np.arange(T)
    row = (t // GRID_W).astype(np.float32)
    col = (t % GRID_W).astype(np.float32)
    nf = 32
    inv = (np.float32(10000.0) ** (-np.arange(nf, dtype=np.float32) / np.float32(nf))).astype(np.float32)
    cosT = np.zeros((128, T), np.float32)
    sinT = np.zeros((128, T), np.float32)
    for d in range(128):
        half = d // 64
        idx = d % 64
        f = idx % 32
        pos = row if half == 0 else col
        ang = (pos * inv[f]).astype(np.float32)
        cosT[d] = np.cos(ang)
        sinT[d] = np.sin(ang) * (-1.0 if idx < 32 else 1.0)
    return cosT, sinT


def rope_perm():
    p = np.zeros(128, np.int64)
    for d in range(128):
        idx = d % 64
        p[d] = d + 32 if idx < 32 else d - 32
    return p


def gdn_masks():
    t = np.arange(128)[:, None]
    j = np.arange(128)[None, :]
    same = (t // 64) == (j // 64)
    m = {}
    m["cum_f"] = (same & (t <= j)).astype(np.float32)
    m["cum_b"] = (same & (t >= j)).astype(np.float32)
    m["same"] = same.astype(np.float32)
    i = t
    m["pos_f"] = np.where(same & (i >= j), 0.0, 30000.0).astype(np.float32)
    m["pos_b"] = np.where(same & (i <= j), 0.0, 30000.0).astype(np.float32)
    m["strict_f"] = (same & (i > j)).astype(np.float32)
    m["strict_b"] = (same & (i < j)).astype(np.float32)
    m["c0"] = np.tile((np.arange(128) < 64).astype(np.float32)[:, None], (1, 128))
    m["c1"] = np.tile((np.arange(128) >= 64).astype(np.float32)[:, None], (1, 128))
    return m


CONST_NAMES = ["ident", "cum_f", "cum_b", "same", "pos_f", "pos_b", "strict_f", "strict_b", "c0", "c1"]


def build(Tp, Ts, debug=False, stop_after=9):
    nc = bass.Bass("TRN2", target_bir_lowering=False)
    P = Prog(nc)
    seqs = [("p", Tp), ("s", Ts)]
    Tmax = max(Tp, Ts)

    def din(name, shape, dt=F32):
        return nc.dram_tensor(name, list(shape), dt, kind="ExternalInput").ap()

    def dscr(name, shape, dt):
        if debug:
            return nc.dram_tensor(name, list(shape), dt, kind="ExternalOutput").ap()
        return nc.dram_tensor(name, list(shape), dt).ap()

    X = {"p": din("x_p", [Tp, D]), "s": din("x_s", [Ts, D])}
    Y = {"p": nc.dram_tensor("y_p", [Tp, D], F32, kind="ExternalOutput").ap(),
         "s": nc.dram_tensor("y_s", [Ts, D], F32, kind="ExternalOutput").ap()}
    w_in = din("w_in_r", [D, WIN_COLS])
    w_out = din("w_out", [D, D])
    w_up = din("w_up", [D, DFF])
    w_down = din("w_down", [DFF, D])
    nrm = din("norms", [4, D])
    conv_w_r = din("conv_w_r", [128, 12, 5])
    nrm_r = din("nrm_r", [128, 4, NKC])
    hnorm_r = din("hnorm_r", [128, 5])
    gparam = din("gparam", [2, 8])
    hnorm = din("hnorm", [5, 128])
    cosT = din("cosT", [128, Tmax])
    sinT = din("sinT", [128, Tmax])
    cst = din("cst", [len(CONST_NAMES), 128, 128])

    S = {}
    for sn, T in seqs:
        S[sn] = dict(
            A=dscr("A_raw_" + sn, [1536, T], BF16),
            QK=dscr("QK_T_" + sn, [768, T], BF16),
            Z=dscr("Z_" + sn, [T, 512], BF16),
            G=dscr("G_" + sn, [T, 16], F32),
            V=dscr("V_" + sn, [T, 256], BF16),
            GQ=dscr("GQ_T_" + sn, [512, T], BF16),
            GK=dscr("GK_T_" + sn, [512, T], BF16),
            GKt=dscr("GKt_" + sn, [T, 512], BF16),
            GVt=dscr("GVt_" + sn, [T, 512], BF16),
            OF=dscr("OF_" + sn, [T, 512], F32),
            OB=dscr("OB_" + sn, [T, 512], F32),
            MIX=dscr("MIX_T_" + sn, [1024, T], BF16),
            X1=dscr("X1_" + sn, [T, 1024], F32),
        )
    dbufs = {}

    def DB(key):
        if key not in dbufs:
            dbufs[key] = Buf(str(key))
        return dbufs[key]

    ps = [Tile(nc.alloc_psum_tensor("ps%d" % i, [128, 512], F32).ap(), "ps%d" % i) for i in range(8)]
    for t in ps:
        t.excl = True

    with ExitStack() as g_es:
        def gtile(name, shape, dt):
            return Tile(_chk(nc, g_es.enter_context(nc.sbuf_tensor(name, list(shape), dt))).ap(), name)

        identb = gtile("identb", [128, 128], BF16)
        onesb = gtile("onesb", [128, 128], BF16)
        P.op("pool", "memset", ap=onesb.ap, constant=1.0, w=[onesb])
        epsc = gtile("epsc", [128, 1], F32)
        P.op("pool", "memset", ap=epsc.ap, constant=EPS, w=[epsc])
        nrm_t = gtile("nrm_t", [128, 4, NKC], F32)
        P.dma("sp", nrm_t.ap, nrm_r, w=[nrm_t])
        hn_t = gtile("hn_t", [128, 5], F32)
        P.dma("sp", hn_t.ap, hnorm_r, w=[hn_t])
        gp_b = gtile("gp_b", [128, 2, 8], F32)
        P.dma("sp", gp_b.ap[:, 0, :], gparam[0:1, :].partition_broadcast(128), w=[gp_b])
        P.dma("sp", gp_b.ap[:, 1, :], gparam[1:2, :].partition_broadcast(128), w=[gp_b])
        negA = gtile("negA", [128, 8], F32)
        P.op("act", "activation", out=negA.ap, in_=gp_b.ap[:, 0, :], func=AF.Exp, r=[gp_b], w=[negA])
        P.op("dve", "tensor_scalar", out=negA.ap, in0=negA.ap, scalar1=-1.0, scalar2=None, op0=ALU.mult,
             r=[negA], w=[negA])

        with ExitStack() as t_es:
            identf = Tile(_chk(nc, t_es.enter_context(nc.sbuf_tensor("identf", [128, 128], F32))).ap(), "identf")
            P.dma("sp", identf.ap, cst[0], w=[identf])
            P.op("dve", "tensor_copy", out=identb.ap, in_=identf.ap, r=[identf], w=[identb])
            P.barrier()
            P.emit()

        def load_weight_bf16(wt, st, src, nk, cols, scale_sel, stage_cols):
            n = 0
            for kc in range(nk):
                for c0 in range(0, cols, stage_cols):
                    cn = min(stage_cols, cols - c0)
                    s_ = st[n % 2]
                    P.dma("sp", s_.ap[:, 0:cn], src[kc * 128:(kc + 1) * 128, c0:c0 + cn], w=[s_])
                    dst = wt.ap[:, kc, c0:c0 + cn]
                    if scale_sel is None:
                        if n % 2 == 0:
                            P.op("act", "copy", out=dst, in_=s_.ap[:, 0:cn], r=[s_], w=[wt])
                        else:
                            P.op("dve", "tensor_copy", out=dst, in_=s_.ap[:, 0:cn], r=[s_], w=[wt])
                    else:
                        sc = nrm_t.ap[:, scale_sel, kc:kc + 1]
                        if n % 2 == 0:
                            P.op("act", "activation", out=dst, in_=s_.ap[:, 0:cn], func=AF.Copy, scale=sc,
                                 r=[s_, nrm_t], w=[wt])
                        else:
                            P.op("dve", "tensor_scalar", out=dst, in0=s_.ap[:, 0:cn], scalar1=sc, scalar2=None, op0=ALU.mult,
                                 r=[s_, nrm_t], w=[wt])
                    n += 1
            return wt

        def rms_rstd(ssq, rstd, n, inv_n):
            P.op("act", "activation", out=rstd.ap[:, 0:n], in_=ssq.ap[:, 0:n], func=AF.Ln, bias=epsc.ap[:, 0:1], scale=inv_n,
                 r=[ssq, epsc], w=[rstd])
            P.op("act", "activation", out=rstd.ap[:, 0:n], in_=rstd.ap[:, 0:n], func=AF.Exp, scale=-0.5,
                 r=[rstd], w=[rstd])

        with ExitStack() as es:
            def tile(name, shape, dt):
                return Tile(_chk(nc, es.enter_context(nc.sbuf_tensor(name, list(shape), dt))).ap(), name)

            wi = tile("wi", [128, NKC, WIN_COLS], BF16)
            with ExitStack() as ses:
                st = [Tile(_chk(nc, ses.enter_context(nc.sbuf_tensor("wst%d" % i, [128, 1928], F32))).ap()) for i in range(2)]
                load_weight_bf16(wi, st, w_in, NKC, WIN_COLS, 0, 1928)
                P.barrier()
                P.emit()
            xt = [tile("xt%d" % i, [128, 4, D], F32) for i in range(2)]
            hb = tile("hb", [128, 4, D], BF16)
            junk = tile("junk", [128, D], BF16)
            ssq = tile("ssq", [128, 4], F32)
            rstd = tile("rstd", [128, 4], F32)
            hT = [tile("hT%d" % i, [128, NKC, TT], BF16) for i in range(1)]
            a_st = [tile("a_st%d" % i, [128, 12, TT], BF16) for i in range(1)]
            qk_st = [tile("qk_st%d" % i, [128, 6, TT], BF16) for i in range(1)]
            z_st = [tile("z_st%d" % i, [128, 4, 512], BF16) for i in range(1)]
            v_st = [tile("v_st%d" % i, [128, 4, 256], BF16) for i in range(1)]
            g_st = [tile("g_st%d" % i, [128, 4, 16], F32) for i in range(1)]
            cs_t = [tile("cs_t%d" % i, [128, 2, TT], F32) for i in range(1)]
            sq_t = [tile("sq_t%d" % i, [128, TT], BF16) for i in range(2)]
            rs_t = [tile("rs_t%d" % i, [128, TT], F32) for i in range(2)]
            t1_t = [tile("t1_t%d" % i, [128, TT], F32) for i in range(2)]
            t2_t = [tile("t2_t%d" % i, [128, TT], F32) for i in range(2)]
            gt_t = [tile("gt_t%d" % i, [128, 16], F32) for i in range(2)]

            it = 0
            for sn, T in seqs:
                sc = S[sn]
                ntile = T // TT
                for ti in range(ntile):
                    t0 = ti * TT
                    x_ = xt[it % 2]
                    hT_ = hT[0]
                    P.dma("sp", x_.ap, X[sn][t0:t0 + TT, :].rearrange("(s p) d -> p s d", p=128), w=[x_])
                    cs_ = cs_t[0]
                    P.dma("sp", cs_.ap[:, 0, :], cosT[:, t0:t0 + TT], w=[cs_])
                    P.dma("sp", cs_.ap[:, 1, :], sinT[:, t0:t0 + TT], w=[cs_])
                    for s in range(4):
                        P.op("act", "activation", out=junk.ap, in_=x_.ap[:, s, :], func=AF.Square,
                                                                     accum_out=ssq.ap[:, s:s + 1],
                             r=[x_], w=[junk, ssq])
                    rms_rstd(ssq, rstd, 4, 1.0 / D)
                    for s in range(4):
                        P.op("dve", "tensor_scalar", out=hb.ap[:, s, :], in0=x_.ap[:, s, :],
                                                                        scalar1=rstd.ap[:, s:s + 1], scalar2=None, op0=ALU.mult,
                             r=[x_, rstd], w=[hb])
                    for s in range(4):
                        pb = ps[s % 2]
                        pv = pb.ap.bitcast(BF16).rearrange("p (k t) -> p k t", k=NKC)
                        for kc in range(NKC):
                            P.op("pe", "transpose", out=pv[:, kc, :], in_=hb.ap[:, s, kc * 128:(kc + 1) * 128],
                                                                               identity=identb.ap,
                                 r=[hb, identb], w=[pb])
                        eng = "act" if s % 2 == 0 else "dve"
                        if eng == "act":
                            P.op("act", "copy", out=hT_.ap[:, :, s * 128:(s + 1) * 128], in_=pv, r=[pb], w=[hT_])
                        else:
                            P.op("dve", "tensor_copy", out=hT_.ap[:, :, s * 128:(s + 1) * 128], in_=pv, r=[pb], w=[hT_])

                    def fm_mm(pb, c):
                        for kc in range(NKC):
                            P.op("pe", "matmul", out=pb.ap, lhsT=wi.ap[:, kc, c * 128:(c + 1) * 128], rhs=hT_.ap[:, kc, :],
                                                                            start=(kc == 0), stop=(kc == NKC - 1),
                                 r=[wi, hT_], w=[pb])

                    a_ = a_st[0]
                    for c in range(12):
                        pb = ps[2 + (c % 2)]
                        fm_mm(pb, c)
                        if c % 2 == 0:
                            P.op("act", "copy", out=a_.ap[:, c, :], in_=pb.ap, r=[pb], w=[a_])
                        else:
                            P.op("dve", "tensor_copy", out=a_.ap[:, c, :], in_=pb.ap, r=[pb], w=[a_])
                    P.dma("sp", sc["A"].rearrange("(c p) t -> p c t", p=128)[:, :, t0:t0 + TT], a_.ap, r=[a_], w=[DB(("A", sn))])
                    qk_ = qk_st[0]
                    for j in range(6):
                        pm = ps[2 + (j % 2)]
                        pp = ps[4 + (j % 2)]
                        pc = ps[6 + (j % 2)]
                        sq_, rs_, t1_, t2_ = sq_t[j % 2], rs_t[j % 2], t1_t[j % 2], t2_t[j % 2]
                        fm_mm(pm, 12 + j)
                        fm_mm(pp, 18 + j)
                        wsel = 1 if j < 4 else 3
                        P.op("act", "activation", out=sq_.ap, in_=pm.ap, func=AF.Square, r=[pm], w=[sq_])
                        P.op("pe", "matmul", out=pc.ap, lhsT=onesb.ap, rhs=sq_.ap, start=True, stop=True,
                             r=[onesb, sq_], w=[pc])
                        P.op("act", "activation", out=rs_.ap, in_=pc.ap, func=AF.Ln, bias=epsc.ap[:, 0:1], scale=1.0 / 128,
                             r=[pc, epsc], w=[rs_])
                        P.op("act", "activation", out=rs_.ap, in_=rs_.ap, func=AF.Exp, scale=-0.5, r=[rs_], w=[rs_])
                        P.op("dve", "scalar_tensor_tensor", out=t1_.ap, in0=pm.ap, scalar=hn_t.ap[:, wsel:wsel + 1],
                                                                                               in1=cs_.ap[:, 0, :], op0=ALU.mult, op1=ALU.mult,
                             r=[pm, hn_t, cs_], w=[t1_])
                        P.op("dve", "scalar_tensor_tensor", out=t2_.ap, in0=pp.ap, scalar=hn_t.ap[:, wsel + 1:wsel + 2],
                                                                                               in1=cs_.ap[:, 1, :], op0=ALU.mult, op1=ALU.mult,
                             r=[pp, hn_t, cs_], w=[t2_])
                        P.op("pool", "tensor_tensor", out=t1_.ap, in0=t1_.ap, in1=t2_.ap, op=ALU.add, r=[t1_, t2_], w=[t1_])
                        P.op("pool", "tensor_tensor", out=qk_.ap[:, j, :], in0=t1_.ap, in1=rs_.ap, op=ALU.mult,
                             r=[t1_, rs_], w=[qk_])
                    P.dma("sp", sc["QK"].rearrange("(c p) t -> p c t", p=128)[:, :, t0:t0 + TT], qk_.ap, r=[qk_], w=[DB(("QK", sn))])
                    z_, v_, g_ = z_st[0], v_st[0], g_st[0]
                    for s in range(4):
                        pz = ps[s % 2]
                        pg = ps[6 + (s % 2)]
                        gt_ = gt_t[s % 2]
                        for kc in range(NKC):
                            P.op("pe", "matmul", out=pz.ap, lhsT=hT_.ap[:, kc, s * 128:(s + 1) * 128], rhs=wi.ap[:, kc, TM0:TM0 + 512],
                                                                            start=(kc == 0), stop=(kc == NKC - 1), r=[hT_, wi], w=[pz])
                        for kc in range(NKC):
                            P.op("pe", "matmul", out=pg.ap[:, 0:272], lhsT=hT_.ap[:, kc, s * 128:(s + 1) * 128],
                                                                            rhs=wi.ap[:, kc, TM0 + 512:TM0 + 784],
                                                                            start=(kc == 0), stop=(kc == NKC - 1), r=[hT_, wi], w=[pg])
                        P.op("act", "copy", out=z_.ap[:, s, :], in_=pz.ap, r=[pz], w=[z_])
                        P.op("dve", "tensor_copy", out=v_.ap[:, s, :], in_=pg.ap[:, 16:272], r=[pg], w=[v_])
                        P.op("act", "activation", out=gt_.ap[:, 0:8], in_=pg.ap[:, 0:8], func=AF.Exp, scale=-1.0, r=[pg], w=[gt_])
                        P.op("dve", "tensor_scalar", out=gt_.ap[:, 0:8], in0=gt_.ap[:, 0:8], scalar1=1.0, scalar2=None, op0=ALU.add,
                             r=[gt_], w=[gt_])
                        P.op("dve", "reciprocal", out=g_.ap[:, s, 0:8], in_=gt_.ap[:, 0:8], r=[gt_], w=[g_])
                        P.op("dve", "tensor_tensor", out=gt_.ap[:, 8:16], in0=pg.ap[:, 8:16], in1=gp_b.ap[:, 1, :], op=ALU.add,
                             r=[pg, gp_b], w=[gt_])
                        P.op("act", "activation", out=gt_.ap[:, 8:16], in_=gt_.ap[:, 8:16], func=AF.Exp, r=[gt_], w=[gt_])
                        P.op("dve", "tensor_scalar", out=gt_.ap[:, 8:16], in0=gt_.ap[:, 8:16], scalar1=1.0, scalar2=None, op0=ALU.add,
                             r=[gt_], w=[gt_])
                        P.op("act", "activation", out=gt_.ap[:, 8:16], in_=gt_.ap[:, 8:16], func=AF.Ln, r=[gt_], w=[gt_])
                        P.op("dve", "tensor_tensor", out=g_.ap[:, s, 8:16], in0=gt_.ap[:, 8:16], in1=negA.ap, op=ALU.mult,
                             r=[gt_, negA], w=[g_])
                    P.dma("sp", sc["Z"][t0:t0 + TT, :].rearrange("(s p) c -> p s c", p=128), z_.ap, r=[z_], w=[DB(("Z", sn))])
                    P.dma("sp", sc["V"][t0:t0 + TT, :].rearrange("(s p) c -> p s c", p=128), v_.ap, r=[v_], w=[DB(("V", sn))])
                    P.dma("sp", sc["G"][t0:t0 + TT, :].rearrange("(s p) c -> p s c", p=128), g_.ap, r=[g_], w=[DB(("G", sn))])
                    it += 1
            P.barrier()
            P.emit()

        if stop_after >= 2:
            phases_rest(locals())
    return nc


def host_inputs(inp, Tmax):
    f = np.float32
    w_in = np.asarray(inp["w_in"][0], f)
    perm = rope_perm()
    qB = w_in[:, 2064:2576]
    kB = w_in[:, 2576:2832]
    qBp = qB.reshape(D, 4, 128)[:, :, perm].reshape(D, 512)
    kBp = kB.reshape(D, 2, 128)[:, :, perm].reshape(D, 256)
    w_in_r = np.ascontiguousarray(np.concatenate(
        [w_in[:, 0:1536], qB, kB, qBp, kBp, w_in[:, 1536:2048], w_in[:, 2048:2064], w_in[:, 2832:3088]], axis=1))
    norms = np.stack([np.asarray(inp[k][0], f) for k in ("norm_mix_pre", "norm_mix_post", "norm_mlp_pre", "norm_mlp_post")])
    gparam = np.stack([np.concatenate([np.asarray(inp["A_log_f"][0], f), np.asarray(inp["A_log_b"][0], f)]),
                       np.concatenate([np.asarray(inp["dt_bias_f"][0], f), np.asarray(inp["dt_bias_b"][0], f)])])
    qn = np.asarray(inp["q_norm_w"][0], f)
    kn = np.asarray(inp["k_norm_w"][0], f)
    hnorm = np.stack([np.asarray(inp["gdn_norm_w"][0], f), qn, qn[perm], kn, kn[perm]])
    cosT, sinT = rope_tables(Tmax)
    m = gdn_masks()
    m["ident"] = np.eye(128, dtype=f)
    cst = np.stack([m[n] for n in CONST_NAMES]).astype(f)
    return dict(w_in_r=w_in_r, w_out=np.ascontiguousarray(np.asarray(inp["w_out"][0], f)),
                w_up=np.ascontiguousarray(np.asarray(inp["w_up"][0], f)),
                w_down=np.ascontiguousarray(np.asarray(inp["w_down"][0], f)),
                norms=np.ascontiguousarray(norms),
                nrm_r=np.ascontiguousarray(norms.reshape(4, NKC, 128).transpose(2, 0, 1)),
                hnorm_r=np.ascontiguousarray(hnorm.T),
                conv_w_r=np.ascontiguousarray(np.asarray(inp["conv_w"][0], f).reshape(5, 12, 128).transpose(2, 1, 0)),
                gparam=np.ascontiguousarray(gparam), hnorm=np.ascontiguousarray(hnorm),
                cosT=cosT, sinT=sinT, cst=cst)


_NC_CACHE = {}


def run(inp, Tp, Ts, ncores, debug=False, stop_after=9):
    key = (Tp, Ts, debug, stop_after)
    if key not in _NC_CACHE:
        _NC_CACHE[key] = build(Tp, Ts, debug, stop_after)
    nc = _NC_CACHE[key]
    shared = host_inputs(inp, max(Tp, Ts))
    xp = np.asarray(inp["x_prompt"], np.float32)
    xs = np.asarray(inp["x_sample"], np.float32)
    in_maps = []
    for c in range(ncores):
        m = dict(shared)
        m["x_p"] = np.ascontiguousarray(xp[c])
        m["x_s"] = np.ascontiguousarray(xs[c])
        in_maps.append(m)
    res = run_bass_kernel_spmd(nc, in_maps, core_ids=list(range(ncores)))
    return res.results


def kernel(**inputs):
    res = run(inputs, 8192, 2048, 8)
    yp = np.stack([np.asarray(r["y_p"], np.float32) for r in res])
    ys = np.stack([np.asarray(r["y_s"], np.float32) for r in res])
    return (yp, ys)


def phase_attn(L):
    nc, P, ps, S, seqs, hnorm, identb = L["nc"], L["P"], L["ps"], L["S"], L["seqs"], L["hnorm"], L["identb"]
    DB = L["DB"]
    with ExitStack() as es:
        def tile(name, shape, dt):
            return Tile(_chk(nc, es.enter_context(nc.sbuf_tensor(name, list(shape), dt))).ap(), name)
        Tmax = max(T for _, T in seqs)
        NBmax = Tmax // 128
        wqk_b = tile("wqk_b", [128, 2, 128], F32)
        P.dma("sp", wqk_b.ap[:, 0, :], hnorm[1:2, :].partition_broadcast(128), w=[wqk_b])
        P.dma("sp", wqk_b.ap[:, 1, :], hnorm[3:4, :].partition_broadcast(128), w=[wqk_b])
        mx = tile("mx", [128, 2], F32)
        P.op("dve", "tensor_reduce", out=mx.ap, in_=wqk_b.ap, axis=AX.X, op=ALU.max, apply_absolute_value=True, r=[wqk_b], w=[mx])
        negM = tile("negM", [128, 1], F32)
        P.op("dve", "tensor_tensor", out=negM.ap, in0=mx.ap[:, 0:1], in1=mx.ap[:, 1:2], op=ALU.mult, r=[mx], w=[negM])
        P.op("dve", "tensor_scalar", out=negM.ap, in0=negM.ap, scalar1=-(128.0 ** 0.5), scalar2=None, op0=ALU.mult, r=[negM], w=[negM])
        kT = tile("kT", [128, Tmax], BF16)
        va = tile("va", [128, NBmax, 129], BF16)
        P.op("pool", "memset", ap=va.ap, constant=1.0, w=[va])
        qT = [tile("qT%d" % i, [128, TT], BF16) for i in range(4)]
        pT = [tile("pT%d" % i, [128, TT], BF16) for i in range(6)]
        osb = [tile("osb%d" % i, [128, 4, 129], F32) for i in range(2)]
        rcp = [tile("rcp%d" % i, [128, 4], F32) for i in range(2)]
        onb = [tile("onb%d" % i, [128, 4, 128], BF16) for i in range(2)]
        mst = [tile("mst%d" % i, [128, TT], BF16) for i in range(2)]
        scale = 128.0 ** -0.5
        qi = 0
        npt = 0
        for sn, T in seqs:
            sc = S[sn]
            NB = T // 128
            for g in range(2):
                P.dma("sp", kT.ap[:, 0:T], sc["QK"][(4 + g) * 128:(5 + g) * 128, :], r=[DB(("QK", sn))], w=[kT])
                P.dma("sp", va.ap[:, 0:NB, 0:128], sc["V"][:, g * 128:(g + 1) * 128].rearrange("(b p) c -> p b c", p=128),
                      r=[DB(("V", sn))], w=[va])
                for h in (2 * g, 2 * g + 1):
                    for qp in range(T // (2 * TT)):
                        q0s = [(2 * qp + s) * TT for s in range(2)]
                        q_s = [qT[(2 * qi + s) % 4] for s in range(2)]
                        for s in range(2):
                            P.dma("sp", q_s[s].ap, sc["QK"][h * 128:(h + 1) * 128, q0s[s]:q0s[s] + TT], r=[DB(("QK", sn))], w=[q_s[s]])
                        sbank = [[ps[2 * s], ps[2 * s + 1]] for s in range(2)]
                        accb = [[ps[4 + 2 * s], ps[5 + 2 * s]] for s in range(2)]
                        accv = [[b.ap[:, 0:258].rearrange("p (a c) -> p a c", a=2) for b in accb[s]] for s in range(2)]

                        def qk(s, kb):
                            sb = sbank[s][kb % 2]
                            P.op("pe", "matmul", out=sb.ap, lhsT=kT.ap[:, kb * 128:(kb + 1) * 128], rhs=q_s[s].ap, start=True, stop=True,
                                 r=[kT, q_s[s]], w=[sb])

                        qk(0, 0)
                        qk(1, 0)
                        for kb in range(NB):
                            p_s = []
                            for s in range(2):
                                if kb + 1 < NB:
                                    qk(s, kb + 1)
                                sb = sbank[s][kb % 2]
                                p_ = pT[npt % 6]
                                npt += 1
                                P.op("act", "activation", out=p_.ap, in_=sb.ap, func=AF.Exp, bias=negM.ap[:, 0:1], scale=scale,
                                     r=[sb, negM], w=[p_])
                                p_s.append(p_)
                            for s in range(2):
                                for qs in range(4):
                                    P.op("pe", "matmul", out=accv[s][qs // 2][:, qs % 2, :], lhsT=p_s[s].ap[:, qs * 128:(qs + 1) * 128], rhs=va.ap[:, kb, :],
                                         start=(kb == 0 and qs % 2 == 0), stop=(kb == NB - 1), skip_group_check=True,
                                         r=[p_s[s], va], w=[accb[s][qs // 2]])
                        for s in range(2):
                            o_, r_, n_, m_ = osb[s], rcp[s], onb[s], mst[s]
                            P.op("dve", "tensor_copy", out=o_.ap[:, 0:2, :], in_=accv[s][0], r=[accb[s][0]], w=[o_])
                            P.op("dve", "tensor_copy", out=o_.ap[:, 2:4, :], in_=accv[s][1], r=[accb[s][1]], w=[o_])
                            P.op("dve", "reciprocal", out=r_.ap, in_=o_.ap[:, :, 128], r=[o_], w=[r_])
                            P.op("dve", "tensor_tensor", out=n_.ap, in0=o_.ap[:, :, 0:128], in1=r_.ap.unsqueeze(2).to_broadcast([128, 4, 128]),
                                 op=ALU.mult, r=[o_, r_], w=[n_])
                            tb = sbank[s][0]
                            tv = tb.ap.bitcast(BF16)[:, 0:512].rearrange("p (a c) -> p a c", a=4)
                            for qs in range(4):
                                P.op("pe", "transpose", out=tv[:, qs, :], in_=n_.ap[:, qs, :], identity=identb.ap, r=[n_, identb], w=[tb])
                            P.op("dve", "tensor_copy", out=m_.ap.rearrange("p (a c) -> p a c", a=4), in_=tv, r=[tb], w=[m_])
                            P.dma("sp", sc["MIX"][512 + h * 128:512 + (h + 1) * 128, q0s[s]:q0s[s] + TT], m_.ap, r=[m_], w=[DB(("MIX", sn))])
                        qi += 1
        P.barrier()
        P.emit()


def phase_mlp(L):
    nc, P, ps, S, seqs, identb = L["nc"], L["P"], L["ps"], L["S"], L["seqs"], L["identb"]
    DB, X, Y, rms_rstd, nrm = L["DB"], L["X"], L["Y"], L["rms_rstd"], L["nrm"]
    load_weight_bf16 = L["load_weight_bf16"]

    def make_norm_resid(mo, junk, ssq, rstd, nrm_b):
        def norm_resid(src_banks, resid_ap, out_ap, which, deps_resid, deps_out):
            P.op("act", "copy", out=mo.ap[:, 0:512], in_=src_banks[0].ap, r=[src_banks[0]], w=[mo])
            P.op("dve", "tensor_copy", out=mo.ap[:, 512:1024], in_=src_banks[1].ap, r=[src_banks[1]], w=[mo])
            P.op("act", "activation", out=junk.ap, in_=mo.ap, func=AF.Square, accum_out=ssq.ap[:, 0:1], r=[mo], w=[junk, ssq])
            rms_rstd(ssq, rstd, 1, 1.0 / D)
            P.op("dve", "scalar_tensor_tensor", out=mo.ap, in0=mo.ap, scalar=rstd.ap[:, 0:1], in1=nrm_b.ap[:, which, :],
                 op0=ALU.mult, op1=ALU.mult, r=[mo, rstd, nrm_b], w=[mo])
            P.op("pool", "tensor_tensor", out=out_ap, in0=mo.ap, in1=resid_ap, op=ALU.add, r=[mo] + deps_resid, w=deps_out)
        return norm_resid

    with ExitStack() as es:
        def tile(name, shape, dt):
            return Tile(_chk(nc, es.enter_context(nc.sbuf_tensor(name, list(shape), dt))).ap(), name)
        wo = tile("wo", [128, 8, D], BF16)
        with ExitStack() as ses:
            st = [Tile(_chk(nc, ses.enter_context(nc.sbuf_tensor("wst4a_%d" % i, [128, 1024], F32))).ap()) for i in range(2)]
            load_weight_bf16(wo, st, L["w_out"], 8, D, None, 1024)
            P.barrier()
            P.emit()
        nrm_b = tile("nrm_b", [128, 2, D], F32)
        P.dma("sp", nrm_b.ap[:, 0, :], nrm[1:2, :].partition_broadcast(128), w=[nrm_b])
        P.dma("sp", nrm_b.ap[:, 1, :], nrm[3:4, :].partition_broadcast(128), w=[nrm_b])
        xt = tile("x4", [128, 4, D], F32)
        mT = [tile("mT%d" % i, [128, 8, TT], BF16) for i in range(2)]
        mo = tile("mo", [128, D], F32)
        junk = tile("junk4", [128, D], BF16)
        ssq = tile("ssq4", [128, 1], F32)
        rstd = tile("rstd4", [128, 1], F32)
        norm_resid = make_norm_resid(mo, junk, ssq, rstd, nrm_b)
        it = 0
        for sn, T in seqs:
            sc = S[sn]
            for ti in range(T // TT):
                t0 = ti * TT
                m_ = mT[it % 2]
                P.dma("sp", xt.ap, X[sn][t0:t0 + TT, :].rearrange("(s p) d -> p s d", p=128), w=[xt])
                P.dma("sp", m_.ap, sc["MIX"].rearrange("(k p) t -> p k t", p=128)[:, :, t0:t0 + TT], r=[DB(("MIX", sn))], w=[m_])
                for s in range(4):
                    banks = [ps[2 * (s % 2)], ps[2 * (s % 2) + 1]]
                    for cg in range(2):
                        for kc in range(8):
                            P.op("pe", "matmul", out=banks[cg].ap, lhsT=m_.ap[:, kc, s * 128:(s + 1) * 128], rhs=wo.ap[:, kc, cg * 512:(cg + 1) * 512],
                                 start=(kc == 0), stop=(kc == 7), r=[m_, wo], w=[banks[cg]])
                    norm_resid(banks, xt.ap[:, s, :], xt.ap[:, s, :], 0, [xt], [xt])
                P.dma("sp", sc["X1"][t0:t0 + TT, :].rearrange("(s p) d -> p s d", p=128), xt.ap, r=[xt], w=[DB(("X1", sn))])
                it += 1
        P.barrier()
        P.emit()
    T4 = 256
    with ExitStack() as es:
        def tile(name, shape, dt):
            return Tile(_chk(nc, es.enter_context(nc.sbuf_tensor(name, list(shape), dt))).ap(), name)
        wu = tile("wu", [128, 8, DFF], BF16)
        wd = tile("wd", [128, 32, D], BF16)
        with ExitStack() as ses:
            st = [Tile(_chk(nc, ses.enter_context(nc.sbuf_tensor("wst4_%d" % i, [128, 2048], F32))).ap()) for i in range(2)]
            load_weight_bf16(wu, st, L["w_up"], 8, DFF, 2, 2048)
            load_weight_bf16(wd, st, L["w_down"], 32, D, None, 1024)
            P.barrier()
            P.emit()
        nrm_b = tile("nrm_b2", [128, D], F32)
        P.dma("sp", nrm_b.ap, nrm[3:4, :].partition_broadcast(128), w=[nrm_b])
        xt = tile("x4b", [128, D], F32)
        mo = tile("mo2", [128, D], F32)
        ssq = tile("ssq4b", [128, 1], F32)
        rstd = tile("rstd4b", [128, 1], F32)
        ssq2 = tile("ssq4c", [128, 1], F32)
        rstd2 = tile("rstd4c", [128, 1], F32)
        h2 = tile("h2", [128, D], BF16)
        h2T = [tile("h2T%d" % i, [128, 8, T4], BF16) for i in range(2)]
        fT = tile("fT", [128, 32, T4], BF16)
        rl = [tile("rl%d" % i, [128, T4], F32) for i in range(2)]
        jk = tile("jk4", [128, D], BF16)
        tiles = [(sn, T, ti) for sn, T in seqs for ti in range(T // T4)]

        def prologue(i):
            sn, T, ti = tiles[i]
            sc = S[sn]
            t0 = ti * T4
            hT_ = h2T[i % 2]
            for s in range(2):
                P.dma("sp", xt.ap, sc["X1"][t0 + s * 128:t0 + (s + 1) * 128, :], r=[DB(("X1", sn))], w=[xt])
                P.op("act", "activation", out=h2.ap, in_=xt.ap, func=AF.Square, accum_out=ssq.ap[:, 0:1], r=[xt], w=[h2, ssq])
                rms_rstd(ssq, rstd, 1, 1.0 / D)
                P.op("dve", "tensor_scalar", out=h2.ap, in0=xt.ap, scalar1=rstd.ap[:, 0:1], scalar2=None, op0=ALU.mult,
                     r=[xt, rstd], w=[h2])
                pb = ps[2 + s]
                pv = pb.ap.bitcast(BF16).rearrange("p (k t) -> p k t", k=8)
                for kc in range(8):
                    P.op("pe", "transpose", out=pv[:, kc, :], in_=h2.ap[:, kc * 128:(kc + 1) * 128], identity=identb.ap,
                         r=[h2, identb], w=[pb])
                P.op("act", "copy", out=hT_.ap[:, :, s * 128:(s + 1) * 128], in_=pv, r=[pb], w=[hT_])

        prologue(0)
        for i, (sn, T, ti) in enumerate(tiles):
            sc = S[sn]
            t0 = ti * T4
            hT_ = h2T[i % 2]
            for fc in range(32):
                pb = ps[4 + (fc % 4)]
                for kc in range(8):
                    P.op("pe", "matmul", out=pb.ap[:, 0:T4], lhsT=wu.ap[:, kc, fc * 128:(fc + 1) * 128], rhs=hT_.ap[:, kc, :],
                         start=(kc == 0), stop=(kc == 7), r=[wu, hT_], w=[pb])
                rl_ = rl[fc % 2]
                P.op("act", "activation", out=rl_.ap, in_=pb.ap[:, 0:T4], func=AF.Relu, r=[pb], w=[rl_])
                P.op("dve" if fc % 2 == 0 else "pool", "tensor_tensor", out=fT.ap[:, fc, :], in0=rl_.ap, in1=rl_.ap, op=ALU.mult, r=[rl_], w=[fT])
            if i + 1 < len(tiles):
                prologue(i + 1)
            for s in range(2):
                banks = [ps[0], ps[1]]
                for cg in range(2):
                    for fc in range(32):
                        P.op("pe", "matmul", out=banks[cg].ap, lhsT=fT.ap[:, fc, s * 128:(s + 1) * 128], rhs=wd.ap[:, fc, cg * 512:(cg + 1) * 512],
                             start=(fc == 0), stop=(fc == 31), r=[fT, wd], w=[banks[cg]])
                P.op("act", "copy", out=mo.ap[:, 0:512], in_=banks[0].ap, r=[banks[0]], w=[mo])
                P.op("dve", "tensor_copy", out=mo.ap[:, 512:1024], in_=banks[1].ap, r=[banks[1]], w=[mo])
                P.op("act", "activation", out=jk.ap, in_=mo.ap, func=AF.Square, accum_out=ssq2.ap[:, 0:1], r=[mo], w=[jk, ssq2])
                rms_rstd(ssq2, rstd2, 1, 1.0 / D)
                P.op("dve", "scalar_tensor_tensor", out=mo.ap, in0=mo.ap, scalar=rstd2.ap[:, 0:1], in1=nrm_b.ap,
                     op0=ALU.mult, op1=ALU.mult, r=[mo, rstd2, nrm_b], w=[mo])
                P.dma("sp", xt.ap, sc["X1"][t0 + s * 128:t0 + (s + 1) * 128, :], r=[DB(("X1", sn))], w=[xt])
                P.op("pool", "tensor_tensor", out=xt.ap, in0=mo.ap, in1=xt.ap, op=ALU.add, r=[mo, xt], w=[xt])
                P.dma("sp", Y[sn][t0 + s * 128:t0 + (s + 1) * 128, :], xt.ap, r=[xt], w=[DB(("Y", sn))])
        P.barrier()
        P.emit()


def phases_rest(L):
    sa = L["stop_after"]
    if sa >= 2:
        phase_gdn(L)
    if sa >= 3:
        phase_attn(L)
    if sa >= 4:
        phase_mlp(L)


def phase_gdn(L):
    nc, P, ps, S, seqs, identb, onesb = L["nc"], L["P"], L["ps"], L["S"], L["seqs"], L["identb"], L["onesb"]
    DB, epsc, rms_rstd = L["DB"], L["epsc"], L["rms_rstd"]
    with ExitStack() as es:
        def tile(name, shape, dt):
            return Tile(_chk(nc, es.enter_context(nc.sbuf_tensor(name, list(shape), dt))).ap(), name)
        cw_t = tile("cw_t", [128, 12, 5], F32)
        P.dma("sp", cw_t.ap, L["conv_w_r"], w=[cw_t])
        dg = tile("dg", [128, 12, 5, 128], BF16)
        for c in range(12):
            for kk in range(5):
                P.op("dve" if (c * 5 + kk) % 2 == 0 else "pool", "tensor_scalar", out=dg.ap[:, c, kk, :], in0=identb.ap,
                     scalar1=cw_t.ap[:, c, kk:kk + 1], scalar2=None, op0=ALU.mult, r=[identb, cw_t], w=[dg])
        raw = [tile("raw%d" % i, [128, 12, TT + 4], BF16) for i in range(2)]
        sl = [tile("sl%d" % i, [128, TT], F32) for i in range(2)]
        sq = [tile("sq%d" % i, [128, TT], BF16) for i in range(2)]
        rs = [tile("rs%d" % i, [128, TT], F32) for i in range(2)]
        fm = [tile("fm%d" % i, [128, TT], BF16) for i in range(3)]
        tk = [tile("tk%d" % i, [128, 4, 128], BF16) for i in range(2)]
        it = 0
        n = 0
        for sn, T in seqs:
            sc = S[sn]
            for ti in range(T // TT):
                t0 = ti * TT
                r_ = raw[it % 2]
                lo, hi = max(t0 - 2, 0), min(t0 + TT + 2, T)
                if t0 == 0:
                    P.op("pool", "memset", ap=r_.ap[:, :, 0:2], constant=0.0, w=[r_])
                if t0 + TT == T:
                    P.op("pool", "memset", ap=r_.ap[:, :, TT + 2:TT + 4], constant=0.0, w=[r_])
                P.dma("sp", r_.ap[:, :, lo - (t0 - 2):hi - (t0 - 2)], sc["A"].rearrange("(c p) t -> p c t", p=128)[:, :, lo:hi],
                      r=[DB(("A", sn))], w=[r_])
                for c in range(12):
                    kind, h = c // 4, c % 4
                    s_ = sl[n % 2]
                    pcv = ps[4 + (n % 4)]
                    for kk in range(5):
                        P.op("pe", "matmul", out=pcv.ap, lhsT=dg.ap[:, c, kk, :], rhs=r_.ap[:, c, kk:kk + TT], start=(kk == 0), stop=(kk == 4),
                             r=[dg, r_], w=[pcv])
                    P.op("act", "activation", out=s_.ap, in_=pcv.ap, func=AF.Silu, r=[pcv], w=[s_])
                    f_ = fm[n % 3]
                    if kind == 2:
                        P.op("pool", "tensor_copy", out=f_.ap, in_=s_.ap, r=[s_], w=[f_])
                    else:
                        q_, rs_ = sq[n % 2], rs[n % 2]
                        pb = ps[n % 2]
                        P.op("act", "activation", out=q_.ap, in_=s_.ap, func=AF.Square, r=[s_], w=[q_])
                        P.op("pe", "matmul", out=pb.ap, lhsT=onesb.ap, rhs=q_.ap, start=True, stop=True, r=[onesb, q_], w=[pb])
                        P.op("act", "activation", out=rs_.ap, in_=pb.ap, func=AF.Ln, bias=epsc.ap[:, 0:1], scale=1.0, r=[pb, epsc], w=[rs_])
                        P.op("act", "activation", out=rs_.ap, in_=rs_.ap, func=AF.Exp, scale=-0.5, r=[rs_], w=[rs_])
                        P.op("dve", "scalar_tensor_tensor", out=f_.ap, in0=s_.ap, scalar=(128.0 ** -0.5 if kind == 0 else 1.0), in1=rs_.ap,
                             op0=ALU.mult, op1=ALU.mult, r=[s_, rs_], w=[f_])
                        dst = sc["GQ"] if kind == 0 else sc["GK"]
                        P.dma("sp", dst[h * 128:(h + 1) * 128, t0:t0 + TT], f_.ap, r=[f_], w=[DB(("GQK", sn))])
                    if kind >= 1:
                        tb = ps[2 + (n % 2)]
                        tv = tb.ap.bitcast(BF16)[:, 0:512].rearrange("p (a c) -> p a c", a=4)
                        for s in range(4):
                            P.op("pe", "transpose", out=tv[:, s, :], in_=f_.ap[:, s * 128:(s + 1) * 128], identity=identb.ap,
                                 r=[f_, identb], w=[tb])
                        t_ = tk[n % 2]
                        P.op("act", "copy", out=t_.ap, in_=tv, r=[tb], w=[t_])
                        dst = sc["GKt"] if kind == 1 else sc["GVt"]
                        P.dma("sp", dst[t0:t0 + TT, h * 128:(h + 1) * 128].rearrange("(s p) d -> p s d", p=128), t_.ap,
                              r=[t_], w=[DB(("GKV", sn))])
                    n += 1
                it += 1
        P.barrier()
        P.emit()
    if L["stop_after"] == 2 and L.get("gdn_pre_only"):
        return
    with ExitStack() as es:
        def tile(name, shape, dt):
            return Tile(_chk(nc, es.enter_context(nc.sbuf_tensor(name, list(shape), dt))).ap(), name)

        cf = tile("cf", [128, len(CONST_NAMES), 128], F32)
        P.dma("sp", cf.ap, L["cst"].rearrange("n p j -> p n j"), w=[cf])
        C = {n: cf.ap[:, i, :] for i, n in enumerate(CONST_NAMES)}
        onesf = tile("onesf", [128, 128], F32)
        P.op("pool", "memset", ap=onesf.ap, constant=1.0, w=[onesf])
        gnw_b = tile("gnw_b", [128, 128], F32)
        P.dma("sp", gnw_b.ap, L["hnorm"][0:1, :].partition_broadcast(128), w=[gnw_b])

        def b3(ap2):
            return ap2.unsqueeze(1).to_broadcast([128, 4, 128])

        def bj(ap2):
            return ap2.unsqueeze(2).to_broadcast([128, 4, 128])

        def v4(t):
            return t.ap.rearrange("p (h j) -> p h j", h=4)

        NU = 2
        Rd = []
        R = []
        for d in range(2):
            rd = {"S4": tile("S4_%d" % d, [128, 512], F32), "S4b": tile("S4b_%d" % d, [128, 512], BF16),
                  }
            Rd.append(rd)
            Ru = []
            for u in range(NU):
                r = {}
                sfx = "%d_%d" % (d, u)
                for nm in ("KT", "QT", "Kt", "Vt", "A0", "A1", "B0", "B1", "Aqk", "AqkT", "PTb", "Vb", "kbg", "ke", "wT", "vn"):
                    r[nm] = tile(nm + sfx, [128, 512], BF16)
                for nm in ("Gm", "Ep", "D4", "Dsb", "PT", "u4"):
                    r[nm] = tile(nm + sfx, [128, 512], F32)
                r["Oq"], r["O4"], r["sz"], r["o2"] = r["Gm"], r["Ep"], r["D4"], r["Dsb"]
                r["zb"], r["onb"], r["mxs"] = r["A0"], r["A1"], r["B0"]
                r["G"] = tile("G" + sfx, [128, 16], F32)
                r["sm"] = tile("sm" + sfx, [128, 16], F32)
                r["ex"] = tile("ex" + sfx, [128, 16], F32)
                r["bgc"] = tile("bgc" + sfx, [128, 4], F32)
                r["ssq"] = tile("gssq" + sfx, [128, 4], F32)
                r["rstd"] = tile("grstd" + sfx, [128, 4], F32)
                r["banks"] = [ps[2 * (NU * d + u)], ps[2 * (NU * d + u) + 1]]
                r["nb"] = 0
                Ru.append(r)
            R.append(Ru)

        stored = {}

        def nbank(r):
            b = r["banks"][r["nb"] % 2]
            r["nb"] += 1
            return b

        def unit(sn, T, d, b, step, NB):
            sc = S[sn]
            r = R[d][step % NU]
            rd = Rd[d]
            t0 = b * 128
            KT, QT, Kt, Vt, G = r["KT"], r["QT"], r["Kt"], r["Vt"], r["G"]
            P.dma("sp", v4(KT), sc["GK"].rearrange("(h p) t -> p h t", p=128)[:, :, t0:t0 + 128], r=[DB(("GQK", sn))], w=[KT])
            P.dma("sp", v4(QT), sc["GQ"].rearrange("(h p) t -> p h t", p=128)[:, :, t0:t0 + 128], r=[DB(("GQK", sn))], w=[QT])
            P.dma("sp", Kt.ap, sc["GKt"][t0:t0 + 128, :], r=[DB(("GKV", sn))], w=[Kt])
            P.dma("sp", Vt.ap, sc["GVt"][t0:t0 + 128, :], r=[DB(("GKV", sn))], w=[Vt])
            P.dma("sp", G.ap, sc["G"][t0:t0 + 128, :], r=[DB(("G", sn))], w=[G])
            yield
            beta = G.ap[:, 4 * d:4 * d + 4]
            gg = G.ap[:, 8 + 4 * d:12 + 4 * d]
            cum = C["cum_f"] if d == 0 else C["cum_b"]
            pos = C["pos_f"] if d == 0 else C["pos_b"]
            strict = C["strict_f"] if d == 0 else C["strict_b"]
            sm, ex, bgc = r["sm"], r["ex"], r["bgc"]
            Gm, Ep, D4, Dsb = r["Gm"], r["Ep"], r["D4"], r["Dsb"]
            pb = nbank(r)
            P.op("pe", "matmul", out=pb.ap[:, 0:4], lhsT=cum, rhs=gg, start=True, stop=True, r=[cf, G], w=[pb])
            P.op("pe", "matmul", out=pb.ap[:, 4:8], lhsT=C["same"], rhs=gg, start=True, stop=True, r=[cf, G], w=[pb])
            P.op("pe", "matmul", out=pb.ap[:, 8:12], lhsT=C["c0"], rhs=gg, start=True, stop=True, r=[cf, G], w=[pb])
            P.op("pe", "matmul", out=pb.ap[:, 12:16], lhsT=C["c1"], rhs=gg, start=True, stop=True, r=[cf, G], w=[pb])
            P.op("pool", "tensor_tensor", out=v4(Gm), in0=b3(cum), in1=bj(gg), op=ALU.mult, r=[cf, G], w=[Gm])
            yield
            P.op("dve", "tensor_copy", out=sm.ap, in_=pb.ap[:, 0:16], r=[pb], w=[sm])
            P.op("dve", "tensor_tensor", out=sm.ap[:, 4:8], in0=sm.ap[:, 4:8], in1=sm.ap[:, 0:4], op=ALU.subtract, r=[sm], w=[sm])
            pe_ = nbank(r)
            P.op("pe", "matmul", out=pe_.ap, lhsT=onesf.ap, rhs=Gm.ap, start=True, stop=True, r=[onesf, Gm], w=[pe_])
            yield
            P.op("act", "activation", out=ex.ap, in_=sm.ap, func=AF.Exp, r=[sm], w=[ex])
            P.op("dve", "tensor_tensor", out=v4(Ep), in0=pe_.ap.rearrange("p (h j) -> p h j", h=4), in1=b3(pos), op=ALU.add,
                 r=[pe_, cf], w=[Ep])
            yield
            P.op("dve", "tensor_tensor", out=bgc.ap, in0=ex.ap[:, 0:4], in1=beta, op=ALU.mult, r=[ex, G], w=[bgc])
            for h in range(4):
                P.op("act", "activation", out=v4(D4)[:, h, :], in_=v4(Ep)[:, h, :], func=AF.Exp, bias=sm.ap[:, h:h + 1], scale=-1.0,
                     r=[Ep, sm], w=[D4])
            pk, pq = nbank(r), nbank(r)
            for h in range(4):
                P.op("pe", "matmul", out=pk.ap[:, h * 128:(h + 1) * 128], lhsT=v4(KT)[:, h, :], rhs=v4(KT)[:, h, :], start=True, stop=True,
                     r=[KT], w=[pk])
            for h in range(4):
                P.op("pe", "matmul", out=pq.ap[:, h * 128:(h + 1) * 128], lhsT=v4(QT)[:, h, :], rhs=v4(KT)[:, h, :], start=True, stop=True,
                     r=[QT, KT], w=[pq])
            yield
            Vb, kbg, ke, u4, wT = r["Vb"], r["kbg"], r["ke"], r["u4"], r["wT"]
            P.op("pool", "tensor_tensor", out=v4(Dsb), in0=v4(D4), in1=b3(strict), op=ALU.mult, r=[D4, cf], w=[Dsb])
            P.op("pool", "tensor_tensor", out=v4(Dsb), in0=v4(Dsb), in1=bj(beta), op=ALU.mult, r=[Dsb, G], w=[Dsb])
            A, B = [r["A0"], r["A1"]], [r["B0"], r["B1"]]
            Aqk, AqkT, PT, PTb = r["Aqk"], r["AqkT"], r["PT"], r["PTb"]
            P.op("dve", "tensor_tensor", out=Aqk.ap, in0=pq.ap, in1=D4.ap, op=ALU.mult, r=[pq, D4], w=[Aqk])
            yield
            P.op("dve", "tensor_tensor", out=A[0].ap, in0=pk.ap, in1=Dsb.ap, op=ALU.mult, r=[pk, Dsb], w=[A[0]])
            P.op("pool", "tensor_tensor", out=v4(Vb), in0=v4(Vt), in1=bj(beta), op=ALU.mult, r=[Vt, G], w=[Vb])
            P.op("pool", "tensor_tensor", out=v4(kbg), in0=v4(Kt), in1=bj(bgc.ap), op=ALU.mult, r=[Kt, bgc], w=[kbg])
            P.op("pool", "tensor_tensor", out=v4(ke), in0=v4(Kt), in1=bj(ex.ap[:, 4:8]), op=ALU.mult, r=[Kt, ex], w=[ke])
            yield
            tb = nbank(r)
            tvA = tb.ap.bitcast(BF16)[:, 0:512].rearrange("p (a c) -> p a c", a=4)
            tvQ = tb.ap.bitcast(BF16)[:, 512:1024].rearrange("p (a c) -> p a c", a=4)
            for h in range(4):
                P.op("pe", "transpose", out=tvA[:, h, :], in_=v4(A[0])[:, h, :], identity=identb.ap, r=[A[0], identb], w=[tb])
            for h in range(4):
                P.op("pe", "transpose", out=tvQ[:, h, :], in_=v4(Aqk)[:, h, :], identity=identb.ap, r=[Aqk, identb], w=[tb])
            yield
            P.op("act", "copy", out=v4(B[0]), in_=tvA, r=[tb], w=[B[0]])
            P.op("act", "copy", out=v4(AqkT), in_=tvQ, r=[tb], w=[AqkT])
            yield
            P.op("pool", "tensor_tensor", out=v4(PT), in0=b3(C["ident"]), in1=v4(B[0]), op=ALU.subtract, r=[cf, B[0]], w=[PT])
            P.op("act", "copy", out=PTb.ap, in_=PT.ap, r=[PT], w=[PTb])
            for k in range(5):
                Ak, Bk, An, Bn = A[k % 2], B[k % 2], A[(k + 1) % 2], B[(k + 1) % 2]
                pa = nbank(r)
                for h in range(4):
                    P.op("pe", "matmul", out=pa.ap[:, h * 128:(h + 1) * 128], lhsT=v4(Bk)[:, h, :], rhs=v4(Ak)[:, h, :], start=True, stop=True,
                         r=[Ak, Bk], w=[pa])
                if k < 4:
                    pbb = nbank(r)
                    for h in range(4):
                        P.op("pe", "matmul", out=pbb.ap[:, h * 128:(h + 1) * 128], lhsT=v4(Ak)[:, h, :], rhs=v4(Bk)[:, h, :], start=True, stop=True,
                             r=[Ak, Bk], w=[pbb])
                yield
                P.op("act", "copy", out=An.ap, in_=pa.ap, r=[pa], w=[An])
                if k < 4:
                    P.op("dve", "tensor_copy", out=Bn.ap, in_=pbb.ap, r=[pbb], w=[Bn])
                yield
                pd = nbank(r)
                for h in range(4):
                    P.op("pe", "matmul", out=pd.ap[:, h * 128:(h + 1) * 128], lhsT=v4(An)[:, h, :], rhs=v4(PTb)[:, h, :], start=True, stop=True,
                         r=[An, PTb], w=[pd])
                yield
                P.op("dve", "tensor_tensor", out=PT.ap, in0=pd.ap, in1=PT.ap, op=ALU.add, r=[pd, PT], w=[PT])
                P.op("act", "copy", out=PTb.ap, in_=PT.ap, r=[PT], w=[PTb])
                yield
            pu, pw = nbank(r), nbank(r)
            for h in range(4):
                P.op("pe", "matmul", out=pu.ap[:, h * 128:(h + 1) * 128], lhsT=v4(PTb)[:, h, :], rhs=v4(Vb)[:, h, :], start=True, stop=True,
                     r=[PTb, Vb], w=[pu])
            for h in range(4):
                P.op("pe", "matmul", out=pw.ap[:, h * 128:(h + 1) * 128], lhsT=v4(kbg)[:, h, :], rhs=v4(PTb)[:, h, :], start=True, stop=True,
                     r=[PTb, kbg], w=[pw])
            yield
            P.op("act", "copy", out=u4.ap, in_=pu.ap, r=[pu], w=[u4])
            P.op("dve", "tensor_copy", out=wT.ap, in_=pw.ap, r=[pw], w=[wT])
            yield
            while rd["rec_done"] < step:
                yield
            S4, S4b, vn, Oq, O4 = rd["S4"], rd["S4b"], r["vn"], r["Oq"], r["O4"]
            for c in ((0, 1) if d == 0 else (1, 0)):
                pc = slice(c * 64, (c + 1) * 64)
                p1, po1 = nbank(r), nbank(r)
                for h in range(4):
                    P.op("pe", "matmul", out=p1.ap[pc, h * 128:(h + 1) * 128], lhsT=v4(wT)[:, h, pc], rhs=v4(S4b)[:, h, :], start=True, stop=True,
                         r=[wT, S4b], w=[p1])
                for h in range(4):
                    P.op("pe", "matmul", out=po1.ap[pc, h * 128:(h + 1) * 128], lhsT=v4(QT)[:, h, pc], rhs=v4(S4b)[:, h, :], start=True, stop=True,
                         r=[QT, S4b], w=[po1])
                yield
                P.op("dve", "tensor_tensor", out=vn.ap[pc, :], in0=u4.ap[pc, :], in1=p1.ap[pc, :], op=ALU.subtract, r=[u4, p1], w=[vn])
                P.op("dve", "tensor_tensor", out=v4(Oq)[pc], in0=po1.ap[pc, :].rearrange("p (h j) -> p h j", h=4),
                     in1=ex.ap[pc, 0:4].unsqueeze(2).to_broadcast([64, 4, 128]), op=ALU.mult, r=[po1, ex], w=[Oq])
                yield
                po2, pds = nbank(r), nbank(r)
                for h in range(4):
                    P.op("pe", "matmul", out=pds.ap[:, h * 128:(h + 1) * 128], lhsT=v4(ke)[pc, h, :], rhs=v4(vn)[pc, h, :], start=True, stop=True,
                         r=[ke, vn], w=[pds])
                for h in range(4):
                    P.op("pe", "matmul", out=po2.ap[pc, h * 128:(h + 1) * 128], lhsT=v4(AqkT)[pc, h, pc], rhs=v4(vn)[pc, h, :], start=True, stop=True,
                         r=[AqkT, vn], w=[po2])
                P.op("pool", "tensor_tensor", out=v4(S4), in0=v4(S4), in1=bj(ex.ap[:, 8 + 4 * c:12 + 4 * c]), op=ALU.mult, r=[S4, ex], w=[S4])
                yield
                P.op("dve", "tensor_tensor", out=S4.ap, in0=pds.ap, in1=S4.ap, op=ALU.add, r=[pds, S4], w=[S4])
                P.op("act", "copy", out=S4b.ap, in_=S4.ap, r=[S4], w=[S4b])
                P.op("dve", "tensor_tensor", out=O4.ap[pc, :], in0=po2.ap[pc, :], in1=Oq.ap[pc, :], op=ALU.add, r=[po2, Oq], w=[O4])
                yield
            rd["rec_done"] = step + 1
            mine, other = ("OF", "OB") if d == 0 else ("OB", "OF")
            if step < NB // 2:
                P.dma("sp", sc[mine][t0:t0 + 128, :], O4.ap, r=[O4], w=[DB((mine, sn, b))])
                stored[(sn, mine, b)] = True
            else:
                o2, sz, zb, onb, mxs = r["o2"], r["sz"], r["zb"], r["onb"], r["mxs"]
                ssq, rstd = r["ssq"], r["rstd"]
                while not stored.get((sn, other, b)):
                    yield
                P.dma("sp", o2.ap, sc[other][t0:t0 + 128, :], r=[DB((other, sn, b))], w=[o2])
                P.dma("sp", zb.ap, sc["Z"][t0:t0 + 128, :], r=[DB(("Z", sn))], w=[zb])
                yield
                P.op("pool", "tensor_tensor", out=o2.ap, in0=o2.ap, in1=O4.ap, op=ALU.add, r=[o2, O4], w=[o2])
                P.op("pool", "tensor_tensor", out=sz.ap, in0=o2.ap, in1=o2.ap, op=ALU.mult, r=[o2], w=[sz])
                yield
                P.op("dve", "tensor_reduce", out=ssq.ap, in_=v4(sz), axis=AX.X, op=ALU.add, r=[sz], w=[ssq])
                rms_rstd(ssq, rstd, 4, 1.0 / 128)
                P.op("act", "activation", out=sz.ap, in_=zb.ap, func=AF.Silu, r=[zb], w=[sz])
                yield
                P.op("dve", "tensor_tensor", out=v4(o2), in0=v4(o2), in1=bj(rstd.ap), op=ALU.mult, r=[o2, rstd], w=[o2])
                P.op("pool", "tensor_tensor", out=v4(o2), in0=v4(o2), in1=b3(gnw_b.ap), op=ALU.mult, r=[o2, gnw_b], w=[o2])
                yield
                P.op("dve", "tensor_tensor", out=onb.ap, in0=o2.ap, in1=sz.ap, op=ALU.mult, r=[o2, sz], w=[onb])
                yield
                tb2 = nbank(r)
                tv2 = tb2.ap.bitcast(BF16)[:, 0:512].rearrange("p (a c) -> p a c", a=4)
                for h in range(4):
                    P.op("pe", "transpose", out=tv2[:, h, :], in_=v4(onb)[:, h, :], identity=identb.ap, r=[onb, identb], w=[tb2])
                yield
                P.op("act", "copy", out=v4(mxs), in_=tv2, r=[tb2], w=[mxs])
                P.dma("sp", sc["MIX"][0:512, :].rearrange("(h p) t -> p h t", p=128)[:, :, t0:t0 + 128], v4(mxs), r=[mxs], w=[DB(("MIX", sn))])

        for sn, T in seqs:
            NB = T // 128
            for d in range(2):
                Rd[d]["rec_done"] = 0
                P.op("pool", "memset", ap=Rd[d]["S4"].ap, constant=0.0, w=[Rd[d]["S4"]])
                P.op("pool", "memset", ap=Rd[d]["S4b"].ap, constant=0.0, w=[Rd[d]["S4b"]])
            active = []
            pending = []
            for st_ in range(NB):
                pending.append((0, st_, st_))
                pending.append((1, NB - 1 - st_, st_))
            STAG = 2
            while pending or active:
                if pending and len(active) < 2 * NU and (not active or active[-1][1] >= STAG):
                    d_, b_, st_ = pending.pop(0)
                    active.append([unit(sn, T, d_, b_, st_, NB), 0])
                for a_ in list(active):
                    try:
                        next(a_[0])
                        a_[1] += 1
                    except StopIteration:
                        active.remove(a_)
        P.barrier()
        P.emit()
```
